# Optimizing a Trainium2 kernel written in Bass

```python
import math
import jax, jax.numpy as jnp
from jax import lax
import numpy as np

D_MODEL = 1024
BATCH = 32
SEQ = 256
DEPTH = 2
DEC_BATCH = 4
DEC_SEQ = 1024
PAST_LEN = 256

GRID_W = 64
CHUNK = 128
MLSTM_HEADS = 4
MLSTM_HD = D_MODEL // (2 * MLSTM_HEADS)
MLSTM_W = MLSTM_HEADS * MLSTM_HD
N_GATE_COLS = 4 * MLSTM_HEADS
CONV_W = D_MODEL // 2
CONV_K = 31
CONV_PAD = CONV_K // 2
RET_HEADS = 4
RET_HD = D_MODEL // (2 * RET_HEADS)
RET_W = RET_HEADS * RET_HD
N_BRANCH = 3
FFN_HIDDEN = -(-8 * D_MODEL // (3 * 256)) * 256
ROPE_BASE = 10000.0
LN_EPS = 1e-5
DEEPNORM_ALPHA = (2.0 * DEPTH) ** 0.25
DEEPNORM_BETA = (8.0 * DEPTH) ** -0.25
IN_SIZES = (MLSTM_W, MLSTM_W, MLSTM_W, MLSTM_W, N_GATE_COLS, CONV_W, CONV_W,
            RET_W, RET_W, RET_W, RET_W, N_BRANCH * D_MODEL)
IN_WIDTH = sum(IN_SIZES)

kernel_name = 'hybrid_mlstm_conformer_retention_diffusion_step'


def layer_norm(x, w, b):
    xf = x.astype(jnp.float32)
    mu = jnp.mean(xf, -1, keepdims=True)
    var = jnp.mean(jnp.square(xf - mu), -1, keepdims=True)
    return ((xf - mu) * lax.rsqrt(var + LN_EPS) * w.astype(jnp.float32) + b.astype(jnp.float32)).astype(x.dtype)


def head_norm(y, w):
    mu = jnp.mean(y, -1, keepdims=True)
    var = jnp.mean(jnp.square(y - mu), -1, keepdims=True)
    yn = (y - mu) * lax.rsqrt(var + LN_EPS)
    return yn.reshape(y.shape[0], y.shape[1], -1) * w.astype(jnp.float32)


def _flip(a):
    return jnp.flip(a, axis=1)


def _to_chunks(a):
    b, t = a.shape[0], a.shape[1]
    return jnp.moveaxis(a.reshape((b, t // CHUNK, CHUNK) + a.shape[2:]), 1, 0)


def _from_chunks(a):
    a = jnp.moveaxis(a, 0, 1)
    return a.reshape((a.shape[0], a.shape[1] * a.shape[2]) + a.shape[3:])


def rope_2d(x):
    t = x.shape[1]
    n_rows = t // GRID_W
    rows, cols = jnp.meshgrid(jnp.arange(n_rows, dtype=jnp.float32),
                              jnp.arange(GRID_W, dtype=jnp.float32), indexing='ij')
    n_pairs = x.shape[-1] // 4
    freqs = ROPE_BASE ** (-jnp.arange(n_pairs, dtype=jnp.float32) / n_pairs)
    ang = jnp.concatenate([rows.reshape(-1)[:, None] * freqs, cols.reshape(-1)[:, None] * freqs], -1)
    cos = jnp.cos(ang)[None, :, None, :]
    sin = jnp.sin(ang)[None, :, None, :]
    x1, x2 = x[..., 0::2], x[..., 1::2]
    return jnp.stack([x1 * cos - x2 * sin, x1 * sin + x2 * cos], -1).reshape(x.shape)


def mlstm_chunked(q, k, v, ig, lf, state):
    causal = jnp.tril(jnp.ones((CHUNK, CHUNK), dtype=bool))

    def step(carry, xs):
        c0, n0, m0 = carry
        qc, kc, vc, igc, lfc = xs
        igc = jnp.transpose(igc, (0, 2, 1))
        b = jnp.cumsum(jnp.transpose(lfc, (0, 2, 1)), axis=-1)
        log_d = jnp.where(causal, b[..., :, None] - b[..., None, :] + igc[..., None, :], -jnp.inf)
        inter = b + m0[..., None]
        m = jnp.maximum(inter, jnp.max(log_d, -1))
        dmat = jnp.exp(log_d - m[..., None])
        s = jnp.einsum('bihd,bjhd->bhij', qc, kc) * dmat
        w_inter = jnp.exp(inter - m)
        num = jnp.einsum('bhij,bjhe->bhie', s, vc) + jnp.einsum('bihd,bhde->bhie', qc, c0) * w_inter[..., None]
        den = jnp.sum(s, -1) + jnp.einsum('bihd,bhd->bhi', qc, n0) * w_inter
        h = num / jnp.maximum(jnp.abs(den), jnp.exp(-m))[..., None]
        b_last = b[..., -1]
        log_w = b_last[..., None] - b + igc
        m_new = jnp.maximum(b_last + m0, jnp.max(log_w, -1))
        w = jnp.exp(log_w - m_new[..., None])
        carry_scale = jnp.exp(b_last + m0 - m_new)
        c_new = c0 * carry_scale[..., None, None] + jnp.einsum('bhj,bjhd,bjhe->bhde', w, kc, vc)
        n_new = n0 * carry_scale[..., None] + jnp.einsum('bhj,bjhd->bhd', w, kc)
        return (c_new, n_new, m_new), jnp.transpose(h, (0, 2, 1, 3))

    final, hs = lax.scan(step, state, (_to_chunks(q), _to_chunks(k), _to_chunks(v), _to_chunks(ig), _to_chunks(lf)))
    return _from_chunks(hs), final


def retention_chunked(q, k, v, log_gamma, s0):
    i = jnp.arange(CHUNK, dtype=jnp.float32)
    diff = i[:, None] - i[None, :]
    decay = jnp.where(diff >= 0, jnp.exp(log_gamma[:, None, None] * jnp.maximum(diff, 0.0)), 0.0)
    xi = jnp.exp(log_gamma[None, :] * (i[:, None] + 1.0))
    zeta = jnp.exp(log_gamma[None, :] * (CHUNK - 1.0 - i[:, None]))
    chunk_decay = jnp.exp(log_gamma * CHUNK)

    def step(s, xs):
        qc, kc, vc = xs
        scores = jnp.einsum('bihd,bjhd->bhij', qc, kc) * decay
        y = jnp.einsum('bhij,bjhe->bihe', scores, vc) + jnp.einsum('bihd,bhde->bihe', qc, s) * xi[None, :, :, None]
        s = s * chunk_decay[None, :, None, None] + jnp.einsum('bjhd,bjhe->bhde', kc * zeta[None, :, :, None], vc)
        return s, y

    s_final, ys = lax.scan(step, s0, (_to_chunks(q), _to_chunks(k), _to_chunks(v)))
    return _from_chunks(ys), s_final


def mixing_sublayer(h, p, init, latent):
    f32 = jnp.float32
    bsz, t, _ = h.shape
    offsets = np.cumsum(IN_SIZES)[:-1].tolist()
    (mq, mk, mv, mo, mg, ca, cg, rq, rk, rv, rg, gm) = jnp.split(h @ p['in_w'], offsets, axis=-1)

    def heads(a, n):
        return a.reshape(bsz, t, n, -1).astype(f32)

    c0, n0, m0, s0 = (s.astype(f32) for s in init)

    q = heads(mq, MLSTM_HEADS)
    k = heads(mk, MLSTM_HEADS) * MLSTM_HD ** -0.5
    v = heads(mv, MLSTM_HEADS)
    gates = mg.reshape(bsz, t, 4, MLSTM_HEADS).astype(f32) + p['mlstm_gate_b'].astype(f32)
    ig_f, lf_f = gates[:, :, 0], jax.nn.log_sigmoid(gates[:, :, 1])
    ig_b, lf_b = gates[:, :, 2], jax.nn.log_sigmoid(gates[:, :, 3])
    h_f, (cf, nf, mf) = mlstm_chunked(q, k, v, ig_f, lf_f, (c0[:, 0], n0[:, 0], m0[:, 0]))
    h_b, (cb, nb, mb) = mlstm_chunked(_flip(q), _flip(k), _flip(v), _flip(ig_b), _flip(lf_b),
                                      (c0[:, 1], n0[:, 1], m0[:, 1]))
    h_a = head_norm(h_f + _flip(h_b), p['mlstm_norm_w']).astype(h.dtype) * jax.nn.sigmoid(mo)
    y_a = h_a @ p['mlstm_out_w']

    u = ca * jax.nn.sigmoid(cg)
    u = lax.conv_general_dilated(u, p['conv_w'][:, None, :], window_strides=(1,), padding=[(CONV_PAD, CONV_PAD)],
                                 dimension_numbers=('NWC', 'WIO', 'NWC'), feature_group_count=CONV_W) + p['conv_b']
    u = jax.nn.silu(layer_norm(u, p['conv_ln_w'], p['conv_ln_b']))
    y_b = u @ p['conv_out_w']

    rq_h = heads(rq, RET_HEADS)
    rk_h = heads(rk, RET_HEADS)
    if latent:
        rq_h, rk_h = rope_2d(rq_h), rope_2d(rk_h)
    rk_h = rk_h * RET_HD ** -0.5
    rv_h = heads(rv, RET_HEADS)
    log_gamma = jax.nn.log_sigmoid(p['ret_decay'].astype(f32))
    y_f, sf = retention_chunked(rq_h, rk_h, rv_h, log_gamma[0], s0[:, 0])
    y_bw, sb = retention_chunked(_flip(rq_h), _flip(rk_h), _flip(rv_h), log_gamma[1], s0[:, 1])
    h_c = head_norm(y_f + _flip(y_bw), p['ret_norm_w']).astype(h.dtype) * jax.nn.silu(rg)
    y_c = h_c @ p['ret_out_w']

    g = jax.nn.sigmoid(gm).reshape(bsz, t, N_BRANCH, D_MODEL)
    merged = g[:, :, 0] * y_a + g[:, :, 1] * y_b + g[:, :, 2] * y_c
    out = merged @ p['out_w']
    new_state = (jnp.stack([cf, cb], 1), jnp.stack([nf, nb], 1), jnp.stack([mf, mb], 1), jnp.stack([sf, sb], 1))
    return out, new_state


def trunk_layer(x, mod, p, init, latent):
    shift1, scale1, gate1, shift2, scale2, gate2 = jnp.split(mod[:, None, :], 6, axis=-1)
    h = x * (1.0 + scale1) + shift1
    mix, new_state = mixing_sublayer(h, p, init, latent)
    x = layer_norm(DEEPNORM_ALPHA * x + gate1 * mix, p['ln1_w'], p['ln1_b'])
    h = x * (1.0 + scale2) + shift2
    a, gt = jnp.split(h @ p['ffn_w13'], 2, axis=-1)
    ff = (jax.nn.silu(gt) * a) @ p['ffn_w2']
    x = layer_norm(DEEPNORM_ALPHA * x + gate2 * ff, p['ln2_w'], p['ln2_b'])
    return x, new_state


def setup_inputs(seed: int = 0) -> dict:
    key = jax.random.key(seed)
    ks = iter(jax.random.split(key, 40))

    def nrm(shape, scale):
        return jax.random.normal(next(ks), shape, jnp.float32) * scale

    fl = np.linspace(3.0, 6.0, MLSTM_HEADS).astype(np.float32)
    zh = np.zeros((MLSTM_HEADS,), np.float32)
    gate_base = jnp.asarray(np.stack([zh, fl, zh, fl], 0))
    gam = np.log(2.0 ** (5.0 + np.arange(RET_HEADS)) - 1.0).astype(np.float32)
    dec_base = jnp.asarray(np.stack([gam, gam], 0))
    return {
        'x_prompt': nrm((BATCH, SEQ, D_MODEL), 1.0),
        'x_sample': nrm((DEC_BATCH, DEC_SEQ, D_MODEL), 1.0),
        'state_mlstm_C': nrm((DEC_BATCH, DEPTH, 2, MLSTM_HEADS, MLSTM_HD, MLSTM_HD), 0.3),
        'state_mlstm_n': nrm((DEC_BATCH, DEPTH, 2, MLSTM_HEADS, MLSTM_HD), 0.3),
        'state_mlstm_m': nrm((DEC_BATCH, DEPTH, 2, MLSTM_HEADS), 1.0),
        'state_ret_S': nrm((DEC_BATCH, DEPTH, 2, RET_HEADS, RET_HD, RET_HD), 0.5),
        'c': nrm((DEC_BATCH, D_MODEL), 1.0),
        'c_ctx': nrm((D_MODEL,), 1.0),
        'ada_w': nrm((DEPTH, D_MODEL, 6 * D_MODEL), 0.5 * D_MODEL ** -0.5),
        'ada_b': nrm((DEPTH, 6 * D_MODEL), 0.02),
        'in_w': nrm((DEPTH, D_MODEL, IN_WIDTH), D_MODEL ** -0.5),
        'mlstm_gate_b': gate_base[None] + nrm((DEPTH, 4, MLSTM_HEADS), 0.1),
        'mlstm_norm_w': 1.0 + nrm((DEPTH, MLSTM_W), 0.05),
        'mlstm_out_w': nrm((DEPTH, MLSTM_W, D_MODEL), MLSTM_W ** -0.5),
        'conv_w': nrm((DEPTH, CONV_K, CONV_W), CONV_K ** -0.5),
        'conv_b': nrm((DEPTH, CONV_W), 0.02),
        'conv_ln_w': 1.0 + nrm((DEPTH, CONV_W), 0.05),
        'conv_ln_b': nrm((DEPTH, CONV_W), 0.02),
        'conv_out_w': nrm((DEPTH, CONV_W, D_MODEL), CONV_W ** -0.5),
        'ret_decay': dec_base[None] + nrm((DEPTH, 2, RET_HEADS), 0.05),
        'ret_norm_w': 1.0 + nrm((DEPTH, RET_W), 0.05),
        'ret_out_w': nrm((DEPTH, RET_W, D_MODEL), RET_W ** -0.5),
        'out_w': nrm((DEPTH, D_MODEL, D_MODEL), DEEPNORM_BETA * D_MODEL ** -0.5),
        'ln1_w': 1.0 + nrm((DEPTH, D_MODEL), 0.05),
        'ln1_b': nrm((DEPTH, D_MODEL), 0.02),
        'ln2_w': 1.0 + nrm((DEPTH, D_MODEL), 0.05),
        'ln2_b': nrm((DEPTH, D_MODEL), 0.02),
        'ffn_w13': nrm((DEPTH, D_MODEL, 2 * FFN_HIDDEN), D_MODEL ** -0.5),
        'ffn_w2': nrm((DEPTH, FFN_HIDDEN, D_MODEL), DEEPNORM_BETA * FFN_HIDDEN ** -0.5),
    }


def reference(x_prompt, x_sample, state_mlstm_C, state_mlstm_n, state_mlstm_m, state_ret_S, c, c_ctx,
              ada_w, ada_b, in_w, mlstm_gate_b, mlstm_norm_w, mlstm_out_w, conv_w, conv_b, conv_ln_w, conv_ln_b,
              conv_out_w, ret_decay, ret_norm_w, ret_out_w, out_w, ln1_w, ln1_b, ln2_w, ln2_b, ffn_w13, ffn_w2):
    f32 = jnp.float32
    bsz = x_prompt.shape[0]
    zero_state = (jnp.zeros((bsz, 2, MLSTM_HEADS, MLSTM_HD, MLSTM_HD), f32),
                  jnp.zeros((bsz, 2, MLSTM_HEADS, MLSTM_HD), f32),
                  jnp.zeros((bsz, 2, MLSTM_HEADS), f32),
                  jnp.zeros((bsz, 2, RET_HEADS, RET_HD, RET_HD), f32))
    y_prompt, y_sample = x_prompt, x_sample
    new_c, new_n, new_m, new_s = [], [], [], []
    for l in range(DEPTH):
        p = {'in_w': in_w[l], 'mlstm_gate_b': mlstm_gate_b[l], 'mlstm_norm_w': mlstm_norm_w[l],
             'mlstm_out_w': mlstm_out_w[l], 'conv_w': conv_w[l], 'conv_b': conv_b[l], 'conv_ln_w': conv_ln_w[l],
             'conv_ln_b': conv_ln_b[l], 'conv_out_w': conv_out_w[l], 'ret_decay': ret_decay[l],
             'ret_norm_w': ret_norm_w[l], 'ret_out_w': ret_out_w[l], 'out_w': out_w[l], 'ln1_w': ln1_w[l],
             'ln1_b': ln1_b[l], 'ln2_w': ln2_w[l], 'ln2_b': ln2_b[l], 'ffn_w13': ffn_w13[l], 'ffn_w2': ffn_w2[l]}
        mod_ctx = jax.nn.silu(c_ctx)[None] @ ada_w[l] + ada_b[l]
        mod_lat = jax.nn.silu(c) @ ada_w[l] + ada_b[l]
        y_prompt, st = trunk_layer(y_prompt, mod_ctx, p, zero_state, False)
        new_c.append(st[0])
        new_n.append(st[1])
        new_m.append(st[2])
        new_s.append(st[3])
        cache = (state_mlstm_C[:, l], state_mlstm_n[:, l], state_mlstm_m[:, l], state_ret_S[:, l])
        y_sample, _ = trunk_layer(y_sample, mod_lat, p, cache, True)
    odt = x_prompt.dtype
    return (y_prompt, y_sample, jnp.stack(new_c, 1).astype(odt), jnp.stack(new_n, 1).astype(odt),
            jnp.stack(new_m, 1).astype(odt), jnp.stack(new_s, 1).astype(odt))
```

```python
import math
import itertools
import numpy as np
import concourse.bass as bass
import concourse.mybir as mybir
from concourse.bass_utils import run_bass_kernel_spmd
from concourse.ap import AP

F32 = mybir.dt.float32
BF16 = mybir.dt.bfloat16
AF = mybir.ActivationFunctionType
ALU = mybir.AluOpType
AX = mybir.AxisListType

D = 1024
L = 2
T = 1536
NCH = 12
NSEG = 6
H = 4
HD = 128
FF = 2816
NFT = 22
CONV_K = 31
EPS = 1e-5
ALPHA = (2.0 * L) ** 0.25
KS = HD ** -0.5
LNKS = math.log(KS)
NEG = -30000.0
NCOLS = 9288
OFF_GATE = 2048
OFF_CA = 2120
OFF_CG = 2632
OFF_RET = 3144
OFF_RVG = 5192
OFF_GM = 6216

DEBUG = {}
STOP = None


class _Stop(Exception):
    pass


def phase_end(name):
    if STOP == name:
        raise _Stop()


def rev(ap):
    a = [list(x) for x in ap.ap]
    step, n = a[-1]
    off = ap.offset + step * (n - 1)
    a[-1] = [-step, n]
    return AP(ap.tensor, off, a)


import types


def _snap(th):
    cl = th.__closure__
    if not cl:
        return th
    cells = []
    for c in cl:
        try:
            cells.append(types.CellType(c.cell_contents))
        except ValueError:
            cells.append(c)
    return types.FunctionType(th.__code__, th.__globals__, th.__name__, th.__defaults__, tuple(cells))


class TT:
    __slots__ = ("name", "w", "r")

    def __init__(self, name=""):
        self.name = name
        self.w = None
        self.r = {}


class Ctx:
    ENG = ["pe", "act", "dve", "pool", "sp"]

    def __init__(self, nc, sems):
        self.nc = nc
        self.sems = sems
        self.prog = {e: [] for e in self.ENG}
        self.cnt = {e: 0 for e in self.ENG}
        self.seen = {e: {} for e in self.ENG}
        self.dkeys = [k for k in sems if k[0] == "d" and k[1:].isdigit()]
        self.wkeys = [k for k in sems if k[0] == "w" and k[1:].isdigit()]
        self.dtot = {k: 0 for k in sems}
        self.dn = 0
        self.wn = 0

    def _wait(self, eng, key, val):
        if val <= 0 or self.seen[eng].get(key, 0) >= val:
            return
        self.seen[eng][key] = val
        sem = self.sems[key]
        self.prog[eng].append(lambda e, sem=sem, val=val: e.wait_ge(sem, val))

    def _deps(self, eng, reads, writes):
        for t in reads:
            if t.w is not None:
                self._wait_dep(eng, t.w)
        for t in writes:
            if t.w is not None:
                self._wait_dep(eng, t.w)
            for k, v in t.r.items():
                self._wait_dep(eng, (k, v))

    def _wait_dep(self, eng, dep):
        k, v = dep
        if k == "pe" and eng == "pe":
            return
        self._wait(eng, k, v)

    def run(self, eng, thunks, reads=(), writes=()):
        if not isinstance(thunks, (list, tuple)):
            thunks = [thunks]
        thunks = [_snap(t) for t in thunks]
        self._deps(eng, reads, writes)
        sem = self.sems[eng]
        n = len(thunks)
        for i, th in enumerate(thunks):
            if i == n - 1:
                self.prog[eng].append(lambda e, th=th, sem=sem: th(e).then_inc(sem, 1))
            else:
                self.prog[eng].append(th)
        self.cnt[eng] += 1
        c = self.cnt[eng]
        for t in reads:
            t.r[eng] = c
        for t in writes:
            t.w = (eng, c)
            t.r = {}

    def dma(self, q, out, in_, reads=(), writes=()):
        if q == "pool":
            key = self.wkeys[self.wn % len(self.wkeys)]
            self.wn += 1
        else:
            key = self.dkeys[self.dn % len(self.dkeys)]
            self.dn += 1
        prev = self.dtot[key]
        self._wait(q, key, prev)
        self._deps(q, reads, writes)
        new = prev + 16
        self.dtot[key] = new
        sem = self.sems[key]
        self.prog[q].append(lambda e, out=out, in_=in_, sem=sem: e.dma_start(out=out, in_=in_).then_inc(sem, 16))
        for t in reads:
            t.r[key] = new
        for t in writes:
            t.w = (key, new)
            t.r = {}

    def barrier(self):
        engs = ["pe", "act", "dve", "sp"]
        for e in engs:
            for f in ["pe", "act", "dve"]:
                if f != e or e != "pe":
                    self._wait(e, f, self.cnt[f])
            for k in self.dkeys:
                self._wait(e, k, self.dtot[k])

    def final(self):
        for k in self.dkeys:
            self._wait("sp", k, self.dtot[k])
        for f in ["pe", "act", "dve"]:
            self._wait("sp", f, self.cnt[f])


def build(dbg=False):
    nc = bass.Bass("TRN2", target_bir_lowering=False)
    dram = {}

    def din(name, shape, dt=F32):
        dram[name] = nc.dram_tensor(name, list(shape), dt, kind="ExternalInput").ap()
        return dram[name]

    def dout(name, shape, dt=F32):
        dram[name] = nc.dram_tensor(name, list(shape), dt, kind="ExternalOutput").ap()
        return dram[name]

    xT_d = din("xT", [128, 8, T])
    cv_d = din("cv", [128, 8, 2])
    chain_d = din("chain", [128, 1])
    Cinit_d = din("Cinit", [L, 2, H, 128, 129])
    minit_d = din("minit", [L, 36, 1])
    Sinit_d = din("Sinit", [L, 2, H, 128, 128])
    ropeCS_d = din("ropeCS", [128, 1024])
    ropeSN_d = din("ropeSN", [128, 1024])
    cst_d = din("cst", [128, 1024])
    sel_d = din("sel", [36, 8, 128])
    ada_w_d = din("ada_w", [L, D, 6 * D])
    ada_b_d = din("ada_bT", [L, 128, 48])
    in_w_d = din("in_wp", [L, D, NCOLS])
    gb_d = din("gb", [L, 36, 2])
    mnw_d = din("mnw", [L, 1, 512])
    rnw_d = din("rnw", [L, 1, 512])
    mow_d = din("mlstm_out_w", [L, 512, D])
    cow_d = din("conv_out_w", [L, 512, D])
    row_d = din("ret_out_w", [L, 512, D])
    convp_d = din("convp", [L, 128, 4, 34])
    rdec_d = din("rdec", [L, 1, 8])
    outw_d = din("out_w", [L, D, D])
    lnp_d = din("lnp", [L, 128, 4, 8])
    w13_d = din("ffn_w13", [L, D, 2 * FF])
    w2_d = din("ffn_w2", [L, FF, D])

    yT_d = dout("yT", [128, 8, T])
    Cfin_d = dout("Cfin", [L, NSEG, 2, H, 128, 129])
    mfin_d = dout("mfin", [L, 36, 12])
    Sfin_d = dout("Sfin", [L, NSEG, 2, H, 128, 128])
    dbg_d = {}

    import contextlib
    es = contextlib.ExitStack()
    with es:
        def sb(name, shape, dt=F32):
            return es.enter_context(nc.sbuf_tensor("s_" + name, list(shape), dt))[:]

        x = sb("x", [128, 8, T])
        hT = sb("hT", [128, 8, T], BF16)
        NW = 3
        Wr = [sb("wr%d" % i, [128, 4096], BF16) for i in range(NW)]
        Wr_T = [TT("wr%d" % i) for i in range(NW)]
        cst = sb("cst", [128, 1024])
        sel = sb("sel", [36, 8, 128])
        identB = sb("identB", [128, 128], BF16)
        onesB = sb("onesB", [128, 128], BF16)
        chain = sb("chain", [128, 1])
        cvs = sb("cvs", [128, 8, 2], BF16)
        cvf = sb("cvf", [128, 8, 2])
        modTs = [sb("modT%d" % i, [128, 48, 2]) for i in range(L)]
        sc1ps = [sb("sc1p%d" % i, [128, 8, 2]) for i in range(L)]
        sc2ps = [sb("sc2p%d" % i, [128, 8, 2]) for i in range(L)]
        adabs = [sb("adab%d" % i, [128, 48]) for i in range(L)]
        M_T = [TT("mod%d" % i) for i in range(L)]
        gb = sb("gb", [36, 2])
        ngb = sb("ngb", [36, 1])
        minit = sb("minit", [36, 1])
        mnw = sb("mnw", [128, 512])
        rnw = sb("rnw", [128, 512])
        convp = sb("convp", [128, 4, 34])
        lnp = sb("lnp", [128, 4, 8])
        rdec = sb("rdec", [128, 8])
        lg = sb("lg", [128, 8])
        rcol = sb("rcol", [128, 8, 4])
        decT = sb("decT", [128, 8, 128])
        ones36 = sb("ones36", [36, 128])
        zeros36 = sb("zeros36", [36, 128])
        AR = 23168
        arena = sb("arena", [128, AR])
        ps = [es.enter_context(nc.psum_tensor("ps%d" % i, [128, 512], F32))[:] for i in range(8)]
        ps_T = [TT("ps%d" % i) for i in range(8)]

        keys = ["pe", "act", "dve", "pool", "sp"] + ["d%d" % i for i in range(8)] + ["w%d" % i for i in range(6)]
        sems = {k: es.enter_context(nc.semaphore(k)) for k in keys}
        cx = Ctx(nc, sems)
        G = TT("globals")
        x_T = TT("x")
        hT_T = TT("hT")

        identF = cst[:, 0:128]
        MASK = [cst[:, 128:256], cst[:, 256:384]]
        DIFF = [cst[:, 384:512], cst[:, 512:640]]
        M01 = [cst[:, 640:768], cst[:, 768:896]]
        POS = cst[:, 896:904]

        pstate = {"i": 0}

        def psum():
            i = pstate["i"] % 8
            pstate["i"] += 1
            return ps[i], ps_T[i]

        wstate = {"i": 0}

        def wload(src, kc, ncol):
            i = wstate["i"] % NW
            wstate["i"] += 1
            dst = Wr[i][:, 0:kc * ncol].rearrange("p (k n) -> p k n", n=ncol)
            cx.dma("pool", dst, src.rearrange("(k p) n -> p k n", p=128), writes=[Wr_T[i]])
            return dst, Wr_T[i]

        ast = {"o": 0}

        def aset(o):
            cx.barrier()
            ast["o"] = o

        def af32(n, shape=None):
            o = ast["o"]
            ast["o"] += n
            assert ast["o"] <= AR, ast["o"]
            v = arena[:, o:o + n]
            return v

        def abf(n):
            n2 = (n + 1) // 2
            return af32(n2).bitcast(BF16)

        def dbg_out(name, ap, shape, reads):
            if not dbg:
                return
            d = dout("dbg_" + name, shape, ap.dtype)
            DEBUG[name] = shape
            cx.dma("sp", d, ap, reads=reads)

        mm = lambda out, lhsT, rhs, st, sp: (lambda e: e.matmul(out, lhsT=lhsT, rhs=rhs, start=st, stop=sp))

        for kc in range(8):
            cx.dma("sp", x[:, kc, :], xT_d[:, kc, :], writes=[x_T])
        for dst, src in ((cst, cst_d), (sel, sel_d), (chain, chain_d), (cvf, cv_d)):
            cx.dma("sp", dst, src, writes=[G])
        cx.run("dve", [lambda e: e.memset(ones36, 1.0), lambda e: e.memset(zeros36, 0.0),
                       lambda e: e.memset(onesB, 1.0),
                       lambda e: e.tensor_copy(out=identB, in_=identF)],
               reads=[G], writes=[G])
        cx.run("act", [lambda e: e.activation(out=cvs, in_=cvf, func=AF.Silu)], reads=[G], writes=[G])
        for i in range(L):
            cx.dma("sp", adabs[i], ada_b_d[i], writes=[M_T[i]])

        def mod_groups(ll, gs):
            mT = modTs[ll]
            for g in gs:
                wv, wT = wload(ada_w_d[ll][:, g * 512:(g + 1) * 512], 8, 512)
                mb, mbT = psum()
                for jj in range(4):
                    cx.run("pe", [mm(mb[:, 2 * jj:2 * jj + 2], wv[:, kc, jj * 128:(jj + 1) * 128], cvs[:, kc, :], kc == 0, kc == 7) for kc in range(8)],
                           reads=[wT, G], writes=[mbT])
                j0 = g * 4
                cx.run("act", [lambda e, mb=mb, j0=j0: e.activation(out=mT[:, j0:j0 + 4, :].rearrange("p j v -> p (j v)"), in_=mb[:, 0:8], func=AF.Copy)],
                       reads=[mbT], writes=[M_T[ll]])
                cx.run("dve", [lambda e, j0=j0: e.tensor_tensor(out=mT[:, j0:j0 + 4, :], in0=mT[:, j0:j0 + 4, :],
                                                                in1=adabs[ll][:, j0:j0 + 4].unsqueeze(2).to_broadcast([128, 4, 2]), op=ALU.add)],
                       reads=[M_T[ll]], writes=[M_T[ll]])
                if g in (2, 3):
                    o = (g - 2) * 4
                    cx.run("dve", [lambda e, j0=j0, o=o: e.tensor_scalar(out=sc1ps[ll][:, o:o + 4, :], in0=mT[:, j0:j0 + 4, :], scalar1=1.0, scalar2=None, op0=ALU.add)],
                           reads=[M_T[ll]], writes=[M_T[ll]])
                if g in (8, 9):
                    o = (g - 8) * 4
                    cx.run("dve", [lambda e, j0=j0, o=o: e.tensor_scalar(out=sc2ps[ll][:, o:o + 4, :], in0=mT[:, j0:j0 + 4, :], scalar1=1.0, scalar2=None, op0=ALU.add)],
                           reads=[M_T[ll]], writes=[M_T[ll]])

        def layer_norm_fm(which, blocks, xb, xq, lnt):
            lw = lnp[:, 2 * which, :]
            lb = lnp[:, 2 * which + 1, :]
            xb_T, xq_T, lnt_T = TT("xb"), TT("xq"), TT("lnt")
            for (a_, b_) in blocks:
                sl = slice(a_, b_)
                cx.run("act", [lambda e, kc=kc: e.activation(out=xb[:, kc, :], in_=x[:, kc, sl], func=AF.Copy) for kc in range(8)],
                       reads=[x_T], writes=[xb_T])
                cx.run("act", [lambda e, kc=kc: e.activation(out=xq[:, kc, :], in_=x[:, kc, sl], func=AF.Square) for kc in range(8)],
                       reads=[x_T], writes=[xq_T])
                b1, b1T = psum()
                cx.run("pe", [mm(b1, onesB, xb[:, kc, :], kc == 0, kc == 7) for kc in range(8)], reads=[xb_T], writes=[b1T])
                b2, b2T = psum()
                cx.run("pe", [mm(b2, onesB, xq[:, kc, :], kc == 0, kc == 7) for kc in range(8)], reads=[xq_T], writes=[b2T])
                mean, msq, rstd = lnt
                cx.run("act", [lambda e: e.activation(out=mean, in_=b1, func=AF.Copy, scale=1.0 / D)], reads=[b1T], writes=[lnt_T])
                cx.run("dve", [lambda e: e.tensor_tensor(out=msq, in0=mean, in1=mean, op=ALU.mult)], reads=[lnt_T], writes=[lnt_T])
                cx.run("dve", [lambda e: e.scalar_tensor_tensor(out=msq, in0=b2, scalar=1.0 / D, in1=msq, op0=ALU.mult, op1=ALU.subtract)],
                       reads=[b2T, lnt_T], writes=[lnt_T])
                cx.run("dve", [lambda e: e.tensor_scalar(out=msq, in0=msq, scalar1=EPS, scalar2=None, op0=ALU.add)], reads=[lnt_T], writes=[lnt_T])
                cx.run("act", [lambda e: e.activation(out=rstd, in_=msq, func=AF.Sqrt)], reads=[lnt_T], writes=[lnt_T])
                cx.run("dve", [lambda e: e.reciprocal(out=rstd, in_=rstd)], reads=[lnt_T], writes=[lnt_T])
                for kc in range(8):
                    cx.run("dve", [lambda e, kc=kc: e.tensor_tensor(out=x[:, kc, sl], in0=x[:, kc, sl], in1=mean, op=ALU.subtract)],
                           reads=[x_T, lnt_T], writes=[x_T])
                    cx.run("dve", [lambda e, kc=kc: e.tensor_tensor(out=x[:, kc, sl], in0=x[:, kc, sl], in1=rstd, op=ALU.mult)],
                           reads=[x_T, lnt_T], writes=[x_T])
                    cx.run("act", [lambda e, kc=kc: e.activation(out=x[:, kc, sl], in_=x[:, kc, sl], func=AF.Identity,
                                                                 scale=lw[:, kc:kc + 1], bias=lb[:, kc:kc + 1])],
                           reads=[x_T, G], writes=[x_T])

        lnks = sb("lnks", [128, 1])
        cx.run("dve", [lambda e: e.memset(lnks, LNKS)], writes=[G])

        def ln_stats(b1, b1T, b2, b2T, mean, msq, rstd, st_T, n):
            cx.run("act", [lambda e: e.activation(out=mean, in_=b1, func=AF.Copy, scale=1.0 / n)], reads=[b1T], writes=[st_T])
            cx.run("dve", [lambda e: e.tensor_tensor(out=msq, in0=mean, in1=mean, op=ALU.mult)], reads=[st_T], writes=[st_T])
            cx.run("dve", [lambda e: e.scalar_tensor_tensor(out=msq, in0=b2, scalar=1.0 / n, in1=msq, op0=ALU.mult, op1=ALU.subtract)],
                   reads=[b2T, st_T], writes=[st_T])
            cx.run("dve", [lambda e: e.tensor_scalar(out=msq, in0=msq, scalar1=EPS, scalar2=None, op0=ALU.add)], reads=[st_T], writes=[st_T])
            cx.run("act", [lambda e: e.activation(out=rstd, in_=msq, func=AF.Sqrt)], reads=[st_T], writes=[st_T])
            cx.run("dve", [lambda e: e.reciprocal(out=rstd, in_=rstd)], reads=[st_T], writes=[st_T])

        def head_norm_out(src3, src_T, normw, h, gate3, gate_T, dstT, dst_T, st6, hn, hn_T):
            for c in range(12):
                cs = slice(c * 128, (c + 1) * 128)
                cx.run("dve", [lambda e, c=c: e.bn_stats(out=st6[:, 0:6], in_=src3[:, c, :])], reads=[src_T[c]], writes=[hn_T])
                cx.run("dve", [lambda e: e.bn_aggr(out=st6[:, 6:8], in_=st6[:, 0:6])], reads=[hn_T], writes=[hn_T])
                cx.run("dve", [lambda e: e.tensor_scalar(out=st6[:, 7:8], in0=st6[:, 7:8], scalar1=EPS, scalar2=None, op0=ALU.add)], reads=[hn_T], writes=[hn_T])
                cx.run("act", [lambda e: e.activation(out=st6[:, 7:8], in_=st6[:, 7:8], func=AF.Sqrt)], reads=[hn_T], writes=[hn_T])
                cx.run("dve", [lambda e: e.reciprocal(out=st6[:, 7:8], in_=st6[:, 7:8])], reads=[hn_T], writes=[hn_T])
                cx.run("dve", [lambda e, c=c: e.tensor_scalar(out=src3[:, c, :], in0=src3[:, c, :], scalar1=st6[:, 6:7], scalar2=st6[:, 7:8],
                                                              op0=ALU.subtract, op1=ALU.mult)], reads=[hn_T, src_T[c]], writes=[src_T[c]])
                cx.run("dve", [lambda e, c=c: e.tensor_tensor(out=src3[:, c, :], in0=src3[:, c, :], in1=normw[:, h * 128:(h + 1) * 128], op=ALU.mult)],
                       reads=[src_T[c], G], writes=[src_T[c]])
                cx.run("dve", [lambda e, c=c: e.tensor_tensor(out=hn, in0=src3[:, c, :], in1=gate3[:, c, :], op=ALU.mult)],
                       reads=[src_T[c], gate_T, hn_T], writes=[hn_T])
                bk, bkT = psum()
                bkb = bk.bitcast(BF16)
                cx.run("pe", [lambda e, bkb=bkb: e.transpose(out=bkb[:, 0:128], in_=hn, identity=identB)], reads=[hn_T, G], writes=[bkT])
                cx.run("act", [lambda e, bkb=bkb, cs=cs: e.activation(out=dstT[:, h, cs], in_=bkb[:, 0:128], func=AF.Copy)], reads=[bkT], writes=[dst_T])

        def layers():
          for l in range(L):
              phase_end('pro%d' % l)
              aset(0)
              for dst, src in ((gb, gb_d[l]), (minit, minit_d[l]), (convp, convp_d[l]), (lnp, lnp_d[l]),
                               (mnw, mnw_d[l].partition_broadcast(128)), (rnw, rnw_d[l].partition_broadcast(128)),
                               (rdec, rdec_d[l].partition_broadcast(128))):
                  cx.dma("sp", dst, src, writes=[G])
              cx.run("dve", [lambda e: e.tensor_scalar(out=ngb, in0=gb[:, 1:2], scalar1=-1.0, scalar2=None, op0=ALU.mult)], reads=[G], writes=[G])
              cx.run("act", [lambda e: e.activation(out=lg, in_=rdec, func=AF.Exp, scale=-1.0)], reads=[G], writes=[G])
              cx.run("act", [lambda e: e.activation(out=lg, in_=lg, func=AF.Ln, bias=1.0)], reads=[G], writes=[G])
              cx.run("dve", [lambda e: e.tensor_scalar(out=lg, in0=lg, scalar1=-1.0, scalar2=None, op0=ALU.mult)], reads=[G], writes=[G])
              for r in range(8):
                  d = r // 4
                  lgc = lg[:, r:r + 1]
                  cx.run("act", [lambda e, r=r, d=d, lgc=lgc: e.activation(out=decT[:, r, :], in_=DIFF[d], func=AF.Exp, scale=lgc)], reads=[G], writes=[G])
                  cx.run("dve", [lambda e, r=r, d=d: e.scalar_tensor_tensor(out=decT[:, r, :], in0=decT[:, r, :], scalar=KS, in1=M01[d],
                                                                           op0=ALU.mult, op1=ALU.mult)], reads=[G], writes=[G])
                  cx.run("act", [lambda e, r=r, d=d, lgc=lgc: e.activation(out=rcol[:, r, 0:1], in_=POS[:, d:d + 1], func=AF.Exp, scale=lgc),
                                 lambda e, r=r, d=d, lgc=lgc: e.activation(out=rcol[:, r, 1:2], in_=POS[:, 2 + d:3 + d], func=AF.Exp, scale=lgc),
                                 lambda e, r=r, d=d, lgc=lgc: e.activation(out=rcol[:, r, 2:3], in_=POS[:, 4:5], func=AF.Exp, scale=lgc)],
                         reads=[G], writes=[G])
                  cx.run("dve", [lambda e, r=r: e.tensor_scalar(out=rcol[:, r, 1:2], in0=rcol[:, r, 1:2], scalar1=KS, scalar2=None, op0=ALU.mult)],
                         reads=[G], writes=[G])

              phase_end('small%d' % l)
              modT, sc1p, sc2p = modTs[l], sc1ps[l], sc2ps[l]
              if l == 0:
                  mod_groups(0, [0, 1, 2, 3])
              phase_end('modmm%d' % l)

              def modulate(scp, shoff):
                  ths = []
                  for kc in range(8):
                      for (a, b, v) in ((0, 1024, 1), (1024, T, 0)):
                          ths.append(lambda e, kc=kc, a=a, b=b, v=v: e.tensor_scalar(
                              out=hT[:, kc, a:b], in0=x[:, kc, a:b], scalar1=scp[:, kc, v:v + 1],
                              scalar2=modT[:, shoff + kc, v:v + 1], op0=ALU.mult, op1=ALU.add))
                  cx.run("dve", ths, reads=[x_T, M_T[l]], writes=[hT_T])

              dbg_out("modT_l%d" % l, modT, [128, 48, 2], [M_T[l]])
              phase_end("mod%d" % l)
              modulate(sc1p, 0)
              phase_end("h%d" % l)
              dbg_out("h_l%d" % l, hT, [128, 8, T], [hT_T])

              aset(3072)
              h_aT = arena[:, 0:3072].bitcast(BF16).rearrange("p (h t) -> p h t", t=T)
              ubT = arena[:, 3072:6144].bitcast(BF16).rearrange("p (h t) -> p h t", t=T)
              h_cT = arena[:, 6144:9216].bitcast(BF16).rearrange("p (h t) -> p h t", t=T)
              merged = arena[:, 9216:15360].bitcast(BF16).rearrange("p (h t) -> p h t", t=T)
              haT_T = TT("h_aT"); ub_T = TT("ubT"); hcT_T = TT("h_cT"); mg_T = TT("merged")
              rIG = af32(T); rA = af32(T); rP = af32(T); rCM = af32(T)
              R_T = TT("rows")
              sm = af32(96).rearrange("p (a c) -> p a c", c=12)
              SM_T = TT("sm")
              cols = af32(3 * 432).rearrange("p (q n) -> p q n", n=432)
              COL_T = TT("cols")
              cbc = af32(96)
              qT = abf(T); kT = abf(T)
              qk_T = TT("qk")
              vext = abf(12 * 130).rearrange("p (c n) -> p c n", n=130)
              v_T = TT("vext")
              osig = abf(T).rearrange("p (c n) -> p c n", n=128)
              o_T = TT("osig")
              hsum = af32(T).rearrange("p (c n) -> p c n", n=128)
              hs_T = [TT("hs%d" % c) for c in range(12)]
              kw = [abf(T).rearrange("p (c n) -> p c n", n=128) for _ in range(2)]
              kw_T = [[TT("kw") for c in range(12)] for _ in range(2)]
              Cx = [[af32(130) for _ in range(2)] for _ in range(2)]
              Cx_T = [[TT("cx"), TT("cx")] for _ in range(2)]
              sc2 = af32(8)
              dmg = [af32(512)] * 2
              dmg_T = [TT("dmg0")] * 2
              sT_all = abf(24 * 128)
              sTa_T = [TT("sTa%d" % i) for i in range(6)]
              Cb_all = [abf(12 * 130).rearrange("p (s n) -> p s n", n=130) for _ in range(2)]
              CbA_T = [[TT("cba") for _ in range(12)] for _ in range(2)]
              Bs2 = [af32(130) for _ in range(2)]
              Bs2_T = [TT("bs2a"), TT("bs2b")]
              tot_all = af32(12 * 130).rearrange("p (c n) -> p c n", n=130)
              tota_T = TT("tot_all")
              dn12 = af32(12)
              dn_T = TT("dn12")
              hn = abf(128)
              hn_T = TT("hn")
              st6 = af32(8)

              wg, wgT = wload(in_w_d[l][:, OFF_GATE:OFF_GATE + 72], 8, 72)
              cx.run("dve", [lambda e: e.memset(rP[0:36, :], 0.0), lambda e: e.memset(rCM[0:36, :], 0.0)], writes=[R_T])
              for which in (0, 1):
                  for nb in range(3):
                      bk, bkT = psum()
                      sl = slice(nb * 512, (nb + 1) * 512)
                      cx.run("pe", [mm(bk[0:36, :], wg[:, kc, which * 36:(which + 1) * 36], hT[:, kc, sl], kc == 0, kc == 7) for kc in range(8)],
                             reads=[wgT, hT_T], writes=[bkT])
                      if which == 0:
                          cx.run("act", [lambda e, bk=bk, sl=sl: e.activation(out=rIG[0:36, sl], in_=bk[0:36, :], func=AF.Identity, bias=gb[:, 0:1])],
                                 reads=[bkT, G], writes=[R_T])
                      else:
                          cx.run("act", [lambda e, bk=bk, sl=sl: e.activation(out=rA[0:36, sl], in_=bk[0:36, :], func=AF.Exp, scale=-1.0, bias=ngb[:, 0:1])],
                                 reads=[bkT, G], writes=[R_T])
              cx.run("act", [lambda e: e.activation(out=rA[0:36, :], in_=rA[0:36, :], func=AF.Ln, bias=1.0)], reads=[R_T], writes=[R_T])
              ths = []
              for c in range(12):
                  cs = slice(c * 128, (c + 1) * 128)
                  ths.append(lambda e, cs=cs: e.tensor_tensor_scan(out=rP[0:4, cs], data0=ones36[0:4, :], data1=rA[0:4, cs], initial=0.0,
                                                                   op0=ALU.mult, op1=ALU.add))
                  ths.append(lambda e, cs=cs: e.tensor_tensor_scan(out=rev(rP[32:36, cs]), data0=ones36[32:36, :], data1=rev(rA[32:36, cs]),
                                                                   initial=0.0, op0=ALU.mult, op1=ALU.add))
              cx.run("dve", ths, reads=[R_T], writes=[R_T])
              cx.run("dve", [lambda e: e.tensor_tensor(out=rA[0:36, :], in0=rIG[0:36, :], in1=rP[0:36, :], op=ALU.add)], reads=[R_T], writes=[R_T])
              ths = []
              for c in range(12):
                  cs = slice(c * 128, (c + 1) * 128)
                  ths.append(lambda e, cs=cs: e.tensor_tensor_scan(out=rCM[0:4, cs], data0=zeros36[0:4, :], data1=rA[0:4, cs], initial=-1e30,
                                                                   op0=ALU.add, op1=ALU.max))
                  ths.append(lambda e, cs=cs: e.tensor_tensor_scan(out=rev(rCM[32:36, cs]), data0=zeros36[32:36, :], data1=rev(rA[32:36, cs]),
                                                                   initial=-1e30, op0=ALU.add, op1=ALU.max))
              cx.run("dve", ths, reads=[R_T], writes=[R_T])
              CM3 = rCM.rearrange("p (c t) -> p c t", t=128)
              P3 = rP.rearrange("p (c t) -> p c t", t=128)
              A3 = rA.rearrange("p (c t) -> p c t", t=128)
              IG3 = rIG.rearrange("p (c t) -> p c t", t=128)
              cx.run("dve", [lambda e: e.memset(sm[0:36, :, :], 0.0)], writes=[SM_T])
              cx.run("dve", [lambda e: e.tensor_copy(out=sm[0:4, 0, :], in_=CM3[0:4, :, 127]),
                             lambda e: e.tensor_copy(out=sm[32:36, 0, :], in_=CM3[32:36, :, 0]),
                             lambda e: e.tensor_copy(out=sm[0:4, 1, :], in_=P3[0:4, :, 127]),
                             lambda e: e.tensor_copy(out=sm[32:36, 1, :], in_=P3[32:36, :, 0])], reads=[R_T, SM_T], writes=[SM_T])
              for d, r0 in ((0, 0), (1, 32)):
                  rs = slice(r0, r0 + 4)
                  order = list(range(12)) if d == 0 else list(range(11, -1, -1))
                  prev = None
                  for c in order:
                      s = c // 2
                      start = (c % 2 == 0) if d == 0 else (c % 2 == 1)
                      m0c = sm[rs, 2, c:c + 1]
                      if start:
                          if (d == 0 and s == 0) or (d == 1 and s == 3):
                              th = lambda e, m0c=m0c, rs=rs: e.tensor_copy(out=m0c, in_=minit[rs, :])
                          elif (d == 0 and s <= 3) or (d == 1 and s <= 2):
                              th = lambda e, m0c=m0c, rs=rs, prev=prev: e.tensor_scalar(out=m0c, in0=sm[rs, 4, prev:prev + 1], scalar1=chain[rs, :],
                                                                                       scalar2=None, op0=ALU.mult)
                          else:
                              th = lambda e, m0c=m0c: e.memset(m0c, 0.0)
                      else:
                          th = lambda e, m0c=m0c, rs=rs, prev=prev: e.tensor_copy(out=m0c, in_=sm[rs, 4, prev:prev + 1])
                      cx.run("dve", [th], reads=[SM_T, G], writes=[SM_T])
                      cx.run("dve", [lambda e, rs=rs, c=c: e.tensor_tensor(out=sm[rs, 3, c:c + 1], in0=sm[rs, 2, c:c + 1], in1=sm[rs, 0, c:c + 1], op=ALU.max)],
                             reads=[SM_T], writes=[SM_T])
                      cx.run("dve", [lambda e, rs=rs, c=c: e.tensor_tensor(out=sm[rs, 4, c:c + 1], in0=sm[rs, 3, c:c + 1], in1=sm[rs, 1, c:c + 1], op=ALU.subtract)],
                             reads=[SM_T], writes=[SM_T])
                      prev = c
              cx.dma("sp", mfin_d[l], sm[0:36, 4, :], reads=[SM_T])
              cx.run("dve", [lambda e: e.tensor_tensor(out=sm[0:36, 6, :], in0=sm[0:36, 3, :], in1=sm[0:36, 2, :], op=ALU.subtract)], reads=[SM_T], writes=[SM_T])
              cx.run("act", [lambda e: e.activation(out=sm[0:36, 5, :], in_=sm[0:36, 6, :], func=AF.Exp, scale=-1.0)], reads=[SM_T], writes=[SM_T])
              m0b = sm[0:36, 2, :].unsqueeze(2).to_broadcast([36, 12, 128])
              mxb = sm[0:36, 3, :].unsqueeze(2).to_broadcast([36, 12, 128])
              cx.run("dve", [lambda e: e.tensor_tensor(out=CM3[0:36], in0=CM3[0:36], in1=m0b, op=ALU.max)], reads=[R_T, SM_T], writes=[R_T])
              cx.run("dve", [lambda e: e.tensor_scalar(out=rCM[0:36, :], in0=rCM[0:36, :], scalar1=-1.0, scalar2=None, op0=ALU.mult)], reads=[R_T], writes=[R_T])
              dbg_out("rA_l%d" % l, rA[0:36, :], [36, T], [R_T])
              dbg_out("rNG_l%d" % l, rCM[0:36, :], [36, T], [R_T])

              def cols_from(q, rows):
                  bk, bkT = psum()
                  cx.run("pe", [lambda e, c=c, bk=bk: e.transpose(out=bk[:, c * 36:(c + 1) * 36], in_=rows[0:36, c * 128:(c + 1) * 128],
                                                                  identity=identF[0:36, 0:36]) for c in range(12)],
                         reads=[R_T, G], writes=[bkT])
                  cx.run("act", [lambda e, bk=bk: e.activation(out=cols[:, q, :], in_=bk[:, 0:432], func=AF.Copy)], reads=[bkT], writes=[COL_T])

              cx.run("dve", [lambda e: e.tensor_tensor(out=IG3[0:36], in0=CM3[0:36], in1=m0b, op=ALU.add)], reads=[R_T, SM_T], writes=[R_T])
              cx.run("act", [lambda e: e.activation(out=rIG[0:36, :], in_=rIG[0:36, :], func=AF.Exp)], reads=[R_T], writes=[R_T])
              cols_from(0, rIG)
              cx.run("dve", [lambda e: e.tensor_tensor(out=rP[0:36, :], in0=rP[0:36, :], in1=rCM[0:36, :], op=ALU.add)], reads=[R_T], writes=[R_T])
              cx.run("act", [lambda e: e.activation(out=rP[0:36, :], in_=rP[0:36, :], func=AF.Exp)], reads=[R_T], writes=[R_T])
              cols_from(1, rP)
              cx.run("dve", [lambda e: e.tensor_tensor(out=IG3[0:36], in0=A3[0:36], in1=mxb, op=ALU.subtract)], reads=[R_T, SM_T], writes=[R_T])
              cx.run("dve", [lambda e: e.tensor_scalar(out=rIG[0:36, :], in0=rIG[0:36, :], scalar1=LNKS, scalar2=None, op0=ALU.add)], reads=[R_T], writes=[R_T])
              cx.run("act", [lambda e: e.activation(out=rIG[0:36, :], in_=rIG[0:36, :], func=AF.Exp)], reads=[R_T], writes=[R_T])
              cols_from(2, rIG)
              bk, bkT = psum()
              cx.run("pe", [mm(bk[:, r * 12:(r + 1) * 12], sel[:, r, :], sm[0:36, 5, :], True, True) for r in range(8)], reads=[SM_T, G], writes=[bkT])
              cx.run("act", [lambda e, bk=bk: e.activation(out=cbc, in_=bk[:, 0:96], func=AF.Copy)], reads=[bkT], writes=[COL_T])
              dbg_out("sm_l%d" % l, sm[0:36, :, :], [36, 8, 12], [SM_T])
              dbg_out("cols_l%d" % l, cols, [128, 3, 432], [COL_T])

              phase_end("gates%d" % l)
              prow = lambda r: (r // 4) * 32 + (r % 4)
              lnks_col = None

              for h in range(H):
                  wv, wT = wload(in_w_d[l][:, h * 512:(h + 1) * 512], 8, 512)
                  for nb in range(3):
                      sl = slice(nb * 512, (nb + 1) * 512)
                      for which, dst in ((0, qT), (1, kT)):
                          bk, bkT = psum()
                          cx.run("pe", [mm(bk, wv[:, kc, which * 128:(which + 1) * 128], hT[:, kc, sl], kc == 0, kc == 7) for kc in range(8)],
                                 reads=[wT, hT_T], writes=[bkT])
                          cx.run("act", [lambda e, bk=bk, dst=dst, sl=sl: e.activation(out=dst[:, sl], in_=bk, func=AF.Copy)], reads=[bkT], writes=[qk_T])
                  cx.run("dve", [lambda e: e.memset(vext[:, :, 128:130], 1.0)], writes=[v_T])
                  for c in range(12):
                      cs = slice(c * 128, (c + 1) * 128)
                      bk, bkT = psum()
                      cx.run("pe", [mm(bk[:, 0:256], hT[:, kc, cs], wv[:, kc, 256:512], kc == 0, kc == 7) for kc in range(8)], reads=[wT, hT_T], writes=[bkT])
                      cx.run("act", [lambda e, bk=bk, c=c: e.activation(out=vext[:, c, 0:128], in_=bk[:, 0:128], func=AF.Copy)], reads=[bkT], writes=[v_T])
                      cx.run("act", [lambda e, bk=bk, c=c: e.activation(out=osig[:, c, :], in_=bk[:, 128:256], func=AF.Sigmoid)], reads=[bkT], writes=[o_T])
                  for c in range(12):
                      cs = slice(c * 128, (c + 1) * 128)
                      bk, bkT = psum()
                      bkb = bk.bitcast(BF16)
                      cx.run("pe", [lambda e, bkb=bkb, cs=cs: e.transpose(out=bkb[:, 0:128], in_=kT[:, cs], identity=identB)], reads=[qk_T, G], writes=[bkT])
                      for d in range(2):
                          r = d * 4 + h
                          col = cols[:, 2, c * 36 + prow(r):c * 36 + prow(r) + 1]
                          cx.run("act", [lambda e, bkb=bkb, d=d, c=c, col=col: e.activation(out=kw[d][:, c, :], in_=bkb[:, 0:128], func=AF.Copy, scale=col)],
                                 reads=[bkT, COL_T], writes=[kw_T[d][c]])
                  orders = [list(range(12)), list(range(11, -1, -1))]
                  for g in range(6):
                      d = g // 3
                      r = d * 4 + h
                      b1, b1T = psum()
                      ths = []
                      css = []
                      for i in range(4):
                          c = orders[d][(g % 3) * 4 + i]
                          cs = slice(c * 128, (c + 1) * 128)
                          css.append(cs)
                          o = b1[:, i * 128:(i + 1) * 128]
                          ths += [mm(o, sel[:, r, :], rCM[0:36, cs], True, False), mm(o, rA[0:36, cs], sel[:, r, :], False, False),
                                  mm(o, identF, MASK[d], False, True)]
                      cx.run("pe", ths, reads=[R_T, G], writes=[b1T])
                      gp = g % 2
                      cx.run("act", [lambda e, b1=b1, gp=gp: e.activation(out=dmg[gp], in_=b1, func=AF.Exp, bias=lnks[:, 0:1])], reads=[b1T, G], writes=[dmg_T[gp]])
                      b2, b2T = psum()
                      cx.run("pe", [mm(b2[:, i * 128:(i + 1) * 128], kT[:, css[i]], qT[:, css[i]], True, True) for i in range(4)], reads=[qk_T], writes=[b2T])
                      cx.run("dve", [lambda e, b2=b2, gp=gp, g=g: e.tensor_tensor(out=sT_all[:, g * 512:(g + 1) * 512], in0=b2, in1=dmg[gp], op=ALU.mult)],
                             reads=[b2T, dmg_T[gp]], writes=[sTa_T[g]])

                  def rec(d, h=h):
                      r = d * 4 + h
                      cur = 0
                      Cxd, CxTd = Cx[d], Cx_T[d]
                      for step, c in enumerate(orders[d]):
                          s = c // 2
                          start = (c % 2 == 0) if d == 0 else (c % 2 == 1)
                          if start:
                              if (d == 0 and s == 0) or (d == 1 and s == 3):
                                  cx.dma("sp", Cxd[cur][:, 0:129], Cinit_d[l, d, h], writes=[CxTd[cur]])
                              elif (d == 0 and s <= 3) or (d == 1 and s <= 2):
                                  cx.run("dve", [lambda e, cur=cur: e.tensor_scalar(out=Cxd[cur][:, 0:129], in0=Cxd[cur][:, 0:129], scalar1=chain[:, 0:1],
                                                                                     scalar2=None, op0=ALU.mult)], reads=[CxTd[cur], G], writes=[CxTd[cur]])
                              else:
                                  cx.run("dve", [lambda e, cur=cur: e.memset(Cxd[cur][:, 0:129], 0.0)], writes=[CxTd[cur]])
                          cx.run("act", [lambda e, cur=cur, step=step: e.activation(out=Cb_all[d][:, step, 0:129], in_=Cxd[cur][:, 0:129], func=AF.Copy)],
                                 reads=[CxTd[cur]], writes=[CbA_T[d][step]])
                          bU, bUT = psum()
                          cx.run("pe", [mm(bU[:, 0:129], kw[d][:, c, :], vext[:, c, 0:129], True, True)], reads=[kw_T[d][c], v_T], writes=[bUT])
                          yield
                          nxt = 1 - cur
                          ccol = cbc[:, r * 12 + c:r * 12 + c + 1]
                          cx.run("dve", [lambda e, cur=cur, nxt=nxt: e.scalar_tensor_tensor(
                              out=Cxd[nxt][:, 0:129], in0=Cxd[cur][:, 0:129], scalar=ccol, in1=bU[:, 0:129], op0=ALU.mult, op1=ALU.add)],
                              reads=[bUT, CxTd[cur], COL_T], writes=[CxTd[nxt]])
                          cur = nxt
                          end = (c % 2 == 1) if d == 0 else (c % 2 == 0)
                          if end:
                              cx.dma("sp", Cfin_d[l, s, d, h], Cxd[cur][:, 0:129], reads=[CxTd[cur]])
                          yield

                  for _ in itertools.zip_longest(rec(0), rec(1)):
                      pass

                  for d in range(2):
                      r = d * 4 + h
                      pr = prow(r)
                      for step, c in enumerate(orders[d]):
                          cs = slice(c * 128, (c + 1) * 128)
                          p = d * 12 + step
                          bA, bAT = psum()
                          cx.run("pe", [mm(bA[:, 0:129], sT_all[:, p * 128:(p + 1) * 128], vext[:, c, 0:129], True, True),
                                        mm(bA[:, 256:385], qT[:, cs], Cb_all[d][:, step, 0:129], True, True)],
                                 reads=[sTa_T[p // 4], v_T, qk_T, CbA_T[d][step]], writes=[bAT])
                          wcol = cols[:, 0, c * 36 + pr:c * 36 + pr + 1]
                          bp = step % 2
                          cx.run("act", [lambda e, bA=bA, bp=bp, wcol=wcol: e.activation(out=Bs2[bp][:, 0:129], in_=bA[:, 256:385], func=AF.Copy, scale=wcol)],
                                 reads=[bAT, COL_T], writes=[Bs2_T[bp]])
                          cx.run("dve", [lambda e, bA=bA, bp=bp, c=c: e.tensor_tensor(out=tot_all[:, c, 0:129], in0=bA[:, 0:129], in1=Bs2[bp][:, 0:129], op=ALU.add)],
                                 reads=[bAT, Bs2_T[bp]], writes=[tota_T])
                      den = tot_all[:, :, 128]
                      ecols = cols[:, 1, :].rearrange("p (c n) -> p c n", n=36)[:, :, pr]
                      cx.run("dve", [lambda e, den=den: e.tensor_scalar(out=dn12, in0=den, scalar1=-1.0, scalar2=None, op0=ALU.mult)], reads=[tota_T], writes=[dn_T])
                      cx.run("dve", [lambda e, den=den: e.tensor_tensor(out=dn12, in0=dn12, in1=den, op=ALU.max)], reads=[tota_T, dn_T], writes=[dn_T])
                      cx.run("dve", [lambda e, ecols=ecols: e.tensor_tensor(out=dn12, in0=dn12, in1=ecols, op=ALU.max)], reads=[dn_T, COL_T], writes=[dn_T])
                      cx.run("dve", [lambda e: e.reciprocal(out=dn12, in_=dn12)], reads=[dn_T], writes=[dn_T])
                      if d == 0:
                          cx.run("act", [lambda e, c=c: e.activation(out=hsum[:, c, :], in_=tot_all[:, c, 0:128], func=AF.Copy, scale=dn12[:, c:c + 1])
                                         for c in range(12)], reads=[tota_T, dn_T], writes=hs_T)
                      else:
                          cx.run("dve", [lambda e, c=c: e.scalar_tensor_tensor(out=hsum[:, c, :], in0=tot_all[:, c, 0:128], scalar=dn12[:, c:c + 1],
                                                                               in1=hsum[:, c, :], op0=ALU.mult, op1=ALU.add) for c in range(12)],
                                 reads=[tota_T, dn_T] + hs_T, writes=hs_T)
                  if l == 0 and h == 0:
                      dbg_out("hsum_l0h0", hsum, [128, 12, 128], hs_T)
                  head_norm_out(hsum, hs_T, mnw, h, osig, o_T, h_aT, haT_T, st6, hn, hn_T)
                  if l == 0:
                      mod_groups(0, [4 + 2 * h, 5 + 2 * h])
                  phase_end("mlstm%d_h%d" % (l, h))
              dbg_out("haT_l%d" % l, h_aT, [128, 4, T], [haT_T])

              phase_end("mlstm%d" % l)
              aset(6144)
              acc = af32(4 * T).rearrange("p (a t) -> p a t", t=T)
              acc_T = TT("acc")
              upad = [abf(6 * 286).rearrange("p (s n) -> p s n", n=286) for _ in range(2)]
              up_T = [TT("upad0"), TT("upad1")]
              Dg = [abf(31 * 128).rearrange("p (k n) -> p k n", n=128) for _ in range(2)]
              Dg_T = [TT("dg0"), TT("dg1")]
              sg = [af32(512) for _ in range(2)]
              sg_T = [TT("sg0"), TT("sg1")]
              cx.run("dve", [lambda e: e.memset(upad[0], 0.0), lambda e: e.memset(upad[1], 0.0)], writes=up_T)
              wa, waT = wload(in_w_d[l][:, OFF_CA:OFF_CA + 512], 8, 512)
              wgc, wgcT = wload(in_w_d[l][:, OFF_CG:OFF_CG + 512], 8, 512)
              k = 0
              for ct in range(4):
                  up, upT, dg, dgT = upad[ct % 2], up_T[ct % 2], Dg[ct % 2], Dg_T[ct % 2]
                  cx.run("act", [lambda e, kk=kk, ct=ct, dg=dg: e.activation(out=dg[:, kk, :], in_=identF, func=AF.Copy, scale=convp[:, ct, kk:kk + 1])
                                 for kk in range(CONV_K)], reads=[G], writes=[dgT])
                  for nb in range(3):
                      sl = slice(nb * 512, (nb + 1) * 512)
                      ba, baT = psum()
                      cx.run("pe", [mm(ba, wa[:, kc, ct * 128:(ct + 1) * 128], hT[:, kc, sl], kc == 0, kc == 7) for kc in range(8)], reads=[waT, hT_T], writes=[baT])
                      bg, bgT = psum()
                      cx.run("pe", [mm(bg, wgc[:, kc, ct * 128:(ct + 1) * 128], hT[:, kc, sl], kc == 0, kc == 7) for kc in range(8)], reads=[wgcT, hT_T], writes=[bgT])
                      pp = k % 2
                      k += 1
                      cx.run("act", [lambda e, bg=bg, pp=pp: e.activation(out=sg[pp], in_=bg, func=AF.Sigmoid)], reads=[bgT], writes=[sg_T[pp]])
                      cx.run("dve", [lambda e, ba=ba, pp=pp, nb=nb, up=up, s2=s2: e.tensor_tensor(
                          out=up[:, 2 * nb + s2, 15:271], in0=ba[:, s2 * 256:(s2 + 1) * 256],
                          in1=sg[pp][:, s2 * 256:(s2 + 1) * 256], op=ALU.mult) for s2 in range(2)], reads=[baT, sg_T[pp]], writes=[upT])
                  ths = []
                  for s in (1, 2, 3):
                      ths.append(lambda e, s=s, up=up: e.tensor_scalar(out=up[:, s, 0:15], in0=up[:, s - 1, 256:271], scalar1=chain[:, 0:1], scalar2=None, op0=ALU.mult))
                  for s in (0, 1, 2):
                      ths.append(lambda e, s=s, up=up: e.tensor_scalar(out=up[:, s, 271:286], in0=up[:, s + 1, 15:30], scalar1=chain[:, 0:1], scalar2=None, op0=ALU.mult))
                  cx.run("dve", ths, reads=[upT, G], writes=[upT])
                  phase_end("convu%d_%d" % (l, ct))
                  for s in range(6):
                      bk, bkT = psum()
                      cx.run("pe", [mm(bk[:, 0:256], dg[:, kk, :], up[:, s, kk:kk + 256], kk == 0, kk == CONV_K - 1) for kk in range(CONV_K)],
                             reads=[dgT, upT], writes=[bkT])
                      cx.run("act", [lambda e, bk=bk, ct=ct, s=s: e.activation(out=acc[:, ct, s * 256:(s + 1) * 256], in_=bk[:, 0:256], func=AF.Identity,
                                                                               bias=convp[:, ct, 31:32])], reads=[bkT, G], writes=[acc_T])
              dbg_out("conv_l%d" % l, acc, [128, 4, T], [acc_T])
              cb = abf(4 * 256).rearrange("p (a n) -> p a n", n=256)
              cq = abf(4 * 256).rearrange("p (a n) -> p a n", n=256)
              cb_T = TT("cb"); cq_T = TT("cq")
              mean = af32(256); msq = af32(256); rstd = af32(256)
              st_T = TT("lnst")
              for nb in range(6):
                  sl = slice(nb * 256, (nb + 1) * 256)
                  cx.run("act", [lambda e, ct=ct, sl=sl: e.activation(out=cb[:, ct, :], in_=acc[:, ct, sl], func=AF.Copy) for ct in range(4)], reads=[acc_T], writes=[cb_T])
                  cx.run("act", [lambda e, ct=ct, sl=sl: e.activation(out=cq[:, ct, :], in_=acc[:, ct, sl], func=AF.Square) for ct in range(4)], reads=[acc_T], writes=[cq_T])
                  b1, b1T = psum()
                  cx.run("pe", [mm(b1[:, 0:256], onesB, cb[:, ct, :], ct == 0, ct == 3) for ct in range(4)], reads=[cb_T, G], writes=[b1T])
                  b2, b2T = psum()
                  cx.run("pe", [mm(b2[:, 0:256], onesB, cq[:, ct, :], ct == 0, ct == 3) for ct in range(4)], reads=[cq_T, G], writes=[b2T])
                  ln_stats(b1[:, 0:256], b1T, b2[:, 0:256], b2T, mean, msq, rstd, st_T, 512)
                  for ct in range(4):
                      cx.run("dve", [lambda e, ct=ct, sl=sl: e.tensor_tensor(out=acc[:, ct, sl], in0=acc[:, ct, sl], in1=mean, op=ALU.subtract)],
                             reads=[acc_T, st_T], writes=[acc_T])
                      cx.run("dve", [lambda e, ct=ct, sl=sl: e.tensor_tensor(out=acc[:, ct, sl], in0=acc[:, ct, sl], in1=rstd, op=ALU.mult)],
                             reads=[acc_T, st_T], writes=[acc_T])
                      cx.run("act", [lambda e, ct=ct, sl=sl: e.activation(out=acc[:, ct, sl], in_=acc[:, ct, sl], func=AF.Identity,
                                                                          scale=convp[:, ct, 32:33], bias=convp[:, ct, 33:34])], reads=[acc_T, G], writes=[acc_T])
                      cx.run("act", [lambda e, ct=ct, sl=sl: e.activation(out=ubT[:, ct, sl], in_=acc[:, ct, sl], func=AF.Silu)], reads=[acc_T], writes=[ub_T])
              dbg_out("ubT_l%d" % l, ubT, [128, 4, T], [ub_T])

              phase_end("conv%d" % l)
              aset(9216)
              ropeCS = af32(1024); ropeSN = af32(1024)
              RP_T = TT("rope")
              cx.dma("sp", ropeCS, ropeCS_d, writes=[RP_T])
              cx.dma("sp", ropeSN, ropeSN_d, writes=[RP_T])
              qT = abf(T); kT = abf(T)
              qk_T = TT("rqk")
              vv = abf(T).rearrange("p (c n) -> p c n", n=128)
              v_T = TT("rv")
              gsil = af32(T).rearrange("p (c n) -> p c n", n=128)
              o_T = TT("gsil")
              ysum = af32(T).rearrange("p (c n) -> p c n", n=128)
              hs_T = [TT("ys%d" % c) for c in range(12)]
              kz = [abf(T).rearrange("p (c n) -> p c n", n=128) for _ in range(2)]
              kz_T = [[TT("kz") for c in range(12)] for _ in range(2)]
              Sx = [[af32(128) for _ in range(2)] for _ in range(2)]
              Sx_T = [[TT("sx"), TT("sx")] for _ in range(2)]
              t1 = af32(512); t2 = af32(512)
              sT_all = abf(24 * 128)
              sTa_T = [TT("rsTa%d" % i) for i in range(6)]
              Sb_all = [abf(12 * 128).rearrange("p (s n) -> p s n", n=128) for _ in range(2)]
              SbA_T = [[TT("sba") for _ in range(12)] for _ in range(2)]
              Bs2 = [af32(128) for _ in range(2)]
              Bs2_T = [TT("rbs2a"), TT("rbs2b")]
              t_T = TT("rt")
              hn = abf(128)
              hn_T = TT("rhn")
              st6 = af32(8)
              for h in range(H):
                  wv, wT = wload(in_w_d[l][:, OFF_RET + h * 512:OFF_RET + (h + 1) * 512], 8, 512)
                  wv2, wT2 = wload(in_w_d[l][:, OFF_RVG + h * 256:OFF_RVG + (h + 1) * 256], 8, 256)
                  for nb in range(3):
                      sl = slice(nb * 512, (nb + 1) * 512)
                      for which, dst in ((0, qT), (1, kT)):
                          bk, bkT = psum()
                          cx.run("pe", [mm(bk, wv[:, kc, which * 256:which * 256 + 128], hT[:, kc, sl], kc == 0, kc == 7) for kc in range(8)],
                                 reads=[wT, hT_T], writes=[bkT])
                          if nb < 2:
                              bs_, bsT = psum()
                              cx.run("pe", [mm(bs_, wv[:, kc, which * 256 + 128:which * 256 + 256], hT[:, kc, sl], kc == 0, kc == 7) for kc in range(8)],
                                     reads=[wT, hT_T], writes=[bsT])
                              cx.run("dve", [lambda e, bk=bk, sl=sl: e.tensor_tensor(out=t1, in0=bk, in1=ropeCS[:, sl], op=ALU.mult)], reads=[bkT, RP_T, t_T], writes=[t_T])
                              cx.run("dve", [lambda e, bs_=bs_, sl=sl: e.tensor_tensor(out=t2, in0=bs_, in1=ropeSN[:, sl], op=ALU.mult)], reads=[bsT, RP_T, t_T], writes=[t_T])
                              cx.run("dve", [lambda e, dst=dst, sl=sl: e.tensor_tensor(out=dst[:, sl], in0=t1, in1=t2, op=ALU.add)], reads=[t_T], writes=[qk_T, t_T])
                          else:
                              cx.run("act", [lambda e, bk=bk, dst=dst, sl=sl: e.activation(out=dst[:, sl], in_=bk, func=AF.Copy)], reads=[bkT], writes=[qk_T])
                  for c in range(12):
                      cs = slice(c * 128, (c + 1) * 128)
                      bk, bkT = psum()
                      cx.run("pe", [mm(bk[:, 0:256], hT[:, kc, cs], wv2[:, kc, :], kc == 0, kc == 7) for kc in range(8)], reads=[wT2, hT_T], writes=[bkT])
                      cx.run("act", [lambda e, bk=bk, c=c: e.activation(out=vv[:, c, :], in_=bk[:, 0:128], func=AF.Copy)], reads=[bkT], writes=[v_T])
                      cx.run("act", [lambda e, bk=bk, c=c: e.activation(out=gsil[:, c, :], in_=bk[:, 128:256], func=AF.Silu)], reads=[bkT], writes=[o_T])
                  for c in range(12):
                      cs = slice(c * 128, (c + 1) * 128)
                      bk, bkT = psum()
                      bkb = bk.bitcast(BF16)
                      cx.run("pe", [lambda e, bkb=bkb, cs=cs: e.transpose(out=bkb[:, 0:128], in_=kT[:, cs], identity=identB)], reads=[qk_T, G], writes=[bkT])
                      for d in range(2):
                          r = d * 4 + h
                          cx.run("act", [lambda e, bkb=bkb, d=d, c=c, r=r: e.activation(out=kz[d][:, c, :], in_=bkb[:, 0:128], func=AF.Copy, scale=rcol[:, r, 1:2])],
                                 reads=[bkT, G], writes=[kz_T[d][c]])
                  orders = [list(range(12)), list(range(11, -1, -1))]
                  for g in range(6):
                      d = g // 3
                      r = d * 4 + h
                      b2, b2T = psum()
                      css = [slice(orders[d][(g % 3) * 4 + i] * 128, (orders[d][(g % 3) * 4 + i] + 1) * 128) for i in range(4)]
                      cx.run("pe", [mm(b2[:, i * 128:(i + 1) * 128], kT[:, css[i]], qT[:, css[i]], True, True) for i in range(4)], reads=[qk_T], writes=[b2T])
                      cx.run("dve", [lambda e, b2=b2, g=g, i=i, r=r: e.tensor_tensor(out=sT_all[:, g * 512 + i * 128:g * 512 + (i + 1) * 128],
                                                                                   in0=b2[:, i * 128:(i + 1) * 128], in1=decT[:, r, :], op=ALU.mult) for i in range(4)],
                             reads=[b2T, G], writes=[sTa_T[g]])

                  def rrec(d, h=h):
                      r = d * 4 + h
                      cur = 0
                      Sxd, SxTd = Sx[d], Sx_T[d]
                      for step, c in enumerate(orders[d]):
                          s = c // 2
                          start = (c % 2 == 0) if d == 0 else (c % 2 == 1)
                          if start:
                              if (d == 0 and s == 0) or (d == 1 and s == 3):
                                  cx.dma("sp", Sxd[cur], Sinit_d[l, d, h], writes=[SxTd[cur]])
                              elif (d == 0 and s <= 3) or (d == 1 and s <= 2):
                                  cx.run("dve", [lambda e, cur=cur: e.tensor_scalar(out=Sxd[cur], in0=Sxd[cur], scalar1=chain[:, 0:1],
                                                                                     scalar2=None, op0=ALU.mult)], reads=[SxTd[cur], G], writes=[SxTd[cur]])
                              else:
                                  cx.run("dve", [lambda e, cur=cur: e.memset(Sxd[cur], 0.0)], writes=[SxTd[cur]])
                          cx.run("act", [lambda e, cur=cur, step=step: e.activation(out=Sb_all[d][:, step, :], in_=Sxd[cur], func=AF.Copy)],
                                 reads=[SxTd[cur]], writes=[SbA_T[d][step]])
                          bU, bUT = psum()
                          cx.run("pe", [mm(bU[:, 0:128], kz[d][:, c, :], vv[:, c, :], True, True)], reads=[kz_T[d][c], v_T], writes=[bUT])
                          yield
                          nxt = 1 - cur
                          cx.run("dve", [lambda e, cur=cur, nxt=nxt: e.scalar_tensor_tensor(
                              out=Sxd[nxt], in0=Sxd[cur], scalar=rcol[:, r, 2:3], in1=bU[:, 0:128], op0=ALU.mult, op1=ALU.add)],
                              reads=[bUT, SxTd[cur], G], writes=[SxTd[nxt]])
                          cur = nxt
                          end = (c % 2 == 1) if d == 0 else (c % 2 == 0)
                          if end:
                              cx.dma("sp", Sfin_d[l, s, d, h], Sxd[cur], reads=[SxTd[cur]])
                          yield

                  for _ in itertools.zip_longest(rrec(0), rrec(1)):
                      pass

                  for d in range(2):
                      r = d * 4 + h
                      for step, c in enumerate(orders[d]):
                          cs = slice(c * 128, (c + 1) * 128)
                          p = d * 12 + step
                          bA, bAT = psum()
                          cx.run("pe", [mm(bA[:, 0:128], sT_all[:, p * 128:(p + 1) * 128], vv[:, c, :], True, True),
                                        mm(bA[:, 256:384], qT[:, cs], Sb_all[d][:, step, :], True, True)],
                                 reads=[sTa_T[p // 4], v_T, qk_T, SbA_T[d][step]], writes=[bAT])
                          bp = step % 2
                          cx.run("act", [lambda e, bA=bA, bp=bp, r=r: e.activation(out=Bs2[bp], in_=bA[:, 256:384], func=AF.Copy, scale=rcol[:, r, 0:1])],
                                 reads=[bAT, G], writes=[Bs2_T[bp]])
                          if d == 0:
                              cx.run("dve", [lambda e, bA=bA, bp=bp, c=c: e.tensor_tensor(out=ysum[:, c, :], in0=bA[:, 0:128], in1=Bs2[bp], op=ALU.add)],
                                     reads=[bAT, Bs2_T[bp]], writes=[hs_T[c]])
                          else:
                              cx.run("dve", [lambda e, bA=bA, bp=bp: e.tensor_tensor(out=Bs2[bp], in0=bA[:, 0:128], in1=Bs2[bp], op=ALU.add)],
                                     reads=[bAT, Bs2_T[bp]], writes=[Bs2_T[bp]])
                              cx.run("dve", [lambda e, bp=bp, c=c: e.tensor_tensor(out=ysum[:, c, :], in0=ysum[:, c, :], in1=Bs2[bp], op=ALU.add)],
                                     reads=[Bs2_T[bp], hs_T[c]], writes=[hs_T[c]])
                  head_norm_out(ysum, hs_T, rnw, h, gsil, o_T, h_cT, hcT_T, st6, hn, hn_T)
                  if l + 1 < L:
                      mod_groups(l + 1, [3 * h, 3 * h + 1, 3 * h + 2])
              dbg_out("hcT_l%d" % l, h_cT, [128, 4, T], [hcT_T])

              phase_end("ret%d" % l)
              aset(15360)
              macc = af32(4 * T).rearrange("p (a t) -> p a t", t=T)
              macc_T = [TT("macc%d" % i) for i in range(4)]
              sgm = [af32(512) for _ in range(2)]
              sgm_T = [TT("sgm0"), TT("sgm1")]
              k = 0
              branches = ((mow_d, h_aT, haT_T), (cow_d, ubT, ub_T), (row_d, h_cT, hcT_T))
              for jg in range(2):
                  for b, (wd, src, srcT) in enumerate(branches):
                      wo, woT = wload(wd[l][:, jg * 512:(jg + 1) * 512], 4, 512)
                      wgm, wgmT = wload(in_w_d[l][:, OFF_GM + b * 1024 + jg * 512:OFF_GM + b * 1024 + (jg + 1) * 512], 8, 512)
                      for jj in range(4):
                          j = jg * 4 + jj
                          for nb in range(3):
                              sl = slice(nb * 512, (nb + 1) * 512)
                              by, byT = psum()
                              cx.run("pe", [mm(by, wo[:, kc, jj * 128:(jj + 1) * 128], src[:, kc, sl], kc == 0, kc == 3) for kc in range(4)], reads=[woT, srcT], writes=[byT])
                              bg, bgT = psum()
                              cx.run("pe", [mm(bg, wgm[:, kc, jj * 128:(jj + 1) * 128], hT[:, kc, sl], kc == 0, kc == 7) for kc in range(8)], reads=[wgmT, hT_T], writes=[bgT])
                              pp = k % 2
                              k += 1
                              cx.run("act", [lambda e, bg=bg, pp=pp: e.activation(out=sgm[pp], in_=bg, func=AF.Sigmoid)], reads=[bgT], writes=[sgm_T[pp]])
                              if b == 0:
                                  cx.run("dve", [lambda e, by=by, pp=pp, sl=sl, jj=jj: e.tensor_tensor(out=macc[:, jj, sl], in0=by, in1=sgm[pp], op=ALU.mult)],
                                         reads=[byT, sgm_T[pp]], writes=[macc_T[jj]])
                              else:
                                  cx.run("dve", [lambda e, by=by, pp=pp: e.tensor_tensor(out=sgm[pp], in0=by, in1=sgm[pp], op=ALU.mult)],
                                         reads=[byT, sgm_T[pp]], writes=[sgm_T[pp]])
                                  if b == 1:
                                      cx.run("dve", [lambda e, pp=pp, sl=sl, jj=jj: e.tensor_tensor(out=macc[:, jj, sl], in0=macc[:, jj, sl], in1=sgm[pp], op=ALU.add)],
                                             reads=[sgm_T[pp], macc_T[jj]], writes=[macc_T[jj]])
                                  else:
                                      cx.run("dve", [lambda e, pp=pp, sl=sl, j=j, jj=jj: e.tensor_tensor(out=merged[:, j, sl], in0=macc[:, jj, sl], in1=sgm[pp], op=ALU.add)],
                                             reads=[sgm_T[pp], macc_T[jj]], writes=[mg_T])
              dbg_out("merged_l%d" % l, merged, [128, 8, T], [mg_T])
              phase_end("merge%d" % l)
              aset(15360)
              xb = abf(8 * 512).rearrange("p (a n) -> p a n", n=512)
              xq = abf(8 * 512).rearrange("p (a n) -> p a n", n=512)
              lnt = (af32(512), af32(512), af32(512))
              cx.run("act", [lambda e, kc=kc: e.activation(out=x[:, kc, :], in_=x[:, kc, :], func=AF.Copy, scale=ALPHA) for kc in range(8)], reads=[x_T], writes=[x_T])
              for jg in range(2):
                  wo, woT = wload(outw_d[l][:, jg * 512:(jg + 1) * 512], 8, 512)
                  for jj in range(4):
                      j = jg * 4 + jj
                      for nb in range(3):
                          sl = slice(nb * 512, (nb + 1) * 512)
                          v = 1 if nb < 2 else 0
                          bk, bkT = psum()
                          cx.run("pe", [mm(bk, wo[:, kc, jj * 128:(jj + 1) * 128], merged[:, kc, sl], kc == 0, kc == 7) for kc in range(8)], reads=[woT, mg_T], writes=[bkT])
                          cx.run("dve", [lambda e, bk=bk, j=j, sl=sl, v=v: e.scalar_tensor_tensor(out=x[:, j, sl], in0=bk, scalar=modT[:, 16 + j, v:v + 1],
                                                                                                 in1=x[:, j, sl], op0=ALU.mult, op1=ALU.add)],
                                 reads=[bkT, x_T, M_T[l]], writes=[x_T])
              layer_norm_fm(0, [(0, 512), (512, 1024), (1024, 1536)], xb, xq, lnt)
              dbg_out("x1_l%d" % l, x, [128, 8, T], [x_T])
              phase_end("outp%d" % l)
              aset(0)
              ffT = abf(NFT * T).rearrange("p (a n) -> p a n", n=T)
              ff_T = TT("ffT")
              a_sb = abf(4 * T).rearrange("p (a n) -> p a n", n=T)
              asb_T = TT("a_sb")
              sgf = [af32(512) for _ in range(2)]
              sgf_T = [TT("sgf0"), TT("sgf1")]
              modulate(sc2p, 24)
              cx.run("act", [lambda e, kc=kc: e.activation(out=x[:, kc, :], in_=x[:, kc, :], func=AF.Copy, scale=ALPHA) for kc in range(8)], reads=[x_T], writes=[x_T])
              k = 0
              for g in range(6):
                  nt = 4 if g < 5 else 2
                  wa, waT = wload(w13_d[l][:, g * 512:g * 512 + nt * 128], 8, nt * 128)
                  wg2, wg2T = wload(w13_d[l][:, FF + g * 512:FF + g * 512 + nt * 128], 8, nt * 128)
                  for jj in range(nt):
                      for nb in range(3):
                          sl = slice(nb * 512, (nb + 1) * 512)
                          bk, bkT = psum()
                          cx.run("pe", [mm(bk, wa[:, kc, jj * 128:(jj + 1) * 128], hT[:, kc, sl], kc == 0, kc == 7) for kc in range(8)],
                                 reads=[waT, hT_T], writes=[bkT])
                          cx.run("act", [lambda e, bk=bk, jj=jj, sl=sl: e.activation(out=a_sb[:, jj, sl], in_=bk, func=AF.Copy)],
                                 reads=[bkT], writes=[asb_T])
                  for jj in range(nt):
                      for nb in range(3):
                          sl = slice(nb * 512, (nb + 1) * 512)
                          bk, bkT = psum()
                          cx.run("pe", [mm(bk, wg2[:, kc, jj * 128:(jj + 1) * 128], hT[:, kc, sl], kc == 0, kc == 7) for kc in range(8)],
                                 reads=[wg2T, hT_T], writes=[bkT])
                          pp = k % 2
                          k += 1
                          cx.run("act", [lambda e, bk=bk, pp=pp: e.activation(out=sgf[pp], in_=bk, func=AF.Silu)], reads=[bkT], writes=[sgf_T[pp]])
                          cx.run("dve", [lambda e, pp=pp, jj=jj, g=g, sl=sl: e.tensor_tensor(out=ffT[:, g * 4 + jj, sl], in0=sgf[pp], in1=a_sb[:, jj, sl], op=ALU.mult)],
                                 reads=[sgf_T[pp], asb_T], writes=[ff_T])
              for j in range(8):
                  w2v, w2T = wload(w2_d[l][:, j * 128:(j + 1) * 128], NFT, 128)
                  for nb in range(3):
                      sl = slice(nb * 512, (nb + 1) * 512)
                      v = 1 if nb < 2 else 0
                      bk, bkT = psum()
                      cx.run("pe", [mm(bk, w2v[:, kc, :], ffT[:, kc, sl], kc == 0, kc == NFT - 1) for kc in range(NFT)], reads=[w2T, ff_T], writes=[bkT])
                      cx.run("dve", [lambda e, bk=bk, j=j, sl=sl, v=v: e.scalar_tensor_tensor(
                          out=x[:, j, sl], in0=bk, scalar=modT[:, 40 + j, v:v + 1], in1=x[:, j, sl], op0=ALU.mult, op1=ALU.add)],
                          reads=[bkT, x_T, M_T[l]], writes=[x_T])
              aset(0)
              xb = abf(8 * 512).rearrange("p (a n) -> p a n", n=512)
              xq = abf(8 * 512).rearrange("p (a n) -> p a n", n=512)
              lnt = (af32(512), af32(512), af32(512))
              layer_norm_fm(1, [(0, 512), (512, 1024), (1024, 1536)], xb, xq, lnt)
              dbg_out("x2_l%d" % l, x, [128, 8, T], [x_T])

        try:
            layers()
        except _Stop:
            pass
        for kc in range(8):
            cx.dma("sp", yT_d[:, kc, :], x[:, kc, :], reads=[x_T])
        cx.final()

        with nc.Block() as block:
            def replay(name):
                def f(e):
                    for th in cx.prog[name]:
                        th(e)
                return f
            block.tensor(replay("pe"))
            block.scalar(replay("act"))
            block.vector(replay("dve"))
            block.gpsimd(replay("pool"))
            block.sync(replay("sp"))
    return nc


_IDX = np.concatenate([np.arange(0, 128, 2), np.arange(1, 128, 2)])
_IDXS = np.concatenate([np.arange(1, 128, 2), np.arange(0, 128, 2)])


def _prow(r):
    return (r // 4) * 32 + (r % 4)


def _in_cols():
    o = dict(mq=0, mk=512, mv=1024, mo=1536, mg=2048, ca=2064, cg=2576, rq=3088, rk=3600, rv=4112, rg=4624, gm=5136)
    cols = []
    for h in range(H):
        for nm in ("mq", "mk", "mv", "mo"):
            cols += list(range(o[nm] + h * 128, o[nm] + (h + 1) * 128))
    z = [-1] * 28
    mg = o["mg"]
    cols += [mg + 0 * 4 + h for h in range(4)] + z + [mg + 2 * 4 + h for h in range(4)]
    cols += [mg + 1 * 4 + h for h in range(4)] + z + [mg + 3 * 4 + h for h in range(4)]
    cols += list(range(o["ca"], o["ca"] + 512)) + list(range(o["cg"], o["cg"] + 512))
    for h in range(H):
        for nm in ("rq", "rk"):
            cols += list(o[nm] + h * 128 + _IDX) + list(o[nm] + h * 128 + _IDXS)
    for h in range(H):
        cols += list(range(o["rv"] + h * 128, o["rv"] + (h + 1) * 128)) + list(range(o["rg"] + h * 128, o["rg"] + (h + 1) * 128))
    cols += list(range(o["gm"], o["gm"] + 3072))
    cols = np.array(cols, dtype=np.int64)
    assert cols.shape[0] == NCOLS, cols.shape
    return cols


_NC_CACHE = {}


def kernel(x_prompt, x_sample, state_mlstm_C, state_mlstm_n, state_mlstm_m, state_ret_S, c, c_ctx,
           ada_w, ada_b, in_w, mlstm_gate_b, mlstm_norm_w, mlstm_out_w, conv_w, conv_b, conv_ln_w, conv_ln_b,
           conv_out_w, ret_decay, ret_norm_w, ret_out_w, out_w, ln1_w, ln1_b, ln2_w, ln2_b, ffn_w13, ffn_w2, _dbg=False):
    f32 = np.float32
    A = lambda a: np.ascontiguousarray(np.asarray(a, dtype=f32))
    x_prompt, x_sample = A(x_prompt), A(x_sample)
    sC, sn, smm, sS = A(state_mlstm_C), A(state_mlstm_n), A(state_mlstm_m), A(state_ret_S)
    c, c_ctx = A(c), A(c_ctx)
    in_w = A(in_w)
    cols = _in_cols()
    in_wp = np.zeros((L, D, NCOLS), f32)
    valid = cols >= 0
    in_wp[:, :, valid] = in_w[:, :, cols[valid]]
    gbias = A(mlstm_gate_b)
    gb = np.zeros((L, 36, 2), f32)
    for h in range(4):
        gb[:, h, 0] = gbias[:, 0, h]; gb[:, h, 1] = gbias[:, 1, h]
        gb[:, 32 + h, 0] = gbias[:, 2, h]; gb[:, 32 + h, 1] = gbias[:, 3, h]
    convp = np.zeros((L, 128, 4, 34), f32)
    cw = A(conv_w)
    convp[:, :, :, 0:31] = cw.reshape(L, CONV_K, 4, 128).transpose(0, 3, 2, 1)
    convp[:, :, :, 31] = A(conv_b).reshape(L, 4, 128).transpose(0, 2, 1)
    convp[:, :, :, 32] = A(conv_ln_w).reshape(L, 4, 128).transpose(0, 2, 1)
    convp[:, :, :, 33] = A(conv_ln_b).reshape(L, 4, 128).transpose(0, 2, 1)
    lnp = np.zeros((L, 128, 4, 8), f32)
    for i, a in enumerate((ln1_w, ln1_b, ln2_w, ln2_b)):
        lnp[:, :, i, :] = A(a).reshape(L, 8, 128).transpose(0, 2, 1)
    ada_bT = np.ascontiguousarray(A(ada_b).reshape(L, 48, 128).transpose(0, 2, 1))
    rdec = A(ret_decay).reshape(L, 1, 8)
    mnw = A(mlstm_norm_w).reshape(L, 1, 512)
    rnw = A(ret_norm_w).reshape(L, 1, 512)
    cst = np.zeros((128, 1024), f32)
    jj, ii = np.meshgrid(np.arange(128), np.arange(128), indexing="ij")
    cst[:, 0:128] = np.eye(128, dtype=f32)
    cst[:, 128:256] = np.where(jj <= ii, 0.0, NEG)
    cst[:, 256:384] = np.where(jj >= ii, 0.0, NEG)
    cst[:, 384:512] = np.maximum(ii - jj, 0)
    cst[:, 512:640] = np.maximum(jj - ii, 0)
    cst[:, 640:768] = (jj <= ii)
    cst[:, 768:896] = (jj >= ii)
    p = np.arange(128)
    cst[:, 896] = p + 1; cst[:, 897] = 128 - p; cst[:, 898] = 127 - p; cst[:, 899] = p; cst[:, 900] = 128
    sel = np.zeros((36, 8, 128), f32)
    for r in range(8):
        sel[_prow(r), r, :] = 1.0
    t = np.arange(1024)
    rows = (t // 64).astype(f32); colsg = (t % 64).astype(f32)
    freqs = (np.float32(10000.0) ** (-np.arange(32, dtype=f32) / np.float32(32))).astype(f32)
    ang = np.concatenate([rows[:, None] * freqs[None, :], colsg[:, None] * freqs[None, :]], -1).astype(f32)
    cs_lat = np.concatenate([np.cos(ang).T, np.cos(ang).T], 0).astype(f32)
    sn_lat = np.concatenate([-np.sin(ang).T, np.sin(ang).T], 0).astype(f32)
    cs_id = np.ones((128, 1024), f32); sn_id = np.zeros((128, 1024), f32)

    shared = dict(cst=cst, sel=sel, ada_w=A(ada_w), ada_bT=ada_bT, in_wp=in_wp, gb=gb, mnw=mnw, rnw=rnw,
                  mlstm_out_w=A(mlstm_out_w), conv_out_w=A(conv_out_w), ret_out_w=A(ret_out_w), convp=convp, rdec=rdec,
                  out_w=A(out_w), lnp=lnp, ffn_w13=A(ffn_w13), ffn_w2=A(ffn_w2))
    in_maps = []
    seg_prompt = []
    for core in range(8):
        if core < 4:
            xs = np.concatenate([x_sample[core], x_prompt[2 * core], x_prompt[2 * core + 1]], 0)
            seg_prompt.append({4: 2 * core, 5: 2 * core + 1})
            cvec = c[core]
            Cinit = np.concatenate([sC[core], sn[core][..., None]], -1)
            minit = np.zeros((L, 36, 1), f32)
            for h in range(4):
                minit[:, h, 0] = smm[core, :, 0, h]; minit[:, 32 + h, 0] = smm[core, :, 1, h]
            Sinit = sS[core][:, :, :, _IDX, :]
            chainv = 1.0
            rcs, rsn = cs_lat, sn_lat
        else:
            base = 8 + 6 * (core - 4)
            xs = np.concatenate([x_prompt[base + s] for s in range(6)], 0)
            seg_prompt.append({s: base + s for s in range(6)})
            cvec = c_ctx
            Cinit = np.zeros((L, 2, H, 128, 129), f32)
            minit = np.zeros((L, 36, 1), f32)
            Sinit = np.zeros((L, 2, H, 128, 128), f32)
            chainv = 0.0
            rcs, rsn = cs_id, sn_id
        xT = np.ascontiguousarray(xs.reshape(T, 8, 128).transpose(2, 1, 0))
        cv = np.stack([c_ctx.reshape(8, 128).T, cvec.reshape(8, 128).T], -1)
        m = dict(shared)
        m.update(xT=xT, cv=np.ascontiguousarray(cv, dtype=f32), chain=np.full((128, 1), chainv, f32),
                 Cinit=np.ascontiguousarray(Cinit, dtype=f32), minit=minit, Sinit=np.ascontiguousarray(Sinit, dtype=f32),
                 ropeCS=rcs, ropeSN=rsn)
        in_maps.append(m)

    key = bool(_dbg)
    nc = build(dbg=key)
    res = run_bass_kernel_spmd(nc, in_maps, core_ids=list(range(8)))
    R = res.results
    y_prompt = np.zeros((32, 256, D), f32)
    y_sample = np.zeros((4, 1024, D), f32)
    new_C = np.zeros((32, L, 2, H, 128, 128), f32)
    new_n = np.zeros((32, L, 2, H, 128), f32)
    new_m = np.zeros((32, L, 2, H), f32)
    new_S = np.zeros((32, L, 2, H, 128, 128), f32)
    for core in range(8):
        r = R[core]
        yt = np.asarray(r["yT"]).transpose(2, 1, 0).reshape(T, D)
        if core < 4:
            y_sample[core] = yt[0:1024]
        Cf = np.asarray(r["Cfin"]); mf = np.asarray(r["mfin"]); Sf = np.asarray(r["Sfin"])
        for s, b in seg_prompt[core].items():
            y_prompt[b] = yt[s * 256:(s + 1) * 256]
            new_C[b] = Cf[:, s, :, :, :, 0:128]
            new_n[b] = Cf[:, s, :, :, :, 128]
            for h in range(4):
                new_m[b, :, 0, h] = mf[:, h, 2 * s + 1]
                new_m[b, :, 1, h] = mf[:, 32 + h, 2 * s]
            Su = np.empty((L, 2, H, 128, 128), f32)
            Su[:, :, :, _IDX, :] = Sf[:, s]
            new_S[b] = Su
    if _dbg:
        return (y_prompt, y_sample, new_C, new_n, new_m, new_S), R
    return (y_prompt, y_sample, new_C, new_n, new_m, new_S)
```

```python
import math
import itertools
import numpy as np
import concourse.bass as bass
import concourse.mybir as mybir
from concourse.bass_utils import run_bass_kernel_spmd
from concourse.ap import AP

F32 = mybir.dt.float32
BF16 = mybir.dt.bfloat16
AF = mybir.ActivationFunctionType
ALU = mybir.AluOpType
AX = mybir.AxisListType

D = 1024
L = 2
T = 1536
NCH = 12
NSEG = 6
H = 4
HD = 128
FF = 2816
NFT = 22
CONV_K = 31
EPS = 1e-5
ALPHA = (2.0 * L) ** 0.25
KS = HD ** -0.5
LNKS = math.log(KS)
NEG = -30000.0
NCOLS = 9288
OFF_GATE = 2048
OFF_CA = 2120
OFF_CG = 2632
OFF_RET = 3144
OFF_RVG = 5192
OFF_GM = 6216

DEBUG = {}
STOP = None


class _Stop(Exception):
    pass


def phase_end(name):
    if STOP == name:
        raise _Stop()


def rev(ap):
    a = [list(x) for x in ap.ap]
    step, n = a[-1]
    off = ap.offset + step * (n - 1)
    a[-1] = [-step, n]
    return AP(ap.tensor, off, a)


import types


def _snap(th):
    cl = th.__closure__
    if not cl:
        return th
    cells = []
    for c in cl:
        try:
            cells.append(types.CellType(c.cell_contents))
        except ValueError:
            cells.append(c)
    return types.FunctionType(th.__code__, th.__globals__, th.__name__, th.__defaults__, tuple(cells))


class TT:
    __slots__ = ("name", "w", "r")

    def __init__(self, name=""):
        self.name = name
        self.w = None
        self.r = {}


class Ctx:
    ENG = ["pe", "act", "dve", "pool", "sp"]

    def __init__(self, nc, sems):
        self.nc = nc
        self.sems = sems
        self.prog = {e: [] for e in self.ENG}
        self.cnt = {e: 0 for e in self.ENG}
        self.seen = {e: {} for e in self.ENG}
        self.dkeys = [k for k in sems if k[0] == "d" and k[1:].isdigit()]
        self.wkeys = [k for k in sems if k[0] == "w" and k[1:].isdigit()]
        self.dtot = {k: 0 for k in sems}
        self.dn = 0
        self.wn = 0

    def _wait(self, eng, key, val):
        if val <= 0 or self.seen[eng].get(key, 0) >= val:
            return
        self.seen[eng][key] = val
        sem = self.sems[key]
        self.prog[eng].append(lambda e, sem=sem, val=val: e.wait_ge(sem, val))

    def _deps(self, eng, reads, writes):
        for t in reads:
            if t.w is not None:
                self._wait_dep(eng, t.w)
        for t in writes:
            if t.w is not None:
                self._wait_dep(eng, t.w)
            for k, v in t.r.items():
                self._wait_dep(eng, (k, v))

    def _wait_dep(self, eng, dep):
        k, v = dep
        if k == "pe" and eng == "pe":
            return
        self._wait(eng, k, v)

    def run(self, eng, thunks, reads=(), writes=()):
        if not isinstance(thunks, (list, tuple)):
            thunks = [thunks]
        thunks = [_snap(t) for t in thunks]
        self._deps(eng, reads, writes)
        sem = self.sems[eng]
        n = len(thunks)
        for i, th in enumerate(thunks):
            if i == n - 1:
                self.prog[eng].append(lambda e, th=th, sem=sem: th(e).then_inc(sem, 1))
            else:
                self.prog[eng].append(th)
        self.cnt[eng] += 1
        c = self.cnt[eng]
        for t in reads:
            t.r[eng] = c
        for t in writes:
            t.w = (eng, c)
            t.r = {}

    def dma(self, q, out, in_, reads=(), writes=()):
        if q == "pool":
            key = self.wkeys[self.wn % len(self.wkeys)]
            self.wn += 1
        else:
            key = self.dkeys[self.dn % len(self.dkeys)]
            self.dn += 1
        prev = self.dtot[key]
        self._wait(q, key, prev)
        self._deps(q, reads, writes)
        new = prev + 16
        self.dtot[key] = new
        sem = self.sems[key]
        self.prog[q].append(lambda e, out=out, in_=in_, sem=sem: e.dma_start(out=out, in_=in_).then_inc(sem, 16))
        for t in reads:
            t.r[key] = new
        for t in writes:
            t.w = (key, new)
            t.r = {}

    def barrier(self):
        engs = ["pe", "act", "dve", "sp"]
        for e in engs:
            for f in ["pe", "act", "dve"]:
                if f != e or e != "pe":
                    self._wait(e, f, self.cnt[f])
            for k in self.dkeys:
                self._wait(e, k, self.dtot[k])

    def final(self):
        for k in self.dkeys:
            self._wait("sp", k, self.dtot[k])
        for f in ["pe", "act", "dve"]:
            self._wait("sp", f, self.cnt[f])


def build(dbg=False):
    nc = bass.Bass("TRN2", target_bir_lowering=False)
    dram = {}

    def din(name, shape, dt=F32):
        dram[name] = nc.dram_tensor(name, list(shape), dt, kind="ExternalInput").ap()
        return dram[name]

    def dout(name, shape, dt=F32):
        dram[name] = nc.dram_tensor(name, list(shape), dt, kind="ExternalOutput").ap()
        return dram[name]

    xT_d = din("xT", [128, 8, T])
    cv_d = din("cv", [128, 8, 2])
    chain_d = din("chain", [128, 1])
    Cinit_d = din("Cinit", [L, 2, H, 128, 129])
    minit_d = din("minit", [L, 36, 1])
    Sinit_d = din("Sinit", [L, 2, H, 128, 128])
    ropeCS_d = din("ropeCS", [128, 1024])
    ropeSN_d = din("ropeSN", [128, 1024])
    cst_d = din("cst", [128, 1024])
    sel_d = din("sel", [36, 8, 128])
    ada_w_d = din("ada_w", [L, D, 6 * D])
    ada_b_d = din("ada_bT", [L, 128, 48])
    in_w_d = din("in_wp", [L, D, NCOLS])
    gb_d = din("gb", [L, 36, 2])
    mnw_d = din("mnw", [L, 1, 512])
    rnw_d = din("rnw", [L, 1, 512])
    mow_d = din("mlstm_out_w", [L, 512, D])
    cow_d = din("conv_out_w", [L, 512, D])
    row_d = din("ret_out_w", [L, 512, D])
    convp_d = din("convp", [L, 128, 4, 34])
    rdec_d = din("rdec", [L, 1, 8])
    outw_d = din("out_w", [L, D, D])
    lnp_d = din("lnp", [L, 128, 4, 8])
    w13_d = din("ffn_w13", [L, D, 2 * FF])
    w2_d = din("ffn_w2", [L, FF, D])

    yT_d = dout("yT", [128, 8, T])
    Cfin_d = dout("Cfin", [L, NSEG, 2, H, 128, 129])
    mfin_d = dout("mfin", [L, 36, 12])
    Sfin_d = dout("Sfin", [L, NSEG, 2, H, 128, 128])
    dbg_d = {}

    import contextlib
    es = contextlib.ExitStack()
    with es:
        def sb(name, shape, dt=F32):
            return es.enter_context(nc.sbuf_tensor("s_" + name, list(shape), dt))[:]

        x = sb("x", [128, 8, T])
        hT = sb("hT", [128, 8, T], BF16)
        NW = 3
        Wr = [sb("wr%d" % i, [128, 4096], BF16) for i in range(NW)]
        Wr_T = [TT("wr%d" % i) for i in range(NW)]
        cst = sb("cst", [128, 1024])
        sel = sb("sel", [36, 8, 128])
        identB = sb("identB", [128, 128], BF16)
        onesB = sb("onesB", [128, 128], BF16)
        chain = sb("chain", [128, 1])
        cvs = sb("cvs", [128, 8, 2], BF16)
        cvf = sb("cvf", [128, 8, 2])
        modTs = [sb("modT%d" % i, [128, 48, 2]) for i in range(L)]
        sc1ps = [sb("sc1p%d" % i, [128, 8, 2]) for i in range(L)]
        sc2ps = [sb("sc2p%d" % i, [128, 8, 2]) for i in range(L)]
        adabs = [sb("adab%d" % i, [128, 48]) for i in range(L)]
        M_T = [TT("mod%d" % i) for i in range(L)]
        gb = sb("gb", [36, 2])
        ngb = sb("ngb", [36, 1])
        minit = sb("minit", [36, 1])
        mnw = sb("mnw", [128, 512])
        rnw = sb("rnw", [128, 512])
        convp = sb("convp", [128, 4, 34])
        lnp = sb("lnp", [128, 4, 8])
        rdec = sb("rdec", [128, 8])
        lg = sb("lg", [128, 8])
        rcol = sb("rcol", [128, 8, 4])
        decT = sb("decT", [128, 8, 128])
        ones36 = sb("ones36", [36, 128])
        zeros36 = sb("zeros36", [36, 128])
        AR = 23168
        arena = sb("arena", [128, AR])
        ps = [es.enter_context(nc.psum_tensor("ps%d" % i, [128, 512], F32))[:] for i in range(8)]
        ps_T = [TT("ps%d" % i) for i in range(8)]

        keys = ["pe", "act", "dve", "pool", "sp"] + ["d%d" % i for i in range(8)] + ["w%d" % i for i in range(6)]
        sems = {k: es.enter_context(nc.semaphore(k)) for k in keys}
        cx = Ctx(nc, sems)
        G = TT("globals")
        x_T = TT("x")
        hT_T = TT("hT")

        identF = cst[:, 0:128]
        MASK = [cst[:, 128:256], cst[:, 256:384]]
        DIFF = [cst[:, 384:512], cst[:, 512:640]]
        M01 = [cst[:, 640:768], cst[:, 768:896]]
        POS = cst[:, 896:904]

        pstate = {"i": 0}

        def psum():
            i = pstate["i"] % 8
            pstate["i"] += 1
            return ps[i], ps_T[i]

        wstate = {"i": 0}

        def wload(src, kc, ncol):
            i = wstate["i"] % NW
            wstate["i"] += 1
            dst = Wr[i][:, 0:kc * ncol].rearrange("p (k n) -> p k n", n=ncol)
            cx.dma("pool", dst, src.rearrange("(k p) n -> p k n", p=128), writes=[Wr_T[i]])
            return dst, Wr_T[i]

        ast = {"o": 0}

        def aset(o):
            cx.barrier()
            ast["o"] = o

        def af32(n, shape=None):
            o = ast["o"]
            ast["o"] += n
            assert ast["o"] <= AR, ast["o"]
            v = arena[:, o:o + n]
            return v

        def abf(n):
            n2 = (n + 1) // 2
            return af32(n2).bitcast(BF16)

        def dbg_out(name, ap, shape, reads):
            if not dbg:
                return
            d = dout("dbg_" + name, shape, ap.dtype)
            DEBUG[name] = shape
            cx.dma("sp", d, ap, reads=reads)

        mm = lambda out, lhsT, rhs, st, sp: (lambda e: e.matmul(out, lhsT=lhsT, rhs=rhs, start=st, stop=sp))

        for kc in range(8):
            cx.dma("sp", x[:, kc, :], xT_d[:, kc, :], writes=[x_T])
        for dst, src in ((cst, cst_d), (sel, sel_d), (chain, chain_d), (cvf, cv_d)):
            cx.dma("sp", dst, src, writes=[G])
        cx.run("dve", [lambda e: e.memset(ones36, 1.0), lambda e: e.memset(zeros36, 0.0),
                       lambda e: e.memset(onesB, 1.0),
                       lambda e: e.tensor_copy(out=identB, in_=identF)],
               reads=[G], writes=[G])
        cx.run("act", [lambda e: e.activation(out=cvs, in_=cvf, func=AF.Silu)], reads=[G], writes=[G])
        for i in range(L):
            cx.dma("sp", adabs[i], ada_b_d[i], writes=[M_T[i]])

        def mod_groups(ll, gs):
            mT = modTs[ll]
            for g in gs:
                wv, wT = wload(ada_w_d[ll][:, g * 512:(g + 1) * 512], 8, 512)
                mb, mbT = psum()
                for jj in range(4):
                    cx.run("pe", [mm(mb[:, 2 * jj:2 * jj + 2], wv[:, kc, jj * 128:(jj + 1) * 128], cvs[:, kc, :], kc == 0, kc == 7) for kc in range(8)],
                           reads=[wT, G], writes=[mbT])
                j0 = g * 4
                cx.run("act", [lambda e, mb=mb, j0=j0: e.activation(out=mT[:, j0:j0 + 4, :].rearrange("p j v -> p (j v)"), in_=mb[:, 0:8], func=AF.Copy)],
                       reads=[mbT], writes=[M_T[ll]])
                cx.run("dve", [lambda e, j0=j0: e.tensor_tensor(out=mT[:, j0:j0 + 4, :], in0=mT[:, j0:j0 + 4, :],
                                                                in1=adabs[ll][:, j0:j0 + 4].unsqueeze(2).to_broadcast([128, 4, 2]), op=ALU.add)],
                       reads=[M_T[ll]], writes=[M_T[ll]])
                if g in (2, 3):
                    o = (g - 2) * 4
                    cx.run("dve", [lambda e, j0=j0, o=o: e.tensor_scalar(out=sc1ps[ll][:, o:o + 4, :], in0=mT[:, j0:j0 + 4, :], scalar1=1.0, scalar2=None, op0=ALU.add)],
                           reads=[M_T[ll]], writes=[M_T[ll]])
                if g in (8, 9):
                    o = (g - 8) * 4
                    cx.run("dve", [lambda e, j0=j0, o=o: e.tensor_scalar(out=sc2ps[ll][:, o:o + 4, :], in0=mT[:, j0:j0 + 4, :], scalar1=1.0, scalar2=None, op0=ALU.add)],
                           reads=[M_T[ll]], writes=[M_T[ll]])

        def layer_norm_fm(which, blocks, xb, xq, lnt):
            lw = lnp[:, 2 * which, :]
            lb = lnp[:, 2 * which + 1, :]
            xb_T, xq_T, lnt_T = TT("xb"), TT("xq"), TT("lnt")
            for (a_, b_) in blocks:
                sl = slice(a_, b_)
                cx.run("act", [lambda e, kc=kc: e.activation(out=xb[:, kc, :], in_=x[:, kc, sl], func=AF.Copy) for kc in range(8)],
                       reads=[x_T], writes=[xb_T])
                cx.run("act", [lambda e, kc=kc: e.activation(out=xq[:, kc, :], in_=x[:, kc, sl], func=AF.Square) for kc in range(8)],
                       reads=[x_T], writes=[xq_T])
                b1, b1T = psum()
                cx.run("pe", [mm(b1, onesB, xb[:, kc, :], kc == 0, kc == 7) for kc in range(8)], reads=[xb_T], writes=[b1T])
                b2, b2T = psum()
                cx.run("pe", [mm(b2, onesB, xq[:, kc, :], kc == 0, kc == 7) for kc in range(8)], reads=[xq_T], writes=[b2T])
                mean, msq, rstd = lnt
                cx.run("act", [lambda e: e.activation(out=mean, in_=b1, func=AF.Copy, scale=1.0 / D)], reads=[b1T], writes=[lnt_T])
                cx.run("dve", [lambda e: e.tensor_tensor(out=msq, in0=mean, in1=mean, op=ALU.mult)], reads=[lnt_T], writes=[lnt_T])
                cx.run("dve", [lambda e: e.scalar_tensor_tensor(out=msq, in0=b2, scalar=1.0 / D, in1=msq, op0=ALU.mult, op1=ALU.subtract)],
                       reads=[b2T, lnt_T], writes=[lnt_T])
                cx.run("dve", [lambda e: e.tensor_scalar(out=msq, in0=msq, scalar1=EPS, scalar2=None, op0=ALU.add)], reads=[lnt_T], writes=[lnt_T])
                cx.run("act", [lambda e: e.activation(out=rstd, in_=msq, func=AF.Sqrt)], reads=[lnt_T], writes=[lnt_T])
                cx.run("dve", [lambda e: e.reciprocal(out=rstd, in_=rstd)], reads=[lnt_T], writes=[lnt_T])
                for kc in range(8):
                    cx.run("dve", [lambda e, kc=kc: e.tensor_tensor(out=x[:, kc, sl], in0=x[:, kc, sl], in1=mean, op=ALU.subtract)],
                           reads=[x_T, lnt_T], writes=[x_T])
                    cx.run("dve", [lambda e, kc=kc: e.tensor_tensor(out=x[:, kc, sl], in0=x[:, kc, sl], in1=rstd, op=ALU.mult)],
                           reads=[x_T, lnt_T], writes=[x_T])
                    cx.run("act", [lambda e, kc=kc: e.activation(out=x[:, kc, sl], in_=x[:, kc, sl], func=AF.Identity,
                                                                 scale=lw[:, kc:kc + 1], bias=lb[:, kc:kc + 1])],
                           reads=[x_T, G], writes=[x_T])

        lnks = sb("lnks", [128, 1])
        cx.run("dve", [lambda e: e.memset(lnks, LNKS)], writes=[G])

        def ln_stats(b1, b1T, b2, b2T, mean, msq, rstd, st_T, n):
            cx.run("act", [lambda e: e.activation(out=mean, in_=b1, func=AF.Copy, scale=1.0 / n)], reads=[b1T], writes=[st_T])
            cx.run("dve", [lambda e: e.tensor_tensor(out=msq, in0=mean, in1=mean, op=ALU.mult)], reads=[st_T], writes=[st_T])
            cx.run("dve", [lambda e: e.scalar_tensor_tensor(out=msq, in0=b2, scalar=1.0 / n, in1=msq, op0=ALU.mult, op1=ALU.subtract)],
                   reads=[b2T, st_T], writes=[st_T])
            cx.run("dve", [lambda e: e.tensor_scalar(out=msq, in0=msq, scalar1=EPS, scalar2=None, op0=ALU.add)], reads=[st_T], writes=[st_T])
            cx.run("act", [lambda e: e.activation(out=rstd, in_=msq, func=AF.Sqrt)], reads=[st_T], writes=[st_T])
            cx.run("dve", [lambda e: e.reciprocal(out=rstd, in_=rstd)], reads=[st_T], writes=[st_T])

        def head_norm_out(src3, src_T, normw, h, gate3, gate_T, dstT, dst_T, scr, scr_T, hn_all, hn_Ts):
            stA = scr[:, 0:72].rearrange("p (c n) -> p c n", n=6)
            mvA = scr[:, 72:96].rearrange("p (c n) -> p c n", n=2)
            src_all = src3
            hn3 = hn_all.rearrange("p (c n) -> p c n", n=128)
            cx.run("dve", [lambda e, c=c: e.bn_stats(out=stA[:, c, :], in_=src3[:, c, :]) for c in range(12)], reads=list(src_T) + [scr_T], writes=[scr_T])
            cx.run("dve", [lambda e, c=c: e.bn_aggr(out=mvA[:, c, :], in_=stA[:, c, :]) for c in range(12)], reads=[scr_T], writes=[scr_T])
            rstd = mvA[:, :, 1]
            cx.run("dve", [lambda e: e.tensor_scalar(out=rstd, in0=rstd, scalar1=EPS, scalar2=None, op0=ALU.add)], reads=[scr_T], writes=[scr_T])
            cx.run("act", [lambda e: e.activation(out=rstd, in_=rstd, func=AF.Sqrt)], reads=[scr_T], writes=[scr_T])
            cx.run("dve", [lambda e: e.reciprocal(out=rstd, in_=rstd)], reads=[scr_T], writes=[scr_T])
            cx.run("dve", [lambda e, c=c: e.tensor_scalar(out=src3[:, c, :], in0=src3[:, c, :], scalar1=mvA[:, c, 0:1], scalar2=mvA[:, c, 1:2],
                                                          op0=ALU.subtract, op1=ALU.mult) for c in range(12)], reads=list(src_T) + [scr_T], writes=list(src_T))
            cx.run("dve", [lambda e, c=c: e.tensor_tensor(out=src3[:, c, :], in0=src3[:, c, :], in1=normw[:, h * 128:(h + 1) * 128], op=ALU.mult) for c in range(12)],
                   reads=list(src_T) + [G], writes=list(src_T))
            cx.run("dve", [lambda e: e.tensor_tensor(out=hn3, in0=src3, in1=gate3, op=ALU.mult)], reads=list(src_T) + [gate_T] + list(hn_Ts), writes=list(hn_Ts))
            for c in range(12):
                bk, bkT = psum()
                bkb = bk.bitcast(BF16)
                cx.run("pe", [lambda e, bkb=bkb, c=c: e.transpose(out=bkb[:, 0:128], in_=hn3[:, c, :], identity=identB)], reads=list(hn_Ts) + [G], writes=[bkT])
                cx.run("act", [lambda e, bkb=bkb, c=c: e.activation(out=dstT[:, h, c * 128:(c + 1) * 128], in_=bkb[:, 0:128], func=AF.Copy)], reads=[bkT], writes=[dst_T])

        def layers():
          for l in range(L):
              phase_end('pro%d' % l)
              aset(0)
              for dst, src in ((gb, gb_d[l]), (minit, minit_d[l]), (convp, convp_d[l]), (lnp, lnp_d[l]),
                               (mnw, mnw_d[l].partition_broadcast(128)), (rnw, rnw_d[l].partition_broadcast(128)),
                               (rdec, rdec_d[l].partition_broadcast(128))):
                  cx.dma("sp", dst, src, writes=[G])
              cx.run("dve", [lambda e: e.tensor_scalar(out=ngb, in0=gb[:, 1:2], scalar1=-1.0, scalar2=None, op0=ALU.mult)], reads=[G], writes=[G])
              cx.run("act", [lambda e: e.activation(out=lg, in_=rdec, func=AF.Exp, scale=-1.0)], reads=[G], writes=[G])
              cx.run("act", [lambda e: e.activation(out=lg, in_=lg, func=AF.Ln, bias=1.0)], reads=[G], writes=[G])
              cx.run("dve", [lambda e: e.tensor_scalar(out=lg, in0=lg, scalar1=-1.0, scalar2=None, op0=ALU.mult)], reads=[G], writes=[G])
              for r in range(8):
                  d = r // 4
                  lgc = lg[:, r:r + 1]
                  cx.run("act", [lambda e, r=r, d=d, lgc=lgc: e.activation(out=decT[:, r, :], in_=DIFF[d], func=AF.Exp, scale=lgc)], reads=[G], writes=[G])
                  cx.run("dve", [lambda e, r=r, d=d: e.scalar_tensor_tensor(out=decT[:, r, :], in0=decT[:, r, :], scalar=KS, in1=M01[d],
                                                                           op0=ALU.mult, op1=ALU.mult)], reads=[G], writes=[G])
                  cx.run("act", [lambda e, r=r, d=d, lgc=lgc: e.activation(out=rcol[:, r, 0:1], in_=POS[:, d:d + 1], func=AF.Exp, scale=lgc),
                                 lambda e, r=r, d=d, lgc=lgc: e.activation(out=rcol[:, r, 1:2], in_=POS[:, 2 + d:3 + d], func=AF.Exp, scale=lgc),
                                 lambda e, r=r, d=d, lgc=lgc: e.activation(out=rcol[:, r, 2:3], in_=POS[:, 4:5], func=AF.Exp, scale=lgc)],
                         reads=[G], writes=[G])
                  cx.run("dve", [lambda e, r=r: e.tensor_scalar(out=rcol[:, r, 1:2], in0=rcol[:, r, 1:2], scalar1=KS, scalar2=None, op0=ALU.mult)],
                         reads=[G], writes=[G])

              phase_end('small%d' % l)
              modT, sc1p, sc2p = modTs[l], sc1ps[l], sc2ps[l]
              if l == 0:
                  mod_groups(0, [0, 1, 2, 3])
              phase_end('modmm%d' % l)

              def modulate(scp, shoff):
                  ths = []
                  for kc in range(8):
                      for (a, b, v) in ((0, 1024, 1), (1024, T, 0)):
                          ths.append(lambda e, kc=kc, a=a, b=b, v=v: e.tensor_scalar(
                              out=hT[:, kc, a:b], in0=x[:, kc, a:b], scalar1=scp[:, kc, v:v + 1],
                              scalar2=modT[:, shoff + kc, v:v + 1], op0=ALU.mult, op1=ALU.add))
                  cx.run("dve", ths, reads=[x_T, M_T[l]], writes=[hT_T])

              dbg_out("modT_l%d" % l, modT, [128, 48, 2], [M_T[l]])
              phase_end("mod%d" % l)
              modulate(sc1p, 0)
              phase_end("h%d" % l)
              dbg_out("h_l%d" % l, hT, [128, 8, T], [hT_T])

              aset(3072)
              h_aT = arena[:, 0:3072].bitcast(BF16).rearrange("p (h t) -> p h t", t=T)
              ubT = arena[:, 3072:6144].bitcast(BF16).rearrange("p (h t) -> p h t", t=T)
              h_cT = arena[:, 6144:9216].bitcast(BF16).rearrange("p (h t) -> p h t", t=T)
              merged = arena[:, 9216:15360].bitcast(BF16).rearrange("p (h t) -> p h t", t=T)
              haT_T = TT("h_aT"); ub_T = TT("ubT"); hcT_T = TT("h_cT"); mg_T = TT("merged")
              rIG = af32(T); rA = af32(T); rP = af32(T); rCM = af32(T)
              R_T = TT("rows")
              sm = af32(96).rearrange("p (a c) -> p a c", c=12)
              SM_T = TT("sm")
              cols = af32(3 * 432).rearrange("p (q n) -> p q n", n=432)
              COL_T = TT("cols")
              cbc = af32(96)
              qT = abf(T); kT = abf(T)
              qk_T = TT("qk")
              vext = abf(12 * 130).rearrange("p (c n) -> p c n", n=130)
              v_T = TT("vext")
              osig = abf(T).rearrange("p (c n) -> p c n", n=128)
              o_T = TT("osig")
              hsum = af32(T).rearrange("p (c n) -> p c n", n=128)
              hs_T = [TT("hs%d" % c) for c in range(12)]
              kw = [abf(T).rearrange("p (c n) -> p c n", n=128) for _ in range(2)]
              kw_T = [[TT("kw") for c in range(12)] for _ in range(2)]
              Cx = [[af32(130) for _ in range(2)] for _ in range(2)]
              Cx_T = [[TT("cx"), TT("cx")] for _ in range(2)]
              sc2 = af32(8)
              dmg = [af32(512)] * 2
              dmg_T = [TT("dmg0")] * 2
              sT_all = abf(24 * 128)
              sTa_T = [TT("sTa%d" % i) for i in range(6)]
              Cb_all = [abf(12 * 130).rearrange("p (s n) -> p s n", n=130) for _ in range(2)]
              CbA_T = [[TT("cba") for _ in range(12)] for _ in range(2)]
              Bs2 = [af32(130) for _ in range(2)]
              Bs2_T = [TT("bs2a"), TT("bs2b")]
              tot_all = af32(12 * 130).rearrange("p (c n) -> p c n", n=130)
              tota_T = TT("tot_all")
              dn12 = af32(12)
              dn_T = TT("dn12")

              wg, wgT = wload(in_w_d[l][:, OFF_GATE:OFF_GATE + 72], 8, 72)
              cx.run("dve", [lambda e: e.memset(rP[0:36, :], 0.0), lambda e: e.memset(rCM[0:36, :], 0.0)], writes=[R_T])
              for which in (0, 1):
                  for nb in range(3):
                      bk, bkT = psum()
                      sl = slice(nb * 512, (nb + 1) * 512)
                      cx.run("pe", [mm(bk[0:36, :], wg[:, kc, which * 36:(which + 1) * 36], hT[:, kc, sl], kc == 0, kc == 7) for kc in range(8)],
                             reads=[wgT, hT_T], writes=[bkT])
                      if which == 0:
                          cx.run("act", [lambda e, bk=bk, sl=sl: e.activation(out=rIG[0:36, sl], in_=bk[0:36, :], func=AF.Identity, bias=gb[:, 0:1])],
                                 reads=[bkT, G], writes=[R_T])
                      else:
                          cx.run("act", [lambda e, bk=bk, sl=sl: e.activation(out=rA[0:36, sl], in_=bk[0:36, :], func=AF.Exp, scale=-1.0, bias=ngb[:, 0:1])],
                                 reads=[bkT, G], writes=[R_T])
              cx.run("act", [lambda e: e.activation(out=rA[0:36, :], in_=rA[0:36, :], func=AF.Ln, bias=1.0)], reads=[R_T], writes=[R_T])
              ths = []
              for c in range(12):
                  cs = slice(c * 128, (c + 1) * 128)
                  ths.append(lambda e, cs=cs: e.tensor_tensor_scan(out=rP[0:4, cs], data0=ones36[0:4, :], data1=rA[0:4, cs], initial=0.0,
                                                                   op0=ALU.mult, op1=ALU.add))
                  ths.append(lambda e, cs=cs: e.tensor_tensor_scan(out=rev(rP[32:36, cs]), data0=ones36[32:36, :], data1=rev(rA[32:36, cs]),
                                                                   initial=0.0, op0=ALU.mult, op1=ALU.add))
              cx.run("dve", ths, reads=[R_T], writes=[R_T])
              cx.run("dve", [lambda e: e.tensor_tensor(out=rA[0:36, :], in0=rIG[0:36, :], in1=rP[0:36, :], op=ALU.add)], reads=[R_T], writes=[R_T])
              ths = []
              for c in range(12):
                  cs = slice(c * 128, (c + 1) * 128)
                  ths.append(lambda e, cs=cs: e.tensor_tensor_scan(out=rCM[0:4, cs], data0=zeros36[0:4, :], data1=rA[0:4, cs], initial=-1e30,
                                                                   op0=ALU.add, op1=ALU.max))
                  ths.append(lambda e, cs=cs: e.tensor_tensor_scan(out=rev(rCM[32:36, cs]), data0=zeros36[32:36, :], data1=rev(rA[32:36, cs]),
                                                                   initial=-1e30, op0=ALU.add, op1=ALU.max))
              cx.run("dve", ths, reads=[R_T], writes=[R_T])
              CM3 = rCM.rearrange("p (c t) -> p c t", t=128)
              P3 = rP.rearrange("p (c t) -> p c t", t=128)
              A3 = rA.rearrange("p (c t) -> p c t", t=128)
              IG3 = rIG.rearrange("p (c t) -> p c t", t=128)
              cx.run("dve", [lambda e: e.memset(sm[0:36, :, :], 0.0)], writes=[SM_T])
              cx.run("dve", [lambda e: e.tensor_copy(out=sm[0:4, 0, :], in_=CM3[0:4, :, 127]),
                             lambda e: e.tensor_copy(out=sm[32:36, 0, :], in_=CM3[32:36, :, 0]),
                             lambda e: e.tensor_copy(out=sm[0:4, 1, :], in_=P3[0:4, :, 127]),
                             lambda e: e.tensor_copy(out=sm[32:36, 1, :], in_=P3[32:36, :, 0])], reads=[R_T, SM_T], writes=[SM_T])
              for d, r0 in ((0, 0), (1, 32)):
                  rs = slice(r0, r0 + 4)
                  order = list(range(12)) if d == 0 else list(range(11, -1, -1))
                  prev = None
                  for c in order:
                      s = c // 2
                      start = (c % 2 == 0) if d == 0 else (c % 2 == 1)
                      m0c = sm[rs, 2, c:c + 1]
                      if start:
                          if (d == 0 and s == 0) or (d == 1 and s == 3):
                              th = lambda e, m0c=m0c, rs=rs: e.tensor_copy(out=m0c, in_=minit[rs, :])
                          elif (d == 0 and s <= 3) or (d == 1 and s <= 2):
                              th = lambda e, m0c=m0c, rs=rs, prev=prev: e.tensor_scalar(out=m0c, in0=sm[rs, 4, prev:prev + 1], scalar1=chain[rs, :],
                                                                                       scalar2=None, op0=ALU.mult)
                          else:
                              th = lambda e, m0c=m0c: e.memset(m0c, 0.0)
                      else:
                          th = lambda e, m0c=m0c, rs=rs, prev=prev: e.tensor_copy(out=m0c, in_=sm[rs, 4, prev:prev + 1])
                      cx.run("dve", [th], reads=[SM_T, G], writes=[SM_T])
                      cx.run("dve", [lambda e, rs=rs, c=c: e.tensor_tensor(out=sm[rs, 3, c:c + 1], in0=sm[rs, 2, c:c + 1], in1=sm[rs, 0, c:c + 1], op=ALU.max)],
                             reads=[SM_T], writes=[SM_T])
                      cx.run("dve", [lambda e, rs=rs, c=c: e.tensor_tensor(out=sm[rs, 4, c:c + 1], in0=sm[rs, 3, c:c + 1], in1=sm[rs, 1, c:c + 1], op=ALU.subtract)],
                             reads=[SM_T], writes=[SM_T])
                      prev = c
              cx.dma("sp", mfin_d[l], sm[0:36, 4, :], reads=[SM_T])
              cx.run("dve", [lambda e: e.tensor_tensor(out=sm[0:36, 6, :], in0=sm[0:36, 3, :], in1=sm[0:36, 2, :], op=ALU.subtract)], reads=[SM_T], writes=[SM_T])
              cx.run("act", [lambda e: e.activation(out=sm[0:36, 5, :], in_=sm[0:36, 6, :], func=AF.Exp, scale=-1.0)], reads=[SM_T], writes=[SM_T])
              m0b = sm[0:36, 2, :].unsqueeze(2).to_broadcast([36, 12, 128])
              mxb = sm[0:36, 3, :].unsqueeze(2).to_broadcast([36, 12, 128])
              cx.run("dve", [lambda e: e.tensor_tensor(out=CM3[0:36], in0=CM3[0:36], in1=m0b, op=ALU.max)], reads=[R_T, SM_T], writes=[R_T])
              cx.run("dve", [lambda e: e.tensor_scalar(out=rCM[0:36, :], in0=rCM[0:36, :], scalar1=-1.0, scalar2=None, op0=ALU.mult)], reads=[R_T], writes=[R_T])
              dbg_out("rA_l%d" % l, rA[0:36, :], [36, T], [R_T])
              dbg_out("rNG_l%d" % l, rCM[0:36, :], [36, T], [R_T])

              def cols_from(q, rows):
                  bk, bkT = psum()
                  cx.run("pe", [lambda e, c=c, bk=bk: e.transpose(out=bk[:, c * 36:(c + 1) * 36], in_=rows[0:36, c * 128:(c + 1) * 128],
                                                                  identity=identF[0:36, 0:36]) for c in range(12)],
                         reads=[R_T, G], writes=[bkT])
                  cx.run("act", [lambda e, bk=bk: e.activation(out=cols[:, q, :], in_=bk[:, 0:432], func=AF.Copy)], reads=[bkT], writes=[COL_T])

              cx.run("dve", [lambda e: e.tensor_tensor(out=IG3[0:36], in0=CM3[0:36], in1=m0b, op=ALU.add)], reads=[R_T, SM_T], writes=[R_T])
              cx.run("act", [lambda e: e.activation(out=rIG[0:36, :], in_=rIG[0:36, :], func=AF.Exp)], reads=[R_T], writes=[R_T])
              cols_from(0, rIG)
              cx.run("dve", [lambda e: e.tensor_tensor(out=rP[0:36, :], in0=rP[0:36, :], in1=rCM[0:36, :], op=ALU.add)], reads=[R_T], writes=[R_T])
              cx.run("act", [lambda e: e.activation(out=rP[0:36, :], in_=rP[0:36, :], func=AF.Exp)], reads=[R_T], writes=[R_T])
              cols_from(1, rP)
              cx.run("dve", [lambda e: e.tensor_tensor(out=IG3[0:36], in0=A3[0:36], in1=mxb, op=ALU.subtract)], reads=[R_T, SM_T], writes=[R_T])
              cx.run("dve", [lambda e: e.tensor_scalar(out=rIG[0:36, :], in0=rIG[0:36, :], scalar1=LNKS, scalar2=None, op0=ALU.add)], reads=[R_T], writes=[R_T])
              cx.run("act", [lambda e: e.activation(out=rIG[0:36, :], in_=rIG[0:36, :], func=AF.Exp)], reads=[R_T], writes=[R_T])
              cols_from(2, rIG)
              bk, bkT = psum()
              cx.run("pe", [mm(bk[:, r * 12:(r + 1) * 12], sel[:, r, :], sm[0:36, 5, :], True, True) for r in range(8)], reads=[SM_T, G], writes=[bkT])
              cx.run("act", [lambda e, bk=bk: e.activation(out=cbc, in_=bk[:, 0:96], func=AF.Copy)], reads=[bkT], writes=[COL_T])
              dbg_out("sm_l%d" % l, sm[0:36, :, :], [36, 8, 12], [SM_T])
              dbg_out("cols_l%d" % l, cols, [128, 3, 432], [COL_T])

              phase_end("gates%d" % l)
              prow = lambda r: (r // 4) * 32 + (r % 4)
              lnks_col = None

              for h in range(H):
                  wv, wT = wload(in_w_d[l][:, h * 512:(h + 1) * 512], 8, 512)
                  for nb in range(3):
                      sl = slice(nb * 512, (nb + 1) * 512)
                      for which, dst in ((0, qT), (1, kT)):
                          bk, bkT = psum()
                          cx.run("pe", [mm(bk, wv[:, kc, which * 128:(which + 1) * 128], hT[:, kc, sl], kc == 0, kc == 7) for kc in range(8)],
                                 reads=[wT, hT_T], writes=[bkT])
                          cx.run("act", [lambda e, bk=bk, dst=dst, sl=sl: e.activation(out=dst[:, sl], in_=bk, func=AF.Copy)], reads=[bkT], writes=[qk_T])
                  cx.run("dve", [lambda e: e.memset(vext[:, :, 128:130], 1.0)], writes=[v_T])
                  for c in range(12):
                      cs = slice(c * 128, (c + 1) * 128)
                      bk, bkT = psum()
                      cx.run("pe", [mm(bk[:, 0:256], hT[:, kc, cs], wv[:, kc, 256:512], kc == 0, kc == 7) for kc in range(8)], reads=[wT, hT_T], writes=[bkT])
                      cx.run("act", [lambda e, bk=bk, c=c: e.activation(out=vext[:, c, 0:128], in_=bk[:, 0:128], func=AF.Copy)], reads=[bkT], writes=[v_T])
                      cx.run("act", [lambda e, bk=bk, c=c: e.activation(out=osig[:, c, :], in_=bk[:, 128:256], func=AF.Sigmoid)], reads=[bkT], writes=[o_T])
                  for c in range(12):
                      cs = slice(c * 128, (c + 1) * 128)
                      bk, bkT = psum()
                      bkb = bk.bitcast(BF16)
                      cx.run("pe", [lambda e, bkb=bkb, cs=cs: e.transpose(out=bkb[:, 0:128], in_=kT[:, cs], identity=identB)], reads=[qk_T, G], writes=[bkT])
                      for d in range(2):
                          r = d * 4 + h
                          col = cols[:, 2, c * 36 + prow(r):c * 36 + prow(r) + 1]
                          cx.run("act", [lambda e, bkb=bkb, d=d, c=c, col=col: e.activation(out=kw[d][:, c, :], in_=bkb[:, 0:128], func=AF.Copy, scale=col)],
                                 reads=[bkT, COL_T], writes=[kw_T[d][c]])
                  orders = [list(range(12)), list(range(11, -1, -1))]
                  for g in range(6):
                      d = g // 3
                      r = d * 4 + h
                      b1, b1T = psum()
                      ths = []
                      css = []
                      for i in range(4):
                          c = orders[d][(g % 3) * 4 + i]
                          cs = slice(c * 128, (c + 1) * 128)
                          css.append(cs)
                          o = b1[:, i * 128:(i + 1) * 128]
                          ths += [mm(o, sel[:, r, :], rCM[0:36, cs], True, False), mm(o, rA[0:36, cs], sel[:, r, :], False, False),
                                  mm(o, identF, MASK[d], False, True)]
                      cx.run("pe", ths, reads=[R_T, G], writes=[b1T])
                      gp = g % 2
                      cx.run("act", [lambda e, b1=b1, gp=gp: e.activation(out=dmg[gp], in_=b1, func=AF.Exp, bias=lnks[:, 0:1])], reads=[b1T, G], writes=[dmg_T[gp]])
                      b2, b2T = psum()
                      cx.run("pe", [mm(b2[:, i * 128:(i + 1) * 128], kT[:, css[i]], qT[:, css[i]], True, True) for i in range(4)], reads=[qk_T], writes=[b2T])
                      cx.run("dve", [lambda e, b2=b2, gp=gp, g=g: e.tensor_tensor(out=sT_all[:, g * 512:(g + 1) * 512], in0=b2, in1=dmg[gp], op=ALU.mult)],
                             reads=[b2T, dmg_T[gp]], writes=[sTa_T[g]])

                  def rec(d, h=h):
                      r = d * 4 + h
                      cur = 0
                      Cxd, CxTd = Cx[d], Cx_T[d]
                      for step, c in enumerate(orders[d]):
                          s = c // 2
                          start = (c % 2 == 0) if d == 0 else (c % 2 == 1)
                          if start:
                              if (d == 0 and s == 0) or (d == 1 and s == 3):
                                  cx.dma("sp", Cxd[cur][:, 0:129], Cinit_d[l, d, h], writes=[CxTd[cur]])
                              elif (d == 0 and s <= 3) or (d == 1 and s <= 2):
                                  cx.run("dve", [lambda e, cur=cur: e.tensor_scalar(out=Cxd[cur][:, 0:129], in0=Cxd[cur][:, 0:129], scalar1=chain[:, 0:1],
                                                                                     scalar2=None, op0=ALU.mult)], reads=[CxTd[cur], G], writes=[CxTd[cur]])
                              else:
                                  cx.run("dve", [lambda e, cur=cur: e.memset(Cxd[cur][:, 0:129], 0.0)], writes=[CxTd[cur]])
                          cx.run("act", [lambda e, cur=cur, step=step: e.activation(out=Cb_all[d][:, step, 0:129], in_=Cxd[cur][:, 0:129], func=AF.Copy)],
                                 reads=[CxTd[cur]], writes=[CbA_T[d][step]])
                          bU, bUT = psum()
                          cx.run("pe", [mm(bU[:, 0:129], kw[d][:, c, :], vext[:, c, 0:129], True, True)], reads=[kw_T[d][c], v_T], writes=[bUT])
                          yield
                          nxt = 1 - cur
                          ccol = cbc[:, r * 12 + c:r * 12 + c + 1]
                          cx.run("dve", [lambda e, cur=cur, nxt=nxt: e.scalar_tensor_tensor(
                              out=Cxd[nxt][:, 0:129], in0=Cxd[cur][:, 0:129], scalar=ccol, in1=bU[:, 0:129], op0=ALU.mult, op1=ALU.add)],
                              reads=[bUT, CxTd[cur], COL_T], writes=[CxTd[nxt]])
                          cur = nxt
                          end = (c % 2 == 1) if d == 0 else (c % 2 == 0)
                          if end:
                              cx.dma("sp", Cfin_d[l, s, d, h], Cxd[cur][:, 0:129], reads=[CxTd[cur]])
                          yield

                  for _ in itertools.zip_longest(rec(0), rec(1)):
                      pass

                  for d in range(2):
                      r = d * 4 + h
                      pr = prow(r)
                      for step, c in enumerate(orders[d]):
                          cs = slice(c * 128, (c + 1) * 128)
                          p = d * 12 + step
                          bA, bAT = psum()
                          cx.run("pe", [mm(bA[:, 0:129], sT_all[:, p * 128:(p + 1) * 128], vext[:, c, 0:129], True, True),
                                        mm(bA[:, 256:385], qT[:, cs], Cb_all[d][:, step, 0:129], True, True)],
                                 reads=[sTa_T[p // 4], v_T, qk_T, CbA_T[d][step]], writes=[bAT])
                          wcol = cols[:, 0, c * 36 + pr:c * 36 + pr + 1]
                          bp = step % 2
                          cx.run("act", [lambda e, bA=bA, bp=bp, wcol=wcol: e.activation(out=Bs2[bp][:, 0:129], in_=bA[:, 256:385], func=AF.Copy, scale=wcol)],
                                 reads=[bAT, COL_T], writes=[Bs2_T[bp]])
                          cx.run("dve", [lambda e, bA=bA, bp=bp, c=c: e.tensor_tensor(out=tot_all[:, c, 0:129], in0=bA[:, 0:129], in1=Bs2[bp][:, 0:129], op=ALU.add)],
                                 reads=[bAT, Bs2_T[bp]], writes=[tota_T])
                      den = tot_all[:, :, 128]
                      ecols = cols[:, 1, :].rearrange("p (c n) -> p c n", n=36)[:, :, pr]
                      cx.run("dve", [lambda e, den=den: e.tensor_scalar(out=dn12, in0=den, scalar1=-1.0, scalar2=None, op0=ALU.mult)], reads=[tota_T], writes=[dn_T])
                      cx.run("dve", [lambda e, den=den: e.tensor_tensor(out=dn12, in0=dn12, in1=den, op=ALU.max)], reads=[tota_T, dn_T], writes=[dn_T])
                      cx.run("dve", [lambda e, ecols=ecols: e.tensor_tensor(out=dn12, in0=dn12, in1=ecols, op=ALU.max)], reads=[dn_T, COL_T], writes=[dn_T])
                      cx.run("dve", [lambda e: e.reciprocal(out=dn12, in_=dn12)], reads=[dn_T], writes=[dn_T])
                      if d == 0:
                          cx.run("act", [lambda e, c=c: e.activation(out=hsum[:, c, :], in_=tot_all[:, c, 0:128], func=AF.Copy, scale=dn12[:, c:c + 1])
                                         for c in range(12)], reads=[tota_T, dn_T], writes=hs_T)
                      else:
                          cx.run("dve", [lambda e, c=c: e.scalar_tensor_tensor(out=hsum[:, c, :], in0=tot_all[:, c, 0:128], scalar=dn12[:, c:c + 1],
                                                                               in1=hsum[:, c, :], op0=ALU.mult, op1=ALU.add) for c in range(12)],
                                 reads=[tota_T, dn_T] + hs_T, writes=hs_T)
                  if l == 0 and h == 0:
                      dbg_out("hsum_l0h0", hsum, [128, 12, 128], hs_T)
                  head_norm_out(hsum, hs_T, mnw, h, osig, o_T, h_aT, haT_T, tot_all.rearrange("p c n -> p (c n)"), tota_T, sT_all[:, 0:1536], sTa_T[0:3])
                  if l == 0:
                      mod_groups(0, [4 + 2 * h, 5 + 2 * h])
                  phase_end("mlstm%d_h%d" % (l, h))
              dbg_out("haT_l%d" % l, h_aT, [128, 4, T], [haT_T])

              phase_end("mlstm%d" % l)
              aset(6144)
              acc = af32(4 * T).rearrange("p (a t) -> p a t", t=T)
              acc_T = TT("acc")
              upad = [abf(6 * 286).rearrange("p (s n) -> p s n", n=286) for _ in range(2)]
              up_T = [TT("upad0"), TT("upad1")]
              Dg = [abf(31 * 128).rearrange("p (k n) -> p k n", n=128) for _ in range(2)]
              Dg_T = [TT("dg0"), TT("dg1")]
              sg = [af32(512) for _ in range(2)]
              sg_T = [TT("sg0"), TT("sg1")]
              cx.run("dve", [lambda e: e.memset(upad[0], 0.0), lambda e: e.memset(upad[1], 0.0)], writes=up_T)
              wa, waT = wload(in_w_d[l][:, OFF_CA:OFF_CA + 512], 8, 512)
              wgc, wgcT = wload(in_w_d[l][:, OFF_CG:OFF_CG + 512], 8, 512)
              k = 0
              for ct in range(4):
                  up, upT, dg, dgT = upad[ct % 2], up_T[ct % 2], Dg[ct % 2], Dg_T[ct % 2]
                  cx.run("act", [lambda e, kk=kk, ct=ct, dg=dg: e.activation(out=dg[:, kk, :], in_=identF, func=AF.Copy, scale=convp[:, ct, kk:kk + 1])
                                 for kk in range(CONV_K)], reads=[G], writes=[dgT])
                  for nb in range(3):
                      sl = slice(nb * 512, (nb + 1) * 512)
                      ba, baT = psum()
                      cx.run("pe", [mm(ba, wa[:, kc, ct * 128:(ct + 1) * 128], hT[:, kc, sl], kc == 0, kc == 7) for kc in range(8)], reads=[waT, hT_T], writes=[baT])
                      bg, bgT = psum()
                      cx.run("pe", [mm(bg, wgc[:, kc, ct * 128:(ct + 1) * 128], hT[:, kc, sl], kc == 0, kc == 7) for kc in range(8)], reads=[wgcT, hT_T], writes=[bgT])
                      pp = k % 2
                      k += 1
                      cx.run("act", [lambda e, bg=bg, pp=pp: e.activation(out=sg[pp], in_=bg, func=AF.Sigmoid)], reads=[bgT], writes=[sg_T[pp]])
                      cx.run("dve", [lambda e, ba=ba, pp=pp, nb=nb, up=up, s2=s2: e.tensor_tensor(
                          out=up[:, 2 * nb + s2, 15:271], in0=ba[:, s2 * 256:(s2 + 1) * 256],
                          in1=sg[pp][:, s2 * 256:(s2 + 1) * 256], op=ALU.mult) for s2 in range(2)], reads=[baT, sg_T[pp]], writes=[upT])
                  ths = []
                  for s in (1, 2, 3):
                      ths.append(lambda e, s=s, up=up: e.tensor_scalar(out=up[:, s, 0:15], in0=up[:, s - 1, 256:271], scalar1=chain[:, 0:1], scalar2=None, op0=ALU.mult))
                  for s in (0, 1, 2):
                      ths.append(lambda e, s=s, up=up: e.tensor_scalar(out=up[:, s, 271:286], in0=up[:, s + 1, 15:30], scalar1=chain[:, 0:1], scalar2=None, op0=ALU.mult))
                  cx.run("dve", ths, reads=[upT, G], writes=[upT])
                  phase_end("convu%d_%d" % (l, ct))
                  for s in range(6):
                      bk, bkT = psum()
                      cx.run("pe", [mm(bk[:, 0:256], dg[:, kk, :], up[:, s, kk:kk + 256], kk == 0, kk == CONV_K - 1) for kk in range(CONV_K)],
                             reads=[dgT, upT], writes=[bkT])
                      cx.run("act", [lambda e, bk=bk, ct=ct, s=s: e.activation(out=acc[:, ct, s * 256:(s + 1) * 256], in_=bk[:, 0:256], func=AF.Identity,
                                                                               bias=convp[:, ct, 31:32])], reads=[bkT, G], writes=[acc_T])
              dbg_out("conv_l%d" % l, acc, [128, 4, T], [acc_T])
              cb = abf(4 * 256).rearrange("p (a n) -> p a n", n=256)
              cq = abf(4 * 256).rearrange("p (a n) -> p a n", n=256)
              cb_T = TT("cb"); cq_T = TT("cq")
              mean = af32(256); msq = af32(256); rstd = af32(256)
              st_T = TT("lnst")
              for nb in range(6):
                  sl = slice(nb * 256, (nb + 1) * 256)
                  cx.run("act", [lambda e, ct=ct, sl=sl: e.activation(out=cb[:, ct, :], in_=acc[:, ct, sl], func=AF.Copy) for ct in range(4)], reads=[acc_T], writes=[cb_T])
                  cx.run("act", [lambda e, ct=ct, sl=sl: e.activation(out=cq[:, ct, :], in_=acc[:, ct, sl], func=AF.Square) for ct in range(4)], reads=[acc_T], writes=[cq_T])
                  b1, b1T = psum()
                  cx.run("pe", [mm(b1[:, 0:256], onesB, cb[:, ct, :], ct == 0, ct == 3) for ct in range(4)], reads=[cb_T, G], writes=[b1T])
                  b2, b2T = psum()
                  cx.run("pe", [mm(b2[:, 0:256], onesB, cq[:, ct, :], ct == 0, ct == 3) for ct in range(4)], reads=[cq_T, G], writes=[b2T])
                  ln_stats(b1[:, 0:256], b1T, b2[:, 0:256], b2T, mean, msq, rstd, st_T, 512)
                  for ct in range(4):
                      cx.run("dve", [lambda e, ct=ct, sl=sl: e.tensor_tensor(out=acc[:, ct, sl], in0=acc[:, ct, sl], in1=mean, op=ALU.subtract)],
                             reads=[acc_T, st_T], writes=[acc_T])
                      cx.run("dve", [lambda e, ct=ct, sl=sl: e.tensor_tensor(out=acc[:, ct, sl], in0=acc[:, ct, sl], in1=rstd, op=ALU.mult)],
                             reads=[acc_T, st_T], writes=[acc_T])
                      cx.run("act", [lambda e, ct=ct, sl=sl: e.activation(out=acc[:, ct, sl], in_=acc[:, ct, sl], func=AF.Identity,
                                                                          scale=convp[:, ct, 32:33], bias=convp[:, ct, 33:34])], reads=[acc_T, G], writes=[acc_T])
                      cx.run("act", [lambda e, ct=ct, sl=sl: e.activation(out=ubT[:, ct, sl], in_=acc[:, ct, sl], func=AF.Silu)], reads=[acc_T], writes=[ub_T])
              dbg_out("ubT_l%d" % l, ubT, [128, 4, T], [ub_T])

              phase_end("conv%d" % l)
              aset(9216)
              ropeCS = af32(1024); ropeSN = af32(1024)
              RP_T = TT("rope")
              cx.dma("sp", ropeCS, ropeCS_d, writes=[RP_T])
              cx.dma("sp", ropeSN, ropeSN_d, writes=[RP_T])
              qT = abf(T); kT = abf(T)
              qk_T = TT("rqk")
              vv = abf(T).rearrange("p (c n) -> p c n", n=128)
              v_T = TT("rv")
              gsil = af32(T).rearrange("p (c n) -> p c n", n=128)
              o_T = TT("gsil")
              ysum = af32(T).rearrange("p (c n) -> p c n", n=128)
              hs_T = [TT("ys%d" % c) for c in range(12)]
              kz = [abf(T).rearrange("p (c n) -> p c n", n=128) for _ in range(2)]
              kz_T = [[TT("kz") for c in range(12)] for _ in range(2)]
              Sx = [[af32(128) for _ in range(2)] for _ in range(2)]
              Sx_T = [[TT("sx"), TT("sx")] for _ in range(2)]
              t1 = af32(512); t2 = af32(512)
              sT_all = abf(24 * 128)
              sTa_T = [TT("rsTa%d" % i) for i in range(6)]
              Sb_all = [abf(12 * 128).rearrange("p (s n) -> p s n", n=128) for _ in range(2)]
              SbA_T = [[TT("sba") for _ in range(12)] for _ in range(2)]
              Bs2 = [af32(128) for _ in range(2)]
              Bs2_T = [TT("rbs2a"), TT("rbs2b")]
              t_T = TT("rt")
              for h in range(H):
                  wv, wT = wload(in_w_d[l][:, OFF_RET + h * 512:OFF_RET + (h + 1) * 512], 8, 512)
                  wv2, wT2 = wload(in_w_d[l][:, OFF_RVG + h * 256:OFF_RVG + (h + 1) * 256], 8, 256)
                  for nb in range(3):
                      sl = slice(nb * 512, (nb + 1) * 512)
                      for which, dst in ((0, qT), (1, kT)):
                          bk, bkT = psum()
                          cx.run("pe", [mm(bk, wv[:, kc, which * 256:which * 256 + 128], hT[:, kc, sl], kc == 0, kc == 7) for kc in range(8)],
                                 reads=[wT, hT_T], writes=[bkT])
                          if nb < 2:
                              bs_, bsT = psum()
                              cx.run("pe", [mm(bs_, wv[:, kc, which * 256 + 128:which * 256 + 256], hT[:, kc, sl], kc == 0, kc == 7) for kc in range(8)],
                                     reads=[wT, hT_T], writes=[bsT])
                              cx.run("dve", [lambda e, bk=bk, sl=sl: e.tensor_tensor(out=t1, in0=bk, in1=ropeCS[:, sl], op=ALU.mult)], reads=[bkT, RP_T, t_T], writes=[t_T])
                              cx.run("dve", [lambda e, bs_=bs_, sl=sl: e.tensor_tensor(out=t2, in0=bs_, in1=ropeSN[:, sl], op=ALU.mult)], reads=[bsT, RP_T, t_T], writes=[t_T])
                              cx.run("dve", [lambda e, dst=dst, sl=sl: e.tensor_tensor(out=dst[:, sl], in0=t1, in1=t2, op=ALU.add)], reads=[t_T], writes=[qk_T, t_T])
                          else:
                              cx.run("act", [lambda e, bk=bk, dst=dst, sl=sl: e.activation(out=dst[:, sl], in_=bk, func=AF.Copy)], reads=[bkT], writes=[qk_T])
                  for c in range(12):
                      cs = slice(c * 128, (c + 1) * 128)
                      bk, bkT = psum()
                      cx.run("pe", [mm(bk[:, 0:256], hT[:, kc, cs], wv2[:, kc, :], kc == 0, kc == 7) for kc in range(8)], reads=[wT2, hT_T], writes=[bkT])
                      cx.run("act", [lambda e, bk=bk, c=c: e.activation(out=vv[:, c, :], in_=bk[:, 0:128], func=AF.Copy)], reads=[bkT], writes=[v_T])
                      cx.run("act", [lambda e, bk=bk, c=c: e.activation(out=gsil[:, c, :], in_=bk[:, 128:256], func=AF.Silu)], reads=[bkT], writes=[o_T])
                  for c in range(12):
                      cs = slice(c * 128, (c + 1) * 128)
                      bk, bkT = psum()
                      bkb = bk.bitcast(BF16)
                      cx.run("pe", [lambda e, bkb=bkb, cs=cs: e.transpose(out=bkb[:, 0:128], in_=kT[:, cs], identity=identB)], reads=[qk_T, G], writes=[bkT])
                      for d in range(2):
                          r = d * 4 + h
                          cx.run("act", [lambda e, bkb=bkb, d=d, c=c, r=r: e.activation(out=kz[d][:, c, :], in_=bkb[:, 0:128], func=AF.Copy, scale=rcol[:, r, 1:2])],
                                 reads=[bkT, G], writes=[kz_T[d][c]])
                  orders = [list(range(12)), list(range(11, -1, -1))]
                  for g in range(6):
                      d = g // 3
                      r = d * 4 + h
                      b2, b2T = psum()
                      css = [slice(orders[d][(g % 3) * 4 + i] * 128, (orders[d][(g % 3) * 4 + i] + 1) * 128) for i in range(4)]
                      cx.run("pe", [mm(b2[:, i * 128:(i + 1) * 128], kT[:, css[i]], qT[:, css[i]], True, True) for i in range(4)], reads=[qk_T], writes=[b2T])
                      cx.run("dve", [lambda e, b2=b2, g=g, i=i, r=r: e.tensor_tensor(out=sT_all[:, g * 512 + i * 128:g * 512 + (i + 1) * 128],
                                                                                   in0=b2[:, i * 128:(i + 1) * 128], in1=decT[:, r, :], op=ALU.mult) for i in range(4)],
                             reads=[b2T, G], writes=[sTa_T[g]])

                  def rrec(d, h=h):
                      r = d * 4 + h
                      cur = 0
                      Sxd, SxTd = Sx[d], Sx_T[d]
                      for step, c in enumerate(orders[d]):
                          s = c // 2
                          start = (c % 2 == 0) if d == 0 else (c % 2 == 1)
                          if start:
                              if (d == 0 and s == 0) or (d == 1 and s == 3):
                                  cx.dma("sp", Sxd[cur], Sinit_d[l, d, h], writes=[SxTd[cur]])
                              elif (d == 0 and s <= 3) or (d == 1 and s <= 2):
                                  cx.run("dve", [lambda e, cur=cur: e.tensor_scalar(out=Sxd[cur], in0=Sxd[cur], scalar1=chain[:, 0:1],
                                                                                     scalar2=None, op0=ALU.mult)], reads=[SxTd[cur], G], writes=[SxTd[cur]])
                              else:
                                  cx.run("dve", [lambda e, cur=cur: e.memset(Sxd[cur], 0.0)], writes=[SxTd[cur]])
                          cx.run("act", [lambda e, cur=cur, step=step: e.activation(out=Sb_all[d][:, step, :], in_=Sxd[cur], func=AF.Copy)],
                                 reads=[SxTd[cur]], writes=[SbA_T[d][step]])
                          bU, bUT = psum()
                          cx.run("pe", [mm(bU[:, 0:128], kz[d][:, c, :], vv[:, c, :], True, True)], reads=[kz_T[d][c], v_T], writes=[bUT])
                          yield
                          nxt = 1 - cur
                          cx.run("dve", [lambda e, cur=cur, nxt=nxt: e.scalar_tensor_tensor(
                              out=Sxd[nxt], in0=Sxd[cur], scalar=rcol[:, r, 2:3], in1=bU[:, 0:128], op0=ALU.mult, op1=ALU.add)],
                              reads=[bUT, SxTd[cur], G], writes=[SxTd[nxt]])
                          cur = nxt
                          end = (c % 2 == 1) if d == 0 else (c % 2 == 0)
                          if end:
                              cx.dma("sp", Sfin_d[l, s, d, h], Sxd[cur], reads=[SxTd[cur]])
                          yield

                  for _ in itertools.zip_longest(rrec(0), rrec(1)):
                      pass

                  for d in range(2):
                      r = d * 4 + h
                      for step, c in enumerate(orders[d]):
                          cs = slice(c * 128, (c + 1) * 128)
                          p = d * 12 + step
                          bA, bAT = psum()
                          cx.run("pe", [mm(bA[:, 0:128], sT_all[:, p * 128:(p + 1) * 128], vv[:, c, :], True, True),
                                        mm(bA[:, 256:384], qT[:, cs], Sb_all[d][:, step, :], True, True)],
                                 reads=[sTa_T[p // 4], v_T, qk_T, SbA_T[d][step]], writes=[bAT])
                          bp = step % 2
                          cx.run("act", [lambda e, bA=bA, bp=bp, r=r: e.activation(out=Bs2[bp], in_=bA[:, 256:384], func=AF.Copy, scale=rcol[:, r, 0:1])],
                                 reads=[bAT, G], writes=[Bs2_T[bp]])
                          if d == 0:
                              cx.run("dve", [lambda e, bA=bA, bp=bp, c=c: e.tensor_tensor(out=ysum[:, c, :], in0=bA[:, 0:128], in1=Bs2[bp], op=ALU.add)],
                                     reads=[bAT, Bs2_T[bp]], writes=[hs_T[c]])
                          else:
                              cx.run("dve", [lambda e, bA=bA, bp=bp: e.tensor_tensor(out=Bs2[bp], in0=bA[:, 0:128], in1=Bs2[bp], op=ALU.add)],
                                     reads=[bAT, Bs2_T[bp]], writes=[Bs2_T[bp]])
                              cx.run("dve", [lambda e, bp=bp, c=c: e.tensor_tensor(out=ysum[:, c, :], in0=ysum[:, c, :], in1=Bs2[bp], op=ALU.add)],
                                     reads=[Bs2_T[bp], hs_T[c]], writes=[hs_T[c]])
                  head_norm_out(ysum, hs_T, rnw, h, gsil, o_T, h_cT, hcT_T, t1, t_T, sT_all[:, 0:1536], sTa_T[0:3])
                  if l + 1 < L:
                      mod_groups(l + 1, [3 * h, 3 * h + 1, 3 * h + 2])
              dbg_out("hcT_l%d" % l, h_cT, [128, 4, T], [hcT_T])

              phase_end("ret%d" % l)
              aset(15360)
              macc = af32(4 * T).rearrange("p (a t) -> p a t", t=T)
              macc_T = [TT("macc%d" % i) for i in range(4)]
              sgm = [af32(512) for _ in range(2)]
              sgm_T = [TT("sgm0"), TT("sgm1")]
              k = 0
              branches = ((mow_d, h_aT, haT_T), (cow_d, ubT, ub_T), (row_d, h_cT, hcT_T))
              for jg in range(2):
                  for b, (wd, src, srcT) in enumerate(branches):
                      wo, woT = wload(wd[l][:, jg * 512:(jg + 1) * 512], 4, 512)
                      wgm, wgmT = wload(in_w_d[l][:, OFF_GM + b * 1024 + jg * 512:OFF_GM + b * 1024 + (jg + 1) * 512], 8, 512)
                      for jj in range(4):
                          j = jg * 4 + jj
                          for nb in range(3):
                              sl = slice(nb * 512, (nb + 1) * 512)
                              by, byT = psum()
                              cx.run("pe", [mm(by, wo[:, kc, jj * 128:(jj + 1) * 128], src[:, kc, sl], kc == 0, kc == 3) for kc in range(4)], reads=[woT, srcT], writes=[byT])
                              bg, bgT = psum()
                              cx.run("pe", [mm(bg, wgm[:, kc, jj * 128:(jj + 1) * 128], hT[:, kc, sl], kc == 0, kc == 7) for kc in range(8)], reads=[wgmT, hT_T], writes=[bgT])
                              pp = k % 2
                              k += 1
                              cx.run("act", [lambda e, bg=bg, pp=pp: e.activation(out=sgm[pp], in_=bg, func=AF.Sigmoid)], reads=[bgT], writes=[sgm_T[pp]])
                              if b == 0:
                                  cx.run("dve", [lambda e, by=by, pp=pp, sl=sl, jj=jj: e.tensor_tensor(out=macc[:, jj, sl], in0=by, in1=sgm[pp], op=ALU.mult)],
                                         reads=[byT, sgm_T[pp]], writes=[macc_T[jj]])
                              else:
                                  cx.run("dve", [lambda e, by=by, pp=pp: e.tensor_tensor(out=sgm[pp], in0=by, in1=sgm[pp], op=ALU.mult)],
                                         reads=[byT, sgm_T[pp]], writes=[sgm_T[pp]])
                                  if b == 1:
                                      cx.run("dve", [lambda e, pp=pp, sl=sl, jj=jj: e.tensor_tensor(out=macc[:, jj, sl], in0=macc[:, jj, sl], in1=sgm[pp], op=ALU.add)],
                                             reads=[sgm_T[pp], macc_T[jj]], writes=[macc_T[jj]])
                                  else:
                                      cx.run("dve", [lambda e, pp=pp, sl=sl, j=j, jj=jj: e.tensor_tensor(out=merged[:, j, sl], in0=macc[:, jj, sl], in1=sgm[pp], op=ALU.add)],
                                             reads=[sgm_T[pp], macc_T[jj]], writes=[mg_T])
              dbg_out("merged_l%d" % l, merged, [128, 8, T], [mg_T])
              phase_end("merge%d" % l)
              aset(15360)
              xb = abf(8 * 512).rearrange("p (a n) -> p a n", n=512)
              xq = abf(8 * 512).rearrange("p (a n) -> p a n", n=512)
              lnt = (af32(512), af32(512), af32(512))
              cx.run("act", [lambda e, kc=kc: e.activation(out=x[:, kc, :], in_=x[:, kc, :], func=AF.Copy, scale=ALPHA) for kc in range(8)], reads=[x_T], writes=[x_T])
              for jg in range(2):
                  wo, woT = wload(outw_d[l][:, jg * 512:(jg + 1) * 512], 8, 512)
                  for jj in range(4):
                      j = jg * 4 + jj
                      for nb in range(3):
                          sl = slice(nb * 512, (nb + 1) * 512)
                          v = 1 if nb < 2 else 0
                          bk, bkT = psum()
                          cx.run("pe", [mm(bk, wo[:, kc, jj * 128:(jj + 1) * 128], merged[:, kc, sl], kc == 0, kc == 7) for kc in range(8)], reads=[woT, mg_T], writes=[bkT])
                          cx.run("dve", [lambda e, bk=bk, j=j, sl=sl, v=v: e.scalar_tensor_tensor(out=x[:, j, sl], in0=bk, scalar=modT[:, 16 + j, v:v + 1],
                                                                                                 in1=x[:, j, sl], op0=ALU.mult, op1=ALU.add)],
                                 reads=[bkT, x_T, M_T[l]], writes=[x_T])
              layer_norm_fm(0, [(0, 512), (512, 1024), (1024, 1536)], xb, xq, lnt)
              dbg_out("x1_l%d" % l, x, [128, 8, T], [x_T])
              phase_end("outp%d" % l)
              aset(0)
              ffT = abf(NFT * T).rearrange("p (a n) -> p a n", n=T)
              ff_T = TT("ffT")
              a_sb = abf(4 * T).rearrange("p (a n) -> p a n", n=T)
              asb_T = TT("a_sb")
              sgf = [af32(512) for _ in range(2)]
              sgf_T = [TT("sgf0"), TT("sgf1")]
              modulate(sc2p, 24)
              cx.run("act", [lambda e, kc=kc: e.activation(out=x[:, kc, :], in_=x[:, kc, :], func=AF.Copy, scale=ALPHA) for kc in range(8)], reads=[x_T], writes=[x_T])
              k = 0
              for g in range(6):
                  nt = 4 if g < 5 else 2
                  wa, waT = wload(w13_d[l][:, g * 512:g * 512 + nt * 128], 8, nt * 128)
                  wg2, wg2T = wload(w13_d[l][:, FF + g * 512:FF + g * 512 + nt * 128], 8, nt * 128)
                  for jj in range(nt):
                      for nb in range(3):
                          sl = slice(nb * 512, (nb + 1) * 512)
                          bk, bkT = psum()
                          cx.run("pe", [mm(bk, wa[:, kc, jj * 128:(jj + 1) * 128], hT[:, kc, sl], kc == 0, kc == 7) for kc in range(8)],
                                 reads=[waT, hT_T], writes=[bkT])
                          cx.run("act", [lambda e, bk=bk, jj=jj, sl=sl: e.activation(out=a_sb[:, jj, sl], in_=bk, func=AF.Copy)],
                                 reads=[bkT], writes=[asb_T])
                  for jj in range(nt):
                      for nb in range(3):
                          sl = slice(nb * 512, (nb + 1) * 512)
                          bk, bkT = psum()
                          cx.run("pe", [mm(bk, wg2[:, kc, jj * 128:(jj + 1) * 128], hT[:, kc, sl], kc == 0, kc == 7) for kc in range(8)],
                                 reads=[wg2T, hT_T], writes=[bkT])
                          pp = k % 2
                          k += 1
                          cx.run("act", [lambda e, bk=bk, pp=pp: e.activation(out=sgf[pp], in_=bk, func=AF.Silu)], reads=[bkT], writes=[sgf_T[pp]])
                          cx.run("dve", [lambda e, pp=pp, jj=jj, g=g, sl=sl: e.tensor_tensor(out=ffT[:, g * 4 + jj, sl], in0=sgf[pp], in1=a_sb[:, jj, sl], op=ALU.mult)],
                                 reads=[sgf_T[pp], asb_T], writes=[ff_T])
              for j in range(8):
                  w2v, w2T = wload(w2_d[l][:, j * 128:(j + 1) * 128], NFT, 128)
                  for nb in range(3):
                      sl = slice(nb * 512, (nb + 1) * 512)
                      v = 1 if nb < 2 else 0
                      bk, bkT = psum()
                      cx.run("pe", [mm(bk, w2v[:, kc, :], ffT[:, kc, sl], kc == 0, kc == NFT - 1) for kc in range(NFT)], reads=[w2T, ff_T], writes=[bkT])
                      cx.run("dve", [lambda e, bk=bk, j=j, sl=sl, v=v: e.scalar_tensor_tensor(
                          out=x[:, j, sl], in0=bk, scalar=modT[:, 40 + j, v:v + 1], in1=x[:, j, sl], op0=ALU.mult, op1=ALU.add)],
                          reads=[bkT, x_T, M_T[l]], writes=[x_T])
              aset(0)
              xb = abf(8 * 512).rearrange("p (a n) -> p a n", n=512)
              xq = abf(8 * 512).rearrange("p (a n) -> p a n", n=512)
              lnt = (af32(512), af32(512), af32(512))
              layer_norm_fm(1, [(0, 512), (512, 1024), (1024, 1536)], xb, xq, lnt)
              dbg_out("x2_l%d" % l, x, [128, 8, T], [x_T])

        try:
            layers()
        except _Stop:
            pass
        for kc in range(8):
            cx.dma("sp", yT_d[:, kc, :], x[:, kc, :], reads=[x_T])
        cx.final()

        with nc.Block() as block:
            def replay(name):
                def f(e):
                    for th in cx.prog[name]:
                        th(e)
                return f
            block.tensor(replay("pe"))
            block.scalar(replay("act"))
            block.vector(replay("dve"))
            block.gpsimd(replay("pool"))
            block.sync(replay("sp"))
    return nc


_IDX = np.concatenate([np.arange(0, 128, 2), np.arange(1, 128, 2)])
_IDXS = np.concatenate([np.arange(1, 128, 2), np.arange(0, 128, 2)])


def _prow(r):
    return (r // 4) * 32 + (r % 4)


def _in_cols():
    o = dict(mq=0, mk=512, mv=1024, mo=1536, mg=2048, ca=2064, cg=2576, rq=3088, rk=3600, rv=4112, rg=4624, gm=5136)
    cols = []
    for h in range(H):
        for nm in ("mq", "mk", "mv", "mo"):
            cols += list(range(o[nm] + h * 128, o[nm] + (h + 1) * 128))
    z = [-1] * 28
    mg = o["mg"]
    cols += [mg + 0 * 4 + h for h in range(4)] + z + [mg + 2 * 4 + h for h in range(4)]
    cols += [mg + 1 * 4 + h for h in range(4)] + z + [mg + 3 * 4 + h for h in range(4)]
    cols += list(range(o["ca"], o["ca"] + 512)) + list(range(o["cg"], o["cg"] + 512))
    for h in range(H):
        for nm in ("rq", "rk"):
            cols += list(o[nm] + h * 128 + _IDX) + list(o[nm] + h * 128 + _IDXS)
    for h in range(H):
        cols += list(range(o["rv"] + h * 128, o["rv"] + (h + 1) * 128)) + list(range(o["rg"] + h * 128, o["rg"] + (h + 1) * 128))
    cols += list(range(o["gm"], o["gm"] + 3072))
    cols = np.array(cols, dtype=np.int64)
    assert cols.shape[0] == NCOLS, cols.shape
    return cols


_NC_CACHE = {}


def kernel(x_prompt, x_sample, state_mlstm_C, state_mlstm_n, state_mlstm_m, state_ret_S, c, c_ctx,
           ada_w, ada_b, in_w, mlstm_gate_b, mlstm_norm_w, mlstm_out_w, conv_w, conv_b, conv_ln_w, conv_ln_b,
           conv_out_w, ret_decay, ret_norm_w, ret_out_w, out_w, ln1_w, ln1_b, ln2_w, ln2_b, ffn_w13, ffn_w2, _dbg=False):
    f32 = np.float32
    A = lambda a: np.ascontiguousarray(np.asarray(a, dtype=f32))
    x_prompt, x_sample = A(x_prompt), A(x_sample)
    sC, sn, smm, sS = A(state_mlstm_C), A(state_mlstm_n), A(state_mlstm_m), A(state_ret_S)
    c, c_ctx = A(c), A(c_ctx)
    in_w = A(in_w)
    cols = _in_cols()
    in_wp = np.zeros((L, D, NCOLS), f32)
    valid = cols >= 0
    in_wp[:, :, valid] = in_w[:, :, cols[valid]]
    gbias = A(mlstm_gate_b)
    gb = np.zeros((L, 36, 2), f32)
    for h in range(4):
        gb[:, h, 0] = gbias[:, 0, h]; gb[:, h, 1] = gbias[:, 1, h]
        gb[:, 32 + h, 0] = gbias[:, 2, h]; gb[:, 32 + h, 1] = gbias[:, 3, h]
    convp = np.zeros((L, 128, 4, 34), f32)
    cw = A(conv_w)
    convp[:, :, :, 0:31] = cw.reshape(L, CONV_K, 4, 128).transpose(0, 3, 2, 1)
    convp[:, :, :, 31] = A(conv_b).reshape(L, 4, 128).transpose(0, 2, 1)
    convp[:, :, :, 32] = A(conv_ln_w).reshape(L, 4, 128).transpose(0, 2, 1)
    convp[:, :, :, 33] = A(conv_ln_b).reshape(L, 4, 128).transpose(0, 2, 1)
    lnp = np.zeros((L, 128, 4, 8), f32)
    for i, a in enumerate((ln1_w, ln1_b, ln2_w, ln2_b)):
        lnp[:, :, i, :] = A(a).reshape(L, 8, 128).transpose(0, 2, 1)
    ada_bT = np.ascontiguousarray(A(ada_b).reshape(L, 48, 128).transpose(0, 2, 1))
    rdec = A(ret_decay).reshape(L, 1, 8)
    mnw = A(mlstm_norm_w).reshape(L, 1, 512)
    rnw = A(ret_norm_w).reshape(L, 1, 512)
    cst = np.zeros((128, 1024), f32)
    jj, ii = np.meshgrid(np.arange(128), np.arange(128), indexing="ij")
    cst[:, 0:128] = np.eye(128, dtype=f32)
    cst[:, 128:256] = np.where(jj <= ii, 0.0, NEG)
    cst[:, 256:384] = np.where(jj >= ii, 0.0, NEG)
    cst[:, 384:512] = np.maximum(ii - jj, 0)
    cst[:, 512:640] = np.maximum(jj - ii, 0)
    cst[:, 640:768] = (jj <= ii)
    cst[:, 768:896] = (jj >= ii)
    p = np.arange(128)
    cst[:, 896] = p + 1; cst[:, 897] = 128 - p; cst[:, 898] = 127 - p; cst[:, 899] = p; cst[:, 900] = 128
    sel = np.zeros((36, 8, 128), f32)
    for r in range(8):
        sel[_prow(r), r, :] = 1.0
    t = np.arange(1024)
    rows = (t // 64).astype(f32); colsg = (t % 64).astype(f32)
    freqs = (np.float32(10000.0) ** (-np.arange(32, dtype=f32) / np.float32(32))).astype(f32)
    ang = np.concatenate([rows[:, None] * freqs[None, :], colsg[:, None] * freqs[None, :]], -1).astype(f32)
    cs_lat = np.concatenate([np.cos(ang).T, np.cos(ang).T], 0).astype(f32)
    sn_lat = np.concatenate([-np.sin(ang).T, np.sin(ang).T], 0).astype(f32)
    cs_id = np.ones((128, 1024), f32); sn_id = np.zeros((128, 1024), f32)

    shared = dict(cst=cst, sel=sel, ada_w=A(ada_w), ada_bT=ada_bT, in_wp=in_wp, gb=gb, mnw=mnw, rnw=rnw,
                  mlstm_out_w=A(mlstm_out_w), conv_out_w=A(conv_out_w), ret_out_w=A(ret_out_w), convp=convp, rdec=rdec,
                  out_w=A(out_w), lnp=lnp, ffn_w13=A(ffn_w13), ffn_w2=A(ffn_w2))
    in_maps = []
    seg_prompt = []
    for core in range(8):
        if core < 4:
            xs = np.concatenate([x_sample[core], x_prompt[2 * core], x_prompt[2 * core + 1]], 0)
            seg_prompt.append({4: 2 * core, 5: 2 * core + 1})
            cvec = c[core]
            Cinit = np.concatenate([sC[core], sn[core][..., None]], -1)
            minit = np.zeros((L, 36, 1), f32)
            for h in range(4):
                minit[:, h, 0] = smm[core, :, 0, h]; minit[:, 32 + h, 0] = smm[core, :, 1, h]
            Sinit = sS[core][:, :, :, _IDX, :]
            chainv = 1.0
            rcs, rsn = cs_lat, sn_lat
        else:
            base = 8 + 6 * (core - 4)
            xs = np.concatenate([x_prompt[base + s] for s in range(6)], 0)
            seg_prompt.append({s: base + s for s in range(6)})
            cvec = c_ctx
            Cinit = np.zeros((L, 2, H, 128, 129), f32)
            minit = np.zeros((L, 36, 1), f32)
            Sinit = np.zeros((L, 2, H, 128, 128), f32)
            chainv = 0.0
            rcs, rsn = cs_id, sn_id
        xT = np.ascontiguousarray(xs.reshape(T, 8, 128).transpose(2, 1, 0))
        cv = np.stack([c_ctx.reshape(8, 128).T, cvec.reshape(8, 128).T], -1)
        m = dict(shared)
        m.update(xT=xT, cv=np.ascontiguousarray(cv, dtype=f32), chain=np.full((128, 1), chainv, f32),
                 Cinit=np.ascontiguousarray(Cinit, dtype=f32), minit=minit, Sinit=np.ascontiguousarray(Sinit, dtype=f32),
                 ropeCS=rcs, ropeSN=rsn)
        in_maps.append(m)

    key = bool(_dbg)
    nc = build(dbg=key)
    res = run_bass_kernel_spmd(nc, in_maps, core_ids=list(range(8)))
    R = res.results
    y_prompt = np.zeros((32, 256, D), f32)
    y_sample = np.zeros((4, 1024, D), f32)
    new_C = np.zeros((32, L, 2, H, 128, 128), f32)
    new_n = np.zeros((32, L, 2, H, 128), f32)
    new_m = np.zeros((32, L, 2, H), f32)
    new_S = np.zeros((32, L, 2, H, 128, 128), f32)
    for core in range(8):
        r = R[core]
        yt = np.asarray(r["yT"]).transpose(2, 1, 0).reshape(T, D)
        if core < 4:
            y_sample[core] = yt[0:1024]
        Cf = np.asarray(r["Cfin"]); mf = np.asarray(r["mfin"]); Sf = np.asarray(r["Sfin"])
        for s, b in seg_prompt[core].items():
            y_prompt[b] = yt[s * 256:(s + 1) * 256]
            new_C[b] = Cf[:, s, :, :, :, 0:128]
            new_n[b] = Cf[:, s, :, :, :, 128]
            for h in range(4):
                new_m[b, :, 0, h] = mf[:, h, 2 * s + 1]
                new_m[b, :, 1, h] = mf[:, 32 + h, 2 * s]
            Su = np.empty((L, 2, H, 128, 128), f32)
            Su[:, :, :, _IDX, :] = Sf[:, s]
            new_S[b] = Su
    if _dbg:
        return (y_prompt, y_sample, new_C, new_n, new_m, new_S), R
    return (y_prompt, y_sample, new_C, new_n, new_m, new_S)
```

```python
import math
import itertools
import numpy as np
import concourse.bass as bass
import concourse.mybir as mybir
from concourse.bass_utils import run_bass_kernel_spmd
from concourse.ap import AP

F32 = mybir.dt.float32
BF16 = mybir.dt.bfloat16
AF = mybir.ActivationFunctionType
ALU = mybir.AluOpType
AX = mybir.AxisListType

D = 1024
L = 2
T = 1536
NCH = 12
NSEG = 6
H = 4
HD = 128
FF = 2816
NFT = 22
CONV_K = 31
EPS = 1e-5
ALPHA = (2.0 * L) ** 0.25
KS = HD ** -0.5
LNKS = math.log(KS)
NEG = -30000.0
NCOLS = 9288
OFF_GATE = 2048
OFF_CA = 2120
OFF_CG = 2632
OFF_RET = 3144
OFF_RVG = 5192
OFF_GM = 6216

DEBUG = {}
STOP = None


class _Stop(Exception):
    pass


def phase_end(name):
    if STOP == name:
        raise _Stop()


def rev(ap):
    a = [list(x) for x in ap.ap]
    step, n = a[-1]
    off = ap.offset + step * (n - 1)
    a[-1] = [-step, n]
    return AP(ap.tensor, off, a)


import types


def _snap(th):
    cl = th.__closure__
    if not cl:
        return th
    cells = []
    for c in cl:
        try:
            cells.append(types.CellType(c.cell_contents))
        except ValueError:
            cells.append(c)
    return types.FunctionType(th.__code__, th.__globals__, th.__name__, th.__defaults__, tuple(cells))


class TT:
    __slots__ = ("name", "w", "r")

    def __init__(self, name=""):
        self.name = name
        self.w = None
        self.r = {}


class Ctx:
    ENG = ["pe", "act", "dve", "pool", "sp"]

    def __init__(self, nc, sems):
        self.nc = nc
        self.sems = sems
        self.prog = {e: [] for e in self.ENG}
        self.cnt = {e: 0 for e in self.ENG}
        self.seen = {e: {} for e in self.ENG}
        self.dkeys = [k for k in sems if k[0] == "d" and k[1:].isdigit()]
        self.wkeys = [k for k in sems if k[0] == "w" and k[1:].isdigit()]
        self.dtot = {k: 0 for k in sems}
        self.dn = 0
        self.wn = 0

    def _wait(self, eng, key, val):
        if val <= 0 or self.seen[eng].get(key, 0) >= val:
            return
        self.seen[eng][key] = val
        sem = self.sems[key]
        self.prog[eng].append(lambda e, sem=sem, val=val: e.wait_ge(sem, val))

    def _deps(self, eng, reads, writes):
        for t in reads:
            if t.w is not None:
                self._wait_dep(eng, t.w)
        for t in writes:
            if t.w is not None:
                self._wait_dep(eng, t.w)
            for k, v in t.r.items():
                self._wait_dep(eng, (k, v))

    def _wait_dep(self, eng, dep):
        k, v = dep
        if k == "pe" and eng == "pe":
            return
        self._wait(eng, k, v)

    def run(self, eng, thunks, reads=(), writes=()):
        if not isinstance(thunks, (list, tuple)):
            thunks = [thunks]
        thunks = [_snap(t) for t in thunks]
        self._deps(eng, reads, writes)
        sem = self.sems[eng]
        n = len(thunks)
        for i, th in enumerate(thunks):
            if i == n - 1:
                self.prog[eng].append(lambda e, th=th, sem=sem: th(e).then_inc(sem, 1))
            else:
                self.prog[eng].append(th)
        self.cnt[eng] += 1
        c = self.cnt[eng]
        for t in reads:
            t.r[eng] = c
        for t in writes:
            t.w = (eng, c)
            t.r = {}

    def dma(self, q, out, in_, reads=(), writes=()):
        if q == "pool":
            key = self.wkeys[self.wn % len(self.wkeys)]
            self.wn += 1
        else:
            key = self.dkeys[self.dn % len(self.dkeys)]
            self.dn += 1
        prev = self.dtot[key]
        self._wait(q, key, prev)
        self._deps(q, reads, writes)
        new = prev + 16
        self.dtot[key] = new
        sem = self.sems[key]
        self.prog[q].append(lambda e, out=out, in_=in_, sem=sem: e.dma_start(out=out, in_=in_).then_inc(sem, 16))
        for t in reads:
            t.r[key] = new
        for t in writes:
            t.w = (key, new)
            t.r = {}

    def barrier(self):
        engs = ["pe", "act", "dve", "sp"]
        for e in engs:
            for f in ["pe", "act", "dve"]:
                if f != e or e != "pe":
                    self._wait(e, f, self.cnt[f])
            for k in self.dkeys:
                self._wait(e, k, self.dtot[k])

    def final(self):
        for k in self.dkeys:
            self._wait("sp", k, self.dtot[k])
        for f in ["pe", "act", "dve"]:
            self._wait("sp", f, self.cnt[f])


def build(dbg=False):
    nc = bass.Bass("TRN2", target_bir_lowering=False)
    dram = {}

    def din(name, shape, dt=F32):
        dram[name] = nc.dram_tensor(name, list(shape), dt, kind="ExternalInput").ap()
        return dram[name]

    def dout(name, shape, dt=F32):
        dram[name] = nc.dram_tensor(name, list(shape), dt, kind="ExternalOutput").ap()
        return dram[name]

    xT_d = din("xT", [128, 8, T])
    cv_d = din("cv", [128, 8, 2])
    chain_d = din("chain", [128, 1])
    Cinit_d = din("Cinit", [L, 2, H, 128, 129])
    minit_d = din("minit", [L, 36, 1])
    Sinit_d = din("Sinit", [L, 2, H, 128, 128])
    ropeCS_d = din("ropeCS", [128, 1024])
    ropeSN_d = din("ropeSN", [128, 1024])
    cst_d = din("cst", [128, 1024])
    sel_d = din("sel", [36, 8, 128])
    ada_w_d = din("ada_w", [L, D, 6 * D])
    ada_b_d = din("ada_bT", [L, 128, 48])
    in_w_d = din("in_wp", [L, D, NCOLS])
    gb_d = din("gb", [L, 36, 2])
    mnw_d = din("mnw", [L, 1, 512])
    rnw_d = din("rnw", [L, 1, 512])
    mow_d = din("mlstm_out_w", [L, 512, D])
    cow_d = din("conv_out_w", [L, 512, D])
    row_d = din("ret_out_w", [L, 512, D])
    convp_d = din("convp", [L, 128, 4, 34])
    rdec_d = din("rdec", [L, 1, 8])
    outw_d = din("out_w", [L, D, D])
    lnp_d = din("lnp", [L, 128, 4, 8])
    w13_d = din("ffn_w13", [L, D, 2 * FF])
    w2_d = din("ffn_w2", [L, FF, D])

    yT_d = dout("yT", [128, 8, T])
    Cfin_d = dout("Cfin", [L, NSEG, 2, H, 128, 129])
    mfin_d = dout("mfin", [L, 36, 12])
    Sfin_d = dout("Sfin", [L, NSEG, 2, H, 128, 128])
    dbg_d = {}

    import contextlib
    es = contextlib.ExitStack()
    with es:
        def sb(name, shape, dt=F32):
            return es.enter_context(nc.sbuf_tensor("s_" + name, list(shape), dt))[:]

        x = sb("x", [128, 8, T])
        hT = sb("hT", [128, 8, T], BF16)
        NW = 3
        Wr = [sb("wr%d" % i, [128, 4096], BF16) for i in range(NW)]
        Wr_T = [TT("wr%d" % i) for i in range(NW)]
        cst = sb("cst", [128, 1024])
        sel = sb("sel", [36, 8, 128])
        identB = sb("identB", [128, 128], BF16)
        onesB = sb("onesB", [128, 128], BF16)
        chain = sb("chain", [128, 1])
        cvs = sb("cvs", [128, 8, 2], BF16)
        cvf = sb("cvf", [128, 8, 2])
        modTs = [sb("modT%d" % i, [128, 48, 2]) for i in range(L)]
        sc1ps = [sb("sc1p%d" % i, [128, 8, 2]) for i in range(L)]
        sc2ps = [sb("sc2p%d" % i, [128, 8, 2]) for i in range(L)]
        adabs = [sb("adab%d" % i, [128, 48]) for i in range(L)]
        M_T = [TT("mod%d" % i) for i in range(L)]
        gb = sb("gb", [36, 2])
        ngb = sb("ngb", [36, 1])
        minit = sb("minit", [36, 1])
        mnw = sb("mnw", [128, 512])
        rnw = sb("rnw", [128, 512])
        convp = sb("convp", [128, 4, 34])
        lnp = sb("lnp", [128, 4, 8])
        rdec = sb("rdec", [128, 8])
        lg = sb("lg", [128, 8])
        rcol = sb("rcol", [128, 8, 4])
        decT = sb("decT", [128, 8, 128])
        ones36 = sb("ones36", [36, 128])
        zeros36 = sb("zeros36", [36, 128])
        AR = 23168
        arena = sb("arena", [128, AR])
        ps = [es.enter_context(nc.psum_tensor("ps%d" % i, [128, 512], F32))[:] for i in range(8)]
        ps_T = [TT("ps%d" % i) for i in range(8)]

        keys = ["pe", "act", "dve", "pool", "sp"] + ["d%d" % i for i in range(8)] + ["w%d" % i for i in range(6)]
        sems = {k: es.enter_context(nc.semaphore(k)) for k in keys}
        cx = Ctx(nc, sems)
        G = TT("globals")
        x_T = TT("x")
        hT_T = TT("hT")

        identF = cst[:, 0:128]
        MASK = [cst[:, 128:256], cst[:, 256:384]]
        DIFF = [cst[:, 384:512], cst[:, 512:640]]
        M01 = [cst[:, 640:768], cst[:, 768:896]]
        POS = cst[:, 896:904]

        pstate = {"i": 0}

        def psum():
            i = pstate["i"] % 8
            pstate["i"] += 1
            return ps[i], ps_T[i]

        wstate = {"i": 0}

        def wload(src, kc, ncol):
            i = wstate["i"] % NW
            wstate["i"] += 1
            dst = Wr[i][:, 0:kc * ncol].rearrange("p (k n) -> p k n", n=ncol)
            cx.dma("pool", dst, src.rearrange("(k p) n -> p k n", p=128), writes=[Wr_T[i]])
            return dst, Wr_T[i]

        ast = {"o": 0}

        def aset(o):
            cx.barrier()
            ast["o"] = o

        def af32(n, shape=None):
            o = ast["o"]
            ast["o"] += n
            assert ast["o"] <= AR, ast["o"]
            v = arena[:, o:o + n]
            return v

        def abf(n):
            n2 = (n + 1) // 2
            return af32(n2).bitcast(BF16)

        def dbg_out(name, ap, shape, reads):
            if not dbg:
                return
            d = dout("dbg_" + name, shape, ap.dtype)
            DEBUG[name] = shape
            cx.dma("sp", d, ap, reads=reads)

        mm = lambda out, lhsT, rhs, st, sp: (lambda e: e.matmul(out, lhsT=lhsT, rhs=rhs, start=st, stop=sp))

        for kc in range(8):
            cx.dma("sp", x[:, kc, :], xT_d[:, kc, :], writes=[x_T])
        for dst, src in ((cst, cst_d), (sel, sel_d), (chain, chain_d), (cvf, cv_d)):
            cx.dma("sp", dst, src, writes=[G])
        cx.run("dve", [lambda e: e.memset(ones36, 1.0), lambda e: e.memset(zeros36, 0.0),
                       lambda e: e.memset(onesB, 1.0),
                       lambda e: e.tensor_copy(out=identB, in_=identF)],
               reads=[G], writes=[G])
        cx.run("act", [lambda e: e.activation(out=cvs, in_=cvf, func=AF.Silu)], reads=[G], writes=[G])
        for i in range(L):
            cx.dma("sp", adabs[i], ada_b_d[i], writes=[M_T[i]])

        def mod_groups(ll, gs):
            mT = modTs[ll]
            for g in gs:
                wv, wT = wload(ada_w_d[ll][:, g * 512:(g + 1) * 512], 8, 512)
                mb, mbT = psum()
                for jj in range(4):
                    cx.run("pe", [mm(mb[:, 2 * jj:2 * jj + 2], wv[:, kc, jj * 128:(jj + 1) * 128], cvs[:, kc, :], kc == 0, kc == 7) for kc in range(8)],
                           reads=[wT, G], writes=[mbT])
                j0 = g * 4
                cx.run("act", [lambda e, mb=mb, j0=j0: e.activation(out=mT[:, j0:j0 + 4, :].rearrange("p j v -> p (j v)"), in_=mb[:, 0:8], func=AF.Copy)],
                       reads=[mbT], writes=[M_T[ll]])
                cx.run("dve", [lambda e, j0=j0: e.tensor_tensor(out=mT[:, j0:j0 + 4, :], in0=mT[:, j0:j0 + 4, :],
                                                                in1=adabs[ll][:, j0:j0 + 4].unsqueeze(2).to_broadcast([128, 4, 2]), op=ALU.add)],
                       reads=[M_T[ll]], writes=[M_T[ll]])
                if g in (2, 3):
                    o = (g - 2) * 4
                    cx.run("dve", [lambda e, j0=j0, o=o: e.tensor_scalar(out=sc1ps[ll][:, o:o + 4, :], in0=mT[:, j0:j0 + 4, :], scalar1=1.0, scalar2=None, op0=ALU.add)],
                           reads=[M_T[ll]], writes=[M_T[ll]])
                if g in (8, 9):
                    o = (g - 8) * 4
                    cx.run("dve", [lambda e, j0=j0, o=o: e.tensor_scalar(out=sc2ps[ll][:, o:o + 4, :], in0=mT[:, j0:j0 + 4, :], scalar1=1.0, scalar2=None, op0=ALU.add)],
                           reads=[M_T[ll]], writes=[M_T[ll]])

        def layer_norm_fm(which, blocks, xb, xq, lnt):
            lw = lnp[:, 2 * which, :]
            lb = lnp[:, 2 * which + 1, :]
            xb_T, xq_T, lnt_T = TT("xb"), TT("xq"), TT("lnt")
            for (a_, b_) in blocks:
                sl = slice(a_, b_)
                cx.run("act", [lambda e, kc=kc: e.activation(out=xb[:, kc, :], in_=x[:, kc, sl], func=AF.Copy) for kc in range(8)],
                       reads=[x_T], writes=[xb_T])
                cx.run("act", [lambda e, kc=kc: e.activation(out=xq[:, kc, :], in_=x[:, kc, sl], func=AF.Square) for kc in range(8)],
                       reads=[x_T], writes=[xq_T])
                b1, b1T = psum()
                cx.run("pe", [mm(b1, onesB, xb[:, kc, :], kc == 0, kc == 7) for kc in range(8)], reads=[xb_T], writes=[b1T])
                b2, b2T = psum()
                cx.run("pe", [mm(b2, onesB, xq[:, kc, :], kc == 0, kc == 7) for kc in range(8)], reads=[xq_T], writes=[b2T])
                mean, msq, rstd = lnt
                cx.run("act", [lambda e: e.activation(out=mean, in_=b1, func=AF.Copy, scale=1.0 / D)], reads=[b1T], writes=[lnt_T])
                cx.run("dve", [lambda e: e.tensor_tensor(out=msq, in0=mean, in1=mean, op=ALU.mult)], reads=[lnt_T], writes=[lnt_T])
                cx.run("dve", [lambda e: e.scalar_tensor_tensor(out=msq, in0=b2, scalar=1.0 / D, in1=msq, op0=ALU.mult, op1=ALU.subtract)],
                       reads=[b2T, lnt_T], writes=[lnt_T])
                cx.run("dve", [lambda e: e.tensor_scalar(out=msq, in0=msq, scalar1=EPS, scalar2=None, op0=ALU.add)], reads=[lnt_T], writes=[lnt_T])
                cx.run("act", [lambda e: e.activation(out=rstd, in_=msq, func=AF.Sqrt)], reads=[lnt_T], writes=[lnt_T])
                cx.run("dve", [lambda e: e.reciprocal(out=rstd, in_=rstd)], reads=[lnt_T], writes=[lnt_T])
                cx.run("dve", [lambda e, kc=kc: e.tensor_tensor(out=x[:, kc, sl], in0=x[:, kc, sl], in1=mean, op=ALU.subtract) for kc in range(8)],
                       reads=[x_T, lnt_T], writes=[x_T])
                cx.run("dve", [lambda e, kc=kc: e.tensor_tensor(out=x[:, kc, sl], in0=x[:, kc, sl], in1=rstd, op=ALU.mult) for kc in range(8)],
                       reads=[x_T, lnt_T], writes=[x_T])
                cx.run("act", [lambda e, kc=kc: e.activation(out=x[:, kc, sl], in_=x[:, kc, sl], func=AF.Identity,
                                                             scale=lw[:, kc:kc + 1], bias=lb[:, kc:kc + 1]) for kc in range(8)],
                       reads=[x_T, G], writes=[x_T])

        lnks = sb("lnks", [128, 1])
        cx.run("dve", [lambda e: e.memset(lnks, LNKS)], writes=[G])

        def ln_stats(b1, b1T, b2, b2T, mean, msq, rstd, st_T, n):
            cx.run("act", [lambda e: e.activation(out=mean, in_=b1, func=AF.Copy, scale=1.0 / n)], reads=[b1T], writes=[st_T])
            cx.run("dve", [lambda e: e.tensor_tensor(out=msq, in0=mean, in1=mean, op=ALU.mult)], reads=[st_T], writes=[st_T])
            cx.run("dve", [lambda e: e.scalar_tensor_tensor(out=msq, in0=b2, scalar=1.0 / n, in1=msq, op0=ALU.mult, op1=ALU.subtract)],
                   reads=[b2T, st_T], writes=[st_T])
            cx.run("dve", [lambda e: e.tensor_scalar(out=msq, in0=msq, scalar1=EPS, scalar2=None, op0=ALU.add)], reads=[st_T], writes=[st_T])
            cx.run("act", [lambda e: e.activation(out=rstd, in_=msq, func=AF.Sqrt)], reads=[st_T], writes=[st_T])
            cx.run("dve", [lambda e: e.reciprocal(out=rstd, in_=rstd)], reads=[st_T], writes=[st_T])

        def head_norm_out(src3, src_T, normw, h, gate3, gate_T, dstT, dst_T, scr, scr_T, hn_all, hn_Ts):
            stA = scr[:, 0:72].rearrange("p (c n) -> p c n", n=6)
            mvA = scr[:, 72:96].rearrange("p (c n) -> p c n", n=2)
            src_all = src3
            hn3 = hn_all.rearrange("p (c n) -> p c n", n=128)
            cx.run("dve", [lambda e, c=c: e.bn_stats(out=stA[:, c, :], in_=src3[:, c, :]) for c in range(12)], reads=list(src_T) + [scr_T], writes=[scr_T])
            cx.run("dve", [lambda e, c=c: e.bn_aggr(out=mvA[:, c, :], in_=stA[:, c, :]) for c in range(12)], reads=[scr_T], writes=[scr_T])
            rstd = mvA[:, :, 1]
            cx.run("dve", [lambda e: e.tensor_scalar(out=rstd, in0=rstd, scalar1=EPS, scalar2=None, op0=ALU.add)], reads=[scr_T], writes=[scr_T])
            cx.run("act", [lambda e: e.activation(out=rstd, in_=rstd, func=AF.Sqrt)], reads=[scr_T], writes=[scr_T])
            cx.run("dve", [lambda e: e.reciprocal(out=rstd, in_=rstd)], reads=[scr_T], writes=[scr_T])
            cx.run("dve", [lambda e, c=c: e.tensor_scalar(out=src3[:, c, :], in0=src3[:, c, :], scalar1=mvA[:, c, 0:1], scalar2=mvA[:, c, 1:2],
                                                          op0=ALU.subtract, op1=ALU.mult) for c in range(12)], reads=list(src_T) + [scr_T], writes=list(src_T))
            cx.run("dve", [lambda e, c=c: e.tensor_tensor(out=src3[:, c, :], in0=src3[:, c, :], in1=normw[:, h * 128:(h + 1) * 128], op=ALU.mult) for c in range(12)],
                   reads=list(src_T) + [G], writes=list(src_T))
            cx.run("dve", [lambda e: e.tensor_tensor(out=hn3, in0=src3, in1=gate3, op=ALU.mult)], reads=list(src_T) + [gate_T] + list(hn_Ts), writes=list(hn_Ts))
            for c in range(12):
                bk, bkT = psum()
                bkb = bk.bitcast(BF16)
                cx.run("pe", [lambda e, bkb=bkb, c=c: e.transpose(out=bkb[:, 0:128], in_=hn3[:, c, :], identity=identB)], reads=list(hn_Ts) + [G], writes=[bkT])
                cx.run("act", [lambda e, bkb=bkb, c=c: e.activation(out=dstT[:, h, c * 128:(c + 1) * 128], in_=bkb[:, 0:128], func=AF.Copy)], reads=[bkT], writes=[dst_T])

        def layers():
          for l in range(L):
              phase_end('pro%d' % l)
              aset(0)
              for dst, src in ((gb, gb_d[l]), (minit, minit_d[l]), (convp, convp_d[l]), (lnp, lnp_d[l]),
                               (mnw, mnw_d[l].partition_broadcast(128)), (rnw, rnw_d[l].partition_broadcast(128)),
                               (rdec, rdec_d[l].partition_broadcast(128))):
                  cx.dma("sp", dst, src, writes=[G])
              cx.run("dve", [lambda e: e.tensor_scalar(out=ngb, in0=gb[:, 1:2], scalar1=-1.0, scalar2=None, op0=ALU.mult)], reads=[G], writes=[G])
              cx.run("act", [lambda e: e.activation(out=lg, in_=rdec, func=AF.Exp, scale=-1.0)], reads=[G], writes=[G])
              cx.run("act", [lambda e: e.activation(out=lg, in_=lg, func=AF.Ln, bias=1.0)], reads=[G], writes=[G])
              cx.run("dve", [lambda e: e.tensor_scalar(out=lg, in0=lg, scalar1=-1.0, scalar2=None, op0=ALU.mult)], reads=[G], writes=[G])
              for r in range(8):
                  d = r // 4
                  lgc = lg[:, r:r + 1]
                  cx.run("act", [lambda e, r=r, d=d, lgc=lgc: e.activation(out=decT[:, r, :], in_=DIFF[d], func=AF.Exp, scale=lgc)], reads=[G], writes=[G])
                  cx.run("dve", [lambda e, r=r, d=d: e.scalar_tensor_tensor(out=decT[:, r, :], in0=decT[:, r, :], scalar=KS, in1=M01[d],
                                                                           op0=ALU.mult, op1=ALU.mult)], reads=[G], writes=[G])
                  cx.run("act", [lambda e, r=r, d=d, lgc=lgc: e.activation(out=rcol[:, r, 0:1], in_=POS[:, d:d + 1], func=AF.Exp, scale=lgc),
                                 lambda e, r=r, d=d, lgc=lgc: e.activation(out=rcol[:, r, 1:2], in_=POS[:, 2 + d:3 + d], func=AF.Exp, scale=lgc),
                                 lambda e, r=r, d=d, lgc=lgc: e.activation(out=rcol[:, r, 2:3], in_=POS[:, 4:5], func=AF.Exp, scale=lgc)],
                         reads=[G], writes=[G])
                  cx.run("dve", [lambda e, r=r: e.tensor_scalar(out=rcol[:, r, 1:2], in0=rcol[:, r, 1:2], scalar1=KS, scalar2=None, op0=ALU.mult)],
                         reads=[G], writes=[G])

              phase_end('small%d' % l)
              modT, sc1p, sc2p = modTs[l], sc1ps[l], sc2ps[l]
              if l == 0:
                  mod_groups(0, [0, 1, 2, 3])
              phase_end('modmm%d' % l)

              def modulate(scp, shoff):
                  ths = []
                  for kc in range(8):
                      for (a, b, v) in ((0, 1024, 1), (1024, T, 0)):
                          ths.append(lambda e, kc=kc, a=a, b=b, v=v: e.tensor_scalar(
                              out=hT[:, kc, a:b], in0=x[:, kc, a:b], scalar1=scp[:, kc, v:v + 1],
                              scalar2=modT[:, shoff + kc, v:v + 1], op0=ALU.mult, op1=ALU.add))
                  cx.run("dve", ths, reads=[x_T, M_T[l]], writes=[hT_T])

              dbg_out("modT_l%d" % l, modT, [128, 48, 2], [M_T[l]])
              phase_end("mod%d" % l)
              modulate(sc1p, 0)
              phase_end("h%d" % l)
              dbg_out("h_l%d" % l, hT, [128, 8, T], [hT_T])

              aset(3072)
              h_aT = arena[:, 0:3072].bitcast(BF16).rearrange("p (h t) -> p h t", t=T)
              ubT = arena[:, 3072:6144].bitcast(BF16).rearrange("p (h t) -> p h t", t=T)
              h_cT = arena[:, 6144:9216].bitcast(BF16).rearrange("p (h t) -> p h t", t=T)
              merged = arena[:, 9216:15360].bitcast(BF16).rearrange("p (h t) -> p h t", t=T)
              haT_T = TT("h_aT"); ub_T = TT("ubT"); hcT_T = TT("h_cT"); mg_T = TT("merged")
              rIG = af32(T); rA = af32(T); rP = af32(T); rCM = af32(T)
              R_T = TT("rows")
              sm = af32(96).rearrange("p (a c) -> p a c", c=12)
              SM_T = TT("sm")
              cols = af32(3 * 432).rearrange("p (q n) -> p q n", n=432)
              COL_T = TT("cols")
              cbc = af32(96)
              qT = abf(T); kT = abf(T)
              qk_T = TT("qk")
              vext = abf(12 * 130).rearrange("p (c n) -> p c n", n=130)
              v_T = TT("vext")
              osig = abf(T).rearrange("p (c n) -> p c n", n=128)
              o_T = TT("osig")
              hsum = af32(T).rearrange("p (c n) -> p c n", n=128)
              hs_T = [TT("hs%d" % c) for c in range(12)]
              kw = [abf(T).rearrange("p (c n) -> p c n", n=128) for _ in range(2)]
              kw_T = [[TT("kw") for c in range(12)] for _ in range(2)]
              Cx = [[af32(130) for _ in range(2)] for _ in range(2)]
              Cx_T = [[TT("cx"), TT("cx")] for _ in range(2)]
              sc2 = af32(8)
              dmg = [af32(512)] * 2
              dmg_T = [TT("dmg0")] * 2
              sT_all = abf(24 * 128)
              sTa_T = [TT("sTa%d" % i) for i in range(6)]
              Cb_all = [abf(12 * 130).rearrange("p (s n) -> p s n", n=130) for _ in range(2)]
              CbA_T = [[TT("cba") for _ in range(12)] for _ in range(2)]
              Bs2 = [af32(130) for _ in range(2)]
              Bs2_T = [TT("bs2a"), TT("bs2b")]
              tot_all = af32(12 * 130).rearrange("p (c n) -> p c n", n=130)
              tota_T = TT("tot_all")
              dn12 = af32(12)
              dn_T = TT("dn12")

              wg, wgT = wload(in_w_d[l][:, OFF_GATE:OFF_GATE + 72], 8, 72)
              cx.run("dve", [lambda e: e.memset(rP[0:36, :], 0.0), lambda e: e.memset(rCM[0:36, :], 0.0)], writes=[R_T])
              for which in (0, 1):
                  for nb in range(3):
                      bk, bkT = psum()
                      sl = slice(nb * 512, (nb + 1) * 512)
                      cx.run("pe", [mm(bk[0:36, :], wg[:, kc, which * 36:(which + 1) * 36], hT[:, kc, sl], kc == 0, kc == 7) for kc in range(8)],
                             reads=[wgT, hT_T], writes=[bkT])
                      if which == 0:
                          cx.run("act", [lambda e, bk=bk, sl=sl: e.activation(out=rIG[0:36, sl], in_=bk[0:36, :], func=AF.Identity, bias=gb[:, 0:1])],
                                 reads=[bkT, G], writes=[R_T])
                      else:
                          cx.run("act", [lambda e, bk=bk, sl=sl: e.activation(out=rA[0:36, sl], in_=bk[0:36, :], func=AF.Exp, scale=-1.0, bias=ngb[:, 0:1])],
                                 reads=[bkT, G], writes=[R_T])
              cx.run("act", [lambda e: e.activation(out=rA[0:36, :], in_=rA[0:36, :], func=AF.Ln, bias=1.0)], reads=[R_T], writes=[R_T])
              ths = []
              for c in range(12):
                  cs = slice(c * 128, (c + 1) * 128)
                  ths.append(lambda e, cs=cs: e.tensor_tensor_scan(out=rP[0:4, cs], data0=ones36[0:4, :], data1=rA[0:4, cs], initial=0.0,
                                                                   op0=ALU.mult, op1=ALU.add))
                  ths.append(lambda e, cs=cs: e.tensor_tensor_scan(out=rev(rP[32:36, cs]), data0=ones36[32:36, :], data1=rev(rA[32:36, cs]),
                                                                   initial=0.0, op0=ALU.mult, op1=ALU.add))
              cx.run("dve", ths, reads=[R_T], writes=[R_T])
              cx.run("dve", [lambda e: e.tensor_tensor(out=rA[0:36, :], in0=rIG[0:36, :], in1=rP[0:36, :], op=ALU.add)], reads=[R_T], writes=[R_T])
              ths = []
              for c in range(12):
                  cs = slice(c * 128, (c + 1) * 128)
                  ths.append(lambda e, cs=cs: e.tensor_tensor_scan(out=rCM[0:4, cs], data0=zeros36[0:4, :], data1=rA[0:4, cs], initial=-1e30,
                                                                   op0=ALU.add, op1=ALU.max))
                  ths.append(lambda e, cs=cs: e.tensor_tensor_scan(out=rev(rCM[32:36, cs]), data0=zeros36[32:36, :], data1=rev(rA[32:36, cs]),
                                                                   initial=-1e30, op0=ALU.add, op1=ALU.max))
              cx.run("dve", ths, reads=[R_T], writes=[R_T])
              CM3 = rCM.rearrange("p (c t) -> p c t", t=128)
              P3 = rP.rearrange("p (c t) -> p c t", t=128)
              A3 = rA.rearrange("p (c t) -> p c t", t=128)
              IG3 = rIG.rearrange("p (c t) -> p c t", t=128)
              cx.run("dve", [lambda e: e.memset(sm[0:36, :, :], 0.0)], writes=[SM_T])
              cx.run("dve", [lambda e: e.tensor_copy(out=sm[0:4, 0, :], in_=CM3[0:4, :, 127]),
                             lambda e: e.tensor_copy(out=sm[32:36, 0, :], in_=CM3[32:36, :, 0]),
                             lambda e: e.tensor_copy(out=sm[0:4, 1, :], in_=P3[0:4, :, 127]),
                             lambda e: e.tensor_copy(out=sm[32:36, 1, :], in_=P3[32:36, :, 0])], reads=[R_T, SM_T], writes=[SM_T])
              for d, r0 in ((0, 0), (1, 32)):
                  rs = slice(r0, r0 + 4)
                  order = list(range(12)) if d == 0 else list(range(11, -1, -1))
                  prev = None
                  for c in order:
                      s = c // 2
                      start = (c % 2 == 0) if d == 0 else (c % 2 == 1)
                      m0c = sm[rs, 2, c:c + 1]
                      if start:
                          if (d == 0 and s == 0) or (d == 1 and s == 3):
                              th = lambda e, m0c=m0c, rs=rs: e.tensor_copy(out=m0c, in_=minit[rs, :])
                          elif (d == 0 and s <= 3) or (d == 1 and s <= 2):
                              th = lambda e, m0c=m0c, rs=rs, prev=prev: e.tensor_scalar(out=m0c, in0=sm[rs, 4, prev:prev + 1], scalar1=chain[rs, :],
                                                                                       scalar2=None, op0=ALU.mult)
                          else:
                              th = lambda e, m0c=m0c: e.memset(m0c, 0.0)
                      else:
                          th = lambda e, m0c=m0c, rs=rs, prev=prev: e.tensor_copy(out=m0c, in_=sm[rs, 4, prev:prev + 1])
                      cx.run("dve", [th], reads=[SM_T, G], writes=[SM_T])
                      cx.run("dve", [lambda e, rs=rs, c=c: e.tensor_tensor(out=sm[rs, 3, c:c + 1], in0=sm[rs, 2, c:c + 1], in1=sm[rs, 0, c:c + 1], op=ALU.max)],
                             reads=[SM_T], writes=[SM_T])
                      cx.run("dve", [lambda e, rs=rs, c=c: e.tensor_tensor(out=sm[rs, 4, c:c + 1], in0=sm[rs, 3, c:c + 1], in1=sm[rs, 1, c:c + 1], op=ALU.subtract)],
                             reads=[SM_T], writes=[SM_T])
                      prev = c
              cx.dma("sp", mfin_d[l], sm[0:36, 4, :], reads=[SM_T])
              cx.run("dve", [lambda e: e.tensor_tensor(out=sm[0:36, 6, :], in0=sm[0:36, 3, :], in1=sm[0:36, 2, :], op=ALU.subtract)], reads=[SM_T], writes=[SM_T])
              cx.run("act", [lambda e: e.activation(out=sm[0:36, 5, :], in_=sm[0:36, 6, :], func=AF.Exp, scale=-1.0)], reads=[SM_T], writes=[SM_T])
              m0b = sm[0:36, 2, :].unsqueeze(2).to_broadcast([36, 12, 128])
              mxb = sm[0:36, 3, :].unsqueeze(2).to_broadcast([36, 12, 128])
              cx.run("dve", [lambda e: e.tensor_tensor(out=CM3[0:36], in0=CM3[0:36], in1=m0b, op=ALU.max)], reads=[R_T, SM_T], writes=[R_T])
              cx.run("dve", [lambda e: e.tensor_scalar(out=rCM[0:36, :], in0=rCM[0:36, :], scalar1=-1.0, scalar2=None, op0=ALU.mult)], reads=[R_T], writes=[R_T])
              dbg_out("rA_l%d" % l, rA[0:36, :], [36, T], [R_T])
              dbg_out("rNG_l%d" % l, rCM[0:36, :], [36, T], [R_T])

              def cols_from(q, rows):
                  bk, bkT = psum()
                  cx.run("pe", [lambda e, c=c, bk=bk: e.transpose(out=bk[:, c * 36:(c + 1) * 36], in_=rows[0:36, c * 128:(c + 1) * 128],
                                                                  identity=identF[0:36, 0:36]) for c in range(12)],
                         reads=[R_T, G], writes=[bkT])
                  cx.run("act", [lambda e, bk=bk: e.activation(out=cols[:, q, :], in_=bk[:, 0:432], func=AF.Copy)], reads=[bkT], writes=[COL_T])

              cx.run("dve", [lambda e: e.tensor_tensor(out=IG3[0:36], in0=CM3[0:36], in1=m0b, op=ALU.add)], reads=[R_T, SM_T], writes=[R_T])
              cx.run("act", [lambda e: e.activation(out=rIG[0:36, :], in_=rIG[0:36, :], func=AF.Exp)], reads=[R_T], writes=[R_T])
              cols_from(0, rIG)
              cx.run("dve", [lambda e: e.tensor_tensor(out=rP[0:36, :], in0=rP[0:36, :], in1=rCM[0:36, :], op=ALU.add)], reads=[R_T], writes=[R_T])
              cx.run("act", [lambda e: e.activation(out=rP[0:36, :], in_=rP[0:36, :], func=AF.Exp)], reads=[R_T], writes=[R_T])
              cols_from(1, rP)
              cx.run("dve", [lambda e: e.tensor_tensor(out=IG3[0:36], in0=A3[0:36], in1=mxb, op=ALU.subtract)], reads=[R_T, SM_T], writes=[R_T])
              cx.run("dve", [lambda e: e.tensor_scalar(out=rIG[0:36, :], in0=rIG[0:36, :], scalar1=LNKS, scalar2=None, op0=ALU.add)], reads=[R_T], writes=[R_T])
              cx.run("act", [lambda e: e.activation(out=rIG[0:36, :], in_=rIG[0:36, :], func=AF.Exp)], reads=[R_T], writes=[R_T])
              cols_from(2, rIG)
              bk, bkT = psum()
              cx.run("pe", [mm(bk[:, r * 12:(r + 1) * 12], sel[:, r, :], sm[0:36, 5, :], True, True) for r in range(8)], reads=[SM_T, G], writes=[bkT])
              cx.run("act", [lambda e, bk=bk: e.activation(out=cbc, in_=bk[:, 0:96], func=AF.Copy)], reads=[bkT], writes=[COL_T])
              dbg_out("sm_l%d" % l, sm[0:36, :, :], [36, 8, 12], [SM_T])
              dbg_out("cols_l%d" % l, cols, [128, 3, 432], [COL_T])

              phase_end("gates%d" % l)
              prow = lambda r: (r // 4) * 32 + (r % 4)
              lnks_col = None

              for h in range(H):
                  wv, wT = wload(in_w_d[l][:, h * 512:(h + 1) * 512], 8, 512)
                  for nb in range(3):
                      sl = slice(nb * 512, (nb + 1) * 512)
                      for which, dst in ((0, qT), (1, kT)):
                          bk, bkT = psum()
                          cx.run("pe", [mm(bk, wv[:, kc, which * 128:(which + 1) * 128], hT[:, kc, sl], kc == 0, kc == 7) for kc in range(8)],
                                 reads=[wT, hT_T], writes=[bkT])
                          cx.run("act", [lambda e, bk=bk, dst=dst, sl=sl: e.activation(out=dst[:, sl], in_=bk, func=AF.Copy)], reads=[bkT], writes=[qk_T])
                  cx.run("dve", [lambda e: e.memset(vext[:, :, 128:130], 1.0)], writes=[v_T])
                  for c in range(12):
                      cs = slice(c * 128, (c + 1) * 128)
                      bk, bkT = psum()
                      cx.run("pe", [mm(bk[:, 0:256], hT[:, kc, cs], wv[:, kc, 256:512], kc == 0, kc == 7) for kc in range(8)], reads=[wT, hT_T], writes=[bkT])
                      cx.run("act", [lambda e, bk=bk, c=c: e.activation(out=vext[:, c, 0:128], in_=bk[:, 0:128], func=AF.Copy)], reads=[bkT], writes=[v_T])
                      cx.run("act", [lambda e, bk=bk, c=c: e.activation(out=osig[:, c, :], in_=bk[:, 128:256], func=AF.Sigmoid)], reads=[bkT], writes=[o_T])
                  for c in range(12):
                      cs = slice(c * 128, (c + 1) * 128)
                      bk, bkT = psum()
                      bkb = bk.bitcast(BF16)
                      cx.run("pe", [lambda e, bkb=bkb, cs=cs: e.transpose(out=bkb[:, 0:128], in_=kT[:, cs], identity=identB)], reads=[qk_T, G], writes=[bkT])
                      for d in range(2):
                          r = d * 4 + h
                          col = cols[:, 2, c * 36 + prow(r):c * 36 + prow(r) + 1]
                          cx.run("act", [lambda e, bkb=bkb, d=d, c=c, col=col: e.activation(out=kw[d][:, c, :], in_=bkb[:, 0:128], func=AF.Copy, scale=col)],
                                 reads=[bkT, COL_T], writes=[kw_T[d][c]])
                  orders = [list(range(12)), list(range(11, -1, -1))]
                  for g in range(6):
                      d = g // 3
                      r = d * 4 + h
                      b1, b1T = psum()
                      ths = []
                      css = []
                      for i in range(4):
                          c = orders[d][(g % 3) * 4 + i]
                          cs = slice(c * 128, (c + 1) * 128)
                          css.append(cs)
                          o = b1[:, i * 128:(i + 1) * 128]
                          ths += [mm(o, sel[:, r, :], rCM[0:36, cs], True, False), mm(o, rA[0:36, cs], sel[:, r, :], False, False),
                                  mm(o, identF, MASK[d], False, True)]
                      cx.run("pe", ths, reads=[R_T, G], writes=[b1T])
                      gp = g % 2
                      cx.run("act", [lambda e, b1=b1, gp=gp: e.activation(out=dmg[gp], in_=b1, func=AF.Exp, bias=lnks[:, 0:1])], reads=[b1T, G], writes=[dmg_T[gp]])
                      b2, b2T = psum()
                      cx.run("pe", [mm(b2[:, i * 128:(i + 1) * 128], kT[:, css[i]], qT[:, css[i]], True, True) for i in range(4)], reads=[qk_T], writes=[b2T])
                      cx.run("dve", [lambda e, b2=b2, gp=gp, g=g: e.tensor_tensor(out=sT_all[:, g * 512:(g + 1) * 512], in0=b2, in1=dmg[gp], op=ALU.mult)],
                             reads=[b2T, dmg_T[gp]], writes=[sTa_T[g]])

                  def rec(d, h=h):
                      r = d * 4 + h
                      cur = 0
                      Cxd, CxTd = Cx[d], Cx_T[d]
                      for step, c in enumerate(orders[d]):
                          s = c // 2
                          start = (c % 2 == 0) if d == 0 else (c % 2 == 1)
                          if start:
                              if (d == 0 and s == 0) or (d == 1 and s == 3):
                                  cx.dma("sp", Cxd[cur][:, 0:129], Cinit_d[l, d, h], writes=[CxTd[cur]])
                              elif (d == 0 and s <= 3) or (d == 1 and s <= 2):
                                  cx.run("dve", [lambda e, cur=cur: e.tensor_scalar(out=Cxd[cur][:, 0:129], in0=Cxd[cur][:, 0:129], scalar1=chain[:, 0:1],
                                                                                     scalar2=None, op0=ALU.mult)], reads=[CxTd[cur], G], writes=[CxTd[cur]])
                              else:
                                  cx.run("dve", [lambda e, cur=cur: e.memset(Cxd[cur][:, 0:129], 0.0)], writes=[CxTd[cur]])
                          cx.run("act", [lambda e, cur=cur, step=step: e.activation(out=Cb_all[d][:, step, 0:129], in_=Cxd[cur][:, 0:129], func=AF.Copy)],
                                 reads=[CxTd[cur]], writes=[CbA_T[d][step]])
                          bU, bUT = psum()
                          cx.run("pe", [mm(bU[:, 0:129], kw[d][:, c, :], vext[:, c, 0:129], True, True)], reads=[kw_T[d][c], v_T], writes=[bUT])
                          yield
                          nxt = 1 - cur
                          ccol = cbc[:, r * 12 + c:r * 12 + c + 1]
                          cx.run("dve", [lambda e, cur=cur, nxt=nxt: e.scalar_tensor_tensor(
                              out=Cxd[nxt][:, 0:129], in0=Cxd[cur][:, 0:129], scalar=ccol, in1=bU[:, 0:129], op0=ALU.mult, op1=ALU.add)],
                              reads=[bUT, CxTd[cur], COL_T], writes=[CxTd[nxt]])
                          cur = nxt
                          end = (c % 2 == 1) if d == 0 else (c % 2 == 0)
                          if end:
                              cx.dma("sp", Cfin_d[l, s, d, h], Cxd[cur][:, 0:129], reads=[CxTd[cur]])
                          yield

                  for _ in itertools.zip_longest(rec(0), rec(1)):
                      pass

                  for d in range(2):
                      r = d * 4 + h
                      pr = prow(r)
                      for step, c in enumerate(orders[d]):
                          cs = slice(c * 128, (c + 1) * 128)
                          p = d * 12 + step
                          bA, bAT = psum()
                          cx.run("pe", [mm(bA[:, 0:129], sT_all[:, p * 128:(p + 1) * 128], vext[:, c, 0:129], True, True),
                                        mm(bA[:, 256:385], qT[:, cs], Cb_all[d][:, step, 0:129], True, True)],
                                 reads=[sTa_T[p // 4], v_T, qk_T, CbA_T[d][step]], writes=[bAT])
                          wcol = cols[:, 0, c * 36 + pr:c * 36 + pr + 1]
                          bp = step % 2
                          cx.run("act", [lambda e, bA=bA, bp=bp, wcol=wcol: e.activation(out=Bs2[bp][:, 0:129], in_=bA[:, 256:385], func=AF.Copy, scale=wcol)],
                                 reads=[bAT, COL_T], writes=[Bs2_T[bp]])
                          cx.run("dve", [lambda e, bA=bA, bp=bp, c=c: e.tensor_tensor(out=tot_all[:, c, 0:129], in0=bA[:, 0:129], in1=Bs2[bp][:, 0:129], op=ALU.add)],
                                 reads=[bAT, Bs2_T[bp]], writes=[tota_T])
                      den = tot_all[:, :, 128]
                      ecols = cols[:, 1, :].rearrange("p (c n) -> p c n", n=36)[:, :, pr]
                      cx.run("dve", [lambda e, den=den: e.tensor_scalar(out=dn12, in0=den, scalar1=-1.0, scalar2=None, op0=ALU.mult)], reads=[tota_T], writes=[dn_T])
                      cx.run("dve", [lambda e, den=den: e.tensor_tensor(out=dn12, in0=dn12, in1=den, op=ALU.max)], reads=[tota_T, dn_T], writes=[dn_T])
                      cx.run("dve", [lambda e, ecols=ecols: e.tensor_tensor(out=dn12, in0=dn12, in1=ecols, op=ALU.max)], reads=[dn_T, COL_T], writes=[dn_T])
                      cx.run("dve", [lambda e: e.reciprocal(out=dn12, in_=dn12)], reads=[dn_T], writes=[dn_T])
                      if d == 0:
                          cx.run("act", [lambda e, c=c: e.activation(out=hsum[:, c, :], in_=tot_all[:, c, 0:128], func=AF.Copy, scale=dn12[:, c:c + 1])
                                         for c in range(12)], reads=[tota_T, dn_T], writes=hs_T)
                      else:
                          cx.run("dve", [lambda e, c=c: e.scalar_tensor_tensor(out=hsum[:, c, :], in0=tot_all[:, c, 0:128], scalar=dn12[:, c:c + 1],
                                                                               in1=hsum[:, c, :], op0=ALU.mult, op1=ALU.add) for c in range(12)],
                                 reads=[tota_T, dn_T] + hs_T, writes=hs_T)
                  if l == 0 and h == 0:
                      dbg_out("hsum_l0h0", hsum, [128, 12, 128], hs_T)
                  head_norm_out(hsum, hs_T, mnw, h, osig, o_T, h_aT, haT_T, tot_all.rearrange("p c n -> p (c n)"), tota_T, sT_all[:, 0:1536], sTa_T[0:3])
                  if l == 0:
                      mod_groups(0, [4 + 2 * h, 5 + 2 * h])
                  phase_end("mlstm%d_h%d" % (l, h))
              dbg_out("haT_l%d" % l, h_aT, [128, 4, T], [haT_T])

              phase_end("mlstm%d" % l)
              aset(6144)
              acc = af32(4 * T).rearrange("p (a t) -> p a t", t=T)
              acc_T = TT("acc")
              upad = [abf(6 * 286).rearrange("p (s n) -> p s n", n=286) for _ in range(2)]
              up_T = [TT("upad0"), TT("upad1")]
              Dg = [abf(31 * 128).rearrange("p (k n) -> p k n", n=128) for _ in range(2)]
              Dg_T = [TT("dg0"), TT("dg1")]
              sg = [af32(512) for _ in range(2)]
              sg_T = [TT("sg0"), TT("sg1")]
              cx.run("dve", [lambda e: e.memset(upad[0], 0.0), lambda e: e.memset(upad[1], 0.0)], writes=up_T)
              wa, waT = wload(in_w_d[l][:, OFF_CA:OFF_CA + 512], 8, 512)
              wgc, wgcT = wload(in_w_d[l][:, OFF_CG:OFF_CG + 512], 8, 512)
              k = 0
              for ct in range(4):
                  up, upT, dg, dgT = upad[ct % 2], up_T[ct % 2], Dg[ct % 2], Dg_T[ct % 2]
                  cx.run("act", [lambda e, kk=kk, ct=ct, dg=dg: e.activation(out=dg[:, kk, :], in_=identF, func=AF.Copy, scale=convp[:, ct, kk:kk + 1])
                                 for kk in range(CONV_K)], reads=[G], writes=[dgT])
                  for nb in range(3):
                      sl = slice(nb * 512, (nb + 1) * 512)
                      ba, baT = psum()
                      cx.run("pe", [mm(ba, wa[:, kc, ct * 128:(ct + 1) * 128], hT[:, kc, sl], kc == 0, kc == 7) for kc in range(8)], reads=[waT, hT_T], writes=[baT])
                      bg, bgT = psum()
                      cx.run("pe", [mm(bg, wgc[:, kc, ct * 128:(ct + 1) * 128], hT[:, kc, sl], kc == 0, kc == 7) for kc in range(8)], reads=[wgcT, hT_T], writes=[bgT])
                      pp = k % 2
                      k += 1
                      cx.run("act", [lambda e, bg=bg, pp=pp: e.activation(out=sg[pp], in_=bg, func=AF.Sigmoid)], reads=[bgT], writes=[sg_T[pp]])
                      cx.run("dve", [lambda e, ba=ba, pp=pp, nb=nb, up=up, s2=s2: e.tensor_tensor(
                          out=up[:, 2 * nb + s2, 15:271], in0=ba[:, s2 * 256:(s2 + 1) * 256],
                          in1=sg[pp][:, s2 * 256:(s2 + 1) * 256], op=ALU.mult) for s2 in range(2)], reads=[baT, sg_T[pp]], writes=[upT])
                  ths = []
                  for s in (1, 2, 3):
                      ths.append(lambda e, s=s, up=up: e.tensor_scalar(out=up[:, s, 0:15], in0=up[:, s - 1, 256:271], scalar1=chain[:, 0:1], scalar2=None, op0=ALU.mult))
                  for s in (0, 1, 2):
                      ths.append(lambda e, s=s, up=up: e.tensor_scalar(out=up[:, s, 271:286], in0=up[:, s + 1, 15:30], scalar1=chain[:, 0:1], scalar2=None, op0=ALU.mult))
                  cx.run("dve", ths, reads=[upT, G], writes=[upT])
                  phase_end("convu%d_%d" % (l, ct))
                  for s in range(6):
                      bk, bkT = psum()
                      cx.run("pe", [mm(bk[:, 0:256], dg[:, kk, :], up[:, s, kk:kk + 256], kk == 0, kk == CONV_K - 1) for kk in range(CONV_K)],
                             reads=[dgT, upT], writes=[bkT])
                      cx.run("act", [lambda e, bk=bk, ct=ct, s=s: e.activation(out=acc[:, ct, s * 256:(s + 1) * 256], in_=bk[:, 0:256], func=AF.Identity,
                                                                               bias=convp[:, ct, 31:32])], reads=[bkT, G], writes=[acc_T])
              dbg_out("conv_l%d" % l, acc, [128, 4, T], [acc_T])
              cb = abf(4 * 256).rearrange("p (a n) -> p a n", n=256)
              cq = abf(4 * 256).rearrange("p (a n) -> p a n", n=256)
              cb_T = TT("cb"); cq_T = TT("cq")
              mean = af32(256); msq = af32(256); rstd = af32(256)
              st_T = TT("lnst")
              for nb in range(6):
                  sl = slice(nb * 256, (nb + 1) * 256)
                  cx.run("act", [lambda e, ct=ct, sl=sl: e.activation(out=cb[:, ct, :], in_=acc[:, ct, sl], func=AF.Copy) for ct in range(4)], reads=[acc_T], writes=[cb_T])
                  cx.run("act", [lambda e, ct=ct, sl=sl: e.activation(out=cq[:, ct, :], in_=acc[:, ct, sl], func=AF.Square) for ct in range(4)], reads=[acc_T], writes=[cq_T])
                  b1, b1T = psum()
                  cx.run("pe", [mm(b1[:, 0:256], onesB, cb[:, ct, :], ct == 0, ct == 3) for ct in range(4)], reads=[cb_T, G], writes=[b1T])
                  b2, b2T = psum()
                  cx.run("pe", [mm(b2[:, 0:256], onesB, cq[:, ct, :], ct == 0, ct == 3) for ct in range(4)], reads=[cq_T, G], writes=[b2T])
                  ln_stats(b1[:, 0:256], b1T, b2[:, 0:256], b2T, mean, msq, rstd, st_T, 512)
                  cx.run("dve", [lambda e, ct=ct, sl=sl: e.tensor_tensor(out=acc[:, ct, sl], in0=acc[:, ct, sl], in1=mean, op=ALU.subtract) for ct in range(4)],
                         reads=[acc_T, st_T], writes=[acc_T])
                  cx.run("dve", [lambda e, ct=ct, sl=sl: e.tensor_tensor(out=acc[:, ct, sl], in0=acc[:, ct, sl], in1=rstd, op=ALU.mult) for ct in range(4)],
                         reads=[acc_T, st_T], writes=[acc_T])
                  cx.run("act", [lambda e, ct=ct, sl=sl: e.activation(out=acc[:, ct, sl], in_=acc[:, ct, sl], func=AF.Identity,
                                                                      scale=convp[:, ct, 32:33], bias=convp[:, ct, 33:34]) for ct in range(4)], reads=[acc_T, G], writes=[acc_T])
                  cx.run("act", [lambda e, ct=ct, sl=sl: e.activation(out=ubT[:, ct, sl], in_=acc[:, ct, sl], func=AF.Silu) for ct in range(4)], reads=[acc_T], writes=[ub_T])
              dbg_out("ubT_l%d" % l, ubT, [128, 4, T], [ub_T])

              phase_end("conv%d" % l)
              aset(9216)
              ropeCS = af32(1024); ropeSN = af32(1024)
              RP_T = TT("rope")
              cx.dma("sp", ropeCS, ropeCS_d, writes=[RP_T])
              cx.dma("sp", ropeSN, ropeSN_d, writes=[RP_T])
              qT = abf(T); kT = abf(T)
              qk_T = TT("rqk")
              vv = abf(T).rearrange("p (c n) -> p c n", n=128)
              v_T = TT("rv")
              gsil = af32(T).rearrange("p (c n) -> p c n", n=128)
              o_T = TT("gsil")
              ysum = af32(T).rearrange("p (c n) -> p c n", n=128)
              hs_T = [TT("ys%d" % c) for c in range(12)]
              kz = [abf(T).rearrange("p (c n) -> p c n", n=128) for _ in range(2)]
              kz_T = [[TT("kz") for c in range(12)] for _ in range(2)]
              Sx = [[af32(128) for _ in range(2)] for _ in range(2)]
              Sx_T = [[TT("sx"), TT("sx")] for _ in range(2)]
              t1 = af32(512); t2 = af32(512)
              sT_all = abf(24 * 128)
              sTa_T = [TT("rsTa%d" % i) for i in range(6)]
              Sb_all = [abf(12 * 128).rearrange("p (s n) -> p s n", n=128) for _ in range(2)]
              SbA_T = [[TT("sba") for _ in range(12)] for _ in range(2)]
              Bs2 = [af32(128) for _ in range(2)]
              Bs2_T = [TT("rbs2a"), TT("rbs2b")]
              t_T = TT("rt")
              for h in range(H):
                  wv, wT = wload(in_w_d[l][:, OFF_RET + h * 512:OFF_RET + (h + 1) * 512], 8, 512)
                  wv2, wT2 = wload(in_w_d[l][:, OFF_RVG + h * 256:OFF_RVG + (h + 1) * 256], 8, 256)
                  for nb in range(3):
                      sl = slice(nb * 512, (nb + 1) * 512)
                      for which, dst in ((0, qT), (1, kT)):
                          bk, bkT = psum()
                          cx.run("pe", [mm(bk, wv[:, kc, which * 256:which * 256 + 128], hT[:, kc, sl], kc == 0, kc == 7) for kc in range(8)],
                                 reads=[wT, hT_T], writes=[bkT])
                          if nb < 2:
                              bs_, bsT = psum()
                              cx.run("pe", [mm(bs_, wv[:, kc, which * 256 + 128:which * 256 + 256], hT[:, kc, sl], kc == 0, kc == 7) for kc in range(8)],
                                     reads=[wT, hT_T], writes=[bsT])
                              cx.run("dve", [lambda e, bk=bk, sl=sl: e.tensor_tensor(out=t1, in0=bk, in1=ropeCS[:, sl], op=ALU.mult)], reads=[bkT, RP_T, t_T], writes=[t_T])
                              cx.run("dve", [lambda e, bs_=bs_, sl=sl: e.tensor_tensor(out=t2, in0=bs_, in1=ropeSN[:, sl], op=ALU.mult)], reads=[bsT, RP_T, t_T], writes=[t_T])
                              cx.run("dve", [lambda e, dst=dst, sl=sl: e.tensor_tensor(out=dst[:, sl], in0=t1, in1=t2, op=ALU.add)], reads=[t_T], writes=[qk_T, t_T])
                          else:
                              cx.run("act", [lambda e, bk=bk, dst=dst, sl=sl: e.activation(out=dst[:, sl], in_=bk, func=AF.Copy)], reads=[bkT], writes=[qk_T])
                  for c in range(12):
                      cs = slice(c * 128, (c + 1) * 128)
                      bk, bkT = psum()
                      cx.run("pe", [mm(bk[:, 0:256], hT[:, kc, cs], wv2[:, kc, :], kc == 0, kc == 7) for kc in range(8)], reads=[wT2, hT_T], writes=[bkT])
                      cx.run("act", [lambda e, bk=bk, c=c: e.activation(out=vv[:, c, :], in_=bk[:, 0:128], func=AF.Copy)], reads=[bkT], writes=[v_T])
                      cx.run("act", [lambda e, bk=bk, c=c: e.activation(out=gsil[:, c, :], in_=bk[:, 128:256], func=AF.Silu)], reads=[bkT], writes=[o_T])
                  for c in range(12):
                      cs = slice(c * 128, (c + 1) * 128)
                      bk, bkT = psum()
                      bkb = bk.bitcast(BF16)
                      cx.run("pe", [lambda e, bkb=bkb, cs=cs: e.transpose(out=bkb[:, 0:128], in_=kT[:, cs], identity=identB)], reads=[qk_T, G], writes=[bkT])
                      for d in range(2):
                          r = d * 4 + h
                          cx.run("act", [lambda e, bkb=bkb, d=d, c=c, r=r: e.activation(out=kz[d][:, c, :], in_=bkb[:, 0:128], func=AF.Copy, scale=rcol[:, r, 1:2])],
                                 reads=[bkT, G], writes=[kz_T[d][c]])
                  orders = [list(range(12)), list(range(11, -1, -1))]
                  for g in range(6):
                      d = g // 3
                      r = d * 4 + h
                      b2, b2T = psum()
                      css = [slice(orders[d][(g % 3) * 4 + i] * 128, (orders[d][(g % 3) * 4 + i] + 1) * 128) for i in range(4)]
                      cx.run("pe", [mm(b2[:, i * 128:(i + 1) * 128], kT[:, css[i]], qT[:, css[i]], True, True) for i in range(4)], reads=[qk_T], writes=[b2T])
                      cx.run("dve", [lambda e, b2=b2, g=g, i=i, r=r: e.tensor_tensor(out=sT_all[:, g * 512 + i * 128:g * 512 + (i + 1) * 128],
                                                                                   in0=b2[:, i * 128:(i + 1) * 128], in1=decT[:, r, :], op=ALU.mult) for i in range(4)],
                             reads=[b2T, G], writes=[sTa_T[g]])

                  def rrec(d, h=h):
                      r = d * 4 + h
                      cur = 0
                      Sxd, SxTd = Sx[d], Sx_T[d]
                      for step, c in enumerate(orders[d]):
                          s = c // 2
                          start = (c % 2 == 0) if d == 0 else (c % 2 == 1)
                          if start:
                              if (d == 0 and s == 0) or (d == 1 and s == 3):
                                  cx.dma("sp", Sxd[cur], Sinit_d[l, d, h], writes=[SxTd[cur]])
                              elif (d == 0 and s <= 3) or (d == 1 and s <= 2):
                                  cx.run("dve", [lambda e, cur=cur: e.tensor_scalar(out=Sxd[cur], in0=Sxd[cur], scalar1=chain[:, 0:1],
                                                                                     scalar2=None, op0=ALU.mult)], reads=[SxTd[cur], G], writes=[SxTd[cur]])
                              else:
                                  cx.run("dve", [lambda e, cur=cur: e.memset(Sxd[cur], 0.0)], writes=[SxTd[cur]])
                          cx.run("act", [lambda e, cur=cur, step=step: e.activation(out=Sb_all[d][:, step, :], in_=Sxd[cur], func=AF.Copy)],
                                 reads=[SxTd[cur]], writes=[SbA_T[d][step]])
                          bU, bUT = psum()
                          cx.run("pe", [mm(bU[:, 0:128], kz[d][:, c, :], vv[:, c, :], True, True)], reads=[kz_T[d][c], v_T], writes=[bUT])
                          yield
                          nxt = 1 - cur
                          cx.run("dve", [lambda e, cur=cur, nxt=nxt: e.scalar_tensor_tensor(
                              out=Sxd[nxt], in0=Sxd[cur], scalar=rcol[:, r, 2:3], in1=bU[:, 0:128], op0=ALU.mult, op1=ALU.add)],
                              reads=[bUT, SxTd[cur], G], writes=[SxTd[nxt]])
                          cur = nxt
                          end = (c % 2 == 1) if d == 0 else (c % 2 == 0)
                          if end:
                              cx.dma("sp", Sfin_d[l, s, d, h], Sxd[cur], reads=[SxTd[cur]])
                          yield

                  for _ in itertools.zip_longest(rrec(0), rrec(1)):
                      pass

                  for d in range(2):
                      r = d * 4 + h
                      for step, c in enumerate(orders[d]):
                          cs = slice(c * 128, (c + 1) * 128)
                          p = d * 12 + step
                          bA, bAT = psum()
                          cx.run("pe", [mm(bA[:, 0:128], sT_all[:, p * 128:(p + 1) * 128], vv[:, c, :], True, True),
                                        mm(bA[:, 256:384], qT[:, cs], Sb_all[d][:, step, :], True, True)],
                                 reads=[sTa_T[p // 4], v_T, qk_T, SbA_T[d][step]], writes=[bAT])
                          bp = step % 2
                          cx.run("act", [lambda e, bA=bA, bp=bp, r=r: e.activation(out=Bs2[bp], in_=bA[:, 256:384], func=AF.Copy, scale=rcol[:, r, 0:1])],
                                 reads=[bAT, G], writes=[Bs2_T[bp]])
                          if d == 0:
                              cx.run("dve", [lambda e, bA=bA, bp=bp, c=c: e.tensor_tensor(out=ysum[:, c, :], in0=bA[:, 0:128], in1=Bs2[bp], op=ALU.add)],
                                     reads=[bAT, Bs2_T[bp]], writes=[hs_T[c]])
                          else:
                              cx.run("dve", [lambda e, bA=bA, bp=bp: e.tensor_tensor(out=Bs2[bp], in0=bA[:, 0:128], in1=Bs2[bp], op=ALU.add)],
                                     reads=[bAT, Bs2_T[bp]], writes=[Bs2_T[bp]])
                              cx.run("dve", [lambda e, bp=bp, c=c: e.tensor_tensor(out=ysum[:, c, :], in0=ysum[:, c, :], in1=Bs2[bp], op=ALU.add)],
                                     reads=[Bs2_T[bp], hs_T[c]], writes=[hs_T[c]])
                  head_norm_out(ysum, hs_T, rnw, h, gsil, o_T, h_cT, hcT_T, t1, t_T, sT_all[:, 0:1536], sTa_T[0:3])
                  if l + 1 < L:
                      mod_groups(l + 1, [3 * h, 3 * h + 1, 3 * h + 2])
              dbg_out("hcT_l%d" % l, h_cT, [128, 4, T], [hcT_T])

              phase_end("ret%d" % l)
              aset(15360)
              macc = af32(4 * T).rearrange("p (a t) -> p a t", t=T)
              macc_T = [TT("macc%d" % i) for i in range(4)]
              sgm = [af32(512) for _ in range(2)]
              sgm_T = [TT("sgm0"), TT("sgm1")]
              k = 0
              branches = ((mow_d, h_aT, haT_T), (cow_d, ubT, ub_T), (row_d, h_cT, hcT_T))
              for jg in range(2):
                  for b, (wd, src, srcT) in enumerate(branches):
                      wo, woT = wload(wd[l][:, jg * 512:(jg + 1) * 512], 4, 512)
                      wgm, wgmT = wload(in_w_d[l][:, OFF_GM + b * 1024 + jg * 512:OFF_GM + b * 1024 + (jg + 1) * 512], 8, 512)
                      for jj in range(4):
                          j = jg * 4 + jj
                          for nb in range(3):
                              sl = slice(nb * 512, (nb + 1) * 512)
                              by, byT = psum()
                              cx.run("pe", [mm(by, wo[:, kc, jj * 128:(jj + 1) * 128], src[:, kc, sl], kc == 0, kc == 3) for kc in range(4)], reads=[woT, srcT], writes=[byT])
                              bg, bgT = psum()
                              cx.run("pe", [mm(bg, wgm[:, kc, jj * 128:(jj + 1) * 128], hT[:, kc, sl], kc == 0, kc == 7) for kc in range(8)], reads=[wgmT, hT_T], writes=[bgT])
                              pp = k % 2
                              k += 1
                              cx.run("act", [lambda e, bg=bg, pp=pp: e.activation(out=sgm[pp], in_=bg, func=AF.Sigmoid)], reads=[bgT], writes=[sgm_T[pp]])
                              if b == 0:
                                  cx.run("dve", [lambda e, by=by, pp=pp, sl=sl, jj=jj: e.tensor_tensor(out=macc[:, jj, sl], in0=by, in1=sgm[pp], op=ALU.mult)],
                                         reads=[byT, sgm_T[pp]], writes=[macc_T[jj]])
                              else:
                                  cx.run("dve", [lambda e, by=by, pp=pp: e.tensor_tensor(out=sgm[pp], in0=by, in1=sgm[pp], op=ALU.mult)],
                                         reads=[byT, sgm_T[pp]], writes=[sgm_T[pp]])
                                  if b == 1:
                                      cx.run("dve", [lambda e, pp=pp, sl=sl, jj=jj: e.tensor_tensor(out=macc[:, jj, sl], in0=macc[:, jj, sl], in1=sgm[pp], op=ALU.add)],
                                             reads=[sgm_T[pp], macc_T[jj]], writes=[macc_T[jj]])
                                  else:
                                      cx.run("dve", [lambda e, pp=pp, sl=sl, j=j, jj=jj: e.tensor_tensor(out=merged[:, j, sl], in0=macc[:, jj, sl], in1=sgm[pp], op=ALU.add)],
                                             reads=[sgm_T[pp], macc_T[jj]], writes=[mg_T])
              dbg_out("merged_l%d" % l, merged, [128, 8, T], [mg_T])
              phase_end("merge%d" % l)
              aset(15360)
              xb = abf(8 * 512).rearrange("p (a n) -> p a n", n=512)
              xq = abf(8 * 512).rearrange("p (a n) -> p a n", n=512)
              lnt = (af32(512), af32(512), af32(512))
              cx.run("act", [lambda e, kc=kc: e.activation(out=x[:, kc, :], in_=x[:, kc, :], func=AF.Copy, scale=ALPHA) for kc in range(8)], reads=[x_T], writes=[x_T])
              for jg in range(2):
                  wo, woT = wload(outw_d[l][:, jg * 512:(jg + 1) * 512], 8, 512)
                  for jj in range(4):
                      j = jg * 4 + jj
                      for nb in range(3):
                          sl = slice(nb * 512, (nb + 1) * 512)
                          v = 1 if nb < 2 else 0
                          bk, bkT = psum()
                          cx.run("pe", [mm(bk, wo[:, kc, jj * 128:(jj + 1) * 128], merged[:, kc, sl], kc == 0, kc == 7) for kc in range(8)], reads=[woT, mg_T], writes=[bkT])
                          cx.run("dve", [lambda e, bk=bk, j=j, sl=sl, v=v: e.scalar_tensor_tensor(out=x[:, j, sl], in0=bk, scalar=modT[:, 16 + j, v:v + 1],
                                                                                                 in1=x[:, j, sl], op0=ALU.mult, op1=ALU.add)],
                                 reads=[bkT, x_T, M_T[l]], writes=[x_T])
              layer_norm_fm(0, [(0, 512), (512, 1024), (1024, 1536)], xb, xq, lnt)
              dbg_out("x1_l%d" % l, x, [128, 8, T], [x_T])
              phase_end("outp%d" % l)
              aset(0)
              ffT = abf(NFT * T).rearrange("p (a n) -> p a n", n=T)
              ff_T = TT("ffT")
              a_sb = abf(4 * T).rearrange("p (a n) -> p a n", n=T)
              asb_T = TT("a_sb")
              sgf = [af32(512) for _ in range(2)]
              sgf_T = [TT("sgf0"), TT("sgf1")]
              modulate(sc2p, 24)
              cx.run("act", [lambda e, kc=kc: e.activation(out=x[:, kc, :], in_=x[:, kc, :], func=AF.Copy, scale=ALPHA) for kc in range(8)], reads=[x_T], writes=[x_T])
              k = 0
              for g in range(6):
                  nt = 4 if g < 5 else 2
                  wa, waT = wload(w13_d[l][:, g * 512:g * 512 + nt * 128], 8, nt * 128)
                  wg2, wg2T = wload(w13_d[l][:, FF + g * 512:FF + g * 512 + nt * 128], 8, nt * 128)
                  for jj in range(nt):
                      for nb in range(3):
                          sl = slice(nb * 512, (nb + 1) * 512)
                          bk, bkT = psum()
                          cx.run("pe", [mm(bk, wa[:, kc, jj * 128:(jj + 1) * 128], hT[:, kc, sl], kc == 0, kc == 7) for kc in range(8)],
                                 reads=[waT, hT_T], writes=[bkT])
                          cx.run("act", [lambda e, bk=bk, jj=jj, sl=sl: e.activation(out=a_sb[:, jj, sl], in_=bk, func=AF.Copy)],
                                 reads=[bkT], writes=[asb_T])
                  for jj in range(nt):
                      for nb in range(3):
                          sl = slice(nb * 512, (nb + 1) * 512)
                          bk, bkT = psum()
                          cx.run("pe", [mm(bk, wg2[:, kc, jj * 128:(jj + 1) * 128], hT[:, kc, sl], kc == 0, kc == 7) for kc in range(8)],
                                 reads=[wg2T, hT_T], writes=[bkT])
                          pp = k % 2
                          k += 1
                          cx.run("act", [lambda e, bk=bk, pp=pp: e.activation(out=sgf[pp], in_=bk, func=AF.Silu)], reads=[bkT], writes=[sgf_T[pp]])
                          cx.run("dve", [lambda e, pp=pp, jj=jj, g=g, sl=sl: e.tensor_tensor(out=ffT[:, g * 4 + jj, sl], in0=sgf[pp], in1=a_sb[:, jj, sl], op=ALU.mult)],
                                 reads=[sgf_T[pp], asb_T], writes=[ff_T])
              for j in range(8):
                  w2v, w2T = wload(w2_d[l][:, j * 128:(j + 1) * 128], NFT, 128)
                  for nb in range(3):
                      sl = slice(nb * 512, (nb + 1) * 512)
                      v = 1 if nb < 2 else 0
                      bk, bkT = psum()
                      cx.run("pe", [mm(bk, w2v[:, kc, :], ffT[:, kc, sl], kc == 0, kc == NFT - 1) for kc in range(NFT)], reads=[w2T, ff_T], writes=[bkT])
                      cx.run("dve", [lambda e, bk=bk, j=j, sl=sl, v=v: e.scalar_tensor_tensor(
                          out=x[:, j, sl], in0=bk, scalar=modT[:, 40 + j, v:v + 1], in1=x[:, j, sl], op0=ALU.mult, op1=ALU.add)],
                          reads=[bkT, x_T, M_T[l]], writes=[x_T])
              aset(0)
              xb = abf(8 * 512).rearrange("p (a n) -> p a n", n=512)
              xq = abf(8 * 512).rearrange("p (a n) -> p a n", n=512)
              lnt = (af32(512), af32(512), af32(512))
              layer_norm_fm(1, [(0, 512), (512, 1024), (1024, 1536)], xb, xq, lnt)
              dbg_out("x2_l%d" % l, x, [128, 8, T], [x_T])

        try:
            layers()
        except _Stop:
            pass
        for kc in range(8):
            cx.dma("sp", yT_d[:, kc, :], x[:, kc, :], reads=[x_T])
        cx.final()

        with nc.Block() as block:
            def replay(name):
                def f(e):
                    for th in cx.prog[name]:
                        th(e)
                return f
            block.tensor(replay("pe"))
            block.scalar(replay("act"))
            block.vector(replay("dve"))
            block.gpsimd(replay("pool"))
            block.sync(replay("sp"))
    return nc


_IDX = np.concatenate([np.arange(0, 128, 2), np.arange(1, 128, 2)])
_IDXS = np.concatenate([np.arange(1, 128, 2), np.arange(0, 128, 2)])


def _prow(r):
    return (r // 4) * 32 + (r % 4)


def _in_cols():
    o = dict(mq=0, mk=512, mv=1024, mo=1536, mg=2048, ca=2064, cg=2576, rq=3088, rk=3600, rv=4112, rg=4624, gm=5136)
    cols = []
    for h in range(H):
        for nm in ("mq", "mk", "mv", "mo"):
            cols += list(range(o[nm] + h * 128, o[nm] + (h + 1) * 128))
    z = [-1] * 28
    mg = o["mg"]
    cols += [mg + 0 * 4 + h for h in range(4)] + z + [mg + 2 * 4 + h for h in range(4)]
    cols += [mg + 1 * 4 + h for h in range(4)] + z + [mg + 3 * 4 + h for h in range(4)]
    cols += list(range(o["ca"], o["ca"] + 512)) + list(range(o["cg"], o["cg"] + 512))
    for h in range(H):
        for nm in ("rq", "rk"):
            cols += list(o[nm] + h * 128 + _IDX) + list(o[nm] + h * 128 + _IDXS)
    for h in range(H):
        cols += list(range(o["rv"] + h * 128, o["rv"] + (h + 1) * 128)) + list(range(o["rg"] + h * 128, o["rg"] + (h + 1) * 128))
    cols += list(range(o["gm"], o["gm"] + 3072))
    cols = np.array(cols, dtype=np.int64)
    assert cols.shape[0] == NCOLS, cols.shape
    return cols


_NC_CACHE = {}


def kernel(x_prompt, x_sample, state_mlstm_C, state_mlstm_n, state_mlstm_m, state_ret_S, c, c_ctx,
           ada_w, ada_b, in_w, mlstm_gate_b, mlstm_norm_w, mlstm_out_w, conv_w, conv_b, conv_ln_w, conv_ln_b,
           conv_out_w, ret_decay, ret_norm_w, ret_out_w, out_w, ln1_w, ln1_b, ln2_w, ln2_b, ffn_w13, ffn_w2, _dbg=False):
    f32 = np.float32
    A = lambda a: np.ascontiguousarray(np.asarray(a, dtype=f32))
    x_prompt, x_sample = A(x_prompt), A(x_sample)
    sC, sn, smm, sS = A(state_mlstm_C), A(state_mlstm_n), A(state_mlstm_m), A(state_ret_S)
    c, c_ctx = A(c), A(c_ctx)
    in_w = A(in_w)
    cols = _in_cols()
    in_wp = np.zeros((L, D, NCOLS), f32)
    valid = cols >= 0
    in_wp[:, :, valid] = in_w[:, :, cols[valid]]
    gbias = A(mlstm_gate_b)
    gb = np.zeros((L, 36, 2), f32)
    for h in range(4):
        gb[:, h, 0] = gbias[:, 0, h]; gb[:, h, 1] = gbias[:, 1, h]
        gb[:, 32 + h, 0] = gbias[:, 2, h]; gb[:, 32 + h, 1] = gbias[:, 3, h]
    convp = np.zeros((L, 128, 4, 34), f32)
    cw = A(conv_w)
    convp[:, :, :, 0:31] = cw.reshape(L, CONV_K, 4, 128).transpose(0, 3, 2, 1)
    convp[:, :, :, 31] = A(conv_b).reshape(L, 4, 128).transpose(0, 2, 1)
    convp[:, :, :, 32] = A(conv_ln_w).reshape(L, 4, 128).transpose(0, 2, 1)
    convp[:, :, :, 33] = A(conv_ln_b).reshape(L, 4, 128).transpose(0, 2, 1)
    lnp = np.zeros((L, 128, 4, 8), f32)
    for i, a in enumerate((ln1_w, ln1_b, ln2_w, ln2_b)):
        lnp[:, :, i, :] = A(a).reshape(L, 8, 128).transpose(0, 2, 1)
    ada_bT = np.ascontiguousarray(A(ada_b).reshape(L, 48, 128).transpose(0, 2, 1))
    rdec = A(ret_decay).reshape(L, 1, 8)
    mnw = A(mlstm_norm_w).reshape(L, 1, 512)
    rnw = A(ret_norm_w).reshape(L, 1, 512)
    cst = np.zeros((128, 1024), f32)
    jj, ii = np.meshgrid(np.arange(128), np.arange(128), indexing="ij")
    cst[:, 0:128] = np.eye(128, dtype=f32)
    cst[:, 128:256] = np.where(jj <= ii, 0.0, NEG)
    cst[:, 256:384] = np.where(jj >= ii, 0.0, NEG)
    cst[:, 384:512] = np.maximum(ii - jj, 0)
    cst[:, 512:640] = np.maximum(jj - ii, 0)
    cst[:, 640:768] = (jj <= ii)
    cst[:, 768:896] = (jj >= ii)
    p = np.arange(128)
    cst[:, 896] = p + 1; cst[:, 897] = 128 - p; cst[:, 898] = 127 - p; cst[:, 899] = p; cst[:, 900] = 128
    sel = np.zeros((36, 8, 128), f32)
    for r in range(8):
        sel[_prow(r), r, :] = 1.0
    t = np.arange(1024)
    rows = (t // 64).astype(f32); colsg = (t % 64).astype(f32)
    freqs = (np.float32(10000.0) ** (-np.arange(32, dtype=f32) / np.float32(32))).astype(f32)
    ang = np.concatenate([rows[:, None] * freqs[None, :], colsg[:, None] * freqs[None, :]], -1).astype(f32)
    cs_lat = np.concatenate([np.cos(ang).T, np.cos(ang).T], 0).astype(f32)
    sn_lat = np.concatenate([-np.sin(ang).T, np.sin(ang).T], 0).astype(f32)
    cs_id = np.ones((128, 1024), f32); sn_id = np.zeros((128, 1024), f32)

    shared = dict(cst=cst, sel=sel, ada_w=A(ada_w), ada_bT=ada_bT, in_wp=in_wp, gb=gb, mnw=mnw, rnw=rnw,
                  mlstm_out_w=A(mlstm_out_w), conv_out_w=A(conv_out_w), ret_out_w=A(ret_out_w), convp=convp, rdec=rdec,
                  out_w=A(out_w), lnp=lnp, ffn_w13=A(ffn_w13), ffn_w2=A(ffn_w2))
    in_maps = []
    seg_prompt = []
    for core in range(8):
        if core < 4:
            xs = np.concatenate([x_sample[core], x_prompt[2 * core], x_prompt[2 * core + 1]], 0)
            seg_prompt.append({4: 2 * core, 5: 2 * core + 1})
            cvec = c[core]
            Cinit = np.concatenate([sC[core], sn[core][..., None]], -1)
            minit = np.zeros((L, 36, 1), f32)
            for h in range(4):
                minit[:, h, 0] = smm[core, :, 0, h]; minit[:, 32 + h, 0] = smm[core, :, 1, h]
            Sinit = sS[core][:, :, :, _IDX, :]
            chainv = 1.0
            rcs, rsn = cs_lat, sn_lat
        else:
            base = 8 + 6 * (core - 4)
            xs = np.concatenate([x_prompt[base + s] for s in range(6)], 0)
            seg_prompt.append({s: base + s for s in range(6)})
            cvec = c_ctx
            Cinit = np.zeros((L, 2, H, 128, 129), f32)
            minit = np.zeros((L, 36, 1), f32)
            Sinit = np.zeros((L, 2, H, 128, 128), f32)
            chainv = 0.0
            rcs, rsn = cs_id, sn_id
        xT = np.ascontiguousarray(xs.reshape(T, 8, 128).transpose(2, 1, 0))
        cv = np.stack([c_ctx.reshape(8, 128).T, cvec.reshape(8, 128).T], -1)
        m = dict(shared)
        m.update(xT=xT, cv=np.ascontiguousarray(cv, dtype=f32), chain=np.full((128, 1), chainv, f32),
                 Cinit=np.ascontiguousarray(Cinit, dtype=f32), minit=minit, Sinit=np.ascontiguousarray(Sinit, dtype=f32),
                 ropeCS=rcs, ropeSN=rsn)
        in_maps.append(m)

    key = bool(_dbg)
    nc = build(dbg=key)
    res = run_bass_kernel_spmd(nc, in_maps, core_ids=list(range(8)))
    R = res.results
    y_prompt = np.zeros((32, 256, D), f32)
    y_sample = np.zeros((4, 1024, D), f32)
    new_C = np.zeros((32, L, 2, H, 128, 128), f32)
    new_n = np.zeros((32, L, 2, H, 128), f32)
    new_m = np.zeros((32, L, 2, H), f32)
    new_S = np.zeros((32, L, 2, H, 128, 128), f32)
    for core in range(8):
        r = R[core]
        yt = np.asarray(r["yT"]).transpose(2, 1, 0).reshape(T, D)
        if core < 4:
            y_sample[core] = yt[0:1024]
        Cf = np.asarray(r["Cfin"]); mf = np.asarray(r["mfin"]); Sf = np.asarray(r["Sfin"])
        for s, b in seg_prompt[core].items():
            y_prompt[b] = yt[s * 256:(s + 1) * 256]
            new_C[b] = Cf[:, s, :, :, :, 0:128]
            new_n[b] = Cf[:, s, :, :, :, 128]
            for h in range(4):
                new_m[b, :, 0, h] = mf[:, h, 2 * s + 1]
                new_m[b, :, 1, h] = mf[:, 32 + h, 2 * s]
            Su = np.empty((L, 2, H, 128, 128), f32)
            Su[:, :, :, _IDX, :] = Sf[:, s]
            new_S[b] = Su
    if _dbg:
        return (y_prompt, y_sample, new_C, new_n, new_m, new_S), R
    return (y_prompt, y_sample, new_C, new_n, new_m, new_S)
```

```python
import math
import itertools
import numpy as np
import concourse.bass as bass
import concourse.mybir as mybir
from concourse.bass_utils import run_bass_kernel_spmd
from concourse.ap import AP

F32 = mybir.dt.float32
BF16 = mybir.dt.bfloat16
AF = mybir.ActivationFunctionType
ALU = mybir.AluOpType
AX = mybir.AxisListType

D = 1024
L = 2
T = 1536
NCH = 12
NSEG = 6
H = 4
HD = 128
FF = 2816
NFT = 22
CONV_K = 31
EPS = 1e-5
ALPHA = (2.0 * L) ** 0.25
KS = HD ** -0.5
LNKS = math.log(KS)
NEG = -30000.0
NCOLS = 9288
OFF_GATE = 2048
OFF_CA = 2120
OFF_CG = 2632
OFF_RET = 3144
OFF_RVG = 5192
OFF_GM = 6216

DEBUG = {}
STOP = None


class _Stop(Exception):
    pass


def phase_end(name):
    if STOP == name:
        raise _Stop()


def rev(ap):
    a = [list(x) for x in ap.ap]
    step, n = a[-1]
    off = ap.offset + step * (n - 1)
    a[-1] = [-step, n]
    return AP(ap.tensor, off, a)


import types


def _snap(th):
    cl = th.__closure__
    if not cl:
        return th
    cells = []
    for c in cl:
        try:
            cells.append(types.CellType(c.cell_contents))
        except ValueError:
            cells.append(c)
    return types.FunctionType(th.__code__, th.__globals__, th.__name__, th.__defaults__, tuple(cells))


class TT:
    __slots__ = ("name", "w", "r")

    def __init__(self, name=""):
        self.name = name
        self.w = None
        self.r = {}


class Ctx:
    ENG = ["pe", "act", "dve", "pool", "sp"]

    def __init__(self, nc, sems):
        self.nc = nc
        self.sems = sems
        self.prog = {e: [] for e in self.ENG}
        self.cnt = {e: 0 for e in self.ENG}
        self.seen = {e: {} for e in self.ENG}
        self.dkeys = [k for k in sems if k[0] == "d" and k[1:].isdigit()]
        self.wkeys = [k for k in sems if k[0] == "w" and k[1:].isdigit()]
        self.dtot = {k: 0 for k in sems}
        self.dn = 0
        self.wn = 0

    def _wait(self, eng, key, val):
        if val <= 0 or self.seen[eng].get(key, 0) >= val:
            return
        self.seen[eng][key] = val
        sem = self.sems[key]
        self.prog[eng].append(lambda e, sem=sem, val=val: e.wait_ge(sem, val))

    def _deps(self, eng, reads, writes):
        for t in reads:
            if t.w is not None:
                self._wait_dep(eng, t.w)
        for t in writes:
            if t.w is not None:
                self._wait_dep(eng, t.w)
            for k, v in t.r.items():
                self._wait_dep(eng, (k, v))

    def _wait_dep(self, eng, dep):
        k, v = dep
        if k == "pe" and eng == "pe":
            return
        self._wait(eng, k, v)

    def run(self, eng, thunks, reads=(), writes=()):
        if not isinstance(thunks, (list, tuple)):
            thunks = [thunks]
        thunks = [_snap(t) for t in thunks]
        self._deps(eng, reads, writes)
        sem = self.sems[eng]
        n = len(thunks)
        for i, th in enumerate(thunks):
            if i == n - 1:
                self.prog[eng].append(lambda e, th=th, sem=sem: th(e).then_inc(sem, 1))
            else:
                self.prog[eng].append(th)
        self.cnt[eng] += 1
        c = self.cnt[eng]
        for t in reads:
            t.r[eng] = c
        for t in writes:
            t.w = (eng, c)
            t.r = {}

    def dma(self, q, out, in_, reads=(), writes=()):
        if q == "pool":
            key = self.wkeys[self.wn % len(self.wkeys)]
            self.wn += 1
        else:
            key = self.dkeys[self.dn % len(self.dkeys)]
            self.dn += 1
        prev = self.dtot[key]
        self._wait(q, key, prev)
        self._deps(q, reads, writes)
        new = prev + 16
        self.dtot[key] = new
        sem = self.sems[key]
        self.prog[q].append(lambda e, out=out, in_=in_, sem=sem: e.dma_start(out=out, in_=in_).then_inc(sem, 16))
        for t in reads:
            t.r[key] = new
        for t in writes:
            t.w = (key, new)
            t.r = {}

    def barrier(self):
        engs = ["pe", "act", "dve", "sp"]
        for e in engs:
            for f in ["pe", "act", "dve"]:
                if f != e or e != "pe":
                    self._wait(e, f, self.cnt[f])
            for k in self.dkeys:
                self._wait(e, k, self.dtot[k])

    def final(self):
        for k in self.dkeys:
            self._wait("sp", k, self.dtot[k])
        for f in ["pe", "act", "dve"]:
            self._wait("sp", f, self.cnt[f])


def build(dbg=False):
    nc = bass.Bass("TRN2", target_bir_lowering=False)
    dram = {}

    def din(name, shape, dt=F32):
        dram[name] = nc.dram_tensor(name, list(shape), dt, kind="ExternalInput").ap()
        return dram[name]

    def dout(name, shape, dt=F32):
        dram[name] = nc.dram_tensor(name, list(shape), dt, kind="ExternalOutput").ap()
        return dram[name]

    xT_d = din("xT", [128, 8, T])
    cv_d = din("cv", [128, 8, 2])
    chain_d = din("chain", [128, 1])
    Cinit_d = din("Cinit", [L, 2, H, 128, 129])
    minit_d = din("minit", [L, 36, 1])
    Sinit_d = din("Sinit", [L, 2, H, 128, 128])
    ropeCS_d = din("ropeCS", [128, 1024])
    ropeSN_d = din("ropeSN", [128, 1024])
    cst_d = din("cst", [128, 1024])
    sel_d = din("sel", [36, 8, 128])
    ada_w_d = din("ada_w", [L, D, 6 * D])
    ada_b_d = din("ada_bT", [L, 128, 48])
    in_w_d = din("in_wp", [L, D, NCOLS])
    gb_d = din("gb", [L, 36, 2])
    mnw_d = din("mnw", [L, 1, 512])
    rnw_d = din("rnw", [L, 1, 512])
    mow_d = din("mlstm_out_w", [L, 512, D])
    cow_d = din("conv_out_w", [L, 512, D])
    row_d = din("ret_out_w", [L, 512, D])
    convp_d = din("convp", [L, 128, 4, 34])
    rdec_d = din("rdec", [L, 1, 8])
    outw_d = din("out_w", [L, D, D])
    lnp_d = din("lnp", [L, 128, 4, 8])
    w13_d = din("ffn_w13", [L, D, 2 * FF])
    w2_d = din("ffn_w2", [L, FF, D])

    yT_d = dout("yT", [128, 8, T])
    Cfin_d = dout("Cfin", [L, NSEG, 2, H, 128, 129])
    mfin_d = dout("mfin", [L, 36, 12])
    Sfin_d = dout("Sfin", [L, NSEG, 2, H, 128, 128])
    dbg_d = {}

    import contextlib
    es = contextlib.ExitStack()
    with es:
        def sb(name, shape, dt=F32):
            return es.enter_context(nc.sbuf_tensor("s_" + name, list(shape), dt))[:]

        x = sb("x", [128, 8, T])
        hT = sb("hT", [128, 8, T], BF16)
        NW = 3
        Wr = [sb("wr%d" % i, [128, 4096], BF16) for i in range(NW)]
        Wr_T = [TT("wr%d" % i) for i in range(NW)]
        cst = sb("cst", [128, 1024])
        sel = sb("sel", [36, 8, 128])
        identB = sb("identB", [128, 128], BF16)
        onesB = sb("onesB", [128, 128], BF16)
        chain = sb("chain", [128, 1])
        cvs = sb("cvs", [128, 8, 2], BF16)
        cvf = sb("cvf", [128, 8, 2])
        modTs = [sb("modT%d" % i, [128, 48, 2]) for i in range(L)]
        sc1ps = [sb("sc1p%d" % i, [128, 8, 2]) for i in range(L)]
        sc2ps = [sb("sc2p%d" % i, [128, 8, 2]) for i in range(L)]
        adabs = [sb("adab%d" % i, [128, 48]) for i in range(L)]
        M_T = [TT("mod%d" % i) for i in range(L)]
        gb = sb("gb", [36, 2])
        ngb = sb("ngb", [36, 1])
        minit = sb("minit", [36, 1])
        mnw = sb("mnw", [128, 512])
        rnw = sb("rnw", [128, 512])
        convp = sb("convp", [128, 4, 34])
        lnp = sb("lnp", [128, 4, 8])
        rdec = sb("rdec", [128, 8])
        lg = sb("lg", [128, 8])
        rcol = sb("rcol", [128, 8, 4])
        decT = sb("decT", [128, 8, 128])
        ones36 = sb("ones36", [36, 128])
        zeros36 = sb("zeros36", [36, 128])
        AR = 23168
        arena = sb("arena", [128, AR])
        ps = [es.enter_context(nc.psum_tensor("ps%d" % i, [128, 512], F32))[:] for i in range(8)]
        ps_T = [TT("ps%d" % i) for i in range(8)]

        keys = ["pe", "act", "dve", "pool", "sp"] + ["d%d" % i for i in range(8)] + ["w%d" % i for i in range(6)]
        sems = {k: es.enter_context(nc.semaphore(k)) for k in keys}
        cx = Ctx(nc, sems)
        G = TT("globals")
        x_T = TT("x")
        hT_T = TT("hT")

        identF = cst[:, 0:128]
        MASK = [cst[:, 128:256], cst[:, 256:384]]
        DIFF = [cst[:, 384:512], cst[:, 512:640]]
        M01 = [cst[:, 640:768], cst[:, 768:896]]
        POS = cst[:, 896:904]

        pstate = {"i": 0}

        def psum():
            i = pstate["i"] % 8
            pstate["i"] += 1
            return ps[i], ps_T[i]

        wstate = {"i": 0}

        def wload(src, kc, ncol):
            i = wstate["i"] % NW
            wstate["i"] += 1
            dst = Wr[i][:, 0:kc * ncol].rearrange("p (k n) -> p k n", n=ncol)
            cx.dma("pool", dst, src.rearrange("(k p) n -> p k n", p=128), writes=[Wr_T[i]])
            return dst, Wr_T[i]

        ast = {"o": 0}

        def aset(o):
            cx.barrier()
            ast["o"] = o

        def af32(n, shape=None):
            o = ast["o"]
            ast["o"] += n
            assert ast["o"] <= AR, ast["o"]
            v = arena[:, o:o + n]
            return v

        def abf(n):
            n2 = (n + 1) // 2
            return af32(n2).bitcast(BF16)

        def dbg_out(name, ap, shape, reads):
            if not dbg:
                return
            d = dout("dbg_" + name, shape, ap.dtype)
            DEBUG[name] = shape
            cx.dma("sp", d, ap, reads=reads)

        mm = lambda out, lhsT, rhs, st, sp: (lambda e: e.matmul(out, lhsT=lhsT, rhs=rhs, start=st, stop=sp))

        for kc in range(8):
            cx.dma("sp", x[:, kc, :], xT_d[:, kc, :], writes=[x_T])
        for dst, src in ((cst, cst_d), (sel, sel_d), (chain, chain_d), (cvf, cv_d)):
            cx.dma("sp", dst, src, writes=[G])
        cx.run("dve", [lambda e: e.memset(ones36, 1.0), lambda e: e.memset(zeros36, 0.0),
                       lambda e: e.memset(onesB, 1.0),
                       lambda e: e.tensor_copy(out=identB, in_=identF)],
               reads=[G], writes=[G])
        cx.run("act", [lambda e: e.activation(out=cvs, in_=cvf, func=AF.Silu)], reads=[G], writes=[G])
        for i in range(L):
            cx.dma("sp", adabs[i], ada_b_d[i], writes=[M_T[i]])

        def mod_groups(ll, gs):
            mT = modTs[ll]
            for g in gs:
                wv, wT = wload(ada_w_d[ll][:, g * 512:(g + 1) * 512], 8, 512)
                mb, mbT = psum()
                for jj in range(4):
                    cx.run("pe", [mm(mb[:, 2 * jj:2 * jj + 2], wv[:, kc, jj * 128:(jj + 1) * 128], cvs[:, kc, :], kc == 0, kc == 7) for kc in range(8)],
                           reads=[wT, G], writes=[mbT])
                j0 = g * 4
                cx.run("act", [lambda e, mb=mb, j0=j0: e.activation(out=mT[:, j0:j0 + 4, :].rearrange("p j v -> p (j v)"), in_=mb[:, 0:8], func=AF.Copy)],
                       reads=[mbT], writes=[M_T[ll]])
                cx.run("dve", [lambda e, j0=j0: e.tensor_tensor(out=mT[:, j0:j0 + 4, :], in0=mT[:, j0:j0 + 4, :],
                                                                in1=adabs[ll][:, j0:j0 + 4].unsqueeze(2).to_broadcast([128, 4, 2]), op=ALU.add)],
                       reads=[M_T[ll]], writes=[M_T[ll]])
                if g in (2, 3):
                    o = (g - 2) * 4
                    cx.run("dve", [lambda e, j0=j0, o=o: e.tensor_scalar(out=sc1ps[ll][:, o:o + 4, :], in0=mT[:, j0:j0 + 4, :], scalar1=1.0, scalar2=None, op0=ALU.add)],
                           reads=[M_T[ll]], writes=[M_T[ll]])
                if g in (8, 9):
                    o = (g - 8) * 4
                    cx.run("dve", [lambda e, j0=j0, o=o: e.tensor_scalar(out=sc2ps[ll][:, o:o + 4, :], in0=mT[:, j0:j0 + 4, :], scalar1=1.0, scalar2=None, op0=ALU.add)],
                           reads=[M_T[ll]], writes=[M_T[ll]])

        def layer_norm_fm(which, blocks, xb, xq, lnt):
            lw = lnp[:, 2 * which, :]
            lb = lnp[:, 2 * which + 1, :]
            xb_T, xq_T, lnt_T = TT("xb"), TT("xq"), TT("lnt")
            for (a_, b_) in blocks:
                sl = slice(a_, b_)
                cx.run("act", [lambda e, kc=kc: e.activation(out=xb[:, kc, :], in_=x[:, kc, sl], func=AF.Copy) for kc in range(8)],
                       reads=[x_T], writes=[xb_T])
                cx.run("act", [lambda e, kc=kc: e.activation(out=xq[:, kc, :], in_=x[:, kc, sl], func=AF.Square) for kc in range(8)],
                       reads=[x_T], writes=[xq_T])
                b1, b1T = psum()
                cx.run("pe", [mm(b1, onesB, xb[:, kc, :], kc == 0, kc == 7) for kc in range(8)], reads=[xb_T], writes=[b1T])
                b2, b2T = psum()
                cx.run("pe", [mm(b2, onesB, xq[:, kc, :], kc == 0, kc == 7) for kc in range(8)], reads=[xq_T], writes=[b2T])
                mean, msq, rstd = lnt
                cx.run("act", [lambda e: e.activation(out=mean, in_=b1, func=AF.Copy, scale=1.0 / D)], reads=[b1T], writes=[lnt_T])
                cx.run("dve", [lambda e: e.tensor_tensor(out=msq, in0=mean, in1=mean, op=ALU.mult)], reads=[lnt_T], writes=[lnt_T])
                cx.run("dve", [lambda e: e.scalar_tensor_tensor(out=msq, in0=b2, scalar=1.0 / D, in1=msq, op0=ALU.mult, op1=ALU.subtract)],
                       reads=[b2T, lnt_T], writes=[lnt_T])
                cx.run("dve", [lambda e: e.tensor_scalar(out=msq, in0=msq, scalar1=EPS, scalar2=None, op0=ALU.add)], reads=[lnt_T], writes=[lnt_T])
                cx.run("act", [lambda e: e.activation(out=rstd, in_=msq, func=AF.Sqrt)], reads=[lnt_T], writes=[lnt_T])
                cx.run("dve", [lambda e: e.reciprocal(out=rstd, in_=rstd)], reads=[lnt_T], writes=[lnt_T])
                cx.run("dve", [lambda e, kc=kc: e.tensor_tensor(out=x[:, kc, sl], in0=x[:, kc, sl], in1=mean, op=ALU.subtract) for kc in range(8)],
                       reads=[x_T, lnt_T], writes=[x_T])
                cx.run("dve", [lambda e, kc=kc: e.tensor_tensor(out=x[:, kc, sl], in0=x[:, kc, sl], in1=rstd, op=ALU.mult) for kc in range(8)],
                       reads=[x_T, lnt_T], writes=[x_T])
                cx.run("act", [lambda e, kc=kc: e.activation(out=x[:, kc, sl], in_=x[:, kc, sl], func=AF.Identity,
                                                             scale=lw[:, kc:kc + 1], bias=lb[:, kc:kc + 1]) for kc in range(8)],
                       reads=[x_T, G], writes=[x_T])

        lnks = sb("lnks", [128, 1])
        cx.run("dve", [lambda e: e.memset(lnks, LNKS)], writes=[G])

        def ln_stats(b1, b1T, b2, b2T, mean, msq, rstd, st_T, n):
            cx.run("act", [lambda e: e.activation(out=mean, in_=b1, func=AF.Copy, scale=1.0 / n)], reads=[b1T], writes=[st_T])
            cx.run("dve", [lambda e: e.tensor_tensor(out=msq, in0=mean, in1=mean, op=ALU.mult)], reads=[st_T], writes=[st_T])
            cx.run("dve", [lambda e: e.scalar_tensor_tensor(out=msq, in0=b2, scalar=1.0 / n, in1=msq, op0=ALU.mult, op1=ALU.subtract)],
                   reads=[b2T, st_T], writes=[st_T])
            cx.run("dve", [lambda e: e.tensor_scalar(out=msq, in0=msq, scalar1=EPS, scalar2=None, op0=ALU.add)], reads=[st_T], writes=[st_T])
            cx.run("act", [lambda e: e.activation(out=rstd, in_=msq, func=AF.Sqrt)], reads=[st_T], writes=[st_T])
            cx.run("dve", [lambda e: e.reciprocal(out=rstd, in_=rstd)], reads=[st_T], writes=[st_T])

        def head_norm_out(src3, src_T, normw, h, gate3, gate_T, dstT, dst_T, scr, scr_T, hn_all, hn_Ts):
            stA = scr[:, 0:72].rearrange("p (c n) -> p c n", n=6)
            mvA = scr[:, 72:96].rearrange("p (c n) -> p c n", n=2)
            src_all = src3
            hn3 = hn_all.rearrange("p (c n) -> p c n", n=128)
            cx.run("dve", [lambda e, c=c: e.bn_stats(out=stA[:, c, :], in_=src3[:, c, :]) for c in range(12)], reads=list(src_T) + [scr_T], writes=[scr_T])
            cx.run("dve", [lambda e, c=c: e.bn_aggr(out=mvA[:, c, :], in_=stA[:, c, :]) for c in range(12)], reads=[scr_T], writes=[scr_T])
            rstd = mvA[:, :, 1]
            cx.run("dve", [lambda e: e.tensor_scalar(out=rstd, in0=rstd, scalar1=EPS, scalar2=None, op0=ALU.add)], reads=[scr_T], writes=[scr_T])
            cx.run("act", [lambda e: e.activation(out=rstd, in_=rstd, func=AF.Sqrt)], reads=[scr_T], writes=[scr_T])
            cx.run("dve", [lambda e: e.reciprocal(out=rstd, in_=rstd)], reads=[scr_T], writes=[scr_T])
            cx.run("dve", [lambda e, c=c: e.tensor_scalar(out=src3[:, c, :], in0=src3[:, c, :], scalar1=mvA[:, c, 0:1], scalar2=mvA[:, c, 1:2],
                                                          op0=ALU.subtract, op1=ALU.mult) for c in range(12)], reads=list(src_T) + [scr_T], writes=list(src_T))
            cx.run("dve", [lambda e, c=c: e.tensor_tensor(out=src3[:, c, :], in0=src3[:, c, :], in1=normw[:, h * 128:(h + 1) * 128], op=ALU.mult) for c in range(12)],
                   reads=list(src_T) + [G], writes=list(src_T))
            cx.run("dve", [lambda e: e.tensor_tensor(out=hn3, in0=src3, in1=gate3, op=ALU.mult)], reads=list(src_T) + list(gate_T) + list(hn_Ts), writes=list(hn_Ts))
            for c in range(12):
                bk, bkT = psum()
                bkb = bk.bitcast(BF16)
                cx.run("pe", [lambda e, bkb=bkb, c=c: e.transpose(out=bkb[:, 0:128], in_=hn3[:, c, :], identity=identB)], reads=list(hn_Ts) + [G], writes=[bkT])
                cx.run("act", [lambda e, bkb=bkb, c=c: e.activation(out=dstT[:, h, c * 128:(c + 1) * 128], in_=bkb[:, 0:128], func=AF.Copy)], reads=[bkT], writes=[dst_T])

        def layers():
          for l in range(L):
              phase_end('pro%d' % l)
              aset(0)
              for dst, src in ((gb, gb_d[l]), (minit, minit_d[l]), (convp, convp_d[l]), (lnp, lnp_d[l]),
                               (mnw, mnw_d[l].partition_broadcast(128)), (rnw, rnw_d[l].partition_broadcast(128)),
                               (rdec, rdec_d[l].partition_broadcast(128))):
                  cx.dma("sp", dst, src, writes=[G])
              cx.run("dve", [lambda e: e.tensor_scalar(out=ngb, in0=gb[:, 1:2], scalar1=-1.0, scalar2=None, op0=ALU.mult)], reads=[G], writes=[G])
              cx.run("act", [lambda e: e.activation(out=lg, in_=rdec, func=AF.Exp, scale=-1.0)], reads=[G], writes=[G])
              cx.run("act", [lambda e: e.activation(out=lg, in_=lg, func=AF.Ln, bias=1.0)], reads=[G], writes=[G])
              cx.run("dve", [lambda e: e.tensor_scalar(out=lg, in0=lg, scalar1=-1.0, scalar2=None, op0=ALU.mult)], reads=[G], writes=[G])
              for r in range(8):
                  d = r // 4
                  lgc = lg[:, r:r + 1]
                  cx.run("act", [lambda e, r=r, d=d, lgc=lgc: e.activation(out=decT[:, r, :], in_=DIFF[d], func=AF.Exp, scale=lgc)], reads=[G], writes=[G])
                  cx.run("dve", [lambda e, r=r, d=d: e.scalar_tensor_tensor(out=decT[:, r, :], in0=decT[:, r, :], scalar=KS, in1=M01[d],
                                                                           op0=ALU.mult, op1=ALU.mult)], reads=[G], writes=[G])
                  cx.run("act", [lambda e, r=r, d=d, lgc=lgc: e.activation(out=rcol[:, r, 0:1], in_=POS[:, d:d + 1], func=AF.Exp, scale=lgc),
                                 lambda e, r=r, d=d, lgc=lgc: e.activation(out=rcol[:, r, 1:2], in_=POS[:, 2 + d:3 + d], func=AF.Exp, scale=lgc),
                                 lambda e, r=r, d=d, lgc=lgc: e.activation(out=rcol[:, r, 2:3], in_=POS[:, 4:5], func=AF.Exp, scale=lgc)],
                         reads=[G], writes=[G])
                  cx.run("dve", [lambda e, r=r: e.tensor_scalar(out=rcol[:, r, 1:2], in0=rcol[:, r, 1:2], scalar1=KS, scalar2=None, op0=ALU.mult)],
                         reads=[G], writes=[G])

              phase_end('small%d' % l)
              modT, sc1p, sc2p = modTs[l], sc1ps[l], sc2ps[l]
              if l == 0:
                  mod_groups(0, [0, 1, 2, 3])
              phase_end('modmm%d' % l)

              def modulate(scp, shoff):
                  ths = []
                  for kc in range(8):
                      for (a, b, v) in ((0, 1024, 1), (1024, T, 0)):
                          ths.append(lambda e, kc=kc, a=a, b=b, v=v: e.tensor_scalar(
                              out=hT[:, kc, a:b], in0=x[:, kc, a:b], scalar1=scp[:, kc, v:v + 1],
                              scalar2=modT[:, shoff + kc, v:v + 1], op0=ALU.mult, op1=ALU.add))
                  cx.run("dve", ths, reads=[x_T, M_T[l]], writes=[hT_T])

              dbg_out("modT_l%d" % l, modT, [128, 48, 2], [M_T[l]])
              phase_end("mod%d" % l)
              modulate(sc1p, 0)
              phase_end("h%d" % l)
              dbg_out("h_l%d" % l, hT, [128, 8, T], [hT_T])

              aset(3072)
              h_aT = arena[:, 0:3072].bitcast(BF16).rearrange("p (h t) -> p h t", t=T)
              ubT = arena[:, 3072:6144].bitcast(BF16).rearrange("p (h t) -> p h t", t=T)
              h_cT = arena[:, 6144:9216].bitcast(BF16).rearrange("p (h t) -> p h t", t=T)
              merged = arena[:, 9216:15360].bitcast(BF16).rearrange("p (h t) -> p h t", t=T)
              haT_T = TT("h_aT"); ub_T = TT("ubT"); hcT_T = TT("h_cT"); mg_T = TT("merged")
              rIG = af32(T); rA = af32(T); rP = af32(T); rCM = af32(T)
              R_T = TT("rows")
              sm = af32(96).rearrange("p (a c) -> p a c", c=12)
              SM_T = TT("sm")
              cols = af32(3 * 432).rearrange("p (q n) -> p q n", n=432)
              COL_T = TT("cols")
              cbc = af32(96)
              qT = abf(T); kT = abf(T)
              qk_T = TT("qk")
              vext = abf(12 * 130).rearrange("p (c n) -> p c n", n=130)
              v_Tc = [TT("vext%d" % c) for c in range(12)]
              o_Tc = [TT("osig%d" % c) for c in range(12)]
              osig = abf(T).rearrange("p (c n) -> p c n", n=128)
              o_T = TT("osig")
              hsum = af32(T).rearrange("p (c n) -> p c n", n=128)
              hs_T = [TT("hs%d" % c) for c in range(12)]
              kw = [abf(T).rearrange("p (c n) -> p c n", n=128) for _ in range(2)]
              kw_T = [[TT("kw") for c in range(12)] for _ in range(2)]
              Cx = [[af32(130) for _ in range(2)] for _ in range(2)]
              Cx_T = [[TT("cx"), TT("cx")] for _ in range(2)]
              sc2 = af32(8)
              dmg = [af32(512)] * 2
              dmg_T = [TT("dmg0")] * 2
              sT_all = abf(24 * 128)
              sTa_T = [TT("sTa%d" % i) for i in range(6)]
              Cb_all = [abf(12 * 130).rearrange("p (s n) -> p s n", n=130) for _ in range(2)]
              CbA_T = [[TT("cba") for _ in range(12)] for _ in range(2)]
              Bs2 = [af32(130) for _ in range(2)]
              Bs2_T = [TT("bs2a"), TT("bs2b")]
              tot_all = af32(12 * 130).rearrange("p (c n) -> p c n", n=130)
              tota_T = TT("tot_all")
              dn12 = af32(12)
              dn_T = TT("dn12")

              wg, wgT = wload(in_w_d[l][:, OFF_GATE:OFF_GATE + 72], 8, 72)
              cx.run("dve", [lambda e: e.memset(rP[0:36, :], 0.0), lambda e: e.memset(rCM[0:36, :], 0.0)], writes=[R_T])
              for which in (0, 1):
                  for nb in range(3):
                      bk, bkT = psum()
                      sl = slice(nb * 512, (nb + 1) * 512)
                      cx.run("pe", [mm(bk[0:36, :], wg[:, kc, which * 36:(which + 1) * 36], hT[:, kc, sl], kc == 0, kc == 7) for kc in range(8)],
                             reads=[wgT, hT_T], writes=[bkT])
                      if which == 0:
                          cx.run("act", [lambda e, bk=bk, sl=sl: e.activation(out=rIG[0:36, sl], in_=bk[0:36, :], func=AF.Identity, bias=gb[:, 0:1])],
                                 reads=[bkT, G], writes=[R_T])
                      else:
                          cx.run("act", [lambda e, bk=bk, sl=sl: e.activation(out=rA[0:36, sl], in_=bk[0:36, :], func=AF.Exp, scale=-1.0, bias=ngb[:, 0:1])],
                                 reads=[bkT, G], writes=[R_T])
              cx.run("act", [lambda e: e.activation(out=rA[0:36, :], in_=rA[0:36, :], func=AF.Ln, bias=1.0)], reads=[R_T], writes=[R_T])
              ths = []
              for c in range(12):
                  cs = slice(c * 128, (c + 1) * 128)
                  ths.append(lambda e, cs=cs: e.tensor_tensor_scan(out=rP[0:4, cs], data0=ones36[0:4, :], data1=rA[0:4, cs], initial=0.0,
                                                                   op0=ALU.mult, op1=ALU.add))
                  ths.append(lambda e, cs=cs: e.tensor_tensor_scan(out=rev(rP[32:36, cs]), data0=ones36[32:36, :], data1=rev(rA[32:36, cs]),
                                                                   initial=0.0, op0=ALU.mult, op1=ALU.add))
              cx.run("dve", ths, reads=[R_T], writes=[R_T])
              cx.run("dve", [lambda e: e.tensor_tensor(out=rA[0:36, :], in0=rIG[0:36, :], in1=rP[0:36, :], op=ALU.add)], reads=[R_T], writes=[R_T])
              ths = []
              for c in range(12):
                  cs = slice(c * 128, (c + 1) * 128)
                  ths.append(lambda e, cs=cs: e.tensor_tensor_scan(out=rCM[0:4, cs], data0=zeros36[0:4, :], data1=rA[0:4, cs], initial=-1e30,
                                                                   op0=ALU.add, op1=ALU.max))
                  ths.append(lambda e, cs=cs: e.tensor_tensor_scan(out=rev(rCM[32:36, cs]), data0=zeros36[32:36, :], data1=rev(rA[32:36, cs]),
                                                                   initial=-1e30, op0=ALU.add, op1=ALU.max))
              cx.run("dve", ths, reads=[R_T], writes=[R_T])
              CM3 = rCM.rearrange("p (c t) -> p c t", t=128)
              P3 = rP.rearrange("p (c t) -> p c t", t=128)
              A3 = rA.rearrange("p (c t) -> p c t", t=128)
              IG3 = rIG.rearrange("p (c t) -> p c t", t=128)
              cx.run("dve", [lambda e: e.memset(sm[0:36, :, :], 0.0)], writes=[SM_T])
              cx.run("dve", [lambda e: e.tensor_copy(out=sm[0:4, 0, :], in_=CM3[0:4, :, 127]),
                             lambda e: e.tensor_copy(out=sm[32:36, 0, :], in_=CM3[32:36, :, 0]),
                             lambda e: e.tensor_copy(out=sm[0:4, 1, :], in_=P3[0:4, :, 127]),
                             lambda e: e.tensor_copy(out=sm[32:36, 1, :], in_=P3[32:36, :, 0])], reads=[R_T, SM_T], writes=[SM_T])
              for d, r0 in ((0, 0), (1, 32)):
                  rs = slice(r0, r0 + 4)
                  order = list(range(12)) if d == 0 else list(range(11, -1, -1))
                  prev = None
                  for c in order:
                      s = c // 2
                      start = (c % 2 == 0) if d == 0 else (c % 2 == 1)
                      m0c = sm[rs, 2, c:c + 1]
                      if start:
                          if (d == 0 and s == 0) or (d == 1 and s == 3):
                              th = lambda e, m0c=m0c, rs=rs: e.tensor_copy(out=m0c, in_=minit[rs, :])
                          elif (d == 0 and s <= 3) or (d == 1 and s <= 2):
                              th = lambda e, m0c=m0c, rs=rs, prev=prev: e.tensor_scalar(out=m0c, in0=sm[rs, 4, prev:prev + 1], scalar1=chain[rs, :],
                                                                                       scalar2=None, op0=ALU.mult)
                          else:
                              th = lambda e, m0c=m0c: e.memset(m0c, 0.0)
                      else:
                          th = lambda e, m0c=m0c, rs=rs, prev=prev: e.tensor_copy(out=m0c, in_=sm[rs, 4, prev:prev + 1])
                      cx.run("dve", [th], reads=[SM_T, G], writes=[SM_T])
                      cx.run("dve", [lambda e, rs=rs, c=c: e.tensor_tensor(out=sm[rs, 3, c:c + 1], in0=sm[rs, 2, c:c + 1], in1=sm[rs, 0, c:c + 1], op=ALU.max)],
                             reads=[SM_T], writes=[SM_T])
                      cx.run("dve", [lambda e, rs=rs, c=c: e.tensor_tensor(out=sm[rs, 4, c:c + 1], in0=sm[rs, 3, c:c + 1], in1=sm[rs, 1, c:c + 1], op=ALU.subtract)],
                             reads=[SM_T], writes=[SM_T])
                      prev = c
              cx.dma("sp", mfin_d[l], sm[0:36, 4, :], reads=[SM_T])
              cx.run("dve", [lambda e: e.tensor_tensor(out=sm[0:36, 6, :], in0=sm[0:36, 3, :], in1=sm[0:36, 2, :], op=ALU.subtract)], reads=[SM_T], writes=[SM_T])
              cx.run("act", [lambda e: e.activation(out=sm[0:36, 5, :], in_=sm[0:36, 6, :], func=AF.Exp, scale=-1.0)], reads=[SM_T], writes=[SM_T])
              m0b = sm[0:36, 2, :].unsqueeze(2).to_broadcast([36, 12, 128])
              mxb = sm[0:36, 3, :].unsqueeze(2).to_broadcast([36, 12, 128])
              cx.run("dve", [lambda e: e.tensor_tensor(out=CM3[0:36], in0=CM3[0:36], in1=m0b, op=ALU.max)], reads=[R_T, SM_T], writes=[R_T])
              cx.run("dve", [lambda e: e.tensor_scalar(out=rCM[0:36, :], in0=rCM[0:36, :], scalar1=-1.0, scalar2=None, op0=ALU.mult)], reads=[R_T], writes=[R_T])
              dbg_out("rA_l%d" % l, rA[0:36, :], [36, T], [R_T])
              dbg_out("rNG_l%d" % l, rCM[0:36, :], [36, T], [R_T])

              def cols_from(q, rows):
                  bk, bkT = psum()
                  cx.run("pe", [lambda e, c=c, bk=bk: e.transpose(out=bk[:, c * 36:(c + 1) * 36], in_=rows[0:36, c * 128:(c + 1) * 128],
                                                                  identity=identF[0:36, 0:36]) for c in range(12)],
                         reads=[R_T, G], writes=[bkT])
                  cx.run("act", [lambda e, bk=bk: e.activation(out=cols[:, q, :], in_=bk[:, 0:432], func=AF.Copy)], reads=[bkT], writes=[COL_T])

              cx.run("dve", [lambda e: e.tensor_tensor(out=IG3[0:36], in0=CM3[0:36], in1=m0b, op=ALU.add)], reads=[R_T, SM_T], writes=[R_T])
              cx.run("act", [lambda e: e.activation(out=rIG[0:36, :], in_=rIG[0:36, :], func=AF.Exp)], reads=[R_T], writes=[R_T])
              cols_from(0, rIG)
              cx.run("dve", [lambda e: e.tensor_tensor(out=rP[0:36, :], in0=rP[0:36, :], in1=rCM[0:36, :], op=ALU.add)], reads=[R_T], writes=[R_T])
              cx.run("act", [lambda e: e.activation(out=rP[0:36, :], in_=rP[0:36, :], func=AF.Exp)], reads=[R_T], writes=[R_T])
              cols_from(1, rP)
              cx.run("dve", [lambda e: e.tensor_tensor(out=IG3[0:36], in0=A3[0:36], in1=mxb, op=ALU.subtract)], reads=[R_T, SM_T], writes=[R_T])
              cx.run("dve", [lambda e: e.tensor_scalar(out=rIG[0:36, :], in0=rIG[0:36, :], scalar1=LNKS, scalar2=None, op0=ALU.add)], reads=[R_T], writes=[R_T])
              cx.run("act", [lambda e: e.activation(out=rIG[0:36, :], in_=rIG[0:36, :], func=AF.Exp)], reads=[R_T], writes=[R_T])
              cols_from(2, rIG)
              bk, bkT = psum()
              cx.run("pe", [mm(bk[:, r * 12:(r + 1) * 12], sel[:, r, :], sm[0:36, 5, :], True, True) for r in range(8)], reads=[SM_T, G], writes=[bkT])
              cx.run("act", [lambda e, bk=bk: e.activation(out=cbc, in_=bk[:, 0:96], func=AF.Copy)], reads=[bkT], writes=[COL_T])
              dbg_out("sm_l%d" % l, sm[0:36, :, :], [36, 8, 12], [SM_T])
              dbg_out("cols_l%d" % l, cols, [128, 3, 432], [COL_T])

              phase_end("gates%d" % l)
              prow = lambda r: (r // 4) * 32 + (r % 4)
              lnks_col = None

              for h in range(H):
                  wv, wT = wload(in_w_d[l][:, h * 512:(h + 1) * 512], 8, 512)
                  for nb in range(3):
                      sl = slice(nb * 512, (nb + 1) * 512)
                      for which, dst in ((0, qT), (1, kT)):
                          bk, bkT = psum()
                          cx.run("pe", [mm(bk, wv[:, kc, which * 128:(which + 1) * 128], hT[:, kc, sl], kc == 0, kc == 7) for kc in range(8)],
                                 reads=[wT, hT_T], writes=[bkT])
                          cx.run("act", [lambda e, bk=bk, dst=dst, sl=sl: e.activation(out=dst[:, sl], in_=bk, func=AF.Copy)], reads=[bkT], writes=[qk_T])
                  cx.run("dve", [lambda e: e.memset(vext[:, :, 128:130], 1.0)], writes=v_Tc)
                  for c in range(12):
                      cs = slice(c * 128, (c + 1) * 128)
                      bk, bkT = psum()
                      cx.run("pe", [mm(bk[:, 0:256], hT[:, kc, cs], wv[:, kc, 256:512], kc == 0, kc == 7) for kc in range(8)], reads=[wT, hT_T], writes=[bkT])
                      cx.run("act", [lambda e, bk=bk, c=c: e.activation(out=vext[:, c, 0:128], in_=bk[:, 0:128], func=AF.Copy)], reads=[bkT], writes=[v_Tc[c]])
                      cx.run("act", [lambda e, bk=bk, c=c: e.activation(out=osig[:, c, :], in_=bk[:, 128:256], func=AF.Sigmoid)], reads=[bkT], writes=[o_Tc[c]])
                  for c in range(12):
                      cs = slice(c * 128, (c + 1) * 128)
                      bk, bkT = psum()
                      bkb = bk.bitcast(BF16)
                      cx.run("pe", [lambda e, bkb=bkb, cs=cs: e.transpose(out=bkb[:, 0:128], in_=kT[:, cs], identity=identB)], reads=[qk_T, G], writes=[bkT])
                      for d in range(2):
                          r = d * 4 + h
                          col = cols[:, 2, c * 36 + prow(r):c * 36 + prow(r) + 1]
                          cx.run("act", [lambda e, bkb=bkb, d=d, c=c, col=col: e.activation(out=kw[d][:, c, :], in_=bkb[:, 0:128], func=AF.Copy, scale=col)],
                                 reads=[bkT, COL_T], writes=[kw_T[d][c]])
                  orders = [list(range(12)), list(range(11, -1, -1))]
                  for g in range(6):
                      d = g // 3
                      r = d * 4 + h
                      b1, b1T = psum()
                      ths = []
                      css = []
                      for i in range(4):
                          c = orders[d][(g % 3) * 4 + i]
                          cs = slice(c * 128, (c + 1) * 128)
                          css.append(cs)
                          o = b1[:, i * 128:(i + 1) * 128]
                          ths += [mm(o, sel[:, r, :], rCM[0:36, cs], True, False), mm(o, rA[0:36, cs], sel[:, r, :], False, False),
                                  mm(o, identF, MASK[d], False, True)]
                      cx.run("pe", ths, reads=[R_T, G], writes=[b1T])
                      gp = g % 2
                      cx.run("act", [lambda e, b1=b1, gp=gp: e.activation(out=dmg[gp], in_=b1, func=AF.Exp, bias=lnks[:, 0:1])], reads=[b1T, G], writes=[dmg_T[gp]])
                      b2, b2T = psum()
                      cx.run("pe", [mm(b2[:, i * 128:(i + 1) * 128], kT[:, css[i]], qT[:, css[i]], True, True) for i in range(4)], reads=[qk_T], writes=[b2T])
                      cx.run("dve", [lambda e, b2=b2, gp=gp, g=g: e.tensor_tensor(out=sT_all[:, g * 512:(g + 1) * 512], in0=b2, in1=dmg[gp], op=ALU.mult)],
                             reads=[b2T, dmg_T[gp]], writes=[sTa_T[g]])

                  def rec(d, h=h):
                      r = d * 4 + h
                      cur = 0
                      Cxd, CxTd = Cx[d], Cx_T[d]
                      for step, c in enumerate(orders[d]):
                          s = c // 2
                          start = (c % 2 == 0) if d == 0 else (c % 2 == 1)
                          if start:
                              if (d == 0 and s == 0) or (d == 1 and s == 3):
                                  cx.dma("sp", Cxd[cur][:, 0:129], Cinit_d[l, d, h], writes=[CxTd[cur]])
                              elif (d == 0 and s <= 3) or (d == 1 and s <= 2):
                                  cx.run("dve", [lambda e, cur=cur: e.tensor_scalar(out=Cxd[cur][:, 0:129], in0=Cxd[cur][:, 0:129], scalar1=chain[:, 0:1],
                                                                                     scalar2=None, op0=ALU.mult)], reads=[CxTd[cur], G], writes=[CxTd[cur]])
                              else:
                                  cx.run("dve", [lambda e, cur=cur: e.memset(Cxd[cur][:, 0:129], 0.0)], writes=[CxTd[cur]])
                          cx.run("act", [lambda e, cur=cur, step=step: e.activation(out=Cb_all[d][:, step, 0:129], in_=Cxd[cur][:, 0:129], func=AF.Copy)],
                                 reads=[CxTd[cur]], writes=[CbA_T[d][step]])
                          bU, bUT = psum()
                          cx.run("pe", [mm(bU[:, 0:129], kw[d][:, c, :], vext[:, c, 0:129], True, True)], reads=[kw_T[d][c], v_Tc[c]], writes=[bUT])
                          yield
                          nxt = 1 - cur
                          ccol = cbc[:, r * 12 + c:r * 12 + c + 1]
                          cx.run("dve", [lambda e, cur=cur, nxt=nxt: e.scalar_tensor_tensor(
                              out=Cxd[nxt][:, 0:129], in0=Cxd[cur][:, 0:129], scalar=ccol, in1=bU[:, 0:129], op0=ALU.mult, op1=ALU.add)],
                              reads=[bUT, CxTd[cur], COL_T], writes=[CxTd[nxt]])
                          cur = nxt
                          end = (c % 2 == 1) if d == 0 else (c % 2 == 0)
                          if end:
                              cx.dma("sp", Cfin_d[l, s, d, h], Cxd[cur][:, 0:129], reads=[CxTd[cur]])
                          yield

                  for _ in itertools.zip_longest(rec(0), rec(1)):
                      pass

                  for d in range(2):
                      r = d * 4 + h
                      pr = prow(r)
                      for step, c in enumerate(orders[d]):
                          cs = slice(c * 128, (c + 1) * 128)
                          p = d * 12 + step
                          bA, bAT = psum()
                          cx.run("pe", [mm(bA[:, 0:129], sT_all[:, p * 128:(p + 1) * 128], vext[:, c, 0:129], True, True),
                                        mm(bA[:, 256:385], qT[:, cs], Cb_all[d][:, step, 0:129], True, True)],
                                 reads=[sTa_T[p // 4], v_Tc[c], qk_T, CbA_T[d][step]], writes=[bAT])
                          wcol = cols[:, 0, c * 36 + pr:c * 36 + pr + 1]
                          bp = step % 2
                          cx.run("act", [lambda e, bA=bA, bp=bp, wcol=wcol: e.activation(out=Bs2[bp][:, 0:129], in_=bA[:, 256:385], func=AF.Copy, scale=wcol)],
                                 reads=[bAT, COL_T], writes=[Bs2_T[bp]])
                          cx.run("dve", [lambda e, bA=bA, bp=bp, c=c: e.tensor_tensor(out=tot_all[:, c, 0:129], in0=bA[:, 0:129], in1=Bs2[bp][:, 0:129], op=ALU.add)],
                                 reads=[bAT, Bs2_T[bp]], writes=[tota_T])
                      den = tot_all[:, :, 128]
                      ecols = cols[:, 1, :].rearrange("p (c n) -> p c n", n=36)[:, :, pr]
                      cx.run("dve", [lambda e, den=den: e.tensor_scalar(out=dn12, in0=den, scalar1=-1.0, scalar2=None, op0=ALU.mult)], reads=[tota_T], writes=[dn_T])
                      cx.run("dve", [lambda e, den=den: e.tensor_tensor(out=dn12, in0=dn12, in1=den, op=ALU.max)], reads=[tota_T, dn_T], writes=[dn_T])
                      cx.run("dve", [lambda e, ecols=ecols: e.tensor_tensor(out=dn12, in0=dn12, in1=ecols, op=ALU.max)], reads=[dn_T, COL_T], writes=[dn_T])
                      cx.run("dve", [lambda e: e.reciprocal(out=dn12, in_=dn12)], reads=[dn_T], writes=[dn_T])
                      if d == 0:
                          cx.run("act", [lambda e, c=c: e.activation(out=hsum[:, c, :], in_=tot_all[:, c, 0:128], func=AF.Copy, scale=dn12[:, c:c + 1])
                                         for c in range(12)], reads=[tota_T, dn_T], writes=hs_T)
                      else:
                          cx.run("dve", [lambda e, c=c: e.scalar_tensor_tensor(out=hsum[:, c, :], in0=tot_all[:, c, 0:128], scalar=dn12[:, c:c + 1],
                                                                               in1=hsum[:, c, :], op0=ALU.mult, op1=ALU.add) for c in range(12)],
                                 reads=[tota_T, dn_T] + hs_T, writes=hs_T)
                  if l == 0 and h == 0:
                      dbg_out("hsum_l0h0", hsum, [128, 12, 128], hs_T)
                  head_norm_out(hsum, hs_T, mnw, h, osig, o_Tc, h_aT, haT_T, tot_all.rearrange("p c n -> p (c n)"), tota_T, sT_all[:, 0:1536], sTa_T[0:3])
                  if l == 0:
                      mod_groups(0, [4 + 2 * h, 5 + 2 * h])
                  phase_end("mlstm%d_h%d" % (l, h))
              dbg_out("haT_l%d" % l, h_aT, [128, 4, T], [haT_T])

              phase_end("mlstm%d" % l)
              aset(6144)
              acc = af32(4 * T).rearrange("p (a t) -> p a t", t=T)
              acc_T = TT("acc")
              upad = [abf(6 * 286).rearrange("p (s n) -> p s n", n=286) for _ in range(2)]
              up_T = [TT("upad0"), TT("upad1")]
              Dg = [abf(31 * 128).rearrange("p (k n) -> p k n", n=128) for _ in range(2)]
              Dg_T = [TT("dg0"), TT("dg1")]
              sg = [af32(512) for _ in range(2)]
              sg_T = [TT("sg0"), TT("sg1")]
              cx.run("dve", [lambda e: e.memset(upad[0], 0.0), lambda e: e.memset(upad[1], 0.0)], writes=up_T)
              wa, waT = wload(in_w_d[l][:, OFF_CA:OFF_CA + 512], 8, 512)
              wgc, wgcT = wload(in_w_d[l][:, OFF_CG:OFF_CG + 512], 8, 512)
              k = 0
              for ct in range(4):
                  up, upT, dg, dgT = upad[ct % 2], up_T[ct % 2], Dg[ct % 2], Dg_T[ct % 2]
                  cx.run("act", [lambda e, kk=kk, ct=ct, dg=dg: e.activation(out=dg[:, kk, :], in_=identF, func=AF.Copy, scale=convp[:, ct, kk:kk + 1])
                                 for kk in range(CONV_K)], reads=[G], writes=[dgT])
                  for nb in range(3):
                      sl = slice(nb * 512, (nb + 1) * 512)
                      ba, baT = psum()
                      cx.run("pe", [mm(ba, wa[:, kc, ct * 128:(ct + 1) * 128], hT[:, kc, sl], kc == 0, kc == 7) for kc in range(8)], reads=[waT, hT_T], writes=[baT])
                      bg, bgT = psum()
                      cx.run("pe", [mm(bg, wgc[:, kc, ct * 128:(ct + 1) * 128], hT[:, kc, sl], kc == 0, kc == 7) for kc in range(8)], reads=[wgcT, hT_T], writes=[bgT])
                      pp = k % 2
                      k += 1
                      cx.run("act", [lambda e, bg=bg, pp=pp: e.activation(out=sg[pp], in_=bg, func=AF.Sigmoid)], reads=[bgT], writes=[sg_T[pp]])
                      cx.run("dve", [lambda e, ba=ba, pp=pp, nb=nb, up=up, s2=s2: e.tensor_tensor(
                          out=up[:, 2 * nb + s2, 15:271], in0=ba[:, s2 * 256:(s2 + 1) * 256],
                          in1=sg[pp][:, s2 * 256:(s2 + 1) * 256], op=ALU.mult) for s2 in range(2)], reads=[baT, sg_T[pp]], writes=[upT])
                  ths = []
                  for s in (1, 2, 3):
                      ths.append(lambda e, s=s, up=up: e.tensor_scalar(out=up[:, s, 0:15], in0=up[:, s - 1, 256:271], scalar1=chain[:, 0:1], scalar2=None, op0=ALU.mult))
                  for s in (0, 1, 2):
                      ths.append(lambda e, s=s, up=up: e.tensor_scalar(out=up[:, s, 271:286], in0=up[:, s + 1, 15:30], scalar1=chain[:, 0:1], scalar2=None, op0=ALU.mult))
                  cx.run("dve", ths, reads=[upT, G], writes=[upT])
                  phase_end("convu%d_%d" % (l, ct))
                  for s in range(6):
                      bk, bkT = psum()
                      cx.run("pe", [mm(bk[:, 0:256], dg[:, kk, :], up[:, s, kk:kk + 256], kk == 0, kk == CONV_K - 1) for kk in range(CONV_K)],
                             reads=[dgT, upT], writes=[bkT])
                      cx.run("act", [lambda e, bk=bk, ct=ct, s=s: e.activation(out=acc[:, ct, s * 256:(s + 1) * 256], in_=bk[:, 0:256], func=AF.Identity,
                                                                               bias=convp[:, ct, 31:32])], reads=[bkT, G], writes=[acc_T])
              dbg_out("conv_l%d" % l, acc, [128, 4, T], [acc_T])
              cb = abf(4 * 256).rearrange("p (a n) -> p a n", n=256)
              cq = abf(4 * 256).rearrange("p (a n) -> p a n", n=256)
              cb_T = TT("cb"); cq_T = TT("cq")
              mean = af32(256); msq = af32(256); rstd = af32(256)
              st_T = TT("lnst")
              for nb in range(6):
                  sl = slice(nb * 256, (nb + 1) * 256)
                  cx.run("act", [lambda e, ct=ct, sl=sl: e.activation(out=cb[:, ct, :], in_=acc[:, ct, sl], func=AF.Copy) for ct in range(4)], reads=[acc_T], writes=[cb_T])
                  cx.run("act", [lambda e, ct=ct, sl=sl: e.activation(out=cq[:, ct, :], in_=acc[:, ct, sl], func=AF.Square) for ct in range(4)], reads=[acc_T], writes=[cq_T])
                  b1, b1T = psum()
                  cx.run("pe", [mm(b1[:, 0:256], onesB, cb[:, ct, :], ct == 0, ct == 3) for ct in range(4)], reads=[cb_T, G], writes=[b1T])
                  b2, b2T = psum()
                  cx.run("pe", [mm(b2[:, 0:256], onesB, cq[:, ct, :], ct == 0, ct == 3) for ct in range(4)], reads=[cq_T, G], writes=[b2T])
                  ln_stats(b1[:, 0:256], b1T, b2[:, 0:256], b2T, mean, msq, rstd, st_T, 512)
                  cx.run("dve", [lambda e, ct=ct, sl=sl: e.tensor_tensor(out=acc[:, ct, sl], in0=acc[:, ct, sl], in1=mean, op=ALU.subtract) for ct in range(4)],
                         reads=[acc_T, st_T], writes=[acc_T])
                  cx.run("dve", [lambda e, ct=ct, sl=sl: e.tensor_tensor(out=acc[:, ct, sl], in0=acc[:, ct, sl], in1=rstd, op=ALU.mult) for ct in range(4)],
                         reads=[acc_T, st_T], writes=[acc_T])
                  cx.run("act", [lambda e, ct=ct, sl=sl: e.activation(out=acc[:, ct, sl], in_=acc[:, ct, sl], func=AF.Identity,
                                                                      scale=convp[:, ct, 32:33], bias=convp[:, ct, 33:34]) for ct in range(4)], reads=[acc_T, G], writes=[acc_T])
                  cx.run("act", [lambda e, ct=ct, sl=sl: e.activation(out=ubT[:, ct, sl], in_=acc[:, ct, sl], func=AF.Silu) for ct in range(4)], reads=[acc_T], writes=[ub_T])
              dbg_out("ubT_l%d" % l, ubT, [128, 4, T], [ub_T])

              phase_end("conv%d" % l)
              aset(9216)
              ropeCS = af32(1024); ropeSN = af32(1024)
              RP_T = TT("rope")
              cx.dma("sp", ropeCS, ropeCS_d, writes=[RP_T])
              cx.dma("sp", ropeSN, ropeSN_d, writes=[RP_T])
              qT = abf(T); kT = abf(T)
              qk_T = TT("rqk")
              vv = abf(T).rearrange("p (c n) -> p c n", n=128)
              v_Tc = [TT("rv%d" % c) for c in range(12)]
              o_Tc = [TT("gsil%d" % c) for c in range(12)]
              gsil = af32(T).rearrange("p (c n) -> p c n", n=128)
              o_T = TT("gsil")
              ysum = af32(T).rearrange("p (c n) -> p c n", n=128)
              hs_T = [TT("ys%d" % c) for c in range(12)]
              kz = [abf(T).rearrange("p (c n) -> p c n", n=128) for _ in range(2)]
              kz_T = [[TT("kz") for c in range(12)] for _ in range(2)]
              Sx = [[af32(128) for _ in range(2)] for _ in range(2)]
              Sx_T = [[TT("sx"), TT("sx")] for _ in range(2)]
              t1 = af32(512); t2 = af32(512)
              sT_all = abf(24 * 128)
              sTa_T = [TT("rsTa%d" % i) for i in range(6)]
              Sb_all = [abf(12 * 128).rearrange("p (s n) -> p s n", n=128) for _ in range(2)]
              SbA_T = [[TT("sba") for _ in range(12)] for _ in range(2)]
              Bs2 = [af32(128) for _ in range(2)]
              Bs2_T = [TT("rbs2a"), TT("rbs2b")]
              t_T = TT("rt")
              for h in range(H):
                  wv, wT = wload(in_w_d[l][:, OFF_RET + h * 512:OFF_RET + (h + 1) * 512], 8, 512)
                  wv2, wT2 = wload(in_w_d[l][:, OFF_RVG + h * 256:OFF_RVG + (h + 1) * 256], 8, 256)
                  for nb in range(3):
                      sl = slice(nb * 512, (nb + 1) * 512)
                      for which, dst in ((0, qT), (1, kT)):
                          bk, bkT = psum()
                          cx.run("pe", [mm(bk, wv[:, kc, which * 256:which * 256 + 128], hT[:, kc, sl], kc == 0, kc == 7) for kc in range(8)],
                                 reads=[wT, hT_T], writes=[bkT])
                          if nb < 2:
                              bs_, bsT = psum()
                              cx.run("pe", [mm(bs_, wv[:, kc, which * 256 + 128:which * 256 + 256], hT[:, kc, sl], kc == 0, kc == 7) for kc in range(8)],
                                     reads=[wT, hT_T], writes=[bsT])
                              cx.run("dve", [lambda e, bk=bk, sl=sl: e.tensor_tensor(out=t1, in0=bk, in1=ropeCS[:, sl], op=ALU.mult)], reads=[bkT, RP_T, t_T], writes=[t_T])
                              cx.run("dve", [lambda e, bs_=bs_, sl=sl: e.tensor_tensor(out=t2, in0=bs_, in1=ropeSN[:, sl], op=ALU.mult)], reads=[bsT, RP_T, t_T], writes=[t_T])
                              cx.run("dve", [lambda e, dst=dst, sl=sl: e.tensor_tensor(out=dst[:, sl], in0=t1, in1=t2, op=ALU.add)], reads=[t_T], writes=[qk_T, t_T])
                          else:
                              cx.run("act", [lambda e, bk=bk, dst=dst, sl=sl: e.activation(out=dst[:, sl], in_=bk, func=AF.Copy)], reads=[bkT], writes=[qk_T])
                  for c in range(12):
                      cs = slice(c * 128, (c + 1) * 128)
                      bk, bkT = psum()
                      cx.run("pe", [mm(bk[:, 0:256], hT[:, kc, cs], wv2[:, kc, :], kc == 0, kc == 7) for kc in range(8)], reads=[wT2, hT_T], writes=[bkT])
                      cx.run("act", [lambda e, bk=bk, c=c: e.activation(out=vv[:, c, :], in_=bk[:, 0:128], func=AF.Copy)], reads=[bkT], writes=[v_Tc[c]])
                      cx.run("act", [lambda e, bk=bk, c=c: e.activation(out=gsil[:, c, :], in_=bk[:, 128:256], func=AF.Silu)], reads=[bkT], writes=[o_Tc[c]])
                  for c in range(12):
                      cs = slice(c * 128, (c + 1) * 128)
                      bk, bkT = psum()
                      bkb = bk.bitcast(BF16)
                      cx.run("pe", [lambda e, bkb=bkb, cs=cs: e.transpose(out=bkb[:, 0:128], in_=kT[:, cs], identity=identB)], reads=[qk_T, G], writes=[bkT])
                      for d in range(2):
                          r = d * 4 + h
                          cx.run("act", [lambda e, bkb=bkb, d=d, c=c, r=r: e.activation(out=kz[d][:, c, :], in_=bkb[:, 0:128], func=AF.Copy, scale=rcol[:, r, 1:2])],
                                 reads=[bkT, G], writes=[kz_T[d][c]])
                  orders = [list(range(12)), list(range(11, -1, -1))]
                  for g in range(6):
                      d = g // 3
                      r = d * 4 + h
                      b2, b2T = psum()
                      css = [slice(orders[d][(g % 3) * 4 + i] * 128, (orders[d][(g % 3) * 4 + i] + 1) * 128) for i in range(4)]
                      cx.run("pe", [mm(b2[:, i * 128:(i + 1) * 128], kT[:, css[i]], qT[:, css[i]], True, True) for i in range(4)], reads=[qk_T], writes=[b2T])
                      cx.run("dve", [lambda e, b2=b2, g=g, i=i, r=r: e.tensor_tensor(out=sT_all[:, g * 512 + i * 128:g * 512 + (i + 1) * 128],
                                                                                   in0=b2[:, i * 128:(i + 1) * 128], in1=decT[:, r, :], op=ALU.mult) for i in range(4)],
                             reads=[b2T, G], writes=[sTa_T[g]])

                  def rrec(d, h=h):
                      r = d * 4 + h
                      cur = 0
                      Sxd, SxTd = Sx[d], Sx_T[d]
                      for step, c in enumerate(orders[d]):
                          s = c // 2
                          start = (c % 2 == 0) if d == 0 else (c % 2 == 1)
                          if start:
                              if (d == 0 and s == 0) or (d == 1 and s == 3):
                                  cx.dma("sp", Sxd[cur], Sinit_d[l, d, h], writes=[SxTd[cur]])
                              elif (d == 0 and s <= 3) or (d == 1 and s <= 2):
                                  cx.run("dve", [lambda e, cur=cur: e.tensor_scalar(out=Sxd[cur], in0=Sxd[cur], scalar1=chain[:, 0:1],
                                                                                     scalar2=None, op0=ALU.mult)], reads=[SxTd[cur], G], writes=[SxTd[cur]])
                              else:
                                  cx.run("dve", [lambda e, cur=cur: e.memset(Sxd[cur], 0.0)], writes=[SxTd[cur]])
                          cx.run("act", [lambda e, cur=cur, step=step: e.activation(out=Sb_all[d][:, step, :], in_=Sxd[cur], func=AF.Copy)],
                                 reads=[SxTd[cur]], writes=[SbA_T[d][step]])
                          bU, bUT = psum()
                          cx.run("pe", [mm(bU[:, 0:128], kz[d][:, c, :], vv[:, c, :], True, True)], reads=[kz_T[d][c], v_Tc[c]], writes=[bUT])
                          yield
                          nxt = 1 - cur
                          cx.run("dve", [lambda e, cur=cur, nxt=nxt: e.scalar_tensor_tensor(
                              out=Sxd[nxt], in0=Sxd[cur], scalar=rcol[:, r, 2:3], in1=bU[:, 0:128], op0=ALU.mult, op1=ALU.add)],
                              reads=[bUT, SxTd[cur], G], writes=[SxTd[nxt]])
                          cur = nxt
                          end = (c % 2 == 1) if d == 0 else (c % 2 == 0)
                          if end:
                              cx.dma("sp", Sfin_d[l, s, d, h], Sxd[cur], reads=[SxTd[cur]])
                          yield

                  for _ in itertools.zip_longest(rrec(0), rrec(1)):
                      pass

                  for d in range(2):
                      r = d * 4 + h
                      for step, c in enumerate(orders[d]):
                          cs = slice(c * 128, (c + 1) * 128)
                          p = d * 12 + step
                          bA, bAT = psum()
                          cx.run("pe", [mm(bA[:, 0:128], sT_all[:, p * 128:(p + 1) * 128], vv[:, c, :], True, True),
                                        mm(bA[:, 256:384], qT[:, cs], Sb_all[d][:, step, :], True, True)],
                                 reads=[sTa_T[p // 4], v_Tc[c], qk_T, SbA_T[d][step]], writes=[bAT])
                          bp = step % 2
                          cx.run("act", [lambda e, bA=bA, bp=bp, r=r: e.activation(out=Bs2[bp], in_=bA[:, 256:384], func=AF.Copy, scale=rcol[:, r, 0:1])],
                                 reads=[bAT, G], writes=[Bs2_T[bp]])
                          if d == 0:
                              cx.run("dve", [lambda e, bA=bA, bp=bp, c=c: e.tensor_tensor(out=ysum[:, c, :], in0=bA[:, 0:128], in1=Bs2[bp], op=ALU.add)],
                                     reads=[bAT, Bs2_T[bp]], writes=[hs_T[c]])
                          else:
                              cx.run("dve", [lambda e, bA=bA, bp=bp: e.tensor_tensor(out=Bs2[bp], in0=bA[:, 0:128], in1=Bs2[bp], op=ALU.add)],
                                     reads=[bAT, Bs2_T[bp]], writes=[Bs2_T[bp]])
                              cx.run("dve", [lambda e, bp=bp, c=c: e.tensor_tensor(out=ysum[:, c, :], in0=ysum[:, c, :], in1=Bs2[bp], op=ALU.add)],
                                     reads=[Bs2_T[bp], hs_T[c]], writes=[hs_T[c]])
                  head_norm_out(ysum, hs_T, rnw, h, gsil, o_Tc, h_cT, hcT_T, t1, t_T, sT_all[:, 0:1536], sTa_T[0:3])
                  if l + 1 < L:
                      mod_groups(l + 1, [3 * h, 3 * h + 1, 3 * h + 2])
              dbg_out("hcT_l%d" % l, h_cT, [128, 4, T], [hcT_T])

              phase_end("ret%d" % l)
              aset(15360)
              macc = af32(4 * T).rearrange("p (a t) -> p a t", t=T)
              macc_T = [TT("macc%d" % i) for i in range(4)]
              sgm = [af32(512) for _ in range(2)]
              sgm_T = [TT("sgm0"), TT("sgm1")]
              k = 0
              branches = ((mow_d, h_aT, haT_T), (cow_d, ubT, ub_T), (row_d, h_cT, hcT_T))
              for jg in range(2):
                  for b, (wd, src, srcT) in enumerate(branches):
                      wo, woT = wload(wd[l][:, jg * 512:(jg + 1) * 512], 4, 512)
                      wgm, wgmT = wload(in_w_d[l][:, OFF_GM + b * 1024 + jg * 512:OFF_GM + b * 1024 + (jg + 1) * 512], 8, 512)
                      for jj in range(4):
                          j = jg * 4 + jj
                          for nb in range(3):
                              sl = slice(nb * 512, (nb + 1) * 512)
                              by, byT = psum()
                              cx.run("pe", [mm(by, wo[:, kc, jj * 128:(jj + 1) * 128], src[:, kc, sl], kc == 0, kc == 3) for kc in range(4)], reads=[woT, srcT], writes=[byT])
                              bg, bgT = psum()
                              cx.run("pe", [mm(bg, wgm[:, kc, jj * 128:(jj + 1) * 128], hT[:, kc, sl], kc == 0, kc == 7) for kc in range(8)], reads=[wgmT, hT_T], writes=[bgT])
                              pp = k % 2
                              k += 1
                              cx.run("act", [lambda e, bg=bg, pp=pp: e.activation(out=sgm[pp], in_=bg, func=AF.Sigmoid)], reads=[bgT], writes=[sgm_T[pp]])
                              if b == 0:
                                  cx.run("dve", [lambda e, by=by, pp=pp, sl=sl, jj=jj: e.tensor_tensor(out=macc[:, jj, sl], in0=by, in1=sgm[pp], op=ALU.mult)],
                                         reads=[byT, sgm_T[pp]], writes=[macc_T[jj]])
                              else:
                                  cx.run("dve", [lambda e, by=by, pp=pp: e.tensor_tensor(out=sgm[pp], in0=by, in1=sgm[pp], op=ALU.mult)],
                                         reads=[byT, sgm_T[pp]], writes=[sgm_T[pp]])
                                  if b == 1:
                                      cx.run("dve", [lambda e, pp=pp, sl=sl, jj=jj: e.tensor_tensor(out=macc[:, jj, sl], in0=macc[:, jj, sl], in1=sgm[pp], op=ALU.add)],
                                             reads=[sgm_T[pp], macc_T[jj]], writes=[macc_T[jj]])
                                  else:
                                      cx.run("dve", [lambda e, pp=pp, sl=sl, j=j, jj=jj: e.tensor_tensor(out=merged[:, j, sl], in0=macc[:, jj, sl], in1=sgm[pp], op=ALU.add)],
                                             reads=[sgm_T[pp], macc_T[jj]], writes=[mg_T])
              dbg_out("merged_l%d" % l, merged, [128, 8, T], [mg_T])
              phase_end("merge%d" % l)
              aset(15360)
              xb = abf(8 * 512).rearrange("p (a n) -> p a n", n=512)
              xq = abf(8 * 512).rearrange("p (a n) -> p a n", n=512)
              lnt = (af32(512), af32(512), af32(512))
              cx.run("act", [lambda e, kc=kc: e.activation(out=x[:, kc, :], in_=x[:, kc, :], func=AF.Copy, scale=ALPHA) for kc in range(8)], reads=[x_T], writes=[x_T])
              for jg in range(2):
                  wo, woT = wload(outw_d[l][:, jg * 512:(jg + 1) * 512], 8, 512)
                  for jj in range(4):
                      j = jg * 4 + jj
                      for nb in range(3):
                          sl = slice(nb * 512, (nb + 1) * 512)
                          v = 1 if nb < 2 else 0
                          bk, bkT = psum()
                          cx.run("pe", [mm(bk, wo[:, kc, jj * 128:(jj + 1) * 128], merged[:, kc, sl], kc == 0, kc == 7) for kc in range(8)], reads=[woT, mg_T], writes=[bkT])
                          cx.run("dve", [lambda e, bk=bk, j=j, sl=sl, v=v: e.scalar_tensor_tensor(out=x[:, j, sl], in0=bk, scalar=modT[:, 16 + j, v:v + 1],
                                                                                                 in1=x[:, j, sl], op0=ALU.mult, op1=ALU.add)],
                                 reads=[bkT, x_T, M_T[l]], writes=[x_T])
              layer_norm_fm(0, [(0, 512), (512, 1024), (1024, 1536)], xb, xq, lnt)
              dbg_out("x1_l%d" % l, x, [128, 8, T], [x_T])
              phase_end("outp%d" % l)
              aset(0)
              ffT = abf(NFT * T).rearrange("p (a n) -> p a n", n=T)
              ff_T = TT("ffT")
              a_sb = abf(4 * T).rearrange("p (a n) -> p a n", n=T)
              asb_T = TT("a_sb")
              sgf = [af32(512) for _ in range(2)]
              sgf_T = [TT("sgf0"), TT("sgf1")]
              modulate(sc2p, 24)
              cx.run("act", [lambda e, kc=kc: e.activation(out=x[:, kc, :], in_=x[:, kc, :], func=AF.Copy, scale=ALPHA) for kc in range(8)], reads=[x_T], writes=[x_T])
              k = 0
              for g in range(6):
                  nt = 4 if g < 5 else 2
                  wa, waT = wload(w13_d[l][:, g * 512:g * 512 + nt * 128], 8, nt * 128)
                  wg2, wg2T = wload(w13_d[l][:, FF + g * 512:FF + g * 512 + nt * 128], 8, nt * 128)
                  for jj in range(nt):
                      for nb in range(3):
                          sl = slice(nb * 512, (nb + 1) * 512)
                          bk, bkT = psum()
                          cx.run("pe", [mm(bk, wa[:, kc, jj * 128:(jj + 1) * 128], hT[:, kc, sl], kc == 0, kc == 7) for kc in range(8)],
                                 reads=[waT, hT_T], writes=[bkT])
                          cx.run("act", [lambda e, bk=bk, jj=jj, sl=sl: e.activation(out=a_sb[:, jj, sl], in_=bk, func=AF.Copy)],
                                 reads=[bkT], writes=[asb_T])
                  for jj in range(nt):
                      for nb in range(3):
                          sl = slice(nb * 512, (nb + 1) * 512)
                          bk, bkT = psum()
                          cx.run("pe", [mm(bk, wg2[:, kc, jj * 128:(jj + 1) * 128], hT[:, kc, sl], kc == 0, kc == 7) for kc in range(8)],
                                 reads=[wg2T, hT_T], writes=[bkT])
                          pp = k % 2
                          k += 1
                          cx.run("act", [lambda e, bk=bk, pp=pp: e.activation(out=sgf[pp], in_=bk, func=AF.Silu)], reads=[bkT], writes=[sgf_T[pp]])
                          cx.run("dve", [lambda e, pp=pp, jj=jj, g=g, sl=sl: e.tensor_tensor(out=ffT[:, g * 4 + jj, sl], in0=sgf[pp], in1=a_sb[:, jj, sl], op=ALU.mult)],
                                 reads=[sgf_T[pp], asb_T], writes=[ff_T])
              for j in range(8):
                  w2v, w2T = wload(w2_d[l][:, j * 128:(j + 1) * 128], NFT, 128)
                  for nb in range(3):
                      sl = slice(nb * 512, (nb + 1) * 512)
                      v = 1 if nb < 2 else 0
                      bk, bkT = psum()
                      cx.run("pe", [mm(bk, w2v[:, kc, :], ffT[:, kc, sl], kc == 0, kc == NFT - 1) for kc in range(NFT)], reads=[w2T, ff_T], writes=[bkT])
                      cx.run("dve", [lambda e, bk=bk, j=j, sl=sl, v=v: e.scalar_tensor_tensor(
                          out=x[:, j, sl], in0=bk, scalar=modT[:, 40 + j, v:v + 1], in1=x[:, j, sl], op0=ALU.mult, op1=ALU.add)],
                          reads=[bkT, x_T, M_T[l]], writes=[x_T])
              aset(0)
              xb = abf(8 * 512).rearrange("p (a n) -> p a n", n=512)
              xq = abf(8 * 512).rearrange("p (a n) -> p a n", n=512)
              lnt = (af32(512), af32(512), af32(512))
              layer_norm_fm(1, [(0, 512), (512, 1024), (1024, 1536)], xb, xq, lnt)
              dbg_out("x2_l%d" % l, x, [128, 8, T], [x_T])

        try:
            layers()
        except _Stop:
            pass
        for kc in range(8):
            cx.dma("sp", yT_d[:, kc, :], x[:, kc, :], reads=[x_T])
        cx.final()

        with nc.Block() as block:
            def replay(name):
                def f(e):
                    for th in cx.prog[name]:
                        th(e)
                return f
            block.tensor(replay("pe"))
            block.scalar(replay("act"))
            block.vector(replay("dve"))
            block.gpsimd(replay("pool"))
            block.sync(replay("sp"))
    return nc


_IDX = np.concatenate([np.arange(0, 128, 2), np.arange(1, 128, 2)])
_IDXS = np.concatenate([np.arange(1, 128, 2), np.arange(0, 128, 2)])


def _prow(r):
    return (r // 4) * 32 + (r % 4)


def _in_cols():
    o = dict(mq=0, mk=512, mv=1024, mo=1536, mg=2048, ca=2064, cg=2576, rq=3088, rk=3600, rv=4112, rg=4624, gm=5136)
    cols = []
    for h in range(H):
        for nm in ("mq", "mk", "mv", "mo"):
            cols += list(range(o[nm] + h * 128, o[nm] + (h + 1) * 128))
    z = [-1] * 28
    mg = o["mg"]
    cols += [mg + 0 * 4 + h for h in range(4)] + z + [mg + 2 * 4 + h for h in range(4)]
    cols += [mg + 1 * 4 + h for h in range(4)] + z + [mg + 3 * 4 + h for h in range(4)]
    cols += list(range(o["ca"], o["ca"] + 512)) + list(range(o["cg"], o["cg"] + 512))
    for h in range(H):
        for nm in ("rq", "rk"):
            cols += list(o[nm] + h * 128 + _IDX) + list(o[nm] + h * 128 + _IDXS)
    for h in range(H):
        cols += list(range(o["rv"] + h * 128, o["rv"] + (h + 1) * 128)) + list(range(o["rg"] + h * 128, o["rg"] + (h + 1) * 128))
    cols += list(range(o["gm"], o["gm"] + 3072))
    cols = np.array(cols, dtype=np.int64)
    assert cols.shape[0] == NCOLS, cols.shape
    return cols


_NC_CACHE = {}


def kernel(x_prompt, x_sample, state_mlstm_C, state_mlstm_n, state_mlstm_m, state_ret_S, c, c_ctx,
           ada_w, ada_b, in_w, mlstm_gate_b, mlstm_norm_w, mlstm_out_w, conv_w, conv_b, conv_ln_w, conv_ln_b,
           conv_out_w, ret_decay, ret_norm_w, ret_out_w, out_w, ln1_w, ln1_b, ln2_w, ln2_b, ffn_w13, ffn_w2, _dbg=False):
    f32 = np.float32
    A = lambda a: np.ascontiguousarray(np.asarray(a, dtype=f32))
    x_prompt, x_sample = A(x_prompt), A(x_sample)
    sC, sn, smm, sS = A(state_mlstm_C), A(state_mlstm_n), A(state_mlstm_m), A(state_ret_S)
    c, c_ctx = A(c), A(c_ctx)
    in_w = A(in_w)
    cols = _in_cols()
    in_wp = np.zeros((L, D, NCOLS), f32)
    valid = cols >= 0
    in_wp[:, :, valid] = in_w[:, :, cols[valid]]
    gbias = A(mlstm_gate_b)
    gb = np.zeros((L, 36, 2), f32)
    for h in range(4):
        gb[:, h, 0] = gbias[:, 0, h]; gb[:, h, 1] = gbias[:, 1, h]
        gb[:, 32 + h, 0] = gbias[:, 2, h]; gb[:, 32 + h, 1] = gbias[:, 3, h]
    convp = np.zeros((L, 128, 4, 34), f32)
    cw = A(conv_w)
    convp[:, :, :, 0:31] = cw.reshape(L, CONV_K, 4, 128).transpose(0, 3, 2, 1)
    convp[:, :, :, 31] = A(conv_b).reshape(L, 4, 128).transpose(0, 2, 1)
    convp[:, :, :, 32] = A(conv_ln_w).reshape(L, 4, 128).transpose(0, 2, 1)
    convp[:, :, :, 33] = A(conv_ln_b).reshape(L, 4, 128).transpose(0, 2, 1)
    lnp = np.zeros((L, 128, 4, 8), f32)
    for i, a in enumerate((ln1_w, ln1_b, ln2_w, ln2_b)):
        lnp[:, :, i, :] = A(a).reshape(L, 8, 128).transpose(0, 2, 1)
    ada_bT = np.ascontiguousarray(A(ada_b).reshape(L, 48, 128).transpose(0, 2, 1))
    rdec = A(ret_decay).reshape(L, 1, 8)
    mnw = A(mlstm_norm_w).reshape(L, 1, 512)
    rnw = A(ret_norm_w).reshape(L, 1, 512)
    cst = np.zeros((128, 1024), f32)
    jj, ii = np.meshgrid(np.arange(128), np.arange(128), indexing="ij")
    cst[:, 0:128] = np.eye(128, dtype=f32)
    cst[:, 128:256] = np.where(jj <= ii, 0.0, NEG)
    cst[:, 256:384] = np.where(jj >= ii, 0.0, NEG)
    cst[:, 384:512] = np.maximum(ii - jj, 0)
    cst[:, 512:640] = np.maximum(jj - ii, 0)
    cst[:, 640:768] = (jj <= ii)
    cst[:, 768:896] = (jj >= ii)
    p = np.arange(128)
    cst[:, 896] = p + 1; cst[:, 897] = 128 - p; cst[:, 898] = 127 - p; cst[:, 899] = p; cst[:, 900] = 128
    sel = np.zeros((36, 8, 128), f32)
    for r in range(8):
        sel[_prow(r), r, :] = 1.0
    t = np.arange(1024)
    rows = (t // 64).astype(f32); colsg = (t % 64).astype(f32)
    freqs = (np.float32(10000.0) ** (-np.arange(32, dtype=f32) / np.float32(32))).astype(f32)
    ang = np.concatenate([rows[:, None] * freqs[None, :], colsg[:, None] * freqs[None, :]], -1).astype(f32)
    cs_lat = np.concatenate([np.cos(ang).T, np.cos(ang).T], 0).astype(f32)
    sn_lat = np.concatenate([-np.sin(ang).T, np.sin(ang).T], 0).astype(f32)
    cs_id = np.ones((128, 1024), f32); sn_id = np.zeros((128, 1024), f32)

    shared = dict(cst=cst, sel=sel, ada_w=A(ada_w), ada_bT=ada_bT, in_wp=in_wp, gb=gb, mnw=mnw, rnw=rnw,
                  mlstm_out_w=A(mlstm_out_w), conv_out_w=A(conv_out_w), ret_out_w=A(ret_out_w), convp=convp, rdec=rdec,
                  out_w=A(out_w), lnp=lnp, ffn_w13=A(ffn_w13), ffn_w2=A(ffn_w2))
    in_maps = []
    seg_prompt = []
    for core in range(8):
        if core < 4:
            xs = np.concatenate([x_sample[core], x_prompt[2 * core], x_prompt[2 * core + 1]], 0)
            seg_prompt.append({4: 2 * core, 5: 2 * core + 1})
            cvec = c[core]
            Cinit = np.concatenate([sC[core], sn[core][..., None]], -1)
            minit = np.zeros((L, 36, 1), f32)
            for h in range(4):
                minit[:, h, 0] = smm[core, :, 0, h]; minit[:, 32 + h, 0] = smm[core, :, 1, h]
            Sinit = sS[core][:, :, :, _IDX, :]
            chainv = 1.0
            rcs, rsn = cs_lat, sn_lat
        else:
            base = 8 + 6 * (core - 4)
            xs = np.concatenate([x_prompt[base + s] for s in range(6)], 0)
            seg_prompt.append({s: base + s for s in range(6)})
            cvec = c_ctx
            Cinit = np.zeros((L, 2, H, 128, 129), f32)
            minit = np.zeros((L, 36, 1), f32)
            Sinit = np.zeros((L, 2, H, 128, 128), f32)
            chainv = 0.0
            rcs, rsn = cs_id, sn_id
        xT = np.ascontiguousarray(xs.reshape(T, 8, 128).transpose(2, 1, 0))
        cv = np.stack([c_ctx.reshape(8, 128).T, cvec.reshape(8, 128).T], -1)
        m = dict(shared)
        m.update(xT=xT, cv=np.ascontiguousarray(cv, dtype=f32), chain=np.full((128, 1), chainv, f32),
                 Cinit=np.ascontiguousarray(Cinit, dtype=f32), minit=minit, Sinit=np.ascontiguousarray(Sinit, dtype=f32),
                 ropeCS=rcs, ropeSN=rsn)
        in_maps.append(m)

    key = bool(_dbg)
    nc = build(dbg=key)
    res = run_bass_kernel_spmd(nc, in_maps, core_ids=list(range(8)))
    R = res.results
    y_prompt = np.zeros((32, 256, D), f32)
    y_sample = np.zeros((4, 1024, D), f32)
    new_C = np.zeros((32, L, 2, H, 128, 128), f32)
    new_n = np.zeros((32, L, 2, H, 128), f32)
    new_m = np.zeros((32, L, 2, H), f32)
    new_S = np.zeros((32, L, 2, H, 128, 128), f32)
    for core in range(8):
        r = R[core]
        yt = np.asarray(r["yT"]).transpose(2, 1, 0).reshape(T, D)
        if core < 4:
            y_sample[core] = yt[0:1024]
        Cf = np.asarray(r["Cfin"]); mf = np.asarray(r["mfin"]); Sf = np.asarray(r["Sfin"])
        for s, b in seg_prompt[core].items():
            y_prompt[b] = yt[s * 256:(s + 1) * 256]
            new_C[b] = Cf[:, s, :, :, :, 0:128]
            new_n[b] = Cf[:, s, :, :, :, 128]
            for h in range(4):
                new_m[b, :, 0, h] = mf[:, h, 2 * s + 1]
                new_m[b, :, 1, h] = mf[:, 32 + h, 2 * s]
            Su = np.empty((L, 2, H, 128, 128), f32)
            Su[:, :, :, _IDX, :] = Sf[:, s]
            new_S[b] = Su
    if _dbg:
        return (y_prompt, y_sample, new_C, new_n, new_m, new_S), R
    return (y_prompt, y_sample, new_C, new_n, new_m, new_S)
```

```python
import math
import itertools
import numpy as np
import concourse.bass as bass
import concourse.mybir as mybir
from concourse.bass_utils import run_bass_kernel_spmd
from concourse.ap import AP

F32 = mybir.dt.float32
BF16 = mybir.dt.bfloat16
AF = mybir.ActivationFunctionType
ALU = mybir.AluOpType
AX = mybir.AxisListType

D = 1024
L = 2
T = 1536
NCH = 12
NSEG = 6
H = 4
HD = 128
FF = 2816
NFT = 22
CONV_K = 31
EPS = 1e-5
ALPHA = (2.0 * L) ** 0.25
KS = HD ** -0.5
LNKS = math.log(KS)
NEG = -30000.0
NCOLS = 9288
OFF_GATE = 2048
OFF_CA = 2120
OFF_CG = 2632
OFF_RET = 3144
OFF_RVG = 5192
OFF_GM = 6216

DEBUG = {}
STOP = None


class _Stop(Exception):
    pass


def phase_end(name):
    if STOP == name:
        raise _Stop()


def rev(ap):
    a = [list(x) for x in ap.ap]
    step, n = a[-1]
    off = ap.offset + step * (n - 1)
    a[-1] = [-step, n]
    return AP(ap.tensor, off, a)


import types


def _snap(th):
    cl = th.__closure__
    if not cl:
        return th
    cells = []
    for c in cl:
        try:
            cells.append(types.CellType(c.cell_contents))
        except ValueError:
            cells.append(c)
    return types.FunctionType(th.__code__, th.__globals__, th.__name__, th.__defaults__, tuple(cells))


class TT:
    __slots__ = ("name", "w", "r")

    def __init__(self, name=""):
        self.name = name
        self.w = None
        self.r = {}


class Ctx:
    ENG = ["pe", "act", "dve", "pool", "sp"]

    def __init__(self, nc, sems):
        self.nc = nc
        self.sems = sems
        self.prog = {e: [] for e in self.ENG}
        self.cnt = {e: 0 for e in self.ENG}
        self.seen = {e: {} for e in self.ENG}
        self.dkeys = [k for k in sems if k[0] == "d" and k[1:].isdigit()]
        self.wkeys = [k for k in sems if k[0] == "w" and k[1:].isdigit()]
        self.dtot = {k: 0 for k in sems}
        self.dn = 0
        self.wn = 0

    def _wait(self, eng, key, val):
        if val <= 0 or self.seen[eng].get(key, 0) >= val:
            return
        self.seen[eng][key] = val
        sem = self.sems[key]
        self.prog[eng].append(lambda e, sem=sem, val=val: e.wait_ge(sem, val))

    def _deps(self, eng, reads, writes):
        for t in reads:
            if t.w is not None:
                self._wait_dep(eng, t.w)
        for t in writes:
            if t.w is not None:
                self._wait_dep(eng, t.w)
            for k, v in t.r.items():
                self._wait_dep(eng, (k, v))

    def _wait_dep(self, eng, dep):
        k, v = dep
        if k == "pe" and eng == "pe":
            return
        self._wait(eng, k, v)

    def run(self, eng, thunks, reads=(), writes=()):
        if not isinstance(thunks, (list, tuple)):
            thunks = [thunks]
        thunks = [_snap(t) for t in thunks]
        self._deps(eng, reads, writes)
        sem = self.sems[eng]
        n = len(thunks)
        for i, th in enumerate(thunks):
            if i == n - 1:
                self.prog[eng].append(lambda e, th=th, sem=sem: th(e).then_inc(sem, 1))
            else:
                self.prog[eng].append(th)
        self.cnt[eng] += 1
        c = self.cnt[eng]
        for t in reads:
            t.r[eng] = c
        for t in writes:
            t.w = (eng, c)
            t.r = {}

    def dma(self, q, out, in_, reads=(), writes=()):
        if q == "pool":
            key = self.wkeys[self.wn % len(self.wkeys)]
            self.wn += 1
        else:
            key = self.dkeys[self.dn % len(self.dkeys)]
            self.dn += 1
        prev = self.dtot[key]
        self._wait(q, key, prev)
        self._deps(q, reads, writes)
        new = prev + 16
        self.dtot[key] = new
        sem = self.sems[key]
        self.prog[q].append(lambda e, out=out, in_=in_, sem=sem: e.dma_start(out=out, in_=in_).then_inc(sem, 16))
        for t in reads:
            t.r[key] = new
        for t in writes:
            t.w = (key, new)
            t.r = {}

    def barrier(self):
        engs = ["pe", "act", "dve", "sp"]
        for e in engs:
            for f in ["pe", "act", "dve"]:
                if f != e or e != "pe":
                    self._wait(e, f, self.cnt[f])
            for k in self.dkeys:
                self._wait(e, k, self.dtot[k])

    def final(self):
        for k in self.dkeys:
            self._wait("sp", k, self.dtot[k])
        for f in ["pe", "act", "dve"]:
            self._wait("sp", f, self.cnt[f])


def build(dbg=False):
    nc = bass.Bass("TRN2", target_bir_lowering=False)
    dram = {}

    def din(name, shape, dt=F32):
        dram[name] = nc.dram_tensor(name, list(shape), dt, kind="ExternalInput").ap()
        return dram[name]

    def dout(name, shape, dt=F32):
        dram[name] = nc.dram_tensor(name, list(shape), dt, kind="ExternalOutput").ap()
        return dram[name]

    xT_d = din("xT", [128, 8, T])
    cv_d = din("cv", [128, 8, 2])
    chain_d = din("chain", [128, 1])
    Cinit_d = din("Cinit", [L, 2, H, 128, 129])
    minit_d = din("minit", [L, 36, 1])
    Sinit_d = din("Sinit", [L, 2, H, 128, 128])
    ropeCS_d = din("ropeCS", [128, 1024])
    ropeSN_d = din("ropeSN", [128, 1024])
    cst_d = din("cst", [128, 1024])
    sel_d = din("sel", [36, 8, 128])
    ada_w_d = din("ada_w", [L, D, 6 * D])
    ada_b_d = din("ada_bT", [L, 128, 48])
    in_w_d = din("in_wp", [L, D, NCOLS])
    gb_d = din("gb", [L, 36, 2])
    mnw_d = din("mnw", [L, 1, 512])
    rnw_d = din("rnw", [L, 1, 512])
    mow_d = din("mlstm_out_w", [L, 512, D])
    cow_d = din("conv_out_w", [L, 512, D])
    row_d = din("ret_out_w", [L, 512, D])
    convp_d = din("convp", [L, 128, 4, 34])
    rdec_d = din("rdec", [L, 1, 8])
    outw_d = din("out_w", [L, D, D])
    lnp_d = din("lnp", [L, 128, 4, 8])
    w13_d = din("ffn_w13", [L, D, 2 * FF])
    w2_d = din("ffn_w2", [L, FF, D])

    yT_d = dout("yT", [128, 8, T])
    Cfin_d = dout("Cfin", [L, NSEG, 2, H, 128, 129])
    mfin_d = dout("mfin", [L, 36, 12])
    Sfin_d = dout("Sfin", [L, NSEG, 2, H, 128, 128])
    dbg_d = {}

    import contextlib
    es = contextlib.ExitStack()
    with es:
        def sb(name, shape, dt=F32):
            return es.enter_context(nc.sbuf_tensor("s_" + name, list(shape), dt))[:]

        x = sb("x", [128, 8, T])
        hT = sb("hT", [128, 8, T], BF16)
        NW = 3
        Wr = [sb("wr%d" % i, [128, 4096], BF16) for i in range(NW)]
        Wr_T = [TT("wr%d" % i) for i in range(NW)]
        cst = sb("cst", [128, 1024])
        sel = sb("sel", [36, 8, 128])
        identB = sb("identB", [128, 128], BF16)
        onesB = sb("onesB", [128, 128], BF16)
        chain = sb("chain", [128, 1])
        cvs = sb("cvs", [128, 8, 2], BF16)
        cvf = sb("cvf", [128, 8, 2])
        modTs = [sb("modT%d" % i, [128, 48, 2]) for i in range(L)]
        sc1ps = [sb("sc1p%d" % i, [128, 8, 2]) for i in range(L)]
        sc2ps = [sb("sc2p%d" % i, [128, 8, 2]) for i in range(L)]
        adabs = [sb("adab%d" % i, [128, 48]) for i in range(L)]
        M_T = [TT("mod%d" % i) for i in range(L)]
        gb = sb("gb", [36, 2])
        ngb = sb("ngb", [36, 1])
        minit = sb("minit", [36, 1])
        mnw = sb("mnw", [128, 512])
        rnw = sb("rnw", [128, 512])
        convp = sb("convp", [128, 4, 34])
        lnp = sb("lnp", [128, 4, 8])
        rdec = sb("rdec", [128, 8])
        lg = sb("lg", [128, 8])
        rcol = sb("rcol", [128, 8, 4])
        decT = sb("decT", [128, 8, 128])
        ones36 = sb("ones36", [36, 128])
        zeros36 = sb("zeros36", [36, 128])
        AR = 23168
        arena = sb("arena", [128, AR])
        ps = [es.enter_context(nc.psum_tensor("ps%d" % i, [128, 512], F32))[:] for i in range(8)]
        ps_T = [TT("ps%d" % i) for i in range(8)]

        keys = ["pe", "act", "dve", "pool", "sp"] + ["d%d" % i for i in range(8)] + ["w%d" % i for i in range(6)]
        sems = {k: es.enter_context(nc.semaphore(k)) for k in keys}
        cx = Ctx(nc, sems)
        G = TT("globals")
        x_T = TT("x")
        hT_T = TT("hT")

        identF = cst[:, 0:128]
        MASK = [cst[:, 128:256], cst[:, 256:384]]
        DIFF = [cst[:, 384:512], cst[:, 512:640]]
        M01 = [cst[:, 640:768], cst[:, 768:896]]
        POS = cst[:, 896:904]

        pstate = {"i": 0}

        def psum():
            i = pstate["i"] % 8
            pstate["i"] += 1
            return ps[i], ps_T[i]

        wstate = {"i": 0}

        def wload(src, kc, ncol):
            i = wstate["i"] % NW
            wstate["i"] += 1
            dst = Wr[i][:, 0:kc * ncol].rearrange("p (k n) -> p k n", n=ncol)
            cx.dma("pool", dst, src.rearrange("(k p) n -> p k n", p=128), writes=[Wr_T[i]])
            return dst, Wr_T[i]

        ast = {"o": 0}

        def aset(o):
            cx.barrier()
            ast["o"] = o

        def af32(n, shape=None):
            o = ast["o"]
            ast["o"] += n
            assert ast["o"] <= AR, ast["o"]
            v = arena[:, o:o + n]
            return v

        def abf(n):
            n2 = (n + 1) // 2
            return af32(n2).bitcast(BF16)

        def dbg_out(name, ap, shape, reads):
            if not dbg:
                return
            d = dout("dbg_" + name, shape, ap.dtype)
            DEBUG[name] = shape
            cx.dma("sp", d, ap, reads=reads)

        mm = lambda out, lhsT, rhs, st, sp: (lambda e: e.matmul(out, lhsT=lhsT, rhs=rhs, start=st, stop=sp))

        for kc in range(8):
            cx.dma("sp", x[:, kc, :], xT_d[:, kc, :], writes=[x_T])
        for dst, src in ((cst, cst_d), (sel, sel_d), (chain, chain_d), (cvf, cv_d)):
            cx.dma("sp", dst, src, writes=[G])
        cx.run("dve", [lambda e: e.memset(ones36, 1.0), lambda e: e.memset(zeros36, 0.0),
                       lambda e: e.memset(onesB, 1.0),
                       lambda e: e.tensor_copy(out=identB, in_=identF)],
               reads=[G], writes=[G])
        cx.run("act", [lambda e: e.activation(out=cvs, in_=cvf, func=AF.Silu)], reads=[G], writes=[G])
        for i in range(L):
            cx.dma("sp", adabs[i], ada_b_d[i], writes=[M_T[i]])

        def mod_groups(ll, gs):
            mT = modTs[ll]
            for g in gs:
                wv, wT = wload(ada_w_d[ll][:, g * 512:(g + 1) * 512], 8, 512)
                mb, mbT = psum()
                for jj in range(4):
                    cx.run("pe", [mm(mb[:, 2 * jj:2 * jj + 2], wv[:, kc, jj * 128:(jj + 1) * 128], cvs[:, kc, :], kc == 0, kc == 7) for kc in range(8)],
                           reads=[wT, G], writes=[mbT])
                j0 = g * 4
                cx.run("act", [lambda e, mb=mb, j0=j0: e.activation(out=mT[:, j0:j0 + 4, :].rearrange("p j v -> p (j v)"), in_=mb[:, 0:8], func=AF.Copy)],
                       reads=[mbT], writes=[M_T[ll]])
                cx.run("dve", [lambda e, j0=j0: e.tensor_tensor(out=mT[:, j0:j0 + 4, :], in0=mT[:, j0:j0 + 4, :],
                                                                in1=adabs[ll][:, j0:j0 + 4].unsqueeze(2).to_broadcast([128, 4, 2]), op=ALU.add)],
                       reads=[M_T[ll]], writes=[M_T[ll]])
                if g in (2, 3):
                    o = (g - 2) * 4
                    cx.run("dve", [lambda e, j0=j0, o=o: e.tensor_scalar(out=sc1ps[ll][:, o:o + 4, :], in0=mT[:, j0:j0 + 4, :], scalar1=1.0, scalar2=None, op0=ALU.add)],
                           reads=[M_T[ll]], writes=[M_T[ll]])
                if g in (8, 9):
                    o = (g - 8) * 4
                    cx.run("dve", [lambda e, j0=j0, o=o: e.tensor_scalar(out=sc2ps[ll][:, o:o + 4, :], in0=mT[:, j0:j0 + 4, :], scalar1=1.0, scalar2=None, op0=ALU.add)],
                           reads=[M_T[ll]], writes=[M_T[ll]])

        def layer_norm_fm(which, blocks, xb, xq, lnt):
            lw = lnp[:, 2 * which, :]
            lb = lnp[:, 2 * which + 1, :]
            xb_T, xq_T, lnt_T = TT("xb"), TT("xq"), TT("lnt")
            for (a_, b_) in blocks:
                sl = slice(a_, b_)
                cx.run("act", [lambda e, kc=kc: e.activation(out=xb[:, kc, :], in_=x[:, kc, sl], func=AF.Copy) for kc in range(8)],
                       reads=[x_T], writes=[xb_T])
                cx.run("act", [lambda e, kc=kc: e.activation(out=xq[:, kc, :], in_=x[:, kc, sl], func=AF.Square) for kc in range(8)],
                       reads=[x_T], writes=[xq_T])
                b1, b1T = psum()
                cx.run("pe", [mm(b1, onesB, xb[:, kc, :], kc == 0, kc == 7) for kc in range(8)], reads=[xb_T], writes=[b1T])
                b2, b2T = psum()
                cx.run("pe", [mm(b2, onesB, xq[:, kc, :], kc == 0, kc == 7) for kc in range(8)], reads=[xq_T], writes=[b2T])
                mean, msq, rstd = lnt
                cx.run("act", [lambda e: e.activation(out=mean, in_=b1, func=AF.Copy, scale=1.0 / D)], reads=[b1T], writes=[lnt_T])
                cx.run("dve", [lambda e: e.tensor_tensor(out=msq, in0=mean, in1=mean, op=ALU.mult)], reads=[lnt_T], writes=[lnt_T])
                cx.run("dve", [lambda e: e.scalar_tensor_tensor(out=msq, in0=b2, scalar=1.0 / D, in1=msq, op0=ALU.mult, op1=ALU.subtract)],
                       reads=[b2T, lnt_T], writes=[lnt_T])
                cx.run("dve", [lambda e: e.tensor_scalar(out=msq, in0=msq, scalar1=EPS, scalar2=None, op0=ALU.add)], reads=[lnt_T], writes=[lnt_T])
                cx.run("act", [lambda e: e.activation(out=rstd, in_=msq, func=AF.Sqrt)], reads=[lnt_T], writes=[lnt_T])
                cx.run("dve", [lambda e: e.reciprocal(out=rstd, in_=rstd)], reads=[lnt_T], writes=[lnt_T])
                cx.run("dve", [lambda e, kc=kc: e.tensor_tensor(out=x[:, kc, sl], in0=x[:, kc, sl], in1=mean, op=ALU.subtract) for kc in range(8)],
                       reads=[x_T, lnt_T], writes=[x_T])
                cx.run("dve", [lambda e, kc=kc: e.tensor_tensor(out=x[:, kc, sl], in0=x[:, kc, sl], in1=rstd, op=ALU.mult) for kc in range(8)],
                       reads=[x_T, lnt_T], writes=[x_T])
                cx.run("act", [lambda e, kc=kc: e.activation(out=x[:, kc, sl], in_=x[:, kc, sl], func=AF.Identity,
                                                             scale=lw[:, kc:kc + 1], bias=lb[:, kc:kc + 1]) for kc in range(8)],
                       reads=[x_T, G], writes=[x_T])

        lnks = sb("lnks", [128, 1])
        cx.run("dve", [lambda e: e.memset(lnks, LNKS)], writes=[G])

        def ln_stats(b1, b1T, b2, b2T, mean, msq, rstd, st_T, n):
            cx.run("act", [lambda e: e.activation(out=mean, in_=b1, func=AF.Copy, scale=1.0 / n)], reads=[b1T], writes=[st_T])
            cx.run("dve", [lambda e: e.tensor_tensor(out=msq, in0=mean, in1=mean, op=ALU.mult)], reads=[st_T], writes=[st_T])
            cx.run("dve", [lambda e: e.scalar_tensor_tensor(out=msq, in0=b2, scalar=1.0 / n, in1=msq, op0=ALU.mult, op1=ALU.subtract)],
                   reads=[b2T, st_T], writes=[st_T])
            cx.run("dve", [lambda e: e.tensor_scalar(out=msq, in0=msq, scalar1=EPS, scalar2=None, op0=ALU.add)], reads=[st_T], writes=[st_T])
            cx.run("act", [lambda e: e.activation(out=rstd, in_=msq, func=AF.Sqrt)], reads=[st_T], writes=[st_T])
            cx.run("dve", [lambda e: e.reciprocal(out=rstd, in_=rstd)], reads=[st_T], writes=[st_T])

        def head_norm_out(src3, src_T, normw, h, gate3, gate_T, dstT, dst_T, scr, scr_T, hn_all, hn_Ts):
            stA = scr[:, 0:72].rearrange("p (c n) -> p c n", n=6)
            mvA = scr[:, 72:96].rearrange("p (c n) -> p c n", n=2)
            src_all = src3
            hn3 = hn_all.rearrange("p (c n) -> p c n", n=128)
            cx.run("dve", [lambda e, c=c: e.bn_stats(out=stA[:, c, :], in_=src3[:, c, :]) for c in range(12)], reads=list(src_T) + [scr_T], writes=[scr_T])
            cx.run("dve", [lambda e, c=c: e.bn_aggr(out=mvA[:, c, :], in_=stA[:, c, :]) for c in range(12)], reads=[scr_T], writes=[scr_T])
            rstd = mvA[:, :, 1]
            cx.run("dve", [lambda e: e.tensor_scalar(out=rstd, in0=rstd, scalar1=EPS, scalar2=None, op0=ALU.add)], reads=[scr_T], writes=[scr_T])
            cx.run("act", [lambda e: e.activation(out=rstd, in_=rstd, func=AF.Sqrt)], reads=[scr_T], writes=[scr_T])
            cx.run("dve", [lambda e: e.reciprocal(out=rstd, in_=rstd)], reads=[scr_T], writes=[scr_T])
            cx.run("dve", [lambda e, c=c: e.tensor_scalar(out=src3[:, c, :], in0=src3[:, c, :], scalar1=mvA[:, c, 0:1], scalar2=mvA[:, c, 1:2],
                                                          op0=ALU.subtract, op1=ALU.mult) for c in range(12)], reads=list(src_T) + [scr_T], writes=list(src_T))
            cx.run("dve", [lambda e, c=c: e.tensor_tensor(out=src3[:, c, :], in0=src3[:, c, :], in1=normw[:, h * 128:(h + 1) * 128], op=ALU.mult) for c in range(12)],
                   reads=list(src_T) + [G], writes=list(src_T))
            cx.run("dve", [lambda e: e.tensor_tensor(out=hn3, in0=src3, in1=gate3, op=ALU.mult)], reads=list(src_T) + [gate_T] + list(hn_Ts), writes=list(hn_Ts))
            for c in range(12):
                bk, bkT = psum()
                bkb = bk.bitcast(BF16)
                cx.run("pe", [lambda e, bkb=bkb, c=c: e.transpose(out=bkb[:, 0:128], in_=hn3[:, c, :], identity=identB)], reads=list(hn_Ts) + [G], writes=[bkT])
                cx.run("act", [lambda e, bkb=bkb, c=c: e.activation(out=dstT[:, h, c * 128:(c + 1) * 128], in_=bkb[:, 0:128], func=AF.Copy)], reads=[bkT], writes=[dst_T])

        def layers():
          for l in range(L):
              phase_end('pro%d' % l)
              aset(0)
              for dst, src in ((gb, gb_d[l]), (minit, minit_d[l]), (convp, convp_d[l]), (lnp, lnp_d[l]),
                               (mnw, mnw_d[l].partition_broadcast(128)), (rnw, rnw_d[l].partition_broadcast(128)),
                               (rdec, rdec_d[l].partition_broadcast(128))):
                  cx.dma("sp", dst, src, writes=[G])
              cx.run("dve", [lambda e: e.tensor_scalar(out=ngb, in0=gb[:, 1:2], scalar1=-1.0, scalar2=None, op0=ALU.mult)], reads=[G], writes=[G])
              cx.run("act", [lambda e: e.activation(out=lg, in_=rdec, func=AF.Exp, scale=-1.0)], reads=[G], writes=[G])
              cx.run("act", [lambda e: e.activation(out=lg, in_=lg, func=AF.Ln, bias=1.0)], reads=[G], writes=[G])
              cx.run("dve", [lambda e: e.tensor_scalar(out=lg, in0=lg, scalar1=-1.0, scalar2=None, op0=ALU.mult)], reads=[G], writes=[G])
              for r in range(8):
                  d = r // 4
                  lgc = lg[:, r:r + 1]
                  cx.run("act", [lambda e, r=r, d=d, lgc=lgc: e.activation(out=decT[:, r, :], in_=DIFF[d], func=AF.Exp, scale=lgc)], reads=[G], writes=[G])
                  cx.run("dve", [lambda e, r=r, d=d: e.scalar_tensor_tensor(out=decT[:, r, :], in0=decT[:, r, :], scalar=KS, in1=M01[d],
                                                                           op0=ALU.mult, op1=ALU.mult)], reads=[G], writes=[G])
                  cx.run("act", [lambda e, r=r, d=d, lgc=lgc: e.activation(out=rcol[:, r, 0:1], in_=POS[:, d:d + 1], func=AF.Exp, scale=lgc),
                                 lambda e, r=r, d=d, lgc=lgc: e.activation(out=rcol[:, r, 1:2], in_=POS[:, 2 + d:3 + d], func=AF.Exp, scale=lgc),
                                 lambda e, r=r, d=d, lgc=lgc: e.activation(out=rcol[:, r, 2:3], in_=POS[:, 4:5], func=AF.Exp, scale=lgc)],
                         reads=[G], writes=[G])
                  cx.run("dve", [lambda e, r=r: e.tensor_scalar(out=rcol[:, r, 1:2], in0=rcol[:, r, 1:2], scalar1=KS, scalar2=None, op0=ALU.mult)],
                         reads=[G], writes=[G])

              phase_end('small%d' % l)
              modT, sc1p, sc2p = modTs[l], sc1ps[l], sc2ps[l]
              if l == 0:
                  mod_groups(0, [0, 1, 2, 3])
              phase_end('modmm%d' % l)

              def modulate(scp, shoff):
                  ths = []
                  for kc in range(8):
                      for (a, b, v) in ((0, 1024, 1), (1024, T, 0)):
                          ths.append(lambda e, kc=kc, a=a, b=b, v=v: e.tensor_scalar(
                              out=hT[:, kc, a:b], in0=x[:, kc, a:b], scalar1=scp[:, kc, v:v + 1],
                              scalar2=modT[:, shoff + kc, v:v + 1], op0=ALU.mult, op1=ALU.add))
                  cx.run("dve", ths, reads=[x_T, M_T[l]], writes=[hT_T])

              dbg_out("modT_l%d" % l, modT, [128, 48, 2], [M_T[l]])
              phase_end("mod%d" % l)
              modulate(sc1p, 0)
              phase_end("h%d" % l)
              dbg_out("h_l%d" % l, hT, [128, 8, T], [hT_T])

              aset(3072)
              h_aT = arena[:, 0:3072].bitcast(BF16).rearrange("p (h t) -> p h t", t=T)
              ubT = arena[:, 3072:6144].bitcast(BF16).rearrange("p (h t) -> p h t", t=T)
              h_cT = arena[:, 6144:9216].bitcast(BF16).rearrange("p (h t) -> p h t", t=T)
              merged = arena[:, 9216:15360].bitcast(BF16).rearrange("p (h t) -> p h t", t=T)
              haT_T = TT("h_aT"); ub_T = TT("ubT"); hcT_T = TT("h_cT"); mg_T = TT("merged")
              rIG = af32(T); rA = af32(T); rP = af32(T); rCM = af32(T)
              R_T = TT("rows")
              sm = af32(96).rearrange("p (a c) -> p a c", c=12)
              SM_T = TT("sm")
              cols = af32(3 * 432).rearrange("p (q n) -> p q n", n=432)
              COL_T = TT("cols")
              cbc = af32(96)
              qT = abf(T); kT = abf(T)
              qk_T = TT("qk")
              vext = abf(12 * 130).rearrange("p (c n) -> p c n", n=130)
              v_T = TT("vext")
              osig = abf(T).rearrange("p (c n) -> p c n", n=128)
              o_T = TT("osig")
              hsum = af32(T).rearrange("p (c n) -> p c n", n=128)
              hs_T = [TT("hs%d" % c) for c in range(12)]
              kw = [abf(T).rearrange("p (c n) -> p c n", n=128) for _ in range(2)]
              kw_T = [[TT("kw") for c in range(12)] for _ in range(2)]
              Cx = [[af32(130) for _ in range(2)] for _ in range(2)]
              Cx_T = [[TT("cx"), TT("cx")] for _ in range(2)]
              sc2 = af32(8)
              dmg = [af32(512)] * 2
              dmg_T = [TT("dmg0")] * 2
              sT_all = abf(24 * 128)
              sTa_T = [TT("sTa%d" % i) for i in range(6)]
              Cb_all = [abf(12 * 130).rearrange("p (s n) -> p s n", n=130) for _ in range(2)]
              CbA_T = [[TT("cba") for _ in range(12)] for _ in range(2)]
              Bs2 = [af32(130) for _ in range(4)]
              Bs2_T = [TT("bs2_%d" % i) for i in range(4)]
              tot_all = af32(12 * 130).rearrange("p (c n) -> p c n", n=130)
              tota_T = TT("tot_all")
              dn12 = af32(12)
              dn_T = TT("dn12")

              wg, wgT = wload(in_w_d[l][:, OFF_GATE:OFF_GATE + 72], 8, 72)
              cx.run("dve", [lambda e: e.memset(rP[0:36, :], 0.0), lambda e: e.memset(rCM[0:36, :], 0.0)], writes=[R_T])
              for which in (0, 1):
                  for nb in range(3):
                      bk, bkT = psum()
                      sl = slice(nb * 512, (nb + 1) * 512)
                      cx.run("pe", [mm(bk[0:36, :], wg[:, kc, which * 36:(which + 1) * 36], hT[:, kc, sl], kc == 0, kc == 7) for kc in range(8)],
                             reads=[wgT, hT_T], writes=[bkT])
                      if which == 0:
                          cx.run("act", [lambda e, bk=bk, sl=sl: e.activation(out=rIG[0:36, sl], in_=bk[0:36, :], func=AF.Identity, bias=gb[:, 0:1])],
                                 reads=[bkT, G], writes=[R_T])
                      else:
                          cx.run("act", [lambda e, bk=bk, sl=sl: e.activation(out=rA[0:36, sl], in_=bk[0:36, :], func=AF.Exp, scale=-1.0, bias=ngb[:, 0:1])],
                                 reads=[bkT, G], writes=[R_T])
              cx.run("act", [lambda e: e.activation(out=rA[0:36, :], in_=rA[0:36, :], func=AF.Ln, bias=1.0)], reads=[R_T], writes=[R_T])
              ths = []
              for c in range(12):
                  cs = slice(c * 128, (c + 1) * 128)
                  ths.append(lambda e, cs=cs: e.tensor_tensor_scan(out=rP[0:4, cs], data0=ones36[0:4, :], data1=rA[0:4, cs], initial=0.0,
                                                                   op0=ALU.mult, op1=ALU.add))
                  ths.append(lambda e, cs=cs: e.tensor_tensor_scan(out=rev(rP[32:36, cs]), data0=ones36[32:36, :], data1=rev(rA[32:36, cs]),
                                                                   initial=0.0, op0=ALU.mult, op1=ALU.add))
              cx.run("dve", ths, reads=[R_T], writes=[R_T])
              cx.run("dve", [lambda e: e.tensor_tensor(out=rA[0:36, :], in0=rIG[0:36, :], in1=rP[0:36, :], op=ALU.add)], reads=[R_T], writes=[R_T])
              ths = []
              for c in range(12):
                  cs = slice(c * 128, (c + 1) * 128)
                  ths.append(lambda e, cs=cs: e.tensor_tensor_scan(out=rCM[0:4, cs], data0=zeros36[0:4, :], data1=rA[0:4, cs], initial=-1e30,
                                                                   op0=ALU.add, op1=ALU.max))
                  ths.append(lambda e, cs=cs: e.tensor_tensor_scan(out=rev(rCM[32:36, cs]), data0=zeros36[32:36, :], data1=rev(rA[32:36, cs]),
                                                                   initial=-1e30, op0=ALU.add, op1=ALU.max))
              cx.run("dve", ths, reads=[R_T], writes=[R_T])
              CM3 = rCM.rearrange("p (c t) -> p c t", t=128)
              P3 = rP.rearrange("p (c t) -> p c t", t=128)
              A3 = rA.rearrange("p (c t) -> p c t", t=128)
              IG3 = rIG.rearrange("p (c t) -> p c t", t=128)
              cx.run("dve", [lambda e: e.memset(sm[0:36, :, :], 0.0)], writes=[SM_T])
              cx.run("dve", [lambda e: e.tensor_copy(out=sm[0:4, 0, :], in_=CM3[0:4, :, 127]),
                             lambda e: e.tensor_copy(out=sm[32:36, 0, :], in_=CM3[32:36, :, 0]),
                             lambda e: e.tensor_copy(out=sm[0:4, 1, :], in_=P3[0:4, :, 127]),
                             lambda e: e.tensor_copy(out=sm[32:36, 1, :], in_=P3[32:36, :, 0])], reads=[R_T, SM_T], writes=[SM_T])
              for d, r0 in ((0, 0), (1, 32)):
                  rs = slice(r0, r0 + 4)
                  order = list(range(12)) if d == 0 else list(range(11, -1, -1))
                  prev = None
                  for c in order:
                      s = c // 2
                      start = (c % 2 == 0) if d == 0 else (c % 2 == 1)
                      m0c = sm[rs, 2, c:c + 1]
                      if start:
                          if (d == 0 and s == 0) or (d == 1 and s == 3):
                              th = lambda e, m0c=m0c, rs=rs: e.tensor_copy(out=m0c, in_=minit[rs, :])
                          elif (d == 0 and s <= 3) or (d == 1 and s <= 2):
                              th = lambda e, m0c=m0c, rs=rs, prev=prev: e.tensor_scalar(out=m0c, in0=sm[rs, 4, prev:prev + 1], scalar1=chain[rs, :],
                                                                                       scalar2=None, op0=ALU.mult)
                          else:
                              th = lambda e, m0c=m0c: e.memset(m0c, 0.0)
                      else:
                          th = lambda e, m0c=m0c, rs=rs, prev=prev: e.tensor_copy(out=m0c, in_=sm[rs, 4, prev:prev + 1])
                      cx.run("dve", [th], reads=[SM_T, G], writes=[SM_T])
                      cx.run("dve", [lambda e, rs=rs, c=c: e.tensor_tensor(out=sm[rs, 3, c:c + 1], in0=sm[rs, 2, c:c + 1], in1=sm[rs, 0, c:c + 1], op=ALU.max)],
                             reads=[SM_T], writes=[SM_T])
                      cx.run("dve", [lambda e, rs=rs, c=c: e.tensor_tensor(out=sm[rs, 4, c:c + 1], in0=sm[rs, 3, c:c + 1], in1=sm[rs, 1, c:c + 1], op=ALU.subtract)],
                             reads=[SM_T], writes=[SM_T])
                      prev = c
              cx.dma("sp", mfin_d[l], sm[0:36, 4, :], reads=[SM_T])
              cx.run("dve", [lambda e: e.tensor_tensor(out=sm[0:36, 6, :], in0=sm[0:36, 3, :], in1=sm[0:36, 2, :], op=ALU.subtract)], reads=[SM_T], writes=[SM_T])
              cx.run("act", [lambda e: e.activation(out=sm[0:36, 5, :], in_=sm[0:36, 6, :], func=AF.Exp, scale=-1.0)], reads=[SM_T], writes=[SM_T])
              m0b = sm[0:36, 2, :].unsqueeze(2).to_broadcast([36, 12, 128])
              mxb = sm[0:36, 3, :].unsqueeze(2).to_broadcast([36, 12, 128])
              cx.run("dve", [lambda e: e.tensor_tensor(out=CM3[0:36], in0=CM3[0:36], in1=m0b, op=ALU.max)], reads=[R_T, SM_T], writes=[R_T])
              cx.run("dve", [lambda e: e.tensor_scalar(out=rCM[0:36, :], in0=rCM[0:36, :], scalar1=-1.0, scalar2=None, op0=ALU.mult)], reads=[R_T], writes=[R_T])
              dbg_out("rA_l%d" % l, rA[0:36, :], [36, T], [R_T])
              dbg_out("rNG_l%d" % l, rCM[0:36, :], [36, T], [R_T])

              def cols_from(q, rows):
                  bk, bkT = psum()
                  cx.run("pe", [lambda e, c=c, bk=bk: e.transpose(out=bk[:, c * 36:(c + 1) * 36], in_=rows[0:36, c * 128:(c + 1) * 128],
                                                                  identity=identF[0:36, 0:36]) for c in range(12)],
                         reads=[R_T, G], writes=[bkT])
                  cx.run("act", [lambda e, bk=bk: e.activation(out=cols[:, q, :], in_=bk[:, 0:432], func=AF.Copy)], reads=[bkT], writes=[COL_T])

              cx.run("dve", [lambda e: e.tensor_tensor(out=IG3[0:36], in0=CM3[0:36], in1=m0b, op=ALU.add)], reads=[R_T, SM_T], writes=[R_T])
              cx.run("act", [lambda e: e.activation(out=rIG[0:36, :], in_=rIG[0:36, :], func=AF.Exp)], reads=[R_T], writes=[R_T])
              cols_from(0, rIG)
              cx.run("dve", [lambda e: e.tensor_tensor(out=rP[0:36, :], in0=rP[0:36, :], in1=rCM[0:36, :], op=ALU.add)], reads=[R_T], writes=[R_T])
              cx.run("act", [lambda e: e.activation(out=rP[0:36, :], in_=rP[0:36, :], func=AF.Exp)], reads=[R_T], writes=[R_T])
              cols_from(1, rP)
              cx.run("dve", [lambda e: e.tensor_tensor(out=IG3[0:36], in0=A3[0:36], in1=mxb, op=ALU.subtract)], reads=[R_T, SM_T], writes=[R_T])
              cx.run("dve", [lambda e: e.tensor_scalar(out=rIG[0:36, :], in0=rIG[0:36, :], scalar1=LNKS, scalar2=None, op0=ALU.add)], reads=[R_T], writes=[R_T])
              cx.run("act", [lambda e: e.activation(out=rIG[0:36, :], in_=rIG[0:36, :], func=AF.Exp)], reads=[R_T], writes=[R_T])
              cols_from(2, rIG)
              bk, bkT = psum()
              cx.run("pe", [mm(bk[:, r * 12:(r + 1) * 12], sel[:, r, :], sm[0:36, 5, :], True, True) for r in range(8)], reads=[SM_T, G], writes=[bkT])
              cx.run("act", [lambda e, bk=bk: e.activation(out=cbc, in_=bk[:, 0:96], func=AF.Copy)], reads=[bkT], writes=[COL_T])
              dbg_out("sm_l%d" % l, sm[0:36, :, :], [36, 8, 12], [SM_T])
              dbg_out("cols_l%d" % l, cols, [128, 3, 432], [COL_T])

              phase_end("gates%d" % l)
              prow = lambda r: (r // 4) * 32 + (r % 4)
              lnks_col = None

              for h in range(H):
                  wv, wT = wload(in_w_d[l][:, h * 512:(h + 1) * 512], 8, 512)
                  for nb in range(3):
                      sl = slice(nb * 512, (nb + 1) * 512)
                      for which, dst in ((0, qT), (1, kT)):
                          bk, bkT = psum()
                          cx.run("pe", [mm(bk, wv[:, kc, which * 128:(which + 1) * 128], hT[:, kc, sl], kc == 0, kc == 7) for kc in range(8)],
                                 reads=[wT, hT_T], writes=[bkT])
                          cx.run("act", [lambda e, bk=bk, dst=dst, sl=sl: e.activation(out=dst[:, sl], in_=bk, func=AF.Copy)], reads=[bkT], writes=[qk_T])
                  cx.run("dve", [lambda e: e.memset(vext[:, :, 128:130], 1.0)], writes=[v_T])
                  for c in range(12):
                      cs = slice(c * 128, (c + 1) * 128)
                      bk, bkT = psum()
                      cx.run("pe", [mm(bk[:, 0:256], hT[:, kc, cs], wv[:, kc, 256:512], kc == 0, kc == 7) for kc in range(8)], reads=[wT, hT_T], writes=[bkT])
                      cx.run("act", [lambda e, bk=bk, c=c: e.activation(out=vext[:, c, 0:128], in_=bk[:, 0:128], func=AF.Copy)], reads=[bkT], writes=[v_T])
                      cx.run("act", [lambda e, bk=bk, c=c: e.activation(out=osig[:, c, :], in_=bk[:, 128:256], func=AF.Sigmoid)], reads=[bkT], writes=[o_T])
                  for c in range(12):
                      cs = slice(c * 128, (c + 1) * 128)
                      bk, bkT = psum()
                      bkb = bk.bitcast(BF16)
                      cx.run("pe", [lambda e, bkb=bkb, cs=cs: e.transpose(out=bkb[:, 0:128], in_=kT[:, cs], identity=identB)], reads=[qk_T, G], writes=[bkT])
                      for d in range(2):
                          r = d * 4 + h
                          col = cols[:, 2, c * 36 + prow(r):c * 36 + prow(r) + 1]
                          cx.run("act", [lambda e, bkb=bkb, d=d, c=c, col=col: e.activation(out=kw[d][:, c, :], in_=bkb[:, 0:128], func=AF.Copy, scale=col)],
                                 reads=[bkT, COL_T], writes=[kw_T[d][c]])
                  orders = [list(range(12)), list(range(11, -1, -1))]
                  for g in range(6):
                      d = g // 3
                      r = d * 4 + h
                      b1, b1T = psum()
                      ths = []
                      css = []
                      for i in range(4):
                          c = orders[d][(g % 3) * 4 + i]
                          cs = slice(c * 128, (c + 1) * 128)
                          css.append(cs)
                          o = b1[:, i * 128:(i + 1) * 128]
                          ths += [mm(o, sel[:, r, :], rCM[0:36, cs], True, False), mm(o, rA[0:36, cs], sel[:, r, :], False, False),
                                  mm(o, identF, MASK[d], False, True)]
                      cx.run("pe", ths, reads=[R_T, G], writes=[b1T])
                      gp = g % 2
                      cx.run("act", [lambda e, b1=b1, gp=gp: e.activation(out=dmg[gp], in_=b1, func=AF.Exp, bias=lnks[:, 0:1])], reads=[b1T, G], writes=[dmg_T[gp]])
                      b2, b2T = psum()
                      cx.run("pe", [mm(b2[:, i * 128:(i + 1) * 128], kT[:, css[i]], qT[:, css[i]], True, True) for i in range(4)], reads=[qk_T], writes=[b2T])
                      cx.run("dve", [lambda e, b2=b2, gp=gp, g=g: e.tensor_tensor(out=sT_all[:, g * 512:(g + 1) * 512], in0=b2, in1=dmg[gp], op=ALU.mult)],
                             reads=[b2T, dmg_T[gp]], writes=[sTa_T[g]])

                  def rec(d, h=h):
                      r = d * 4 + h
                      cur = 0
                      Cxd, CxTd = Cx[d], Cx_T[d]
                      for step, c in enumerate(orders[d]):
                          s = c // 2
                          start = (c % 2 == 0) if d == 0 else (c % 2 == 1)
                          if start:
                              if (d == 0 and s == 0) or (d == 1 and s == 3):
                                  cx.dma("sp", Cxd[cur][:, 0:129], Cinit_d[l, d, h], writes=[CxTd[cur]])
                              elif (d == 0 and s <= 3) or (d == 1 and s <= 2):
                                  cx.run("dve", [lambda e, cur=cur: e.tensor_scalar(out=Cxd[cur][:, 0:129], in0=Cxd[cur][:, 0:129], scalar1=chain[:, 0:1],
                                                                                     scalar2=None, op0=ALU.mult)], reads=[CxTd[cur], G], writes=[CxTd[cur]])
                              else:
                                  cx.run("dve", [lambda e, cur=cur: e.memset(Cxd[cur][:, 0:129], 0.0)], writes=[CxTd[cur]])
                          cx.run("act", [lambda e, cur=cur, step=step: e.activation(out=Cb_all[d][:, step, 0:129], in_=Cxd[cur][:, 0:129], func=AF.Copy)],
                                 reads=[CxTd[cur]], writes=[CbA_T[d][step]])
                          bU, bUT = psum()
                          cx.run("pe", [mm(bU[:, 0:129], kw[d][:, c, :], vext[:, c, 0:129], True, True)], reads=[kw_T[d][c], v_T], writes=[bUT])
                          yield
                          nxt = 1 - cur
                          ccol = cbc[:, r * 12 + c:r * 12 + c + 1]
                          cx.run("dve", [lambda e, cur=cur, nxt=nxt: e.scalar_tensor_tensor(
                              out=Cxd[nxt][:, 0:129], in0=Cxd[cur][:, 0:129], scalar=ccol, in1=bU[:, 0:129], op0=ALU.mult, op1=ALU.add)],
                              reads=[bUT, CxTd[cur], COL_T], writes=[CxTd[nxt]])
                          cur = nxt
                          end = (c % 2 == 1) if d == 0 else (c % 2 == 0)
                          if end:
                              cx.dma("sp", Cfin_d[l, s, d, h], Cxd[cur][:, 0:129], reads=[CxTd[cur]])
                          yield

                  for _ in itertools.zip_longest(rec(0), rec(1)):
                      pass

                  for d in range(2):
                      r = d * 4 + h
                      pr = prow(r)
                      for step, c in enumerate(orders[d]):
                          cs = slice(c * 128, (c + 1) * 128)
                          p = d * 12 + step
                          bA, bAT = psum()
                          cx.run("pe", [mm(bA[:, 0:129], sT_all[:, p * 128:(p + 1) * 128], vext[:, c, 0:129], True, True),
                                        mm(bA[:, 256:385], qT[:, cs], Cb_all[d][:, step, 0:129], True, True)],
                                 reads=[sTa_T[p // 4], v_T, qk_T, CbA_T[d][step]], writes=[bAT])
                          wcol = cols[:, 0, c * 36 + pr:c * 36 + pr + 1]
                          bp = step % 4
                          cx.run("act", [lambda e, bA=bA, bp=bp, wcol=wcol: e.activation(out=Bs2[bp][:, 0:129], in_=bA[:, 256:385], func=AF.Copy, scale=wcol)],
                                 reads=[bAT, COL_T], writes=[Bs2_T[bp]])
                          cx.run("dve", [lambda e, bA=bA, bp=bp, c=c: e.tensor_tensor(out=tot_all[:, c, 0:129], in0=bA[:, 0:129], in1=Bs2[bp][:, 0:129], op=ALU.add)],
                                 reads=[bAT, Bs2_T[bp]], writes=[tota_T])
                      den = tot_all[:, :, 128]
                      ecols = cols[:, 1, :].rearrange("p (c n) -> p c n", n=36)[:, :, pr]
                      cx.run("dve", [lambda e, den=den: e.tensor_scalar(out=dn12, in0=den, scalar1=-1.0, scalar2=None, op0=ALU.mult)], reads=[tota_T], writes=[dn_T])
                      cx.run("dve", [lambda e, den=den: e.tensor_tensor(out=dn12, in0=dn12, in1=den, op=ALU.max)], reads=[tota_T, dn_T], writes=[dn_T])
                      cx.run("dve", [lambda e, ecols=ecols: e.tensor_tensor(out=dn12, in0=dn12, in1=ecols, op=ALU.max)], reads=[dn_T, COL_T], writes=[dn_T])
                      cx.run("dve", [lambda e: e.reciprocal(out=dn12, in_=dn12)], reads=[dn_T], writes=[dn_T])
                      if d == 0:
                          cx.run("act", [lambda e, c=c: e.activation(out=hsum[:, c, :], in_=tot_all[:, c, 0:128], func=AF.Copy, scale=dn12[:, c:c + 1])
                                         for c in range(12)], reads=[tota_T, dn_T], writes=hs_T)
                      else:
                          cx.run("dve", [lambda e, c=c: e.scalar_tensor_tensor(out=hsum[:, c, :], in0=tot_all[:, c, 0:128], scalar=dn12[:, c:c + 1],
                                                                               in1=hsum[:, c, :], op0=ALU.mult, op1=ALU.add) for c in range(12)],
                                 reads=[tota_T, dn_T] + hs_T, writes=hs_T)
                  if l == 0 and h == 0:
                      dbg_out("hsum_l0h0", hsum, [128, 12, 128], hs_T)
                  head_norm_out(hsum, hs_T, mnw, h, osig, o_T, h_aT, haT_T, tot_all.rearrange("p c n -> p (c n)"), tota_T, sT_all[:, 0:1536], sTa_T[0:3])
                  if l == 0:
                      mod_groups(0, [4 + 2 * h, 5 + 2 * h])
                  phase_end("mlstm%d_h%d" % (l, h))
              dbg_out("haT_l%d" % l, h_aT, [128, 4, T], [haT_T])

              phase_end("mlstm%d" % l)
              aset(6144)
              acc = af32(4 * T).rearrange("p (a t) -> p a t", t=T)
              acc_T = TT("acc")
              upad = [abf(6 * 286).rearrange("p (s n) -> p s n", n=286) for _ in range(2)]
              up_T = [TT("upad0"), TT("upad1")]
              Dg = [abf(31 * 128).rearrange("p (k n) -> p k n", n=128) for _ in range(2)]
              Dg_T = [TT("dg0"), TT("dg1")]
              sg = [af32(512) for _ in range(2)]
              sg_T = [TT("sg0"), TT("sg1")]
              cx.run("dve", [lambda e: e.memset(upad[0], 0.0), lambda e: e.memset(upad[1], 0.0)], writes=up_T)
              wa, waT = wload(in_w_d[l][:, OFF_CA:OFF_CA + 512], 8, 512)
              wgc, wgcT = wload(in_w_d[l][:, OFF_CG:OFF_CG + 512], 8, 512)
              k = 0
              for ct in range(4):
                  up, upT, dg, dgT = upad[ct % 2], up_T[ct % 2], Dg[ct % 2], Dg_T[ct % 2]
                  cx.run("act", [lambda e, kk=kk, ct=ct, dg=dg: e.activation(out=dg[:, kk, :], in_=identF, func=AF.Copy, scale=convp[:, ct, kk:kk + 1])
                                 for kk in range(CONV_K)], reads=[G], writes=[dgT])
                  for nb in range(3):
                      sl = slice(nb * 512, (nb + 1) * 512)
                      ba, baT = psum()
                      cx.run("pe", [mm(ba, wa[:, kc, ct * 128:(ct + 1) * 128], hT[:, kc, sl], kc == 0, kc == 7) for kc in range(8)], reads=[waT, hT_T], writes=[baT])
                      bg, bgT = psum()
                      cx.run("pe", [mm(bg, wgc[:, kc, ct * 128:(ct + 1) * 128], hT[:, kc, sl], kc == 0, kc == 7) for kc in range(8)], reads=[wgcT, hT_T], writes=[bgT])
                      pp = k % 2
                      k += 1
                      cx.run("act", [lambda e, bg=bg, pp=pp: e.activation(out=sg[pp], in_=bg, func=AF.Sigmoid)], reads=[bgT], writes=[sg_T[pp]])
                      cx.run("dve", [lambda e, ba=ba, pp=pp, nb=nb, up=up, s2=s2: e.tensor_tensor(
                          out=up[:, 2 * nb + s2, 15:271], in0=ba[:, s2 * 256:(s2 + 1) * 256],
                          in1=sg[pp][:, s2 * 256:(s2 + 1) * 256], op=ALU.mult) for s2 in range(2)], reads=[baT, sg_T[pp]], writes=[upT])
                  ths = []
                  for s in (1, 2, 3):
                      ths.append(lambda e, s=s, up=up: e.tensor_scalar(out=up[:, s, 0:15], in0=up[:, s - 1, 256:271], scalar1=chain[:, 0:1], scalar2=None, op0=ALU.mult))
                  for s in (0, 1, 2):
                      ths.append(lambda e, s=s, up=up: e.tensor_scalar(out=up[:, s, 271:286], in0=up[:, s + 1, 15:30], scalar1=chain[:, 0:1], scalar2=None, op0=ALU.mult))
                  cx.run("dve", ths, reads=[upT, G], writes=[upT])
                  phase_end("convu%d_%d" % (l, ct))
                  for s in range(6):
                      bk, bkT = psum()
                      cx.run("pe", [mm(bk[:, 0:256], dg[:, kk, :], up[:, s, kk:kk + 256], kk == 0, kk == CONV_K - 1) for kk in range(CONV_K)],
                             reads=[dgT, upT], writes=[bkT])
                      cx.run("act", [lambda e, bk=bk, ct=ct, s=s: e.activation(out=acc[:, ct, s * 256:(s + 1) * 256], in_=bk[:, 0:256], func=AF.Identity,
                                                                               bias=convp[:, ct, 31:32])], reads=[bkT, G], writes=[acc_T])
              dbg_out("conv_l%d" % l, acc, [128, 4, T], [acc_T])
              cb = abf(4 * 256).rearrange("p (a n) -> p a n", n=256)
              cq = abf(4 * 256).rearrange("p (a n) -> p a n", n=256)
              cb_T = TT("cb"); cq_T = TT("cq")
              mean = af32(256); msq = af32(256); rstd = af32(256)
              st_T = TT("lnst")
              for nb in range(6):
                  sl = slice(nb * 256, (nb + 1) * 256)
                  cx.run("act", [lambda e, ct=ct, sl=sl: e.activation(out=cb[:, ct, :], in_=acc[:, ct, sl], func=AF.Copy) for ct in range(4)], reads=[acc_T], writes=[cb_T])
                  cx.run("act", [lambda e, ct=ct, sl=sl: e.activation(out=cq[:, ct, :], in_=acc[:, ct, sl], func=AF.Square) for ct in range(4)], reads=[acc_T], writes=[cq_T])
                  b1, b1T = psum()
                  cx.run("pe", [mm(b1[:, 0:256], onesB, cb[:, ct, :], ct == 0, ct == 3) for ct in range(4)], reads=[cb_T, G], writes=[b1T])
                  b2, b2T = psum()
                  cx.run("pe", [mm(b2[:, 0:256], onesB, cq[:, ct, :], ct == 0, ct == 3) for ct in range(4)], reads=[cq_T, G], writes=[b2T])
                  ln_stats(b1[:, 0:256], b1T, b2[:, 0:256], b2T, mean, msq, rstd, st_T, 512)
                  cx.run("dve", [lambda e, ct=ct, sl=sl: e.tensor_tensor(out=acc[:, ct, sl], in0=acc[:, ct, sl], in1=mean, op=ALU.subtract) for ct in range(4)],
                         reads=[acc_T, st_T], writes=[acc_T])
                  cx.run("dve", [lambda e, ct=ct, sl=sl: e.tensor_tensor(out=acc[:, ct, sl], in0=acc[:, ct, sl], in1=rstd, op=ALU.mult) for ct in range(4)],
                         reads=[acc_T, st_T], writes=[acc_T])
                  cx.run("act", [lambda e, ct=ct, sl=sl: e.activation(out=acc[:, ct, sl], in_=acc[:, ct, sl], func=AF.Identity,
                                                                      scale=convp[:, ct, 32:33], bias=convp[:, ct, 33:34]) for ct in range(4)], reads=[acc_T, G], writes=[acc_T])
                  cx.run("act", [lambda e, ct=ct, sl=sl: e.activation(out=ubT[:, ct, sl], in_=acc[:, ct, sl], func=AF.Silu) for ct in range(4)], reads=[acc_T], writes=[ub_T])
              dbg_out("ubT_l%d" % l, ubT, [128, 4, T], [ub_T])

              phase_end("conv%d" % l)
              aset(9216)
              ropeCS = af32(1024); ropeSN = af32(1024)
              RP_T = TT("rope")
              cx.dma("sp", ropeCS, ropeCS_d, writes=[RP_T])
              cx.dma("sp", ropeSN, ropeSN_d, writes=[RP_T])
              qT = abf(T); kT = abf(T)
              qk_T = TT("rqk")
              vv = abf(T).rearrange("p (c n) -> p c n", n=128)
              v_T = TT("rv")
              gsil = af32(T).rearrange("p (c n) -> p c n", n=128)
              o_T = TT("gsil")
              ysum = af32(T).rearrange("p (c n) -> p c n", n=128)
              hs_T = [TT("ys%d" % c) for c in range(12)]
              kz = [abf(T).rearrange("p (c n) -> p c n", n=128) for _ in range(2)]
              kz_T = [[TT("kz") for c in range(12)] for _ in range(2)]
              Sx = [[af32(128) for _ in range(2)] for _ in range(2)]
              Sx_T = [[TT("sx"), TT("sx")] for _ in range(2)]
              t1 = af32(512); t2 = af32(512)
              sT_all = abf(24 * 128)
              sTa_T = [TT("rsTa%d" % i) for i in range(6)]
              Sb_all = [abf(12 * 128).rearrange("p (s n) -> p s n", n=128) for _ in range(2)]
              SbA_T = [[TT("sba") for _ in range(12)] for _ in range(2)]
              Bs2 = [af32(128) for _ in range(3)]
              Bs2_T = [TT("rbs2_%d" % i) for i in range(3)]
              t_T = TT("rt")
              for h in range(H):
                  wv, wT = wload(in_w_d[l][:, OFF_RET + h * 512:OFF_RET + (h + 1) * 512], 8, 512)
                  wv2, wT2 = wload(in_w_d[l][:, OFF_RVG + h * 256:OFF_RVG + (h + 1) * 256], 8, 256)
                  for nb in range(3):
                      sl = slice(nb * 512, (nb + 1) * 512)
                      for which, dst in ((0, qT), (1, kT)):
                          bk, bkT = psum()
                          cx.run("pe", [mm(bk, wv[:, kc, which * 256:which * 256 + 128], hT[:, kc, sl], kc == 0, kc == 7) for kc in range(8)],
                                 reads=[wT, hT_T], writes=[bkT])
                          if nb < 2:
                              bs_, bsT = psum()
                              cx.run("pe", [mm(bs_, wv[:, kc, which * 256 + 128:which * 256 + 256], hT[:, kc, sl], kc == 0, kc == 7) for kc in range(8)],
                                     reads=[wT, hT_T], writes=[bsT])
                              cx.run("dve", [lambda e, bk=bk, sl=sl: e.tensor_tensor(out=t1, in0=bk, in1=ropeCS[:, sl], op=ALU.mult)], reads=[bkT, RP_T, t_T], writes=[t_T])
                              cx.run("dve", [lambda e, bs_=bs_, sl=sl: e.tensor_tensor(out=t2, in0=bs_, in1=ropeSN[:, sl], op=ALU.mult)], reads=[bsT, RP_T, t_T], writes=[t_T])
                              cx.run("dve", [lambda e, dst=dst, sl=sl: e.tensor_tensor(out=dst[:, sl], in0=t1, in1=t2, op=ALU.add)], reads=[t_T], writes=[qk_T, t_T])
                          else:
                              cx.run("act", [lambda e, bk=bk, dst=dst, sl=sl: e.activation(out=dst[:, sl], in_=bk, func=AF.Copy)], reads=[bkT], writes=[qk_T])
                  for c in range(12):
                      cs = slice(c * 128, (c + 1) * 128)
                      bk, bkT = psum()
                      cx.run("pe", [mm(bk[:, 0:256], hT[:, kc, cs], wv2[:, kc, :], kc == 0, kc == 7) for kc in range(8)], reads=[wT2, hT_T], writes=[bkT])
                      cx.run("act", [lambda e, bk=bk, c=c: e.activation(out=vv[:, c, :], in_=bk[:, 0:128], func=AF.Copy)], reads=[bkT], writes=[v_T])
                      cx.run("act", [lambda e, bk=bk, c=c: e.activation(out=gsil[:, c, :], in_=bk[:, 128:256], func=AF.Silu)], reads=[bkT], writes=[o_T])
                  for c in range(12):
                      cs = slice(c * 128, (c + 1) * 128)
                      bk, bkT = psum()
                      bkb = bk.bitcast(BF16)
                      cx.run("pe", [lambda e, bkb=bkb, cs=cs: e.transpose(out=bkb[:, 0:128], in_=kT[:, cs], identity=identB)], reads=[qk_T, G], writes=[bkT])
                      for d in range(2):
                          r = d * 4 + h
                          cx.run("act", [lambda e, bkb=bkb, d=d, c=c, r=r: e.activation(out=kz[d][:, c, :], in_=bkb[:, 0:128], func=AF.Copy, scale=rcol[:, r, 1:2])],
                                 reads=[bkT, G], writes=[kz_T[d][c]])
                  orders = [list(range(12)), list(range(11, -1, -1))]
                  for g in range(6):
                      d = g // 3
                      r = d * 4 + h
                      b2, b2T = psum()
                      css = [slice(orders[d][(g % 3) * 4 + i] * 128, (orders[d][(g % 3) * 4 + i] + 1) * 128) for i in range(4)]
                      cx.run("pe", [mm(b2[:, i * 128:(i + 1) * 128], kT[:, css[i]], qT[:, css[i]], True, True) for i in range(4)], reads=[qk_T], writes=[b2T])
                      cx.run("dve", [lambda e, b2=b2, g=g, i=i, r=r: e.tensor_tensor(out=sT_all[:, g * 512 + i * 128:g * 512 + (i + 1) * 128],
                                                                                   in0=b2[:, i * 128:(i + 1) * 128], in1=decT[:, r, :], op=ALU.mult) for i in range(4)],
                             reads=[b2T, G], writes=[sTa_T[g]])

                  def rrec(d, h=h):
                      r = d * 4 + h
                      cur = 0
                      Sxd, SxTd = Sx[d], Sx_T[d]
                      for step, c in enumerate(orders[d]):
                          s = c // 2
                          start = (c % 2 == 0) if d == 0 else (c % 2 == 1)
                          if start:
                              if (d == 0 and s == 0) or (d == 1 and s == 3):
                                  cx.dma("sp", Sxd[cur], Sinit_d[l, d, h], writes=[SxTd[cur]])
                              elif (d == 0 and s <= 3) or (d == 1 and s <= 2):
                                  cx.run("dve", [lambda e, cur=cur: e.tensor_scalar(out=Sxd[cur], in0=Sxd[cur], scalar1=chain[:, 0:1],
                                                                                     scalar2=None, op0=ALU.mult)], reads=[SxTd[cur], G], writes=[SxTd[cur]])
                              else:
                                  cx.run("dve", [lambda e, cur=cur: e.memset(Sxd[cur], 0.0)], writes=[SxTd[cur]])
                          cx.run("act", [lambda e, cur=cur, step=step: e.activation(out=Sb_all[d][:, step, :], in_=Sxd[cur], func=AF.Copy)],
                                 reads=[SxTd[cur]], writes=[SbA_T[d][step]])
                          bU, bUT = psum()
                          cx.run("pe", [mm(bU[:, 0:128], kz[d][:, c, :], vv[:, c, :], True, True)], reads=[kz_T[d][c], v_T], writes=[bUT])
                          yield
                          nxt = 1 - cur
                          cx.run("dve", [lambda e, cur=cur, nxt=nxt: e.scalar_tensor_tensor(
                              out=Sxd[nxt], in0=Sxd[cur], scalar=rcol[:, r, 2:3], in1=bU[:, 0:128], op0=ALU.mult, op1=ALU.add)],
                              reads=[bUT, SxTd[cur], G], writes=[SxTd[nxt]])
                          cur = nxt
                          end = (c % 2 == 1) if d == 0 else (c % 2 == 0)
                          if end:
                              cx.dma("sp", Sfin_d[l, s, d, h], Sxd[cur], reads=[SxTd[cur]])
                          yield

                  for _ in itertools.zip_longest(rrec(0), rrec(1)):
                      pass

                  for d in range(2):
                      r = d * 4 + h
                      for step, c in enumerate(orders[d]):
                          cs = slice(c * 128, (c + 1) * 128)
                          p = d * 12 + step
                          bA, bAT = psum()
                          cx.run("pe", [mm(bA[:, 0:128], sT_all[:, p * 128:(p + 1) * 128], vv[:, c, :], True, True),
                                        mm(bA[:, 256:384], qT[:, cs], Sb_all[d][:, step, :], True, True)],
                                 reads=[sTa_T[p // 4], v_T, qk_T, SbA_T[d][step]], writes=[bAT])
                          bp = step % 3
                          cx.run("act", [lambda e, bA=bA, bp=bp, r=r: e.activation(out=Bs2[bp], in_=bA[:, 256:384], func=AF.Copy, scale=rcol[:, r, 0:1])],
                                 reads=[bAT, G], writes=[Bs2_T[bp]])
                          if d == 0:
                              cx.run("dve", [lambda e, bA=bA, bp=bp, c=c: e.tensor_tensor(out=ysum[:, c, :], in0=bA[:, 0:128], in1=Bs2[bp], op=ALU.add)],
                                     reads=[bAT, Bs2_T[bp]], writes=[hs_T[c]])
                          else:
                              cx.run("dve", [lambda e, bA=bA, bp=bp: e.tensor_tensor(out=Bs2[bp], in0=bA[:, 0:128], in1=Bs2[bp], op=ALU.add)],
                                     reads=[bAT, Bs2_T[bp]], writes=[Bs2_T[bp]])
                              cx.run("dve", [lambda e, bp=bp, c=c: e.tensor_tensor(out=ysum[:, c, :], in0=ysum[:, c, :], in1=Bs2[bp], op=ALU.add)],
                                     reads=[Bs2_T[bp], hs_T[c]], writes=[hs_T[c]])
                  head_norm_out(ysum, hs_T, rnw, h, gsil, o_T, h_cT, hcT_T, t1, t_T, sT_all[:, 0:1536], sTa_T[0:3])
                  if l + 1 < L:
                      mod_groups(l + 1, [3 * h, 3 * h + 1, 3 * h + 2])
              dbg_out("hcT_l%d" % l, h_cT, [128, 4, T], [hcT_T])

              phase_end("ret%d" % l)
              aset(15360)
              macc = af32(4 * T).rearrange("p (a t) -> p a t", t=T)
              macc_T = [TT("macc%d" % i) for i in range(4)]
              sgm = [af32(512) for _ in range(2)]
              sgm_T = [TT("sgm0"), TT("sgm1")]
              k = 0
              branches = ((mow_d, h_aT, haT_T), (cow_d, ubT, ub_T), (row_d, h_cT, hcT_T))
              for jg in range(2):
                  for b, (wd, src, srcT) in enumerate(branches):
                      wo, woT = wload(wd[l][:, jg * 512:(jg + 1) * 512], 4, 512)
                      wgm, wgmT = wload(in_w_d[l][:, OFF_GM + b * 1024 + jg * 512:OFF_GM + b * 1024 + (jg + 1) * 512], 8, 512)
                      for jj in range(4):
                          j = jg * 4 + jj
                          for nb in range(3):
                              sl = slice(nb * 512, (nb + 1) * 512)
                              by, byT = psum()
                              cx.run("pe", [mm(by, wo[:, kc, jj * 128:(jj + 1) * 128], src[:, kc, sl], kc == 0, kc == 3) for kc in range(4)], reads=[woT, srcT], writes=[byT])
                              bg, bgT = psum()
                              cx.run("pe", [mm(bg, wgm[:, kc, jj * 128:(jj + 1) * 128], hT[:, kc, sl], kc == 0, kc == 7) for kc in range(8)], reads=[wgmT, hT_T], writes=[bgT])
                              pp = k % 2
                              k += 1
                              cx.run("act", [lambda e, bg=bg, pp=pp: e.activation(out=sgm[pp], in_=bg, func=AF.Sigmoid)], reads=[bgT], writes=[sgm_T[pp]])
                              if b == 0:
                                  cx.run("dve", [lambda e, by=by, pp=pp, sl=sl, jj=jj: e.tensor_tensor(out=macc[:, jj, sl], in0=by, in1=sgm[pp], op=ALU.mult)],
                                         reads=[byT, sgm_T[pp]], writes=[macc_T[jj]])
                              else:
                                  cx.run("dve", [lambda e, by=by, pp=pp: e.tensor_tensor(out=sgm[pp], in0=by, in1=sgm[pp], op=ALU.mult)],
                                         reads=[byT, sgm_T[pp]], writes=[sgm_T[pp]])
                                  if b == 1:
                                      cx.run("dve", [lambda e, pp=pp, sl=sl, jj=jj: e.tensor_tensor(out=macc[:, jj, sl], in0=macc[:, jj, sl], in1=sgm[pp], op=ALU.add)],
                                             reads=[sgm_T[pp], macc_T[jj]], writes=[macc_T[jj]])
                                  else:
                                      cx.run("dve", [lambda e, pp=pp, sl=sl, j=j, jj=jj: e.tensor_tensor(out=merged[:, j, sl], in0=macc[:, jj, sl], in1=sgm[pp], op=ALU.add)],
                                             reads=[sgm_T[pp], macc_T[jj]], writes=[mg_T])
              dbg_out("merged_l%d" % l, merged, [128, 8, T], [mg_T])
              phase_end("merge%d" % l)
              aset(15360)
              xb = abf(8 * 512).rearrange("p (a n) -> p a n", n=512)
              xq = abf(8 * 512).rearrange("p (a n) -> p a n", n=512)
              lnt = (af32(512), af32(512), af32(512))
              cx.run("act", [lambda e, kc=kc: e.activation(out=x[:, kc, :], in_=x[:, kc, :], func=AF.Copy, scale=ALPHA) for kc in range(8)], reads=[x_T], writes=[x_T])
              for jg in range(2):
                  wo, woT = wload(outw_d[l][:, jg * 512:(jg + 1) * 512], 8, 512)
                  for jj in range(4):
                      j = jg * 4 + jj
                      for nb in range(3):
                          sl = slice(nb * 512, (nb + 1) * 512)
                          v = 1 if nb < 2 else 0
                          bk, bkT = psum()
                          cx.run("pe", [mm(bk, wo[:, kc, jj * 128:(jj + 1) * 128], merged[:, kc, sl], kc == 0, kc == 7) for kc in range(8)], reads=[woT, mg_T], writes=[bkT])
                          cx.run("dve", [lambda e, bk=bk, j=j, sl=sl, v=v: e.scalar_tensor_tensor(out=x[:, j, sl], in0=bk, scalar=modT[:, 16 + j, v:v + 1],
                                                                                                 in1=x[:, j, sl], op0=ALU.mult, op1=ALU.add)],
                                 reads=[bkT, x_T, M_T[l]], writes=[x_T])
              layer_norm_fm(0, [(0, 512), (512, 1024), (1024, 1536)], xb, xq, lnt)
              dbg_out("x1_l%d" % l, x, [128, 8, T], [x_T])
              phase_end("outp%d" % l)
              aset(0)
              ffT = abf(NFT * T).rearrange("p (a n) -> p a n", n=T)
              ff_T = TT("ffT")
              a_sb = abf(4 * T).rearrange("p (a n) -> p a n", n=T)
              asb_T = TT("a_sb")
              sgf = [af32(512) for _ in range(2)]
              sgf_T = [TT("sgf0"), TT("sgf1")]
              modulate(sc2p, 24)
              cx.run("act", [lambda e, kc=kc: e.activation(out=x[:, kc, :], in_=x[:, kc, :], func=AF.Copy, scale=ALPHA) for kc in range(8)], reads=[x_T], writes=[x_T])
              k = 0
              for g in range(6):
                  nt = 4 if g < 5 else 2
                  wa, waT = wload(w13_d[l][:, g * 512:g * 512 + nt * 128], 8, nt * 128)
                  wg2, wg2T = wload(w13_d[l][:, FF + g * 512:FF + g * 512 + nt * 128], 8, nt * 128)
                  for jj in range(nt):
                      for nb in range(3):
                          sl = slice(nb * 512, (nb + 1) * 512)
                          bk, bkT = psum()
                          cx.run("pe", [mm(bk, wa[:, kc, jj * 128:(jj + 1) * 128], hT[:, kc, sl], kc == 0, kc == 7) for kc in range(8)],
                                 reads=[waT, hT_T], writes=[bkT])
                          cx.run("act", [lambda e, bk=bk, jj=jj, sl=sl: e.activation(out=a_sb[:, jj, sl], in_=bk, func=AF.Copy)],
                                 reads=[bkT], writes=[asb_T])
                  for jj in range(nt):
                      for nb in range(3):
                          sl = slice(nb * 512, (nb + 1) * 512)
                          bk, bkT = psum()
                          cx.run("pe", [mm(bk, wg2[:, kc, jj * 128:(jj + 1) * 128], hT[:, kc, sl], kc == 0, kc == 7) for kc in range(8)],
                                 reads=[wg2T, hT_T], writes=[bkT])
                          pp = k % 2
                          k += 1
                          cx.run("act", [lambda e, bk=bk, pp=pp: e.activation(out=sgf[pp], in_=bk, func=AF.Silu)], reads=[bkT], writes=[sgf_T[pp]])
                          cx.run("dve", [lambda e, pp=pp, jj=jj, g=g, sl=sl: e.tensor_tensor(out=ffT[:, g * 4 + jj, sl], in0=sgf[pp], in1=a_sb[:, jj, sl], op=ALU.mult)],
                                 reads=[sgf_T[pp], asb_T], writes=[ff_T])
              for j in range(8):
                  w2v, w2T = wload(w2_d[l][:, j * 128:(j + 1) * 128], NFT, 128)
                  for nb in range(3):
                      sl = slice(nb * 512, (nb + 1) * 512)
                      v = 1 if nb < 2 else 0
                      bk, bkT = psum()
                      cx.run("pe", [mm(bk, w2v[:, kc, :], ffT[:, kc, sl], kc == 0, kc == NFT - 1) for kc in range(NFT)], reads=[w2T, ff_T], writes=[bkT])
                      cx.run("dve", [lambda e, bk=bk, j=j, sl=sl, v=v: e.scalar_tensor_tensor(
                          out=x[:, j, sl], in0=bk, scalar=modT[:, 40 + j, v:v + 1], in1=x[:, j, sl], op0=ALU.mult, op1=ALU.add)],
                          reads=[bkT, x_T, M_T[l]], writes=[x_T])
              aset(0)
              xb = abf(8 * 512).rearrange("p (a n) -> p a n", n=512)
              xq = abf(8 * 512).rearrange("p (a n) -> p a n", n=512)
              lnt = (af32(512), af32(512), af32(512))
              layer_norm_fm(1, [(0, 512), (512, 1024), (1024, 1536)], xb, xq, lnt)
              dbg_out("x2_l%d" % l, x, [128, 8, T], [x_T])

        try:
            layers()
        except _Stop:
            pass
        for kc in range(8):
            cx.dma("sp", yT_d[:, kc, :], x[:, kc, :], reads=[x_T])
        cx.final()

        with nc.Block() as block:
            def replay(name):
                def f(e):
                    for th in cx.prog[name]:
                        th(e)
                return f
            block.tensor(replay("pe"))
            block.scalar(replay("act"))
            block.vector(replay("dve"))
            block.gpsimd(replay("pool"))
            block.sync(replay("sp"))
    return nc


_IDX = np.concatenate([np.arange(0, 128, 2), np.arange(1, 128, 2)])
_IDXS = np.concatenate([np.arange(1, 128, 2), np.arange(0, 128, 2)])


def _prow(r):
    return (r // 4) * 32 + (r % 4)


def _in_cols():
    o = dict(mq=0, mk=512, mv=1024, mo=1536, mg=2048, ca=2064, cg=2576, rq=3088, rk=3600, rv=4112, rg=4624, gm=5136)
    cols = []
    for h in range(H):
        for nm in ("mq", "mk", "mv", "mo"):
            cols += list(range(o[nm] + h * 128, o[nm] + (h + 1) * 128))
    z = [-1] * 28
    mg = o["mg"]
    cols += [mg + 0 * 4 + h for h in range(4)] + z + [mg + 2 * 4 + h for h in range(4)]
    cols += [mg + 1 * 4 + h for h in range(4)] + z + [mg + 3 * 4 + h for h in range(4)]
    cols += list(range(o["ca"], o["ca"] + 512)) + list(range(o["cg"], o["cg"] + 512))
    for h in range(H):
        for nm in ("rq", "rk"):
            cols += list(o[nm] + h * 128 + _IDX) + list(o[nm] + h * 128 + _IDXS)
    for h in range(H):
        cols += list(range(o["rv"] + h * 128, o["rv"] + (h + 1) * 128)) + list(range(o["rg"] + h * 128, o["rg"] + (h + 1) * 128))
    cols += list(range(o["gm"], o["gm"] + 3072))
    cols = np.array(cols, dtype=np.int64)
    assert cols.shape[0] == NCOLS, cols.shape
    return cols


_NC_CACHE = {}


def kernel(x_prompt, x_sample, state_mlstm_C, state_mlstm_n, state_mlstm_m, state_ret_S, c, c_ctx,
           ada_w, ada_b, in_w, mlstm_gate_b, mlstm_norm_w, mlstm_out_w, conv_w, conv_b, conv_ln_w, conv_ln_b,
           conv_out_w, ret_decay, ret_norm_w, ret_out_w, out_w, ln1_w, ln1_b, ln2_w, ln2_b, ffn_w13, ffn_w2, _dbg=False):
    f32 = np.float32
    A = lambda a: np.ascontiguousarray(np.asarray(a, dtype=f32))
    x_prompt, x_sample = A(x_prompt), A(x_sample)
    sC, sn, smm, sS = A(state_mlstm_C), A(state_mlstm_n), A(state_mlstm_m), A(state_ret_S)
    c, c_ctx = A(c), A(c_ctx)
    in_w = A(in_w)
    cols = _in_cols()
    in_wp = np.zeros((L, D, NCOLS), f32)
    valid = cols >= 0
    in_wp[:, :, valid] = in_w[:, :, cols[valid]]
    gbias = A(mlstm_gate_b)
    gb = np.zeros((L, 36, 2), f32)
    for h in range(4):
        gb[:, h, 0] = gbias[:, 0, h]; gb[:, h, 1] = gbias[:, 1, h]
        gb[:, 32 + h, 0] = gbias[:, 2, h]; gb[:, 32 + h, 1] = gbias[:, 3, h]
    convp = np.zeros((L, 128, 4, 34), f32)
    cw = A(conv_w)
    convp[:, :, :, 0:31] = cw.reshape(L, CONV_K, 4, 128).transpose(0, 3, 2, 1)
    convp[:, :, :, 31] = A(conv_b).reshape(L, 4, 128).transpose(0, 2, 1)
    convp[:, :, :, 32] = A(conv_ln_w).reshape(L, 4, 128).transpose(0, 2, 1)
    convp[:, :, :, 33] = A(conv_ln_b).reshape(L, 4, 128).transpose(0, 2, 1)
    lnp = np.zeros((L, 128, 4, 8), f32)
    for i, a in enumerate((ln1_w, ln1_b, ln2_w, ln2_b)):
        lnp[:, :, i, :] = A(a).reshape(L, 8, 128).transpose(0, 2, 1)
    ada_bT = np.ascontiguousarray(A(ada_b).reshape(L, 48, 128).transpose(0, 2, 1))
    rdec = A(ret_decay).reshape(L, 1, 8)
    mnw = A(mlstm_norm_w).reshape(L, 1, 512)
    rnw = A(ret_norm_w).reshape(L, 1, 512)
    cst = np.zeros((128, 1024), f32)
    jj, ii = np.meshgrid(np.arange(128), np.arange(128), indexing="ij")
    cst[:, 0:128] = np.eye(128, dtype=f32)
    cst[:, 128:256] = np.where(jj <= ii, 0.0, NEG)
    cst[:, 256:384] = np.where(jj >= ii, 0.0, NEG)
    cst[:, 384:512] = np.maximum(ii - jj, 0)
    cst[:, 512:640] = np.maximum(jj - ii, 0)
    cst[:, 640:768] = (jj <= ii)
    cst[:, 768:896] = (jj >= ii)
    p = np.arange(128)
    cst[:, 896] = p + 1; cst[:, 897] = 128 - p; cst[:, 898] = 127 - p; cst[:, 899] = p; cst[:, 900] = 128
    sel = np.zeros((36, 8, 128), f32)
    for r in range(8):
        sel[_prow(r), r, :] = 1.0
    t = np.arange(1024)
    rows = (t // 64).astype(f32); colsg = (t % 64).astype(f32)
    freqs = (np.float32(10000.0) ** (-np.arange(32, dtype=f32) / np.float32(32))).astype(f32)
    ang = np.concatenate([rows[:, None] * freqs[None, :], colsg[:, None] * freqs[None, :]], -1).astype(f32)
    cs_lat = np.concatenate([np.cos(ang).T, np.cos(ang).T], 0).astype(f32)
    sn_lat = np.concatenate([-np.sin(ang).T, np.sin(ang).T], 0).astype(f32)
    cs_id = np.ones((128, 1024), f32); sn_id = np.zeros((128, 1024), f32)

    shared = dict(cst=cst, sel=sel, ada_w=A(ada_w), ada_bT=ada_bT, in_wp=in_wp, gb=gb, mnw=mnw, rnw=rnw,
                  mlstm_out_w=A(mlstm_out_w), conv_out_w=A(conv_out_w), ret_out_w=A(ret_out_w), convp=convp, rdec=rdec,
                  out_w=A(out_w), lnp=lnp, ffn_w13=A(ffn_w13), ffn_w2=A(ffn_w2))
    in_maps = []
    seg_prompt = []
    for core in range(8):
        if core < 4:
            xs = np.concatenate([x_sample[core], x_prompt[2 * core], x_prompt[2 * core + 1]], 0)
            seg_prompt.append({4: 2 * core, 5: 2 * core + 1})
            cvec = c[core]
            Cinit = np.concatenate([sC[core], sn[core][..., None]], -1)
            minit = np.zeros((L, 36, 1), f32)
            for h in range(4):
                minit[:, h, 0] = smm[core, :, 0, h]; minit[:, 32 + h, 0] = smm[core, :, 1, h]
            Sinit = sS[core][:, :, :, _IDX, :]
            chainv = 1.0
            rcs, rsn = cs_lat, sn_lat
        else:
            base = 8 + 6 * (core - 4)
            xs = np.concatenate([x_prompt[base + s] for s in range(6)], 0)
            seg_prompt.append({s: base + s for s in range(6)})
            cvec = c_ctx
            Cinit = np.zeros((L, 2, H, 128, 129), f32)
            minit = np.zeros((L, 36, 1), f32)
            Sinit = np.zeros((L, 2, H, 128, 128), f32)
            chainv = 0.0
            rcs, rsn = cs_id, sn_id
        xT = np.ascontiguousarray(xs.reshape(T, 8, 128).transpose(2, 1, 0))
        cv = np.stack([c_ctx.reshape(8, 128).T, cvec.reshape(8, 128).T], -1)
        m = dict(shared)
        m.update(xT=xT, cv=np.ascontiguousarray(cv, dtype=f32), chain=np.full((128, 1), chainv, f32),
                 Cinit=np.ascontiguousarray(Cinit, dtype=f32), minit=minit, Sinit=np.ascontiguousarray(Sinit, dtype=f32),
                 ropeCS=rcs, ropeSN=rsn)
        in_maps.append(m)

    key = bool(_dbg)
    nc = build(dbg=key)
    res = run_bass_kernel_spmd(nc, in_maps, core_ids=list(range(8)))
    R = res.results
    y_prompt = np.zeros((32, 256, D), f32)
    y_sample = np.zeros((4, 1024, D), f32)
    new_C = np.zeros((32, L, 2, H, 128, 128), f32)
    new_n = np.zeros((32, L, 2, H, 128), f32)
    new_m = np.zeros((32, L, 2, H), f32)
    new_S = np.zeros((32, L, 2, H, 128, 128), f32)
    for core in range(8):
        r = R[core]
        yt = np.asarray(r["yT"]).transpose(2, 1, 0).reshape(T, D)
        if core < 4:
            y_sample[core] = yt[0:1024]
        Cf = np.asarray(r["Cfin"]); mf = np.asarray(r["mfin"]); Sf = np.asarray(r["Sfin"])
        for s, b in seg_prompt[core].items():
            y_prompt[b] = yt[s * 256:(s + 1) * 256]
            new_C[b] = Cf[:, s, :, :, :, 0:128]
            new_n[b] = Cf[:, s, :, :, :, 128]
            for h in range(4):
                new_m[b, :, 0, h] = mf[:, h, 2 * s + 1]
                new_m[b, :, 1, h] = mf[:, 32 + h, 2 * s]
            Su = np.empty((L, 2, H, 128, 128), f32)
            Su[:, :, :, _IDX, :] = Sf[:, s]
            new_S[b] = Su
    if _dbg:
        return (y_prompt, y_sample, new_C, new_n, new_m, new_S), R
    return (y_prompt, y_sample, new_C, new_n, new_m, new_S)
```

```python
import math
import itertools
import numpy as np
import concourse.bass as bass
import concourse.mybir as mybir
from concourse.bass_utils import run_bass_kernel_spmd
from concourse.ap import AP

F32 = mybir.dt.float32
BF16 = mybir.dt.bfloat16
AF = mybir.ActivationFunctionType
ALU = mybir.AluOpType
AX = mybir.AxisListType

D = 1024
L = 2
T = 1536
NCH = 12
NSEG = 6
H = 4
HD = 128
FF = 2816
NFT = 22
CONV_K = 31
EPS = 1e-5
ALPHA = (2.0 * L) ** 0.25
KS = HD ** -0.5
LNKS = math.log(KS)
NEG = -30000.0
NCOLS = 9288
OFF_GATE = 2048
OFF_CA = 2120
OFF_CG = 2632
OFF_RET = 3144
OFF_RVG = 5192
OFF_GM = 6216

DEBUG = {}
STOP = None


class _Stop(Exception):
    pass


def phase_end(name):
    if STOP == name:
        raise _Stop()


def rev(ap):
    a = [list(x) for x in ap.ap]
    step, n = a[-1]
    off = ap.offset + step * (n - 1)
    a[-1] = [-step, n]
    return AP(ap.tensor, off, a)


import types


def _snap(th):
    cl = th.__closure__
    if not cl:
        return th
    cells = []
    for c in cl:
        try:
            cells.append(types.CellType(c.cell_contents))
        except ValueError:
            cells.append(c)
    return types.FunctionType(th.__code__, th.__globals__, th.__name__, th.__defaults__, tuple(cells))


class TT:
    __slots__ = ("name", "w", "r")

    def __init__(self, name=""):
        self.name = name
        self.w = None
        self.r = {}


class Ctx:
    ENG = ["pe", "act", "dve", "pool", "sp"]

    def __init__(self, nc, sems):
        self.nc = nc
        self.sems = sems
        self.prog = {e: [] for e in self.ENG}
        self.cnt = {e: 0 for e in self.ENG}
        self.seen = {e: {} for e in self.ENG}
        self.dkeys = [k for k in sems if k[0] == "d" and k[1:].isdigit()]
        self.wkeys = [k for k in sems if k[0] == "w" and k[1:].isdigit()]
        self.dtot = {k: 0 for k in sems}
        self.dn = 0
        self.wn = 0

    def _wait(self, eng, key, val):
        if val <= 0 or self.seen[eng].get(key, 0) >= val:
            return
        self.seen[eng][key] = val
        sem = self.sems[key]
        self.prog[eng].append(lambda e, sem=sem, val=val: e.wait_ge(sem, val))

    def _deps(self, eng, reads, writes):
        for t in reads:
            if t.w is not None:
                self._wait_dep(eng, t.w)
        for t in writes:
            if t.w is not None:
                self._wait_dep(eng, t.w)
            for k, v in t.r.items():
                self._wait_dep(eng, (k, v))

    def _wait_dep(self, eng, dep):
        k, v = dep
        if k == "pe" and eng == "pe":
            return
        self._wait(eng, k, v)

    def run(self, eng, thunks, reads=(), writes=()):
        if not isinstance(thunks, (list, tuple)):
            thunks = [thunks]
        thunks = [_snap(t) for t in thunks]
        self._deps(eng, reads, writes)
        sem = self.sems[eng]
        n = len(thunks)
        for i, th in enumerate(thunks):
            if i == n - 1:
                self.prog[eng].append(lambda e, th=th, sem=sem: th(e).then_inc(sem, 1))
            else:
                self.prog[eng].append(th)
        self.cnt[eng] += 1
        c = self.cnt[eng]
        for t in reads:
            t.r[eng] = c
        for t in writes:
            t.w = (eng, c)
            t.r = {}

    def dma(self, q, out, in_, reads=(), writes=()):
        if q == "pool":
            key = self.wkeys[self.wn % len(self.wkeys)]
            self.wn += 1
        else:
            key = self.dkeys[self.dn % len(self.dkeys)]
            self.dn += 1
        prev = self.dtot[key]
        self._wait(q, key, prev)
        self._deps(q, reads, writes)
        new = prev + 16
        self.dtot[key] = new
        sem = self.sems[key]
        self.prog[q].append(lambda e, out=out, in_=in_, sem=sem: e.dma_start(out=out, in_=in_).then_inc(sem, 16))
        for t in reads:
            t.r[key] = new
        for t in writes:
            t.w = (key, new)
            t.r = {}

    def barrier(self):
        engs = ["pe", "act", "dve", "sp"]
        for e in engs:
            for f in ["pe", "act", "dve"]:
                if f != e or e != "pe":
                    self._wait(e, f, self.cnt[f])
            for k in self.dkeys:
                self._wait(e, k, self.dtot[k])

    def final(self):
        for k in self.dkeys:
            self._wait("sp", k, self.dtot[k])
        for f in ["pe", "act", "dve"]:
            self._wait("sp", f, self.cnt[f])


def build(dbg=False):
    nc = bass.Bass("TRN2", target_bir_lowering=False)
    dram = {}

    def din(name, shape, dt=F32):
        dram[name] = nc.dram_tensor(name, list(shape), dt, kind="ExternalInput").ap()
        return dram[name]

    def dout(name, shape, dt=F32):
        dram[name] = nc.dram_tensor(name, list(shape), dt, kind="ExternalOutput").ap()
        return dram[name]

    xT_d = din("xT", [128, 8, T])
    cv_d = din("cv", [128, 8, 2])
    chain_d = din("chain", [128, 1])
    Cinit_d = din("Cinit", [L, 2, H, 128, 129])
    minit_d = din("minit", [L, 36, 1])
    Sinit_d = din("Sinit", [L, 2, H, 128, 128])
    ropeCS_d = din("ropeCS", [128, 1024])
    ropeSN_d = din("ropeSN", [128, 1024])
    cst_d = din("cst", [128, 1024])
    sel_d = din("sel", [36, 8, 128])
    ada_w_d = din("ada_w", [L, D, 6 * D])
    ada_b_d = din("ada_bT", [L, 128, 48])
    in_w_d = din("in_wp", [L, D, NCOLS])
    gb_d = din("gb", [L, 36, 2])
    mnw_d = din("mnw", [L, 1, 512])
    rnw_d = din("rnw", [L, 1, 512])
    mow_d = din("mlstm_out_w", [L, 512, D])
    cow_d = din("conv_out_w", [L, 512, D])
    row_d = din("ret_out_w", [L, 512, D])
    convp_d = din("convp", [L, 128, 4, 34])
    rdec_d = din("rdec", [L, 1, 8])
    outw_d = din("out_w", [L, D, D])
    lnp_d = din("lnp", [L, 128, 4, 8])
    w13_d = din("ffn_w13", [L, D, 2 * FF])
    w2_d = din("ffn_w2", [L, FF, D])

    yT_d = dout("yT", [128, 8, T])
    Cfin_d = dout("Cfin", [L, NSEG, 2, H, 128, 129])
    mfin_d = dout("mfin", [L, 36, 12])
    Sfin_d = dout("Sfin", [L, NSEG, 2, H, 128, 128])
    dbg_d = {}

    import contextlib
    es = contextlib.ExitStack()
    with es:
        def sb(name, shape, dt=F32):
            return es.enter_context(nc.sbuf_tensor("s_" + name, list(shape), dt))[:]

        x = sb("x", [128, 8, T])
        hT = sb("hT", [128, 8, T], BF16)
        NW = 3
        Wr = [sb("wr%d" % i, [128, 4096], BF16) for i in range(NW)]
        Wr_T = [TT("wr%d" % i) for i in range(NW)]
        cst = sb("cst", [128, 1024])
        sel = sb("sel", [36, 8, 128])
        identB = sb("identB", [128, 128], BF16)
        onesB = sb("onesB", [128, 128], BF16)
        chain = sb("chain", [128, 1])
        cvs = sb("cvs", [128, 8, 2], BF16)
        cvf = sb("cvf", [128, 8, 2])
        modTs = [sb("modT%d" % i, [128, 48, 2]) for i in range(L)]
        sc1ps = [sb("sc1p%d" % i, [128, 8, 2]) for i in range(L)]
        sc2ps = [sb("sc2p%d" % i, [128, 8, 2]) for i in range(L)]
        adabs = [sb("adab%d" % i, [128, 48]) for i in range(L)]
        M_T = [TT("mod%d" % i) for i in range(L)]
        gb = sb("gb", [36, 2])
        ngb = sb("ngb", [36, 1])
        minit = sb("minit", [36, 1])
        mnw = sb("mnw", [128, 512])
        rnw = sb("rnw", [128, 512])
        convp = sb("convp", [128, 4, 34])
        lnp = sb("lnp", [128, 4, 8])
        rdec = sb("rdec", [128, 8])
        lg = sb("lg", [128, 8])
        rcol = sb("rcol", [128, 8, 4])
        decT = sb("decT", [128, 8, 128])
        ones36 = sb("ones36", [36, 128])
        zeros36 = sb("zeros36", [36, 128])
        AR = 23168
        arena = sb("arena", [128, AR])
        ps = [es.enter_context(nc.psum_tensor("ps%d" % i, [128, 512], F32))[:] for i in range(8)]
        ps_T = [TT("ps%d" % i) for i in range(8)]

        keys = ["pe", "act", "dve", "pool", "sp"] + ["d%d" % i for i in range(8)] + ["w%d" % i for i in range(6)]
        sems = {k: es.enter_context(nc.semaphore(k)) for k in keys}
        cx = Ctx(nc, sems)
        G = TT("globals")
        x_T = TT("x")
        hT_T = TT("hT")

        identF = cst[:, 0:128]
        MASK = [cst[:, 128:256], cst[:, 256:384]]
        DIFF = [cst[:, 384:512], cst[:, 512:640]]
        M01 = [cst[:, 640:768], cst[:, 768:896]]
        POS = cst[:, 896:904]

        pstate = {"i": 0}

        def psum():
            i = pstate["i"] % 8
            pstate["i"] += 1
            return ps[i], ps_T[i]

        wstate = {"i": 0}

        def wload(src, kc, ncol):
            i = wstate["i"] % NW
            wstate["i"] += 1
            dst = Wr[i][:, 0:kc * ncol].rearrange("p (k n) -> p k n", n=ncol)
            cx.dma("pool", dst, src.rearrange("(k p) n -> p k n", p=128), writes=[Wr_T[i]])
            return dst, Wr_T[i]

        ast = {"o": 0}

        def aset(o):
            cx.barrier()
            ast["o"] = o

        def af32(n, shape=None):
            o = ast["o"]
            ast["o"] += n
            assert ast["o"] <= AR, ast["o"]
            v = arena[:, o:o + n]
            return v

        def abf(n):
            n2 = (n + 1) // 2
            return af32(n2).bitcast(BF16)

        def dbg_out(name, ap, shape, reads):
            if not dbg:
                return
            d = dout("dbg_" + name, shape, ap.dtype)
            DEBUG[name] = shape
            cx.dma("sp", d, ap, reads=reads)

        mm = lambda out, lhsT, rhs, st, sp: (lambda e: e.matmul(out, lhsT=lhsT, rhs=rhs, start=st, stop=sp))

        for kc in range(8):
            cx.dma("sp", x[:, kc, :], xT_d[:, kc, :], writes=[x_T])
        for dst, src in ((cst, cst_d), (sel, sel_d), (chain, chain_d), (cvf, cv_d)):
            cx.dma("sp", dst, src, writes=[G])
        cx.run("dve", [lambda e: e.memset(ones36, 1.0), lambda e: e.memset(zeros36, 0.0),
                       lambda e: e.memset(onesB, 1.0),
                       lambda e: e.tensor_copy(out=identB, in_=identF)],
               reads=[G], writes=[G])
        cx.run("act", [lambda e: e.activation(out=cvs, in_=cvf, func=AF.Silu)], reads=[G], writes=[G])
        for i in range(L):
            cx.dma("sp", adabs[i], ada_b_d[i], writes=[M_T[i]])

        def mod_groups(ll, gs):
            mT = modTs[ll]
            for g in gs:
                wv, wT = wload(ada_w_d[ll][:, g * 512:(g + 1) * 512], 8, 512)
                mb, mbT = psum()
                for jj in range(4):
                    cx.run("pe", [mm(mb[:, 2 * jj:2 * jj + 2], wv[:, kc, jj * 128:(jj + 1) * 128], cvs[:, kc, :], kc == 0, kc == 7) for kc in range(8)],
                           reads=[wT, G], writes=[mbT])
                j0 = g * 4
                cx.run("act", [lambda e, mb=mb, j0=j0: e.activation(out=mT[:, j0:j0 + 4, :].rearrange("p j v -> p (j v)"), in_=mb[:, 0:8], func=AF.Copy)],
                       reads=[mbT], writes=[M_T[ll]])
                cx.run("dve", [lambda e, j0=j0: e.tensor_tensor(out=mT[:, j0:j0 + 4, :], in0=mT[:, j0:j0 + 4, :],
                                                                in1=adabs[ll][:, j0:j0 + 4].unsqueeze(2).to_broadcast([128, 4, 2]), op=ALU.add)],
                       reads=[M_T[ll]], writes=[M_T[ll]])
                if g in (2, 3):
                    o = (g - 2) * 4
                    cx.run("dve", [lambda e, j0=j0, o=o: e.tensor_scalar(out=sc1ps[ll][:, o:o + 4, :], in0=mT[:, j0:j0 + 4, :], scalar1=1.0, scalar2=None, op0=ALU.add)],
                           reads=[M_T[ll]], writes=[M_T[ll]])
                if g in (8, 9):
                    o = (g - 8) * 4
                    cx.run("dve", [lambda e, j0=j0, o=o: e.tensor_scalar(out=sc2ps[ll][:, o:o + 4, :], in0=mT[:, j0:j0 + 4, :], scalar1=1.0, scalar2=None, op0=ALU.add)],
                           reads=[M_T[ll]], writes=[M_T[ll]])

        def layer_norm_fm(which, blocks, xb, xq, lnt):
            lw = lnp[:, 2 * which, :]
            lb = lnp[:, 2 * which + 1, :]
            xb_T, xq_T, lnt_T = TT("xb"), TT("xq"), TT("lnt")
            for (a_, b_) in blocks:
                sl = slice(a_, b_)
                cx.run("act", [lambda e, kc=kc: e.activation(out=xb[:, kc, :], in_=x[:, kc, sl], func=AF.Copy) for kc in range(8)],
                       reads=[x_T], writes=[xb_T])
                cx.run("act", [lambda e, kc=kc: e.activation(out=xq[:, kc, :], in_=x[:, kc, sl], func=AF.Square) for kc in range(8)],
                       reads=[x_T], writes=[xq_T])
                b1, b1T = psum()
                cx.run("pe", [mm(b1, onesB, xb[:, kc, :], kc == 0, kc == 7) for kc in range(8)], reads=[xb_T], writes=[b1T])
                b2, b2T = psum()
                cx.run("pe", [mm(b2, onesB, xq[:, kc, :], kc == 0, kc == 7) for kc in range(8)], reads=[xq_T], writes=[b2T])
                mean, msq, rstd = lnt
                cx.run("act", [lambda e: e.activation(out=mean, in_=b1, func=AF.Copy, scale=1.0 / D)], reads=[b1T], writes=[lnt_T])
                cx.run("dve", [lambda e: e.tensor_tensor(out=msq, in0=mean, in1=mean, op=ALU.mult)], reads=[lnt_T], writes=[lnt_T])
                cx.run("dve", [lambda e: e.scalar_tensor_tensor(out=msq, in0=b2, scalar=1.0 / D, in1=msq, op0=ALU.mult, op1=ALU.subtract)],
                       reads=[b2T, lnt_T], writes=[lnt_T])
                cx.run("dve", [lambda e: e.tensor_scalar(out=msq, in0=msq, scalar1=EPS, scalar2=None, op0=ALU.add)], reads=[lnt_T], writes=[lnt_T])
                cx.run("act", [lambda e: e.activation(out=rstd, in_=msq, func=AF.Sqrt)], reads=[lnt_T], writes=[lnt_T])
                cx.run("dve", [lambda e: e.reciprocal(out=rstd, in_=rstd)], reads=[lnt_T], writes=[lnt_T])
                cx.run("dve", [lambda e, kc=kc: e.tensor_tensor(out=x[:, kc, sl], in0=x[:, kc, sl], in1=mean, op=ALU.subtract) for kc in range(8)],
                       reads=[x_T, lnt_T], writes=[x_T])
                cx.run("dve", [lambda e, kc=kc: e.tensor_tensor(out=x[:, kc, sl], in0=x[:, kc, sl], in1=rstd, op=ALU.mult) for kc in range(8)],
                       reads=[x_T, lnt_T], writes=[x_T])
                cx.run("act", [lambda e, kc=kc: e.activation(out=x[:, kc, sl], in_=x[:, kc, sl], func=AF.Identity,
                                                             scale=lw[:, kc:kc + 1], bias=lb[:, kc:kc + 1]) for kc in range(8)],
                       reads=[x_T, G], writes=[x_T])

        lnks = sb("lnks", [128, 1])
        cx.run("dve", [lambda e: e.memset(lnks, LNKS)], writes=[G])

        def ln_stats(b1, b1T, b2, b2T, mean, msq, rstd, st_T, n):
            cx.run("act", [lambda e: e.activation(out=mean, in_=b1, func=AF.Copy, scale=1.0 / n)], reads=[b1T], writes=[st_T])
            cx.run("dve", [lambda e: e.tensor_tensor(out=msq, in0=mean, in1=mean, op=ALU.mult)], reads=[st_T], writes=[st_T])
            cx.run("dve", [lambda e: e.scalar_tensor_tensor(out=msq, in0=b2, scalar=1.0 / n, in1=msq, op0=ALU.mult, op1=ALU.subtract)],
                   reads=[b2T, st_T], writes=[st_T])
            cx.run("dve", [lambda e: e.tensor_scalar(out=msq, in0=msq, scalar1=EPS, scalar2=None, op0=ALU.add)], reads=[st_T], writes=[st_T])
            cx.run("act", [lambda e: e.activation(out=rstd, in_=msq, func=AF.Sqrt)], reads=[st_T], writes=[st_T])
            cx.run("dve", [lambda e: e.reciprocal(out=rstd, in_=rstd)], reads=[st_T], writes=[st_T])

        def head_norm_out(src3, src_T, normw, h, gate3, gate_T, dstT, dst_T, scr, scr_T, hn_all, hn_Ts):
            stA = scr[:, 0:72].rearrange("p (c n) -> p c n", n=6)
            mvA = scr[:, 72:96].rearrange("p (c n) -> p c n", n=2)
            src_all = src3
            hn3 = hn_all.rearrange("p (c n) -> p c n", n=128)
            cx.run("dve", [lambda e, c=c: e.bn_stats(out=stA[:, c, :], in_=src3[:, c, :]) for c in range(12)], reads=list(src_T) + [scr_T], writes=[scr_T])
            cx.run("dve", [lambda e, c=c: e.bn_aggr(out=mvA[:, c, :], in_=stA[:, c, :]) for c in range(12)], reads=[scr_T], writes=[scr_T])
            rstd = mvA[:, :, 1]
            cx.run("dve", [lambda e: e.tensor_scalar(out=rstd, in0=rstd, scalar1=EPS, scalar2=None, op0=ALU.add)], reads=[scr_T], writes=[scr_T])
            cx.run("act", [lambda e: e.activation(out=rstd, in_=rstd, func=AF.Sqrt)], reads=[scr_T], writes=[scr_T])
            cx.run("dve", [lambda e: e.reciprocal(out=rstd, in_=rstd)], reads=[scr_T], writes=[scr_T])
            cx.run("dve", [lambda e, c=c: e.tensor_scalar(out=src3[:, c, :], in0=src3[:, c, :], scalar1=mvA[:, c, 0:1], scalar2=mvA[:, c, 1:2],
                                                          op0=ALU.subtract, op1=ALU.mult) for c in range(12)], reads=list(src_T) + [scr_T], writes=list(src_T))
            cx.run("dve", [lambda e, c=c: e.tensor_tensor(out=src3[:, c, :], in0=src3[:, c, :], in1=normw[:, h * 128:(h + 1) * 128], op=ALU.mult) for c in range(12)],
                   reads=list(src_T) + [G], writes=list(src_T))
            cx.run("dve", [lambda e: e.tensor_tensor(out=hn3, in0=src3, in1=gate3, op=ALU.mult)], reads=list(src_T) + [gate_T] + list(hn_Ts), writes=list(hn_Ts))
            for c in range(12):
                bk, bkT = psum()
                bkb = bk.bitcast(BF16)
                cx.run("pe", [lambda e, bkb=bkb, c=c: e.transpose(out=bkb[:, 0:128], in_=hn3[:, c, :], identity=identB)], reads=list(hn_Ts) + [G], writes=[bkT])
                cx.run("act", [lambda e, bkb=bkb, c=c: e.activation(out=dstT[:, h, c * 128:(c + 1) * 128], in_=bkb[:, 0:128], func=AF.Copy)], reads=[bkT], writes=[dst_T])

        def layers():
          for l in range(L):
              phase_end('pro%d' % l)
              aset(0)
              for dst, src in ((gb, gb_d[l]), (minit, minit_d[l]), (convp, convp_d[l]), (lnp, lnp_d[l]),
                               (mnw, mnw_d[l].partition_broadcast(128)), (rnw, rnw_d[l].partition_broadcast(128)),
                               (rdec, rdec_d[l].partition_broadcast(128))):
                  cx.dma("sp", dst, src, writes=[G])
              cx.run("dve", [lambda e: e.tensor_scalar(out=ngb, in0=gb[:, 1:2], scalar1=-1.0, scalar2=None, op0=ALU.mult)], reads=[G], writes=[G])
              cx.run("act", [lambda e: e.activation(out=lg, in_=rdec, func=AF.Exp, scale=-1.0)], reads=[G], writes=[G])
              cx.run("act", [lambda e: e.activation(out=lg, in_=lg, func=AF.Ln, bias=1.0)], reads=[G], writes=[G])
              cx.run("dve", [lambda e: e.tensor_scalar(out=lg, in0=lg, scalar1=-1.0, scalar2=None, op0=ALU.mult)], reads=[G], writes=[G])
              for r in range(8):
                  d = r // 4
                  lgc = lg[:, r:r + 1]
                  cx.run("act", [lambda e, r=r, d=d, lgc=lgc: e.activation(out=decT[:, r, :], in_=DIFF[d], func=AF.Exp, scale=lgc)], reads=[G], writes=[G])
                  cx.run("dve", [lambda e, r=r, d=d: e.scalar_tensor_tensor(out=decT[:, r, :], in0=decT[:, r, :], scalar=KS, in1=M01[d],
                                                                           op0=ALU.mult, op1=ALU.mult)], reads=[G], writes=[G])
                  cx.run("act", [lambda e, r=r, d=d, lgc=lgc: e.activation(out=rcol[:, r, 0:1], in_=POS[:, d:d + 1], func=AF.Exp, scale=lgc),
                                 lambda e, r=r, d=d, lgc=lgc: e.activation(out=rcol[:, r, 1:2], in_=POS[:, 2 + d:3 + d], func=AF.Exp, scale=lgc),
                                 lambda e, r=r, d=d, lgc=lgc: e.activation(out=rcol[:, r, 2:3], in_=POS[:, 4:5], func=AF.Exp, scale=lgc)],
                         reads=[G], writes=[G])
                  cx.run("dve", [lambda e, r=r: e.tensor_scalar(out=rcol[:, r, 1:2], in0=rcol[:, r, 1:2], scalar1=KS, scalar2=None, op0=ALU.mult)],
                         reads=[G], writes=[G])

              phase_end('small%d' % l)
              modT, sc1p, sc2p = modTs[l], sc1ps[l], sc2ps[l]
              if l == 0:
                  mod_groups(0, [0, 1, 2, 3])
              phase_end('modmm%d' % l)

              def modulate(scp, shoff):
                  ths = []
                  for kc in range(8):
                      for (a, b, v) in ((0, 1024, 1), (1024, T, 0)):
                          ths.append(lambda e, kc=kc, a=a, b=b, v=v: e.tensor_scalar(
                              out=hT[:, kc, a:b], in0=x[:, kc, a:b], scalar1=scp[:, kc, v:v + 1],
                              scalar2=modT[:, shoff + kc, v:v + 1], op0=ALU.mult, op1=ALU.add))
                  cx.run("dve", ths, reads=[x_T, M_T[l]], writes=[hT_T])

              dbg_out("modT_l%d" % l, modT, [128, 48, 2], [M_T[l]])
              phase_end("mod%d" % l)
              modulate(sc1p, 0)
              phase_end("h%d" % l)
              dbg_out("h_l%d" % l, hT, [128, 8, T], [hT_T])

              aset(3072)
              h_aT = arena[:, 0:3072].bitcast(BF16).rearrange("p (h t) -> p h t", t=T)
              ubT = arena[:, 3072:6144].bitcast(BF16).rearrange("p (h t) -> p h t", t=T)
              h_cT = arena[:, 6144:9216].bitcast(BF16).rearrange("p (h t) -> p h t", t=T)
              merged = arena[:, 9216:15360].bitcast(BF16).rearrange("p (h t) -> p h t", t=T)
              haT_T = TT("h_aT"); ub_T = TT("ubT"); hcT_T = TT("h_cT"); mg_T = TT("merged")
              rIG = af32(T); rA = af32(T); rP = af32(T); rCM = af32(T)
              R_T = TT("rows")
              sm = af32(96).rearrange("p (a c) -> p a c", c=12)
              SM_T = TT("sm")
              cols = af32(3 * 432).rearrange("p (q n) -> p q n", n=432)
              COL_T = TT("cols")
              cbc = af32(96)
              qT = abf(T); kT = abf(T)
              qk_T = TT("qk")
              vext = abf(12 * 130).rearrange("p (c n) -> p c n", n=130)
              v_T = TT("vext")
              osig = abf(T).rearrange("p (c n) -> p c n", n=128)
              o_T = TT("osig")
              hsum = af32(T).rearrange("p (c n) -> p c n", n=128)
              hs_T = [TT("hs%d" % c) for c in range(12)]
              kw = [abf(T).rearrange("p (c n) -> p c n", n=128) for _ in range(2)]
              kw_T = [[TT("kw") for c in range(12)] for _ in range(2)]
              Cx = [[af32(130) for _ in range(2)] for _ in range(2)]
              Cx_T = [[TT("cx"), TT("cx")] for _ in range(2)]
              sc2 = af32(8)
              dmg = [af32(512)] * 2
              dmg_T = [TT("dmg0")] * 2
              sT_all = abf(24 * 128)
              sTa_T = [TT("sTa%d" % i) for i in range(6)]
              Cb_all = [abf(12 * 130).rearrange("p (s n) -> p s n", n=130) for _ in range(2)]
              CbA_T = [[TT("cba") for _ in range(12)] for _ in range(2)]
              Bs2 = [af32(130) for _ in range(2)]
              Bs2_T = [TT("bs2a"), TT("bs2b")]
              tot_all = af32(12 * 130).rearrange("p (c n) -> p c n", n=130)
              tota_T = TT("tot_all")
              dn12 = af32(12)
              dn_T = TT("dn12")

              wg, wgT = wload(in_w_d[l][:, OFF_GATE:OFF_GATE + 72], 8, 72)
              cx.run("dve", [lambda e: e.memset(rP[0:36, :], 0.0), lambda e: e.memset(rCM[0:36, :], 0.0)], writes=[R_T])
              for which in (0, 1):
                  for nb in range(3):
                      bk, bkT = psum()
                      sl = slice(nb * 512, (nb + 1) * 512)
                      cx.run("pe", [mm(bk[0:36, :], wg[:, kc, which * 36:(which + 1) * 36], hT[:, kc, sl], kc == 0, kc == 7) for kc in range(8)],
                             reads=[wgT, hT_T], writes=[bkT])
                      if which == 0:
                          cx.run("act", [lambda e, bk=bk, sl=sl: e.activation(out=rIG[0:36, sl], in_=bk[0:36, :], func=AF.Identity, bias=gb[:, 0:1])],
                                 reads=[bkT, G], writes=[R_T])
                      else:
                          cx.run("act", [lambda e, bk=bk, sl=sl: e.activation(out=rA[0:36, sl], in_=bk[0:36, :], func=AF.Exp, scale=-1.0, bias=ngb[:, 0:1])],
                                 reads=[bkT, G], writes=[R_T])
              cx.run("act", [lambda e: e.activation(out=rA[0:36, :], in_=rA[0:36, :], func=AF.Ln, bias=1.0)], reads=[R_T], writes=[R_T])
              ths = []
              for c in range(12):
                  cs = slice(c * 128, (c + 1) * 128)
                  ths.append(lambda e, cs=cs: e.tensor_tensor_scan(out=rP[0:4, cs], data0=ones36[0:4, :], data1=rA[0:4, cs], initial=0.0,
                                                                   op0=ALU.mult, op1=ALU.add))
                  ths.append(lambda e, cs=cs: e.tensor_tensor_scan(out=rev(rP[32:36, cs]), data0=ones36[32:36, :], data1=rev(rA[32:36, cs]),
                                                                   initial=0.0, op0=ALU.mult, op1=ALU.add))
              cx.run("dve", ths, reads=[R_T], writes=[R_T])
              cx.run("dve", [lambda e: e.tensor_tensor(out=rA[0:36, :], in0=rIG[0:36, :], in1=rP[0:36, :], op=ALU.add)], reads=[R_T], writes=[R_T])
              ths = []
              for c in range(12):
                  cs = slice(c * 128, (c + 1) * 128)
                  ths.append(lambda e, cs=cs: e.tensor_tensor_scan(out=rCM[0:4, cs], data0=zeros36[0:4, :], data1=rA[0:4, cs], initial=-1e30,
                                                                   op0=ALU.add, op1=ALU.max))
                  ths.append(lambda e, cs=cs: e.tensor_tensor_scan(out=rev(rCM[32:36, cs]), data0=zeros36[32:36, :], data1=rev(rA[32:36, cs]),
                                                                   initial=-1e30, op0=ALU.add, op1=ALU.max))
              cx.run("dve", ths, reads=[R_T], writes=[R_T])
              CM3 = rCM.rearrange("p (c t) -> p c t", t=128)
              P3 = rP.rearrange("p (c t) -> p c t", t=128)
              A3 = rA.rearrange("p (c t) -> p c t", t=128)
              IG3 = rIG.rearrange("p (c t) -> p c t", t=128)
              cx.run("dve", [lambda e: e.memset(sm[0:36, :, :], 0.0)], writes=[SM_T])
              cx.run("dve", [lambda e: e.tensor_copy(out=sm[0:4, 0, :], in_=CM3[0:4, :, 127]),
                             lambda e: e.tensor_copy(out=sm[32:36, 0, :], in_=CM3[32:36, :, 0]),
                             lambda e: e.tensor_copy(out=sm[0:4, 1, :], in_=P3[0:4, :, 127]),
                             lambda e: e.tensor_copy(out=sm[32:36, 1, :], in_=P3[32:36, :, 0])], reads=[R_T, SM_T], writes=[SM_T])
              for d, r0 in ((0, 0), (1, 32)):
                  rs = slice(r0, r0 + 4)
                  order = list(range(12)) if d == 0 else list(range(11, -1, -1))
                  prev = None
                  for c in order:
                      s = c // 2
                      start = (c % 2 == 0) if d == 0 else (c % 2 == 1)
                      m0c = sm[rs, 2, c:c + 1]
                      if start:
                          if (d == 0 and s == 0) or (d == 1 and s == 3):
                              th = lambda e, m0c=m0c, rs=rs: e.tensor_copy(out=m0c, in_=minit[rs, :])
                          elif (d == 0 and s <= 3) or (d == 1 and s <= 2):
                              th = lambda e, m0c=m0c, rs=rs, prev=prev: e.tensor_scalar(out=m0c, in0=sm[rs, 4, prev:prev + 1], scalar1=chain[rs, :],
                                                                                       scalar2=None, op0=ALU.mult)
                          else:
                              th = lambda e, m0c=m0c: e.memset(m0c, 0.0)
                      else:
                          th = lambda e, m0c=m0c, rs=rs, prev=prev: e.tensor_copy(out=m0c, in_=sm[rs, 4, prev:prev + 1])
                      cx.run("dve", [th], reads=[SM_T, G], writes=[SM_T])
                      cx.run("dve", [lambda e, rs=rs, c=c: e.tensor_tensor(out=sm[rs, 3, c:c + 1], in0=sm[rs, 2, c:c + 1], in1=sm[rs, 0, c:c + 1], op=ALU.max)],
                             reads=[SM_T], writes=[SM_T])
                      cx.run("dve", [lambda e, rs=rs, c=c: e.tensor_tensor(out=sm[rs, 4, c:c + 1], in0=sm[rs, 3, c:c + 1], in1=sm[rs, 1, c:c + 1], op=ALU.subtract)],
                             reads=[SM_T], writes=[SM_T])
                      prev = c
              cx.dma("sp", mfin_d[l], sm[0:36, 4, :], reads=[SM_T])
              cx.run("dve", [lambda e: e.tensor_tensor(out=sm[0:36, 6, :], in0=sm[0:36, 3, :], in1=sm[0:36, 2, :], op=ALU.subtract)], reads=[SM_T], writes=[SM_T])
              cx.run("act", [lambda e: e.activation(out=sm[0:36, 5, :], in_=sm[0:36, 6, :], func=AF.Exp, scale=-1.0)], reads=[SM_T], writes=[SM_T])
              m0b = sm[0:36, 2, :].unsqueeze(2).to_broadcast([36, 12, 128])
              mxb = sm[0:36, 3, :].unsqueeze(2).to_broadcast([36, 12, 128])
              cx.run("dve", [lambda e: e.tensor_tensor(out=CM3[0:36], in0=CM3[0:36], in1=m0b, op=ALU.max)], reads=[R_T, SM_T], writes=[R_T])
              cx.run("dve", [lambda e: e.tensor_scalar(out=rCM[0:36, :], in0=rCM[0:36, :], scalar1=-1.0, scalar2=None, op0=ALU.mult)], reads=[R_T], writes=[R_T])
              dbg_out("rA_l%d" % l, rA[0:36, :], [36, T], [R_T])
              dbg_out("rNG_l%d" % l, rCM[0:36, :], [36, T], [R_T])

              def cols_from(q, rows):
                  bk, bkT = psum()
                  cx.run("pe", [lambda e, c=c, bk=bk: e.transpose(out=bk[:, c * 36:(c + 1) * 36], in_=rows[0:36, c * 128:(c + 1) * 128],
                                                                  identity=identF[0:36, 0:36]) for c in range(12)],
                         reads=[R_T, G], writes=[bkT])
                  cx.run("act", [lambda e, bk=bk: e.activation(out=cols[:, q, :], in_=bk[:, 0:432], func=AF.Copy)], reads=[bkT], writes=[COL_T])

              cx.run("dve", [lambda e: e.tensor_tensor(out=IG3[0:36], in0=CM3[0:36], in1=m0b, op=ALU.add)], reads=[R_T, SM_T], writes=[R_T])
              cx.run("act", [lambda e: e.activation(out=rIG[0:36, :], in_=rIG[0:36, :], func=AF.Exp)], reads=[R_T], writes=[R_T])
              cols_from(0, rIG)
              cx.run("dve", [lambda e: e.tensor_tensor(out=rP[0:36, :], in0=rP[0:36, :], in1=rCM[0:36, :], op=ALU.add)], reads=[R_T], writes=[R_T])
              cx.run("act", [lambda e: e.activation(out=rP[0:36, :], in_=rP[0:36, :], func=AF.Exp)], reads=[R_T], writes=[R_T])
              cols_from(1, rP)
              cx.run("dve", [lambda e: e.tensor_tensor(out=IG3[0:36], in0=A3[0:36], in1=mxb, op=ALU.subtract)], reads=[R_T, SM_T], writes=[R_T])
              cx.run("dve", [lambda e: e.tensor_scalar(out=rIG[0:36, :], in0=rIG[0:36, :], scalar1=LNKS, scalar2=None, op0=ALU.add)], reads=[R_T], writes=[R_T])
              cx.run("act", [lambda e: e.activation(out=rIG[0:36, :], in_=rIG[0:36, :], func=AF.Exp)], reads=[R_T], writes=[R_T])
              cols_from(2, rIG)
              bk, bkT = psum()
              cx.run("pe", [mm(bk[:, r * 12:(r + 1) * 12], sel[:, r, :], sm[0:36, 5, :], True, True) for r in range(8)], reads=[SM_T, G], writes=[bkT])
              cx.run("act", [lambda e, bk=bk: e.activation(out=cbc, in_=bk[:, 0:96], func=AF.Copy)], reads=[bkT], writes=[COL_T])
              dbg_out("sm_l%d" % l, sm[0:36, :, :], [36, 8, 12], [SM_T])
              dbg_out("cols_l%d" % l, cols, [128, 3, 432], [COL_T])

              phase_end("gates%d" % l)
              prow = lambda r: (r // 4) * 32 + (r % 4)
              lnks_col = None

              WH = {}

              def proj_qk(h):
                  wv, wT = wload(in_w_d[l][:, h * 512:(h + 1) * 512], 8, 512)
                  WH[h] = (wv, wT)
                  for nb in range(3):
                      sl = slice(nb * 512, (nb + 1) * 512)
                      for which, dst in ((0, qT), (1, kT)):
                          bk, bkT = psum()
                          cx.run("pe", [mm(bk, wv[:, kc, which * 128:(which + 1) * 128], hT[:, kc, sl], kc == 0, kc == 7) for kc in range(8)],
                                 reads=[wT, hT_T], writes=[bkT])
                          cx.run("act", [lambda e, bk=bk, dst=dst, sl=sl: e.activation(out=dst[:, sl], in_=bk, func=AF.Copy)], reads=[bkT], writes=[qk_T])
                  for c in range(12):
                      cs = slice(c * 128, (c + 1) * 128)
                      bk, bkT = psum()
                      bkb = bk.bitcast(BF16)
                      cx.run("pe", [lambda e, bkb=bkb, cs=cs: e.transpose(out=bkb[:, 0:128], in_=kT[:, cs], identity=identB)], reads=[qk_T, G], writes=[bkT])
                      for d in range(2):
                          r = d * 4 + h
                          col = cols[:, 2, c * 36 + prow(r):c * 36 + prow(r) + 1]
                          cx.run("act", [lambda e, bkb=bkb, d=d, c=c, col=col: e.activation(out=kw[d][:, c, :], in_=bkb[:, 0:128], func=AF.Copy, scale=col)],
                                 reads=[bkT, COL_T], writes=[kw_T[d][c]])

              def proj_vo(h):
                  wv, wT = WH[h]
                  cx.run("dve", [lambda e: e.memset(vext[:, :, 128:130], 1.0)], writes=[v_T])
                  for c in range(12):
                      cs = slice(c * 128, (c + 1) * 128)
                      bk, bkT = psum()
                      cx.run("pe", [mm(bk[:, 0:256], hT[:, kc, cs], wv[:, kc, 256:512], kc == 0, kc == 7) for kc in range(8)], reads=[wT, hT_T], writes=[bkT])
                      cx.run("act", [lambda e, bk=bk, c=c: e.activation(out=vext[:, c, 0:128], in_=bk[:, 0:128], func=AF.Copy)], reads=[bkT], writes=[v_T])
                      cx.run("act", [lambda e, bk=bk, c=c: e.activation(out=osig[:, c, :], in_=bk[:, 128:256], func=AF.Sigmoid)], reads=[bkT], writes=[o_T])

              proj_qk(0)
              for h in range(H):
                  proj_vo(h)
                  orders = [list(range(12)), list(range(11, -1, -1))]
                  for g in range(6):
                      d = g // 3
                      r = d * 4 + h
                      b1, b1T = psum()
                      ths = []
                      css = []
                      for i in range(4):
                          c = orders[d][(g % 3) * 4 + i]
                          cs = slice(c * 128, (c + 1) * 128)
                          css.append(cs)
                          o = b1[:, i * 128:(i + 1) * 128]
                          ths += [mm(o, sel[:, r, :], rCM[0:36, cs], True, False), mm(o, rA[0:36, cs], sel[:, r, :], False, False),
                                  mm(o, identF, MASK[d], False, True)]
                      cx.run("pe", ths, reads=[R_T, G], writes=[b1T])
                      gp = g % 2
                      cx.run("act", [lambda e, b1=b1, gp=gp: e.activation(out=dmg[gp], in_=b1, func=AF.Exp, bias=lnks[:, 0:1])], reads=[b1T, G], writes=[dmg_T[gp]])
                      b2, b2T = psum()
                      cx.run("pe", [mm(b2[:, i * 128:(i + 1) * 128], kT[:, css[i]], qT[:, css[i]], True, True) for i in range(4)], reads=[qk_T], writes=[b2T])
                      cx.run("dve", [lambda e, b2=b2, gp=gp, g=g: e.tensor_tensor(out=sT_all[:, g * 512:(g + 1) * 512], in0=b2, in1=dmg[gp], op=ALU.mult)],
                             reads=[b2T, dmg_T[gp]], writes=[sTa_T[g]])

                  def rec(d, h=h):
                      r = d * 4 + h
                      cur = 0
                      Cxd, CxTd = Cx[d], Cx_T[d]
                      for step, c in enumerate(orders[d]):
                          s = c // 2
                          start = (c % 2 == 0) if d == 0 else (c % 2 == 1)
                          if start:
                              if (d == 0 and s == 0) or (d == 1 and s == 3):
                                  cx.dma("sp", Cxd[cur][:, 0:129], Cinit_d[l, d, h], writes=[CxTd[cur]])
                              elif (d == 0 and s <= 3) or (d == 1 and s <= 2):
                                  cx.run("dve", [lambda e, cur=cur: e.tensor_scalar(out=Cxd[cur][:, 0:129], in0=Cxd[cur][:, 0:129], scalar1=chain[:, 0:1],
                                                                                     scalar2=None, op0=ALU.mult)], reads=[CxTd[cur], G], writes=[CxTd[cur]])
                              else:
                                  cx.run("dve", [lambda e, cur=cur: e.memset(Cxd[cur][:, 0:129], 0.0)], writes=[CxTd[cur]])
                          cx.run("act", [lambda e, cur=cur, step=step: e.activation(out=Cb_all[d][:, step, 0:129], in_=Cxd[cur][:, 0:129], func=AF.Copy)],
                                 reads=[CxTd[cur]], writes=[CbA_T[d][step]])
                          bU, bUT = psum()
                          cx.run("pe", [mm(bU[:, 0:129], kw[d][:, c, :], vext[:, c, 0:129], True, True)], reads=[kw_T[d][c], v_T], writes=[bUT])
                          yield
                          nxt = 1 - cur
                          ccol = cbc[:, r * 12 + c:r * 12 + c + 1]
                          cx.run("dve", [lambda e, cur=cur, nxt=nxt: e.scalar_tensor_tensor(
                              out=Cxd[nxt][:, 0:129], in0=Cxd[cur][:, 0:129], scalar=ccol, in1=bU[:, 0:129], op0=ALU.mult, op1=ALU.add)],
                              reads=[bUT, CxTd[cur], COL_T], writes=[CxTd[nxt]])
                          cur = nxt
                          end = (c % 2 == 1) if d == 0 else (c % 2 == 0)
                          if end:
                              cx.dma("sp", Cfin_d[l, s, d, h], Cxd[cur][:, 0:129], reads=[CxTd[cur]])
                          yield

                  for _ in itertools.zip_longest(rec(0), rec(1)):
                      pass

                  for d in range(2):
                      r = d * 4 + h
                      pr = prow(r)
                      for step, c in enumerate(orders[d]):
                          cs = slice(c * 128, (c + 1) * 128)
                          p = d * 12 + step
                          bA, bAT = psum()
                          cx.run("pe", [mm(bA[:, 0:129], sT_all[:, p * 128:(p + 1) * 128], vext[:, c, 0:129], True, True),
                                        mm(bA[:, 256:385], qT[:, cs], Cb_all[d][:, step, 0:129], True, True)],
                                 reads=[sTa_T[p // 4], v_T, qk_T, CbA_T[d][step]], writes=[bAT])
                          wcol = cols[:, 0, c * 36 + pr:c * 36 + pr + 1]
                          bp = step % 2
                          cx.run("act", [lambda e, bA=bA, bp=bp, wcol=wcol: e.activation(out=Bs2[bp][:, 0:129], in_=bA[:, 256:385], func=AF.Copy, scale=wcol)],
                                 reads=[bAT, COL_T], writes=[Bs2_T[bp]])
                          cx.run("dve", [lambda e, bA=bA, bp=bp, c=c: e.tensor_tensor(out=tot_all[:, c, 0:129], in0=bA[:, 0:129], in1=Bs2[bp][:, 0:129], op=ALU.add)],
                                 reads=[bAT, Bs2_T[bp]], writes=[tota_T])
                      den = tot_all[:, :, 128]
                      ecols = cols[:, 1, :].rearrange("p (c n) -> p c n", n=36)[:, :, pr]
                      cx.run("dve", [lambda e, den=den: e.tensor_scalar(out=dn12, in0=den, scalar1=-1.0, scalar2=None, op0=ALU.mult)], reads=[tota_T], writes=[dn_T])
                      cx.run("dve", [lambda e, den=den: e.tensor_tensor(out=dn12, in0=dn12, in1=den, op=ALU.max)], reads=[tota_T, dn_T], writes=[dn_T])
                      cx.run("dve", [lambda e, ecols=ecols: e.tensor_tensor(out=dn12, in0=dn12, in1=ecols, op=ALU.max)], reads=[dn_T, COL_T], writes=[dn_T])
                      cx.run("dve", [lambda e: e.reciprocal(out=dn12, in_=dn12)], reads=[dn_T], writes=[dn_T])
                      if d == 0:
                          cx.run("act", [lambda e, c=c: e.activation(out=hsum[:, c, :], in_=tot_all[:, c, 0:128], func=AF.Copy, scale=dn12[:, c:c + 1])
                                         for c in range(12)], reads=[tota_T, dn_T], writes=hs_T)
                      else:
                          cx.run("dve", [lambda e, c=c: e.scalar_tensor_tensor(out=hsum[:, c, :], in0=tot_all[:, c, 0:128], scalar=dn12[:, c:c + 1],
                                                                               in1=hsum[:, c, :], op0=ALU.mult, op1=ALU.add) for c in range(12)],
                                 reads=[tota_T, dn_T] + hs_T, writes=hs_T)
                  if l == 0 and h == 0:
                      dbg_out("hsum_l0h0", hsum, [128, 12, 128], hs_T)
                  if h + 1 < H:
                      proj_qk(h + 1)
                  head_norm_out(hsum, hs_T, mnw, h, osig, o_T, h_aT, haT_T, tot_all.rearrange("p c n -> p (c n)"), tota_T, sT_all[:, 0:1536], sTa_T[0:3])
                  if l == 0:
                      mod_groups(0, [4 + 2 * h, 5 + 2 * h])
                  phase_end("mlstm%d_h%d" % (l, h))
              dbg_out("haT_l%d" % l, h_aT, [128, 4, T], [haT_T])

              phase_end("mlstm%d" % l)
              aset(6144)
              acc = af32(4 * T).rearrange("p (a t) -> p a t", t=T)
              acc_T = TT("acc")
              upad = [abf(6 * 286).rearrange("p (s n) -> p s n", n=286) for _ in range(2)]
              up_T = [TT("upad0"), TT("upad1")]
              Dg = [abf(31 * 128).rearrange("p (k n) -> p k n", n=128) for _ in range(2)]
              Dg_T = [TT("dg0"), TT("dg1")]
              sg = [af32(512) for _ in range(2)]
              sg_T = [TT("sg0"), TT("sg1")]
              cx.run("dve", [lambda e: e.memset(upad[0], 0.0), lambda e: e.memset(upad[1], 0.0)], writes=up_T)
              wa, waT = wload(in_w_d[l][:, OFF_CA:OFF_CA + 512], 8, 512)
              wgc, wgcT = wload(in_w_d[l][:, OFF_CG:OFF_CG + 512], 8, 512)
              k = 0
              for ct in range(4):
                  up, upT, dg, dgT = upad[ct % 2], up_T[ct % 2], Dg[ct % 2], Dg_T[ct % 2]
                  cx.run("act", [lambda e, kk=kk, ct=ct, dg=dg: e.activation(out=dg[:, kk, :], in_=identF, func=AF.Copy, scale=convp[:, ct, kk:kk + 1])
                                 for kk in range(CONV_K)], reads=[G], writes=[dgT])
                  for nb in range(3):
                      sl = slice(nb * 512, (nb + 1) * 512)
                      ba, baT = psum()
                      cx.run("pe", [mm(ba, wa[:, kc, ct * 128:(ct + 1) * 128], hT[:, kc, sl], kc == 0, kc == 7) for kc in range(8)], reads=[waT, hT_T], writes=[baT])
                      bg, bgT = psum()
                      cx.run("pe", [mm(bg, wgc[:, kc, ct * 128:(ct + 1) * 128], hT[:, kc, sl], kc == 0, kc == 7) for kc in range(8)], reads=[wgcT, hT_T], writes=[bgT])
                      pp = k % 2
                      k += 1
                      cx.run("act", [lambda e, bg=bg, pp=pp: e.activation(out=sg[pp], in_=bg, func=AF.Sigmoid)], reads=[bgT], writes=[sg_T[pp]])
                      cx.run("dve", [lambda e, ba=ba, pp=pp, nb=nb, up=up, s2=s2: e.tensor_tensor(
                          out=up[:, 2 * nb + s2, 15:271], in0=ba[:, s2 * 256:(s2 + 1) * 256],
                          in1=sg[pp][:, s2 * 256:(s2 + 1) * 256], op=ALU.mult) for s2 in range(2)], reads=[baT, sg_T[pp]], writes=[upT])
                  ths = []
                  for s in (1, 2, 3):
                      ths.append(lambda e, s=s, up=up: e.tensor_scalar(out=up[:, s, 0:15], in0=up[:, s - 1, 256:271], scalar1=chain[:, 0:1], scalar2=None, op0=ALU.mult))
                  for s in (0, 1, 2):
                      ths.append(lambda e, s=s, up=up: e.tensor_scalar(out=up[:, s, 271:286], in0=up[:, s + 1, 15:30], scalar1=chain[:, 0:1], scalar2=None, op0=ALU.mult))
                  cx.run("dve", ths, reads=[upT, G], writes=[upT])
                  phase_end("convu%d_%d" % (l, ct))
                  for s in range(6):
                      bk, bkT = psum()
                      cx.run("pe", [mm(bk[:, 0:256], dg[:, kk, :], up[:, s, kk:kk + 256], kk == 0, kk == CONV_K - 1) for kk in range(CONV_K)],
                             reads=[dgT, upT], writes=[bkT])
                      cx.run("act", [lambda e, bk=bk, ct=ct, s=s: e.activation(out=acc[:, ct, s * 256:(s + 1) * 256], in_=bk[:, 0:256], func=AF.Identity,
                                                                               bias=convp[:, ct, 31:32])], reads=[bkT, G], writes=[acc_T])
              dbg_out("conv_l%d" % l, acc, [128, 4, T], [acc_T])
              cb = abf(4 * 256).rearrange("p (a n) -> p a n", n=256)
              cq = abf(4 * 256).rearrange("p (a n) -> p a n", n=256)
              cb_T = TT("cb"); cq_T = TT("cq")
              mean = af32(256); msq = af32(256); rstd = af32(256)
              st_T = TT("lnst")
              for nb in range(6):
                  sl = slice(nb * 256, (nb + 1) * 256)
                  cx.run("act", [lambda e, ct=ct, sl=sl: e.activation(out=cb[:, ct, :], in_=acc[:, ct, sl], func=AF.Copy) for ct in range(4)], reads=[acc_T], writes=[cb_T])
                  cx.run("act", [lambda e, ct=ct, sl=sl: e.activation(out=cq[:, ct, :], in_=acc[:, ct, sl], func=AF.Square) for ct in range(4)], reads=[acc_T], writes=[cq_T])
                  b1, b1T = psum()
                  cx.run("pe", [mm(b1[:, 0:256], onesB, cb[:, ct, :], ct == 0, ct == 3) for ct in range(4)], reads=[cb_T, G], writes=[b1T])
                  b2, b2T = psum()
                  cx.run("pe", [mm(b2[:, 0:256], onesB, cq[:, ct, :], ct == 0, ct == 3) for ct in range(4)], reads=[cq_T, G], writes=[b2T])
                  ln_stats(b1[:, 0:256], b1T, b2[:, 0:256], b2T, mean, msq, rstd, st_T, 512)
                  cx.run("dve", [lambda e, ct=ct, sl=sl: e.tensor_tensor(out=acc[:, ct, sl], in0=acc[:, ct, sl], in1=mean, op=ALU.subtract) for ct in range(4)],
                         reads=[acc_T, st_T], writes=[acc_T])
                  cx.run("dve", [lambda e, ct=ct, sl=sl: e.tensor_tensor(out=acc[:, ct, sl], in0=acc[:, ct, sl], in1=rstd, op=ALU.mult) for ct in range(4)],
                         reads=[acc_T, st_T], writes=[acc_T])
                  cx.run("act", [lambda e, ct=ct, sl=sl: e.activation(out=acc[:, ct, sl], in_=acc[:, ct, sl], func=AF.Identity,
                                                                      scale=convp[:, ct, 32:33], bias=convp[:, ct, 33:34]) for ct in range(4)], reads=[acc_T, G], writes=[acc_T])
                  cx.run("act", [lambda e, ct=ct, sl=sl: e.activation(out=ubT[:, ct, sl], in_=acc[:, ct, sl], func=AF.Silu) for ct in range(4)], reads=[acc_T], writes=[ub_T])
              dbg_out("ubT_l%d" % l, ubT, [128, 4, T], [ub_T])

              phase_end("conv%d" % l)
              aset(9216)
              ropeCS = af32(1024); ropeSN = af32(1024)
              RP_T = TT("rope")
              cx.dma("sp", ropeCS, ropeCS_d, writes=[RP_T])
              cx.dma("sp", ropeSN, ropeSN_d, writes=[RP_T])
              qT = abf(T); kT = abf(T)
              qk_T = TT("rqk")
              vv = abf(T).rearrange("p (c n) -> p c n", n=128)
              v_T = TT("rv")
              gsil = af32(T).rearrange("p (c n) -> p c n", n=128)
              o_T = TT("gsil")
              ysum = af32(T).rearrange("p (c n) -> p c n", n=128)
              hs_T = [TT("ys%d" % c) for c in range(12)]
              kz = [abf(T).rearrange("p (c n) -> p c n", n=128) for _ in range(2)]
              kz_T = [[TT("kz") for c in range(12)] for _ in range(2)]
              Sx = [[af32(128) for _ in range(2)] for _ in range(2)]
              Sx_T = [[TT("sx"), TT("sx")] for _ in range(2)]
              t1 = af32(512); t2 = af32(512)
              sT_all = abf(24 * 128)
              sTa_T = [TT("rsTa%d" % i) for i in range(6)]
              Sb_all = [abf(12 * 128).rearrange("p (s n) -> p s n", n=128) for _ in range(2)]
              SbA_T = [[TT("sba") for _ in range(12)] for _ in range(2)]
              Bs2 = [af32(128) for _ in range(2)]
              Bs2_T = [TT("rbs2a"), TT("rbs2b")]
              t_T = TT("rt")
              def rproj_qk(h):
                  wv, wT = wload(in_w_d[l][:, OFF_RET + h * 512:OFF_RET + (h + 1) * 512], 8, 512)
                  for nb in range(3):
                      sl = slice(nb * 512, (nb + 1) * 512)
                      for which, dst in ((0, qT), (1, kT)):
                          bk, bkT = psum()
                          cx.run("pe", [mm(bk, wv[:, kc, which * 256:which * 256 + 128], hT[:, kc, sl], kc == 0, kc == 7) for kc in range(8)],
                                 reads=[wT, hT_T], writes=[bkT])
                          if nb < 2:
                              bs_, bsT = psum()
                              cx.run("pe", [mm(bs_, wv[:, kc, which * 256 + 128:which * 256 + 256], hT[:, kc, sl], kc == 0, kc == 7) for kc in range(8)],
                                     reads=[wT, hT_T], writes=[bsT])
                              cx.run("dve", [lambda e, bk=bk, sl=sl: e.tensor_tensor(out=t1, in0=bk, in1=ropeCS[:, sl], op=ALU.mult)], reads=[bkT, RP_T, t_T], writes=[t_T])
                              cx.run("dve", [lambda e, bs_=bs_, sl=sl: e.tensor_tensor(out=t2, in0=bs_, in1=ropeSN[:, sl], op=ALU.mult)], reads=[bsT, RP_T, t_T], writes=[t_T])
                              cx.run("dve", [lambda e, dst=dst, sl=sl: e.tensor_tensor(out=dst[:, sl], in0=t1, in1=t2, op=ALU.add)], reads=[t_T], writes=[qk_T, t_T])
                          else:
                              cx.run("act", [lambda e, bk=bk, dst=dst, sl=sl: e.activation(out=dst[:, sl], in_=bk, func=AF.Copy)], reads=[bkT], writes=[qk_T])
                  for c in range(12):
                      cs = slice(c * 128, (c + 1) * 128)
                      bk, bkT = psum()
                      bkb = bk.bitcast(BF16)
                      cx.run("pe", [lambda e, bkb=bkb, cs=cs: e.transpose(out=bkb[:, 0:128], in_=kT[:, cs], identity=identB)], reads=[qk_T, G], writes=[bkT])
                      for d in range(2):
                          r = d * 4 + h
                          cx.run("act", [lambda e, bkb=bkb, d=d, c=c, r=r: e.activation(out=kz[d][:, c, :], in_=bkb[:, 0:128], func=AF.Copy, scale=rcol[:, r, 1:2])],
                                 reads=[bkT, G], writes=[kz_T[d][c]])

              def rproj_vg(h):
                  wv2, wT2 = wload(in_w_d[l][:, OFF_RVG + h * 256:OFF_RVG + (h + 1) * 256], 8, 256)
                  for c in range(12):
                      cs = slice(c * 128, (c + 1) * 128)
                      bk, bkT = psum()
                      cx.run("pe", [mm(bk[:, 0:256], hT[:, kc, cs], wv2[:, kc, :], kc == 0, kc == 7) for kc in range(8)], reads=[wT2, hT_T], writes=[bkT])
                      cx.run("act", [lambda e, bk=bk, c=c: e.activation(out=vv[:, c, :], in_=bk[:, 0:128], func=AF.Copy)], reads=[bkT], writes=[v_T])
                      cx.run("act", [lambda e, bk=bk, c=c: e.activation(out=gsil[:, c, :], in_=bk[:, 128:256], func=AF.Silu)], reads=[bkT], writes=[o_T])

              rproj_qk(0)
              for h in range(H):
                  rproj_vg(h)
                  orders = [list(range(12)), list(range(11, -1, -1))]
                  for g in range(6):
                      d = g // 3
                      r = d * 4 + h
                      b2, b2T = psum()
                      css = [slice(orders[d][(g % 3) * 4 + i] * 128, (orders[d][(g % 3) * 4 + i] + 1) * 128) for i in range(4)]
                      cx.run("pe", [mm(b2[:, i * 128:(i + 1) * 128], kT[:, css[i]], qT[:, css[i]], True, True) for i in range(4)], reads=[qk_T], writes=[b2T])
                      cx.run("dve", [lambda e, b2=b2, g=g, i=i, r=r: e.tensor_tensor(out=sT_all[:, g * 512 + i * 128:g * 512 + (i + 1) * 128],
                                                                                   in0=b2[:, i * 128:(i + 1) * 128], in1=decT[:, r, :], op=ALU.mult) for i in range(4)],
                             reads=[b2T, G], writes=[sTa_T[g]])

                  def rrec(d, h=h):
                      r = d * 4 + h
                      cur = 0
                      Sxd, SxTd = Sx[d], Sx_T[d]
                      for step, c in enumerate(orders[d]):
                          s = c // 2
                          start = (c % 2 == 0) if d == 0 else (c % 2 == 1)
                          if start:
                              if (d == 0 and s == 0) or (d == 1 and s == 3):
                                  cx.dma("sp", Sxd[cur], Sinit_d[l, d, h], writes=[SxTd[cur]])
                              elif (d == 0 and s <= 3) or (d == 1 and s <= 2):
                                  cx.run("dve", [lambda e, cur=cur: e.tensor_scalar(out=Sxd[cur], in0=Sxd[cur], scalar1=chain[:, 0:1],
                                                                                     scalar2=None, op0=ALU.mult)], reads=[SxTd[cur], G], writes=[SxTd[cur]])
                              else:
                                  cx.run("dve", [lambda e, cur=cur: e.memset(Sxd[cur], 0.0)], writes=[SxTd[cur]])
                          cx.run("act", [lambda e, cur=cur, step=step: e.activation(out=Sb_all[d][:, step, :], in_=Sxd[cur], func=AF.Copy)],
                                 reads=[SxTd[cur]], writes=[SbA_T[d][step]])
                          bU, bUT = psum()
                          cx.run("pe", [mm(bU[:, 0:128], kz[d][:, c, :], vv[:, c, :], True, True)], reads=[kz_T[d][c], v_T], writes=[bUT])
                          yield
                          nxt = 1 - cur
                          cx.run("dve", [lambda e, cur=cur, nxt=nxt: e.scalar_tensor_tensor(
                              out=Sxd[nxt], in0=Sxd[cur], scalar=rcol[:, r, 2:3], in1=bU[:, 0:128], op0=ALU.mult, op1=ALU.add)],
                              reads=[bUT, SxTd[cur], G], writes=[SxTd[nxt]])
                          cur = nxt
                          end = (c % 2 == 1) if d == 0 else (c % 2 == 0)
                          if end:
                              cx.dma("sp", Sfin_d[l, s, d, h], Sxd[cur], reads=[SxTd[cur]])
                          yield

                  for _ in itertools.zip_longest(rrec(0), rrec(1)):
                      pass

                  for d in range(2):
                      r = d * 4 + h
                      for step, c in enumerate(orders[d]):
                          cs = slice(c * 128, (c + 1) * 128)
                          p = d * 12 + step
                          bA, bAT = psum()
                          cx.run("pe", [mm(bA[:, 0:128], sT_all[:, p * 128:(p + 1) * 128], vv[:, c, :], True, True),
                                        mm(bA[:, 256:384], qT[:, cs], Sb_all[d][:, step, :], True, True)],
                                 reads=[sTa_T[p // 4], v_T, qk_T, SbA_T[d][step]], writes=[bAT])
                          bp = step % 2
                          cx.run("act", [lambda e, bA=bA, bp=bp, r=r: e.activation(out=Bs2[bp], in_=bA[:, 256:384], func=AF.Copy, scale=rcol[:, r, 0:1])],
                                 reads=[bAT, G], writes=[Bs2_T[bp]])
                          if d == 0:
                              cx.run("dve", [lambda e, bA=bA, bp=bp, c=c: e.tensor_tensor(out=ysum[:, c, :], in0=bA[:, 0:128], in1=Bs2[bp], op=ALU.add)],
                                     reads=[bAT, Bs2_T[bp]], writes=[hs_T[c]])
                          else:
                              cx.run("dve", [lambda e, bA=bA, bp=bp: e.tensor_tensor(out=Bs2[bp], in0=bA[:, 0:128], in1=Bs2[bp], op=ALU.add)],
                                     reads=[bAT, Bs2_T[bp]], writes=[Bs2_T[bp]])
                              cx.run("dve", [lambda e, bp=bp, c=c: e.tensor_tensor(out=ysum[:, c, :], in0=ysum[:, c, :], in1=Bs2[bp], op=ALU.add)],
                                     reads=[Bs2_T[bp], hs_T[c]], writes=[hs_T[c]])
                  if h + 1 < H:
                      rproj_qk(h + 1)
                  head_norm_out(ysum, hs_T, rnw, h, gsil, o_T, h_cT, hcT_T, t1, t_T, sT_all[:, 0:1536], sTa_T[0:3])
                  if l + 1 < L:
                      mod_groups(l + 1, [3 * h, 3 * h + 1, 3 * h + 2])
              dbg_out("hcT_l%d" % l, h_cT, [128, 4, T], [hcT_T])

              phase_end("ret%d" % l)
              aset(15360)
              macc = af32(4 * T).rearrange("p (a t) -> p a t", t=T)
              macc_T = [TT("macc%d" % i) for i in range(4)]
              sgm = [af32(512) for _ in range(2)]
              sgm_T = [TT("sgm0"), TT("sgm1")]
              k = 0
              branches = ((mow_d, h_aT, haT_T), (cow_d, ubT, ub_T), (row_d, h_cT, hcT_T))
              for jg in range(2):
                  for b, (wd, src, srcT) in enumerate(branches):
                      wo, woT = wload(wd[l][:, jg * 512:(jg + 1) * 512], 4, 512)
                      wgm, wgmT = wload(in_w_d[l][:, OFF_GM + b * 1024 + jg * 512:OFF_GM + b * 1024 + (jg + 1) * 512], 8, 512)
                      for jj in range(4):
                          j = jg * 4 + jj
                          for nb in range(3):
                              sl = slice(nb * 512, (nb + 1) * 512)
                              by, byT = psum()
                              cx.run("pe", [mm(by, wo[:, kc, jj * 128:(jj + 1) * 128], src[:, kc, sl], kc == 0, kc == 3) for kc in range(4)], reads=[woT, srcT], writes=[byT])
                              bg, bgT = psum()
                              cx.run("pe", [mm(bg, wgm[:, kc, jj * 128:(jj + 1) * 128], hT[:, kc, sl], kc == 0, kc == 7) for kc in range(8)], reads=[wgmT, hT_T], writes=[bgT])
                              pp = k % 2
                              k += 1
                              cx.run("act", [lambda e, bg=bg, pp=pp: e.activation(out=sgm[pp], in_=bg, func=AF.Sigmoid)], reads=[bgT], writes=[sgm_T[pp]])
                              if b == 0:
                                  cx.run("dve", [lambda e, by=by, pp=pp, sl=sl, jj=jj: e.tensor_tensor(out=macc[:, jj, sl], in0=by, in1=sgm[pp], op=ALU.mult)],
                                         reads=[byT, sgm_T[pp]], writes=[macc_T[jj]])
                              else:
                                  cx.run("dve", [lambda e, by=by, pp=pp: e.tensor_tensor(out=sgm[pp], in0=by, in1=sgm[pp], op=ALU.mult)],
                                         reads=[byT, sgm_T[pp]], writes=[sgm_T[pp]])
                                  if b == 1:
                                      cx.run("dve", [lambda e, pp=pp, sl=sl, jj=jj: e.tensor_tensor(out=macc[:, jj, sl], in0=macc[:, jj, sl], in1=sgm[pp], op=ALU.add)],
                                             reads=[sgm_T[pp], macc_T[jj]], writes=[macc_T[jj]])
                                  else:
                                      cx.run("dve", [lambda e, pp=pp, sl=sl, j=j, jj=jj: e.tensor_tensor(out=merged[:, j, sl], in0=macc[:, jj, sl], in1=sgm[pp], op=ALU.add)],
                                             reads=[sgm_T[pp], macc_T[jj]], writes=[mg_T])
              dbg_out("merged_l%d" % l, merged, [128, 8, T], [mg_T])
              phase_end("merge%d" % l)
              aset(15360)
              xb = abf(8 * 512).rearrange("p (a n) -> p a n", n=512)
              xq = abf(8 * 512).rearrange("p (a n) -> p a n", n=512)
              lnt = (af32(512), af32(512), af32(512))
              cx.run("act", [lambda e, kc=kc: e.activation(out=x[:, kc, :], in_=x[:, kc, :], func=AF.Copy, scale=ALPHA) for kc in range(8)], reads=[x_T], writes=[x_T])
              for jg in range(2):
                  wo, woT = wload(outw_d[l][:, jg * 512:(jg + 1) * 512], 8, 512)
                  for jj in range(4):
                      j = jg * 4 + jj
                      for nb in range(3):
                          sl = slice(nb * 512, (nb + 1) * 512)
                          v = 1 if nb < 2 else 0
                          bk, bkT = psum()
                          cx.run("pe", [mm(bk, wo[:, kc, jj * 128:(jj + 1) * 128], merged[:, kc, sl], kc == 0, kc == 7) for kc in range(8)], reads=[woT, mg_T], writes=[bkT])
                          cx.run("dve", [lambda e, bk=bk, j=j, sl=sl, v=v: e.scalar_tensor_tensor(out=x[:, j, sl], in0=bk, scalar=modT[:, 16 + j, v:v + 1],
                                                                                                 in1=x[:, j, sl], op0=ALU.mult, op1=ALU.add)],
                                 reads=[bkT, x_T, M_T[l]], writes=[x_T])
              layer_norm_fm(0, [(0, 512), (512, 1024), (1024, 1536)], xb, xq, lnt)
              dbg_out("x1_l%d" % l, x, [128, 8, T], [x_T])
              phase_end("outp%d" % l)
              aset(0)
              ffT = abf(NFT * T).rearrange("p (a n) -> p a n", n=T)
              ff_T = TT("ffT")
              a_sb = abf(4 * T).rearrange("p (a n) -> p a n", n=T)
              asb_T = TT("a_sb")
              sgf = [af32(512) for _ in range(2)]
              sgf_T = [TT("sgf0"), TT("sgf1")]
              modulate(sc2p, 24)
              cx.run("act", [lambda e, kc=kc: e.activation(out=x[:, kc, :], in_=x[:, kc, :], func=AF.Copy, scale=ALPHA) for kc in range(8)], reads=[x_T], writes=[x_T])
              k = 0
              for g in range(6):
                  nt = 4 if g < 5 else 2
                  wa, waT = wload(w13_d[l][:, g * 512:g * 512 + nt * 128], 8, nt * 128)
                  wg2, wg2T = wload(w13_d[l][:, FF + g * 512:FF + g * 512 + nt * 128], 8, nt * 128)
                  for jj in range(nt):
                      for nb in range(3):
                          sl = slice(nb * 512, (nb + 1) * 512)
                          bk, bkT = psum()
                          cx.run("pe", [mm(bk, wa[:, kc, jj * 128:(jj + 1) * 128], hT[:, kc, sl], kc == 0, kc == 7) for kc in range(8)],
                                 reads=[waT, hT_T], writes=[bkT])
                          cx.run("act", [lambda e, bk=bk, jj=jj, sl=sl: e.activation(out=a_sb[:, jj, sl], in_=bk, func=AF.Copy)],
                                 reads=[bkT], writes=[asb_T])
                  for jj in range(nt):
                      for nb in range(3):
                          sl = slice(nb * 512, (nb + 1) * 512)
                          bk, bkT = psum()
                          cx.run("pe", [mm(bk, wg2[:, kc, jj * 128:(jj + 1) * 128], hT[:, kc, sl], kc == 0, kc == 7) for kc in range(8)],
                                 reads=[wg2T, hT_T], writes=[bkT])
                          pp = k % 2
                          k += 1
                          cx.run("act", [lambda e, bk=bk, pp=pp: e.activation(out=sgf[pp], in_=bk, func=AF.Silu)], reads=[bkT], writes=[sgf_T[pp]])
                          cx.run("dve", [lambda e, pp=pp, jj=jj, g=g, sl=sl: e.tensor_tensor(out=ffT[:, g * 4 + jj, sl], in0=sgf[pp], in1=a_sb[:, jj, sl], op=ALU.mult)],
                                 reads=[sgf_T[pp], asb_T], writes=[ff_T])
              for j in range(8):
                  w2v, w2T = wload(w2_d[l][:, j * 128:(j + 1) * 128], NFT, 128)
                  for nb in range(3):
                      sl = slice(nb * 512, (nb + 1) * 512)
                      v = 1 if nb < 2 else 0
                      bk, bkT = psum()
                      cx.run("pe", [mm(bk, w2v[:, kc, :], ffT[:, kc, sl], kc == 0, kc == NFT - 1) for kc in range(NFT)], reads=[w2T, ff_T], writes=[bkT])
                      cx.run("dve", [lambda e, bk=bk, j=j, sl=sl, v=v: e.scalar_tensor_tensor(
                          out=x[:, j, sl], in0=bk, scalar=modT[:, 40 + j, v:v + 1], in1=x[:, j, sl], op0=ALU.mult, op1=ALU.add)],
                          reads=[bkT, x_T, M_T[l]], writes=[x_T])
              aset(0)
              xb = abf(8 * 512).rearrange("p (a n) -> p a n", n=512)
              xq = abf(8 * 512).rearrange("p (a n) -> p a n", n=512)
              lnt = (af32(512), af32(512), af32(512))
              layer_norm_fm(1, [(0, 512), (512, 1024), (1024, 1536)], xb, xq, lnt)
              dbg_out("x2_l%d" % l, x, [128, 8, T], [x_T])

        try:
            layers()
        except _Stop:
            pass
        for kc in range(8):
            cx.dma("sp", yT_d[:, kc, :], x[:, kc, :], reads=[x_T])
        cx.final()

        with nc.Block() as block:
            def replay(name):
                def f(e):
                    for th in cx.prog[name]:
                        th(e)
                return f
            block.tensor(replay("pe"))
            block.scalar(replay("act"))
            block.vector(replay("dve"))
            block.gpsimd(replay("pool"))
            block.sync(replay("sp"))
    return nc


_IDX = np.concatenate([np.arange(0, 128, 2), np.arange(1, 128, 2)])
_IDXS = np.concatenate([np.arange(1, 128, 2), np.arange(0, 128, 2)])


def _prow(r):
    return (r // 4) * 32 + (r % 4)


def _in_cols():
    o = dict(mq=0, mk=512, mv=1024, mo=1536, mg=2048, ca=2064, cg=2576, rq=3088, rk=3600, rv=4112, rg=4624, gm=5136)
    cols = []
    for h in range(H):
        for nm in ("mq", "mk", "mv", "mo"):
            cols += list(range(o[nm] + h * 128, o[nm] + (h + 1) * 128))
    z = [-1] * 28
    mg = o["mg"]
    cols += [mg + 0 * 4 + h for h in range(4)] + z + [mg + 2 * 4 + h for h in range(4)]
    cols += [mg + 1 * 4 + h for h in range(4)] + z + [mg + 3 * 4 + h for h in range(4)]
    cols += list(range(o["ca"], o["ca"] + 512)) + list(range(o["cg"], o["cg"] + 512))
    for h in range(H):
        for nm in ("rq", "rk"):
            cols += list(o[nm] + h * 128 + _IDX) + list(o[nm] + h * 128 + _IDXS)
    for h in range(H):
        cols += list(range(o["rv"] + h * 128, o["rv"] + (h + 1) * 128)) + list(range(o["rg"] + h * 128, o["rg"] + (h + 1) * 128))
    cols += list(range(o["gm"], o["gm"] + 3072))
    cols = np.array(cols, dtype=np.int64)
    assert cols.shape[0] == NCOLS, cols.shape
    return cols


_NC_CACHE = {}


def kernel(x_prompt, x_sample, state_mlstm_C, state_mlstm_n, state_mlstm_m, state_ret_S, c, c_ctx,
           ada_w, ada_b, in_w, mlstm_gate_b, mlstm_norm_w, mlstm_out_w, conv_w, conv_b, conv_ln_w, conv_ln_b,
           conv_out_w, ret_decay, ret_norm_w, ret_out_w, out_w, ln1_w, ln1_b, ln2_w, ln2_b, ffn_w13, ffn_w2, _dbg=False):
    f32 = np.float32
    A = lambda a: np.ascontiguousarray(np.asarray(a, dtype=f32))
    x_prompt, x_sample = A(x_prompt), A(x_sample)
    sC, sn, smm, sS = A(state_mlstm_C), A(state_mlstm_n), A(state_mlstm_m), A(state_ret_S)
    c, c_ctx = A(c), A(c_ctx)
    in_w = A(in_w)
    cols = _in_cols()
    in_wp = np.zeros((L, D, NCOLS), f32)
    valid = cols >= 0
    in_wp[:, :, valid] = in_w[:, :, cols[valid]]
    gbias = A(mlstm_gate_b)
    gb = np.zeros((L, 36, 2), f32)
    for h in range(4):
        gb[:, h, 0] = gbias[:, 0, h]; gb[:, h, 1] = gbias[:, 1, h]
        gb[:, 32 + h, 0] = gbias[:, 2, h]; gb[:, 32 + h, 1] = gbias[:, 3, h]
    convp = np.zeros((L, 128, 4, 34), f32)
    cw = A(conv_w)
    convp[:, :, :, 0:31] = cw.reshape(L, CONV_K, 4, 128).transpose(0, 3, 2, 1)
    convp[:, :, :, 31] = A(conv_b).reshape(L, 4, 128).transpose(0, 2, 1)
    convp[:, :, :, 32] = A(conv_ln_w).reshape(L, 4, 128).transpose(0, 2, 1)
    convp[:, :, :, 33] = A(conv_ln_b).reshape(L, 4, 128).transpose(0, 2, 1)
    lnp = np.zeros((L, 128, 4, 8), f32)
    for i, a in enumerate((ln1_w, ln1_b, ln2_w, ln2_b)):
        lnp[:, :, i, :] = A(a).reshape(L, 8, 128).transpose(0, 2, 1)
    ada_bT = np.ascontiguousarray(A(ada_b).reshape(L, 48, 128).transpose(0, 2, 1))
    rdec = A(ret_decay).reshape(L, 1, 8)
    mnw = A(mlstm_norm_w).reshape(L, 1, 512)
    rnw = A(ret_norm_w).reshape(L, 1, 512)
    cst = np.zeros((128, 1024), f32)
    jj, ii = np.meshgrid(np.arange(128), np.arange(128), indexing="ij")
    cst[:, 0:128] = np.eye(128, dtype=f32)
    cst[:, 128:256] = np.where(jj <= ii, 0.0, NEG)
    cst[:, 256:384] = np.where(jj >= ii, 0.0, NEG)
    cst[:, 384:512] = np.maximum(ii - jj, 0)
    cst[:, 512:640] = np.maximum(jj - ii, 0)
    cst[:, 640:768] = (jj <= ii)
    cst[:, 768:896] = (jj >= ii)
    p = np.arange(128)
    cst[:, 896] = p + 1; cst[:, 897] = 128 - p; cst[:, 898] = 127 - p; cst[:, 899] = p; cst[:, 900] = 128
    sel = np.zeros((36, 8, 128), f32)
    for r in range(8):
        sel[_prow(r), r, :] = 1.0
    t = np.arange(1024)
    rows = (t // 64).astype(f32); colsg = (t % 64).astype(f32)
    freqs = (np.float32(10000.0) ** (-np.arange(32, dtype=f32) / np.float32(32))).astype(f32)
    ang = np.concatenate([rows[:, None] * freqs[None, :], colsg[:, None] * freqs[None, :]], -1).astype(f32)
    cs_lat = np.concatenate([np.cos(ang).T, np.cos(ang).T], 0).astype(f32)
    sn_lat = np.concatenate([-np.sin(ang).T, np.sin(ang).T], 0).astype(f32)
    cs_id = np.ones((128, 1024), f32); sn_id = np.zeros((128, 1024), f32)

    shared = dict(cst=cst, sel=sel, ada_w=A(ada_w), ada_bT=ada_bT, in_wp=in_wp, gb=gb, mnw=mnw, rnw=rnw,
                  mlstm_out_w=A(mlstm_out_w), conv_out_w=A(conv_out_w), ret_out_w=A(ret_out_w), convp=convp, rdec=rdec,
                  out_w=A(out_w), lnp=lnp, ffn_w13=A(ffn_w13), ffn_w2=A(ffn_w2))
    in_maps = []
    seg_prompt = []
    for core in range(8):
        if core < 4:
            xs = np.concatenate([x_sample[core], x_prompt[2 * core], x_prompt[2 * core + 1]], 0)
            seg_prompt.append({4: 2 * core, 5: 2 * core + 1})
            cvec = c[core]
            Cinit = np.concatenate([sC[core], sn[core][..., None]], -1)
            minit = np.zeros((L, 36, 1), f32)
            for h in range(4):
                minit[:, h, 0] = smm[core, :, 0, h]; minit[:, 32 + h, 0] = smm[core, :, 1, h]
            Sinit = sS[core][:, :, :, _IDX, :]
            chainv = 1.0
            rcs, rsn = cs_lat, sn_lat
        else:
            base = 8 + 6 * (core - 4)
            xs = np.concatenate([x_prompt[base + s] for s in range(6)], 0)
            seg_prompt.append({s: base + s for s in range(6)})
            cvec = c_ctx
            Cinit = np.zeros((L, 2, H, 128, 129), f32)
            minit = np.zeros((L, 36, 1), f32)
            Sinit = np.zeros((L, 2, H, 128, 128), f32)
            chainv = 0.0
            rcs, rsn = cs_id, sn_id
        xT = np.ascontiguousarray(xs.reshape(T, 8, 128).transpose(2, 1, 0))
        cv = np.stack([c_ctx.reshape(8, 128).T, cvec.reshape(8, 128).T], -1)
        m = dict(shared)
        m.update(xT=xT, cv=np.ascontiguousarray(cv, dtype=f32), chain=np.full((128, 1), chainv, f32),
                 Cinit=np.ascontiguousarray(Cinit, dtype=f32), minit=minit, Sinit=np.ascontiguousarray(Sinit, dtype=f32),
                 ropeCS=rcs, ropeSN=rsn)
        in_maps.append(m)

    key = bool(_dbg)
    nc = build(dbg=key)
    res = run_bass_kernel_spmd(nc, in_maps, core_ids=list(range(8)))
    R = res.results
    y_prompt = np.zeros((32, 256, D), f32)
    y_sample = np.zeros((4, 1024, D), f32)
    new_C = np.zeros((32, L, 2, H, 128, 128), f32)
    new_n = np.zeros((32, L, 2, H, 128), f32)
    new_m = np.zeros((32, L, 2, H), f32)
    new_S = np.zeros((32, L, 2, H, 128, 128), f32)
    for core in range(8):
        r = R[core]
        yt = np.asarray(r["yT"]).transpose(2, 1, 0).reshape(T, D)
        if core < 4:
            y_sample[core] = yt[0:1024]
        Cf = np.asarray(r["Cfin"]); mf = np.asarray(r["mfin"]); Sf = np.asarray(r["Sfin"])
        for s, b in seg_prompt[core].items():
            y_prompt[b] = yt[s * 256:(s + 1) * 256]
            new_C[b] = Cf[:, s, :, :, :, 0:128]
            new_n[b] = Cf[:, s, :, :, :, 128]
            for h in range(4):
                new_m[b, :, 0, h] = mf[:, h, 2 * s + 1]
                new_m[b, :, 1, h] = mf[:, 32 + h, 2 * s]
            Su = np.empty((L, 2, H, 128, 128), f32)
            Su[:, :, :, _IDX, :] = Sf[:, s]
            new_S[b] = Su
    if _dbg:
        return (y_prompt, y_sample, new_C, new_n, new_m, new_S), R
    return (y_prompt, y_sample, new_C, new_n, new_m, new_S)
```

```python
import math
import itertools
import numpy as np
import concourse.bass as bass
import concourse.mybir as mybir
from concourse.bass_utils import run_bass_kernel_spmd
from concourse.ap import AP

F32 = mybir.dt.float32
BF16 = mybir.dt.bfloat16
AF = mybir.ActivationFunctionType
ALU = mybir.AluOpType
AX = mybir.AxisListType

D = 1024
L = 2
T = 1536
NCH = 12
NSEG = 6
H = 4
HD = 128
FF = 2816
NFT = 22
CONV_K = 31
EPS = 1e-5
ALPHA = (2.0 * L) ** 0.25
KS = HD ** -0.5
LNKS = math.log(KS)
NEG = -30000.0
NCOLS = 9288
OFF_GATE = 2048
OFF_CA = 2120
OFF_CG = 2632
OFF_RET = 3144
OFF_RVG = 5192
OFF_GM = 6216

DEBUG = {}
STOP = None


class _Stop(Exception):
    pass


def phase_end(name):
    if STOP == name:
        raise _Stop()


def rev(ap):
    a = [list(x) for x in ap.ap]
    step, n = a[-1]
    off = ap.offset + step * (n - 1)
    a[-1] = [-step, n]
    return AP(ap.tensor, off, a)


import types


def _snap(th):
    cl = th.__closure__
    if not cl:
        return th
    cells = []
    for c in cl:
        try:
            cells.append(types.CellType(c.cell_contents))
        except ValueError:
            cells.append(c)
    return types.FunctionType(th.__code__, th.__globals__, th.__name__, th.__defaults__, tuple(cells))


class TT:
    __slots__ = ("name", "w", "r")

    def __init__(self, name=""):
        self.name = name
        self.w = None
        self.r = {}


class Ctx:
    ENG = ["pe", "act", "dve", "pool", "sp"]

    def __init__(self, nc, sems):
        self.nc = nc
        self.sems = sems
        self.prog = {e: [] for e in self.ENG}
        self.cnt = {e: 0 for e in self.ENG}
        self.seen = {e: {} for e in self.ENG}
        self.dkeys = [k for k in sems if k[0] == "d" and k[1:].isdigit()]
        self.wkeys = [k for k in sems if k[0] == "w" and k[1:].isdigit()]
        self.dtot = {k: 0 for k in sems}
        self.dn = 0
        self.wn = 0

    def _wait(self, eng, key, val):
        if val <= 0 or self.seen[eng].get(key, 0) >= val:
            return
        self.seen[eng][key] = val
        sem = self.sems[key]
        self.prog[eng].append(lambda e, sem=sem, val=val: e.wait_ge(sem, val))

    def _deps(self, eng, reads, writes):
        for t in reads:
            if t.w is not None:
                self._wait_dep(eng, t.w)
        for t in writes:
            if t.w is not None:
                self._wait_dep(eng, t.w)
            for k, v in t.r.items():
                self._wait_dep(eng, (k, v))

    def _wait_dep(self, eng, dep):
        k, v = dep
        if k == "pe" and eng == "pe":
            return
        self._wait(eng, k, v)

    def run(self, eng, thunks, reads=(), writes=()):
        if not isinstance(thunks, (list, tuple)):
            thunks = [thunks]
        thunks = [_snap(t) for t in thunks]
        self._deps(eng, reads, writes)
        sem = self.sems[eng]
        n = len(thunks)
        for i, th in enumerate(thunks):
            if i == n - 1:
                self.prog[eng].append(lambda e, th=th, sem=sem: th(e).then_inc(sem, 1))
            else:
                self.prog[eng].append(th)
        self.cnt[eng] += 1
        c = self.cnt[eng]
        for t in reads:
            t.r[eng] = c
        for t in writes:
            t.w = (eng, c)
            t.r = {}

    def dma(self, q, out, in_, reads=(), writes=()):
        if q == "pool":
            key = self.wkeys[self.wn % len(self.wkeys)]
            self.wn += 1
        else:
            key = self.dkeys[self.dn % len(self.dkeys)]
            self.dn += 1
        prev = self.dtot[key]
        self._wait(q, key, prev)
        self._deps(q, reads, writes)
        new = prev + 16
        self.dtot[key] = new
        sem = self.sems[key]
        self.prog[q].append(lambda e, out=out, in_=in_, sem=sem: e.dma_start(out=out, in_=in_).then_inc(sem, 16))
        for t in reads:
            t.r[key] = new
        for t in writes:
            t.w = (key, new)
            t.r = {}

    def barrier(self):
        engs = ["pe", "act", "dve", "sp"]
        for e in engs:
            for f in ["pe", "act", "dve"]:
                if f != e or e != "pe":
                    self._wait(e, f, self.cnt[f])
            for k in self.dkeys:
                self._wait(e, k, self.dtot[k])

    def final(self):
        for k in self.dkeys:
            self._wait("sp", k, self.dtot[k])
        for f in ["pe", "act", "dve"]:
            self._wait("sp", f, self.cnt[f])


def build(dbg=False):
    nc = bass.Bass("TRN2", target_bir_lowering=False)
    dram = {}

    def din(name, shape, dt=F32):
        dram[name] = nc.dram_tensor(name, list(shape), dt, kind="ExternalInput").ap()
        return dram[name]

    def dout(name, shape, dt=F32):
        dram[name] = nc.dram_tensor(name, list(shape), dt, kind="ExternalOutput").ap()
        return dram[name]

    xT_d = din("xT", [128, 8, T])
    cv_d = din("cv", [128, 8, 2])
    chain_d = din("chain", [128, 1])
    Cinit_d = din("Cinit", [L, 2, H, 128, 129])
    minit_d = din("minit", [L, 36, 1])
    Sinit_d = din("Sinit", [L, 2, H, 128, 128])
    ropeCS_d = din("ropeCS", [128, 1024])
    ropeSN_d = din("ropeSN", [128, 1024])
    cst_d = din("cst", [128, 1024])
    sel_d = din("sel", [36, 8, 128])
    ada_w_d = din("ada_w", [L, D, 6 * D])
    ada_b_d = din("ada_bT", [L, 128, 48])
    in_w_d = din("in_wp", [L, D, NCOLS])
    gb_d = din("gb", [L, 36, 2])
    mnw_d = din("mnw", [L, 1, 512])
    rnw_d = din("rnw", [L, 1, 512])
    mow_d = din("mlstm_out_w", [L, 512, D])
    cow_d = din("conv_out_w", [L, 512, D])
    row_d = din("ret_out_w", [L, 512, D])
    convp_d = din("convp", [L, 128, 4, 34])
    rdec_d = din("rdec", [L, 1, 8])
    outw_d = din("out_w", [L, D, D])
    lnp_d = din("lnp", [L, 128, 4, 8])
    w13_d = din("ffn_w13", [L, D, 2 * FF])
    w2_d = din("ffn_w2", [L, FF, D])

    yT_d = dout("yT", [128, 8, T])
    Cfin_d = dout("Cfin", [L, NSEG, 2, H, 128, 129])
    mfin_d = dout("mfin", [L, 36, 12])
    Sfin_d = dout("Sfin", [L, NSEG, 2, H, 128, 128])
    dbg_d = {}

    import contextlib
    es = contextlib.ExitStack()
    with es:
        def sb(name, shape, dt=F32):
            return es.enter_context(nc.sbuf_tensor("s_" + name, list(shape), dt))[:]

        x = sb("x", [128, 8, T])
        hT = sb("hT", [128, 8, T], BF16)
        NW = 3
        Wr = [sb("wr%d" % i, [128, 4096], BF16) for i in range(NW)]
        Wr_T = [TT("wr%d" % i) for i in range(NW)]
        cst = sb("cst", [128, 1024])
        sel = sb("sel", [36, 8, 128])
        identB = sb("identB", [128, 128], BF16)
        onesB = sb("onesB", [128, 128], BF16)
        chain = sb("chain", [128, 1])
        cvs = sb("cvs", [128, 8, 2], BF16)
        cvf = sb("cvf", [128, 8, 2])
        modTs = [sb("modT%d" % i, [128, 48, 2]) for i in range(L)]
        sc1ps = [sb("sc1p%d" % i, [128, 8, 2]) for i in range(L)]
        sc2ps = [sb("sc2p%d" % i, [128, 8, 2]) for i in range(L)]
        adabs = [sb("adab%d" % i, [128, 48]) for i in range(L)]
        M_T = [TT("mod%d" % i) for i in range(L)]
        gb = sb("gb", [36, 2])
        ngb = sb("ngb", [36, 1])
        minit = sb("minit", [36, 1])
        mnw = sb("mnw", [128, 512])
        rnw = sb("rnw", [128, 512])
        convp = sb("convp", [128, 4, 34])
        lnp = sb("lnp", [128, 4, 8])
        rdec = sb("rdec", [128, 8])
        lg = sb("lg", [128, 8])
        rcol = sb("rcol", [128, 8, 4])
        decT = sb("decT", [128, 8, 128])
        ones36 = sb("ones36", [36, 128])
        zeros36 = sb("zeros36", [36, 128])
        AR = 23168
        arena = sb("arena", [128, AR])
        ps = [es.enter_context(nc.psum_tensor("ps%d" % i, [128, 512], F32))[:] for i in range(8)]
        ps_T = [TT("ps%d" % i) for i in range(8)]

        keys = ["pe", "act", "dve", "pool", "sp"] + ["d%d" % i for i in range(8)] + ["w%d" % i for i in range(6)]
        sems = {k: es.enter_context(nc.semaphore(k)) for k in keys}
        cx = Ctx(nc, sems)
        G = TT("globals")
        x_T = TT("x")
        hT_T = TT("hT")

        identF = cst[:, 0:128]
        MASK = [cst[:, 128:256], cst[:, 256:384]]
        DIFF = [cst[:, 384:512], cst[:, 512:640]]
        M01 = [cst[:, 640:768], cst[:, 768:896]]
        POS = cst[:, 896:904]

        pstate = {"i": 0}

        def psum():
            i = pstate["i"] % 8
            pstate["i"] += 1
            return ps[i], ps_T[i]

        wstate = {"i": 0}

        def wload(src, kc, ncol):
            i = wstate["i"] % NW
            wstate["i"] += 1
            dst = Wr[i][:, 0:kc * ncol].rearrange("p (k n) -> p k n", n=ncol)
            cx.dma("pool", dst, src.rearrange("(k p) n -> p k n", p=128), writes=[Wr_T[i]])
            return dst, Wr_T[i]

        ast = {"o": 0}

        def aset(o):
            cx.barrier()
            ast["o"] = o

        def af32(n, shape=None):
            o = ast["o"]
            ast["o"] += n
            assert ast["o"] <= AR, ast["o"]
            v = arena[:, o:o + n]
            return v

        def abf(n):
            n2 = (n + 1) // 2
            return af32(n2).bitcast(BF16)

        def dbg_out(name, ap, shape, reads):
            if not dbg:
                return
            d = dout("dbg_" + name, shape, ap.dtype)
            DEBUG[name] = shape
            cx.dma("sp", d, ap, reads=reads)

        mm = lambda out, lhsT, rhs, st, sp: (lambda e: e.matmul(out, lhsT=lhsT, rhs=rhs, start=st, stop=sp))

        for kc in range(8):
            cx.dma("sp", x[:, kc, :], xT_d[:, kc, :], writes=[x_T])
        for dst, src in ((cst, cst_d), (sel, sel_d), (chain, chain_d), (cvf, cv_d)):
            cx.dma("sp", dst, src, writes=[G])
        cx.run("dve", [lambda e: e.memset(ones36, 1.0), lambda e: e.memset(zeros36, 0.0),
                       lambda e: e.memset(onesB, 1.0),
                       lambda e: e.tensor_copy(out=identB, in_=identF)],
               reads=[G], writes=[G])
        cx.run("act", [lambda e: e.activation(out=cvs, in_=cvf, func=AF.Silu)], reads=[G], writes=[G])
        for i in range(L):
            cx.dma("sp", adabs[i], ada_b_d[i], writes=[M_T[i]])

        def mod_groups(ll, gs):
            mT = modTs[ll]
            for g in gs:
                wv, wT = wload(ada_w_d[ll][:, g * 512:(g + 1) * 512], 8, 512)
                mb, mbT = psum()
                for jj in range(4):
                    cx.run("pe", [mm(mb[:, 2 * jj:2 * jj + 2], wv[:, kc, jj * 128:(jj + 1) * 128], cvs[:, kc, :], kc == 0, kc == 7) for kc in range(8)],
                           reads=[wT, G], writes=[mbT])
                j0 = g * 4
                cx.run("act", [lambda e, mb=mb, j0=j0: e.activation(out=mT[:, j0:j0 + 4, :].rearrange("p j v -> p (j v)"), in_=mb[:, 0:8], func=AF.Copy)],
                       reads=[mbT], writes=[M_T[ll]])
                cx.run("dve", [lambda e, j0=j0: e.tensor_tensor(out=mT[:, j0:j0 + 4, :], in0=mT[:, j0:j0 + 4, :],
                                                                in1=adabs[ll][:, j0:j0 + 4].unsqueeze(2).to_broadcast([128, 4, 2]), op=ALU.add)],
                       reads=[M_T[ll]], writes=[M_T[ll]])
                if g in (2, 3):
                    o = (g - 2) * 4
                    cx.run("dve", [lambda e, j0=j0, o=o: e.tensor_scalar(out=sc1ps[ll][:, o:o + 4, :], in0=mT[:, j0:j0 + 4, :], scalar1=1.0, scalar2=None, op0=ALU.add)],
                           reads=[M_T[ll]], writes=[M_T[ll]])
                if g in (8, 9):
                    o = (g - 8) * 4
                    cx.run("dve", [lambda e, j0=j0, o=o: e.tensor_scalar(out=sc2ps[ll][:, o:o + 4, :], in0=mT[:, j0:j0 + 4, :], scalar1=1.0, scalar2=None, op0=ALU.add)],
                           reads=[M_T[ll]], writes=[M_T[ll]])

        def layer_norm_fm(which, blocks, xb, xq, lnt):
            lw = lnp[:, 2 * which, :]
            lb = lnp[:, 2 * which + 1, :]
            xb_T, xq_T, lnt_T = TT("xb"), TT("xq"), TT("lnt")
            for (a_, b_) in blocks:
                sl = slice(a_, b_)
                cx.run("act", [lambda e, kc=kc: e.activation(out=xb[:, kc, :], in_=x[:, kc, sl], func=AF.Copy) for kc in range(8)],
                       reads=[x_T], writes=[xb_T])
                cx.run("act", [lambda e, kc=kc: e.activation(out=xq[:, kc, :], in_=x[:, kc, sl], func=AF.Square) for kc in range(8)],
                       reads=[x_T], writes=[xq_T])
                b1, b1T = psum()
                cx.run("pe", [mm(b1, onesB, xb[:, kc, :], kc == 0, kc == 7) for kc in range(8)], reads=[xb_T], writes=[b1T])
                b2, b2T = psum()
                cx.run("pe", [mm(b2, onesB, xq[:, kc, :], kc == 0, kc == 7) for kc in range(8)], reads=[xq_T], writes=[b2T])
                mean, msq, rstd = lnt
                cx.run("act", [lambda e: e.activation(out=mean, in_=b1, func=AF.Copy, scale=1.0 / D)], reads=[b1T], writes=[lnt_T])
                cx.run("dve", [lambda e: e.tensor_tensor(out=msq, in0=mean, in1=mean, op=ALU.mult)], reads=[lnt_T], writes=[lnt_T])
                cx.run("dve", [lambda e: e.scalar_tensor_tensor(out=msq, in0=b2, scalar=1.0 / D, in1=msq, op0=ALU.mult, op1=ALU.subtract)],
                       reads=[b2T, lnt_T], writes=[lnt_T])
                cx.run("dve", [lambda e: e.tensor_scalar(out=msq, in0=msq, scalar1=EPS, scalar2=None, op0=ALU.add)], reads=[lnt_T], writes=[lnt_T])
                cx.run("act", [lambda e: e.activation(out=rstd, in_=msq, func=AF.Sqrt)], reads=[lnt_T], writes=[lnt_T])
                cx.run("dve", [lambda e: e.reciprocal(out=rstd, in_=rstd)], reads=[lnt_T], writes=[lnt_T])
                cx.run("dve", [lambda e, kc=kc: e.tensor_tensor(out=x[:, kc, sl], in0=x[:, kc, sl], in1=mean, op=ALU.subtract) for kc in range(8)],
                       reads=[x_T, lnt_T], writes=[x_T])
                cx.run("dve", [lambda e, kc=kc: e.tensor_tensor(out=x[:, kc, sl], in0=x[:, kc, sl], in1=rstd, op=ALU.mult) for kc in range(8)],
                       reads=[x_T, lnt_T], writes=[x_T])
                cx.run("act", [lambda e, kc=kc: e.activation(out=x[:, kc, sl], in_=x[:, kc, sl], func=AF.Identity,
                                                             scale=lw[:, kc:kc + 1], bias=lb[:, kc:kc + 1]) for kc in range(8)],
                       reads=[x_T, G], writes=[x_T])

        lnks = sb("lnks", [128, 1])
        cx.run("dve", [lambda e: e.memset(lnks, LNKS)], writes=[G])

        def ln_stats(b1, b1T, b2, b2T, mean, msq, rstd, st_T, n):
            cx.run("act", [lambda e: e.activation(out=mean, in_=b1, func=AF.Copy, scale=1.0 / n)], reads=[b1T], writes=[st_T])
            cx.run("dve", [lambda e: e.tensor_tensor(out=msq, in0=mean, in1=mean, op=ALU.mult)], reads=[st_T], writes=[st_T])
            cx.run("dve", [lambda e: e.scalar_tensor_tensor(out=msq, in0=b2, scalar=1.0 / n, in1=msq, op0=ALU.mult, op1=ALU.subtract)],
                   reads=[b2T, st_T], writes=[st_T])
            cx.run("dve", [lambda e: e.tensor_scalar(out=msq, in0=msq, scalar1=EPS, scalar2=None, op0=ALU.add)], reads=[st_T], writes=[st_T])
            cx.run("act", [lambda e: e.activation(out=rstd, in_=msq, func=AF.Sqrt)], reads=[st_T], writes=[st_T])
            cx.run("dve", [lambda e: e.reciprocal(out=rstd, in_=rstd)], reads=[st_T], writes=[st_T])

        def head_norm_out(src3, src_T, normw, h, gate3, gate_T, dstT, dst_T, scr, scr_T, hn_all, hn_Ts):
            stA = scr[:, 0:72].rearrange("p (c n) -> p c n", n=6)
            mvA = scr[:, 72:96].rearrange("p (c n) -> p c n", n=2)
            src_all = src3
            hn3 = hn_all.rearrange("p (c n) -> p c n", n=128)
            cx.run("dve", [lambda e, c=c: e.bn_stats(out=stA[:, c, :], in_=src3[:, c, :]) for c in range(12)], reads=list(src_T) + [scr_T], writes=[scr_T])
            cx.run("dve", [lambda e, c=c: e.bn_aggr(out=mvA[:, c, :], in_=stA[:, c, :]) for c in range(12)], reads=[scr_T], writes=[scr_T])
            rstd = mvA[:, :, 1]
            cx.run("dve", [lambda e: e.tensor_scalar(out=rstd, in0=rstd, scalar1=EPS, scalar2=None, op0=ALU.add)], reads=[scr_T], writes=[scr_T])
            cx.run("act", [lambda e: e.activation(out=rstd, in_=rstd, func=AF.Sqrt)], reads=[scr_T], writes=[scr_T])
            cx.run("dve", [lambda e: e.reciprocal(out=rstd, in_=rstd)], reads=[scr_T], writes=[scr_T])
            cx.run("dve", [lambda e, c=c: e.tensor_scalar(out=src3[:, c, :], in0=src3[:, c, :], scalar1=mvA[:, c, 0:1], scalar2=mvA[:, c, 1:2],
                                                          op0=ALU.subtract, op1=ALU.mult) for c in range(12)], reads=list(src_T) + [scr_T], writes=list(src_T))
            cx.run("dve", [lambda e, c=c: e.tensor_tensor(out=src3[:, c, :], in0=src3[:, c, :], in1=normw[:, h * 128:(h + 1) * 128], op=ALU.mult) for c in range(12)],
                   reads=list(src_T) + [G], writes=list(src_T))
            cx.run("dve", [lambda e: e.tensor_tensor(out=hn3, in0=src3, in1=gate3, op=ALU.mult)], reads=list(src_T) + [gate_T] + list(hn_Ts), writes=list(hn_Ts))
            for c in range(12):
                bk, bkT = psum()
                bkb = bk.bitcast(BF16)
                cx.run("pe", [lambda e, bkb=bkb, c=c: e.transpose(out=bkb[:, 0:128], in_=hn3[:, c, :], identity=identB)], reads=list(hn_Ts) + [G], writes=[bkT])
                cx.run("act", [lambda e, bkb=bkb, c=c: e.activation(out=dstT[:, h, c * 128:(c + 1) * 128], in_=bkb[:, 0:128], func=AF.Copy)], reads=[bkT], writes=[dst_T])

        def layers():
          for l in range(L):
              phase_end('pro%d' % l)
              aset(0)
              for dst, src in ((gb, gb_d[l]), (minit, minit_d[l]), (convp, convp_d[l]), (lnp, lnp_d[l]),
                               (mnw, mnw_d[l].partition_broadcast(128)), (rnw, rnw_d[l].partition_broadcast(128)),
                               (rdec, rdec_d[l].partition_broadcast(128))):
                  cx.dma("sp", dst, src, writes=[G])
              cx.run("dve", [lambda e: e.tensor_scalar(out=ngb, in0=gb[:, 1:2], scalar1=-1.0, scalar2=None, op0=ALU.mult)], reads=[G], writes=[G])
              cx.run("act", [lambda e: e.activation(out=lg, in_=rdec, func=AF.Exp, scale=-1.0)], reads=[G], writes=[G])
              cx.run("act", [lambda e: e.activation(out=lg, in_=lg, func=AF.Ln, bias=1.0)], reads=[G], writes=[G])
              cx.run("dve", [lambda e: e.tensor_scalar(out=lg, in0=lg, scalar1=-1.0, scalar2=None, op0=ALU.mult)], reads=[G], writes=[G])
              for r in range(8):
                  d = r // 4
                  lgc = lg[:, r:r + 1]
                  cx.run("act", [lambda e, r=r, d=d, lgc=lgc: e.activation(out=decT[:, r, :], in_=DIFF[d], func=AF.Exp, scale=lgc)], reads=[G], writes=[G])
                  cx.run("dve", [lambda e, r=r, d=d: e.scalar_tensor_tensor(out=decT[:, r, :], in0=decT[:, r, :], scalar=KS, in1=M01[d],
                                                                           op0=ALU.mult, op1=ALU.mult)], reads=[G], writes=[G])
                  cx.run("act", [lambda e, r=r, d=d, lgc=lgc: e.activation(out=rcol[:, r, 0:1], in_=POS[:, d:d + 1], func=AF.Exp, scale=lgc),
                                 lambda e, r=r, d=d, lgc=lgc: e.activation(out=rcol[:, r, 1:2], in_=POS[:, 2 + d:3 + d], func=AF.Exp, scale=lgc),
                                 lambda e, r=r, d=d, lgc=lgc: e.activation(out=rcol[:, r, 2:3], in_=POS[:, 4:5], func=AF.Exp, scale=lgc)],
                         reads=[G], writes=[G])
                  cx.run("dve", [lambda e, r=r: e.tensor_scalar(out=rcol[:, r, 1:2], in0=rcol[:, r, 1:2], scalar1=KS, scalar2=None, op0=ALU.mult)],
                         reads=[G], writes=[G])

              phase_end('small%d' % l)
              modT, sc1p, sc2p = modTs[l], sc1ps[l], sc2ps[l]
              if l == 0:
                  mod_groups(0, [0, 1, 2, 3])
              phase_end('modmm%d' % l)

              def modulate(scp, shoff):
                  ths = []
                  for kc in range(8):
                      for (a, b, v) in ((0, 1024, 1), (1024, T, 0)):
                          ths.append(lambda e, kc=kc, a=a, b=b, v=v: e.tensor_scalar(
                              out=hT[:, kc, a:b], in0=x[:, kc, a:b], scalar1=scp[:, kc, v:v + 1],
                              scalar2=modT[:, shoff + kc, v:v + 1], op0=ALU.mult, op1=ALU.add))
                  cx.run("dve", ths, reads=[x_T, M_T[l]], writes=[hT_T])

              dbg_out("modT_l%d" % l, modT, [128, 48, 2], [M_T[l]])
              phase_end("mod%d" % l)
              modulate(sc1p, 0)
              phase_end("h%d" % l)
              dbg_out("h_l%d" % l, hT, [128, 8, T], [hT_T])

              aset(3072)
              h_aT = arena[:, 0:3072].bitcast(BF16).rearrange("p (h t) -> p h t", t=T)
              ubT = arena[:, 3072:6144].bitcast(BF16).rearrange("p (h t) -> p h t", t=T)
              h_cT = arena[:, 6144:9216].bitcast(BF16).rearrange("p (h t) -> p h t", t=T)
              merged = arena[:, 9216:15360].bitcast(BF16).rearrange("p (h t) -> p h t", t=T)
              haT_T = TT("h_aT"); ub_T = TT("ubT"); hcT_T = TT("h_cT"); mg_T = TT("merged")
              rIG = af32(T); rA = af32(T); rP = af32(T); rCM = af32(T)
              R_T = TT("rows")
              sm = af32(96).rearrange("p (a c) -> p a c", c=12)
              SM_T = TT("sm")
              cols = af32(3 * 432).rearrange("p (q n) -> p q n", n=432)
              COL_T = TT("cols")
              cbc = af32(96)
              qT = abf(T); kT = abf(T)
              qk_T = TT("qk")
              vext = abf(12 * 130).rearrange("p (c n) -> p c n", n=130)
              v_T = TT("vext")
              osig = abf(T).rearrange("p (c n) -> p c n", n=128)
              o_T = TT("osig")
              hsum = af32(T).rearrange("p (c n) -> p c n", n=128)
              hs_T = [TT("hs%d" % c) for c in range(12)]
              kw = [abf(T).rearrange("p (c n) -> p c n", n=128) for _ in range(2)]
              kw_T = [[TT("kw") for c in range(12)] for _ in range(2)]
              Cx = [[af32(130) for _ in range(2)] for _ in range(2)]
              Cx_T = [[TT("cx"), TT("cx")] for _ in range(2)]
              sc2 = af32(8)
              dmg = [af32(512)] * 2
              dmg_T = [TT("dmg0")] * 2
              sT_all = abf(24 * 128)
              sTa_T = [TT("sTa%d" % i) for i in range(6)]
              Cb_all = [abf(12 * 130).rearrange("p (s n) -> p s n", n=130) for _ in range(2)]
              CbA_T = [[TT("cba") for _ in range(12)] for _ in range(2)]
              Bs2 = [af32(130) for _ in range(2)]
              Bs2_T = [TT("bs2a"), TT("bs2b")]
              tot_all = af32(12 * 130).rearrange("p (c n) -> p c n", n=130)
              tota_T = TT("tot_all")
              dn12 = af32(12)
              dn_T = TT("dn12")

              wg, wgT = wload(in_w_d[l][:, OFF_GATE:OFF_GATE + 72], 8, 72)
              cx.run("dve", [lambda e: e.memset(rP[0:36, :], 0.0), lambda e: e.memset(rCM[0:36, :], 0.0)], writes=[R_T])
              for which in (0, 1):
                  for nb in range(3):
                      bk, bkT = psum()
                      sl = slice(nb * 512, (nb + 1) * 512)
                      cx.run("pe", [mm(bk[0:36, :], wg[:, kc, which * 36:(which + 1) * 36], hT[:, kc, sl], kc == 0, kc == 7) for kc in range(8)],
                             reads=[wgT, hT_T], writes=[bkT])
                      if which == 0:
                          cx.run("act", [lambda e, bk=bk, sl=sl: e.activation(out=rIG[0:36, sl], in_=bk[0:36, :], func=AF.Identity, bias=gb[:, 0:1])],
                                 reads=[bkT, G], writes=[R_T])
                      else:
                          cx.run("act", [lambda e, bk=bk, sl=sl: e.activation(out=rA[0:36, sl], in_=bk[0:36, :], func=AF.Exp, scale=-1.0, bias=ngb[:, 0:1])],
                                 reads=[bkT, G], writes=[R_T])
              cx.run("act", [lambda e: e.activation(out=rA[0:36, :], in_=rA[0:36, :], func=AF.Ln, bias=1.0)], reads=[R_T], writes=[R_T])
              ths = []
              for c in range(12):
                  cs = slice(c * 128, (c + 1) * 128)
                  ths.append(lambda e, cs=cs: e.tensor_tensor_scan(out=rP[0:4, cs], data0=ones36[0:4, :], data1=rA[0:4, cs], initial=0.0,
                                                                   op0=ALU.mult, op1=ALU.add))
                  ths.append(lambda e, cs=cs: e.tensor_tensor_scan(out=rev(rP[32:36, cs]), data0=ones36[32:36, :], data1=rev(rA[32:36, cs]),
                                                                   initial=0.0, op0=ALU.mult, op1=ALU.add))
              cx.run("dve", ths, reads=[R_T], writes=[R_T])
              cx.run("dve", [lambda e: e.tensor_tensor(out=rA[0:36, :], in0=rIG[0:36, :], in1=rP[0:36, :], op=ALU.add)], reads=[R_T], writes=[R_T])
              ths = []
              for c in range(12):
                  cs = slice(c * 128, (c + 1) * 128)
                  ths.append(lambda e, cs=cs: e.tensor_tensor_scan(out=rCM[0:4, cs], data0=zeros36[0:4, :], data1=rA[0:4, cs], initial=-1e30,
                                                                   op0=ALU.add, op1=ALU.max))
                  ths.append(lambda e, cs=cs: e.tensor_tensor_scan(out=rev(rCM[32:36, cs]), data0=zeros36[32:36, :], data1=rev(rA[32:36, cs]),
                                                                   initial=-1e30, op0=ALU.add, op1=ALU.max))
              cx.run("dve", ths, reads=[R_T], writes=[R_T])
              CM3 = rCM.rearrange("p (c t) -> p c t", t=128)
              P3 = rP.rearrange("p (c t) -> p c t", t=128)
              A3 = rA.rearrange("p (c t) -> p c t", t=128)
              IG3 = rIG.rearrange("p (c t) -> p c t", t=128)
              cx.run("dve", [lambda e: e.memset(sm[0:36, :, :], 0.0)], writes=[SM_T])
              cx.run("dve", [lambda e: e.tensor_copy(out=sm[0:4, 0, :], in_=CM3[0:4, :, 127]),
                             lambda e: e.tensor_copy(out=sm[32:36, 0, :], in_=CM3[32:36, :, 0]),
                             lambda e: e.tensor_copy(out=sm[0:4, 1, :], in_=P3[0:4, :, 127]),
                             lambda e: e.tensor_copy(out=sm[32:36, 1, :], in_=P3[32:36, :, 0])], reads=[R_T, SM_T], writes=[SM_T])
              for d, r0 in ((0, 0), (1, 32)):
                  rs = slice(r0, r0 + 4)
                  order = list(range(12)) if d == 0 else list(range(11, -1, -1))
                  prev = None
                  for c in order:
                      s = c // 2
                      start = (c % 2 == 0) if d == 0 else (c % 2 == 1)
                      m0c = sm[rs, 2, c:c + 1]
                      if start:
                          if (d == 0 and s == 0) or (d == 1 and s == 3):
                              th = lambda e, m0c=m0c, rs=rs: e.tensor_copy(out=m0c, in_=minit[rs, :])
                          elif (d == 0 and s <= 3) or (d == 1 and s <= 2):
                              th = lambda e, m0c=m0c, rs=rs, prev=prev: e.tensor_scalar(out=m0c, in0=sm[rs, 4, prev:prev + 1], scalar1=chain[rs, :],
                                                                                       scalar2=None, op0=ALU.mult)
                          else:
                              th = lambda e, m0c=m0c: e.memset(m0c, 0.0)
                      else:
                          th = lambda e, m0c=m0c, rs=rs, prev=prev: e.tensor_copy(out=m0c, in_=sm[rs, 4, prev:prev + 1])
                      cx.run("dve", [th], reads=[SM_T, G], writes=[SM_T])
                      cx.run("dve", [lambda e, rs=rs, c=c: e.tensor_tensor(out=sm[rs, 3, c:c + 1], in0=sm[rs, 2, c:c + 1], in1=sm[rs, 0, c:c + 1], op=ALU.max)],
                             reads=[SM_T], writes=[SM_T])
                      cx.run("dve", [lambda e, rs=rs, c=c: e.tensor_tensor(out=sm[rs, 4, c:c + 1], in0=sm[rs, 3, c:c + 1], in1=sm[rs, 1, c:c + 1], op=ALU.subtract)],
                             reads=[SM_T], writes=[SM_T])
                      prev = c
              cx.dma("sp", mfin_d[l], sm[0:36, 4, :], reads=[SM_T])
              cx.run("dve", [lambda e: e.tensor_tensor(out=sm[0:36, 6, :], in0=sm[0:36, 3, :], in1=sm[0:36, 2, :], op=ALU.subtract)], reads=[SM_T], writes=[SM_T])
              cx.run("act", [lambda e: e.activation(out=sm[0:36, 5, :], in_=sm[0:36, 6, :], func=AF.Exp, scale=-1.0)], reads=[SM_T], writes=[SM_T])
              m0b = sm[0:36, 2, :].unsqueeze(2).to_broadcast([36, 12, 128])
              mxb = sm[0:36, 3, :].unsqueeze(2).to_broadcast([36, 12, 128])
              cx.run("dve", [lambda e: e.tensor_tensor(out=CM3[0:36], in0=CM3[0:36], in1=m0b, op=ALU.max)], reads=[R_T, SM_T], writes=[R_T])
              cx.run("dve", [lambda e: e.tensor_scalar(out=rCM[0:36, :], in0=rCM[0:36, :], scalar1=-1.0, scalar2=None, op0=ALU.mult)], reads=[R_T], writes=[R_T])
              dbg_out("rA_l%d" % l, rA[0:36, :], [36, T], [R_T])
              dbg_out("rNG_l%d" % l, rCM[0:36, :], [36, T], [R_T])

              def cols_from(q, rows):
                  bk, bkT = psum()
                  cx.run("pe", [lambda e, c=c, bk=bk: e.transpose(out=bk[:, c * 36:(c + 1) * 36], in_=rows[0:36, c * 128:(c + 1) * 128],
                                                                  identity=identF[0:36, 0:36]) for c in range(12)],
                         reads=[R_T, G], writes=[bkT])
                  cx.run("act", [lambda e, bk=bk: e.activation(out=cols[:, q, :], in_=bk[:, 0:432], func=AF.Copy)], reads=[bkT], writes=[COL_T])

              cx.run("dve", [lambda e: e.tensor_tensor(out=IG3[0:36], in0=CM3[0:36], in1=m0b, op=ALU.add)], reads=[R_T, SM_T], writes=[R_T])
              cx.run("act", [lambda e: e.activation(out=rIG[0:36, :], in_=rIG[0:36, :], func=AF.Exp)], reads=[R_T], writes=[R_T])
              cols_from(0, rIG)
              cx.run("dve", [lambda e: e.tensor_tensor(out=rP[0:36, :], in0=rP[0:36, :], in1=rCM[0:36, :], op=ALU.add)], reads=[R_T], writes=[R_T])
              cx.run("act", [lambda e: e.activation(out=rP[0:36, :], in_=rP[0:36, :], func=AF.Exp)], reads=[R_T], writes=[R_T])
              cols_from(1, rP)
              cx.run("dve", [lambda e: e.tensor_tensor(out=IG3[0:36], in0=A3[0:36], in1=mxb, op=ALU.subtract)], reads=[R_T, SM_T], writes=[R_T])
              cx.run("dve", [lambda e: e.tensor_scalar(out=rIG[0:36, :], in0=rIG[0:36, :], scalar1=LNKS, scalar2=None, op0=ALU.add)], reads=[R_T], writes=[R_T])
              cx.run("act", [lambda e: e.activation(out=rIG[0:36, :], in_=rIG[0:36, :], func=AF.Exp)], reads=[R_T], writes=[R_T])
              cols_from(2, rIG)
              bk, bkT = psum()
              cx.run("pe", [mm(bk[:, r * 12:(r + 1) * 12], sel[:, r, :], sm[0:36, 5, :], True, True) for r in range(8)], reads=[SM_T, G], writes=[bkT])
              cx.run("act", [lambda e, bk=bk: e.activation(out=cbc, in_=bk[:, 0:96], func=AF.Copy)], reads=[bkT], writes=[COL_T])
              dbg_out("sm_l%d" % l, sm[0:36, :, :], [36, 8, 12], [SM_T])
              dbg_out("cols_l%d" % l, cols, [128, 3, 432], [COL_T])

              phase_end("gates%d" % l)
              prow = lambda r: (r // 4) * 32 + (r % 4)
              lnks_col = None

              WH = {}

              def proj_qk(h):
                  wv, wT = wload(in_w_d[l][:, h * 512:(h + 1) * 512], 8, 512)
                  WH[h] = (wv, wT)
                  for nb in range(3):
                      sl = slice(nb * 512, (nb + 1) * 512)
                      for which, dst in ((0, qT), (1, kT)):
                          bk, bkT = psum()
                          cx.run("pe", [mm(bk, wv[:, kc, which * 128:(which + 1) * 128], hT[:, kc, sl], kc == 0, kc == 7) for kc in range(8)],
                                 reads=[wT, hT_T], writes=[bkT])
                          cx.run("act", [lambda e, bk=bk, dst=dst, sl=sl: e.activation(out=dst[:, sl], in_=bk, func=AF.Copy)], reads=[bkT], writes=[qk_T])
                  for c in range(12):
                      cs = slice(c * 128, (c + 1) * 128)
                      bk, bkT = psum()
                      bkb = bk.bitcast(BF16)
                      cx.run("pe", [lambda e, bkb=bkb, cs=cs: e.transpose(out=bkb[:, 0:128], in_=kT[:, cs], identity=identB)], reads=[qk_T, G], writes=[bkT])
                      for d in range(2):
                          r = d * 4 + h
                          col = cols[:, 2, c * 36 + prow(r):c * 36 + prow(r) + 1]
                          cx.run("act", [lambda e, bkb=bkb, d=d, c=c, col=col: e.activation(out=kw[d][:, c, :], in_=bkb[:, 0:128], func=AF.Copy, scale=col)],
                                 reads=[bkT, COL_T], writes=[kw_T[d][c]])

              def proj_vo(h):
                  wv, wT = WH[h]
                  cx.run("dve", [lambda e: e.memset(vext[:, :, 128:130], 1.0)], writes=[v_T])
                  for c in range(12):
                      cs = slice(c * 128, (c + 1) * 128)
                      bk, bkT = psum()
                      cx.run("pe", [mm(bk[:, 0:256], hT[:, kc, cs], wv[:, kc, 256:512], kc == 0, kc == 7) for kc in range(8)], reads=[wT, hT_T], writes=[bkT])
                      cx.run("act", [lambda e, bk=bk, c=c: e.activation(out=vext[:, c, 0:128], in_=bk[:, 0:128], func=AF.Copy)], reads=[bkT], writes=[v_T])
                      cx.run("act", [lambda e, bk=bk, c=c: e.activation(out=osig[:, c, :], in_=bk[:, 128:256], func=AF.Sigmoid)], reads=[bkT], writes=[o_T])

              proj_qk(0)
              for h in range(H):
                  proj_vo(h)
                  orders = [list(range(12)), list(range(11, -1, -1))]
                  for g in range(6):
                      d = g // 3
                      r = d * 4 + h
                      b1, b1T = psum()
                      ths = []
                      css = []
                      for i in range(4):
                          c = orders[d][(g % 3) * 4 + i]
                          cs = slice(c * 128, (c + 1) * 128)
                          css.append(cs)
                          o = b1[:, i * 128:(i + 1) * 128]
                          ths += [mm(o, sel[:, r, :], rCM[0:36, cs], True, False), mm(o, rA[0:36, cs], sel[:, r, :], False, False),
                                  mm(o, identF, MASK[d], False, True)]
                      cx.run("pe", ths, reads=[R_T, G], writes=[b1T])
                      gp = g % 2
                      cx.run("act", [lambda e, b1=b1, gp=gp: e.activation(out=dmg[gp], in_=b1, func=AF.Exp, bias=lnks[:, 0:1])], reads=[b1T, G], writes=[dmg_T[gp]])
                      b2, b2T = psum()
                      cx.run("pe", [mm(b2[:, i * 128:(i + 1) * 128], kT[:, css[i]], qT[:, css[i]], True, True) for i in range(4)], reads=[qk_T], writes=[b2T])
                      cx.run("dve", [lambda e, b2=b2, gp=gp, g=g: e.tensor_tensor(out=sT_all[:, g * 512:(g + 1) * 512], in0=b2, in1=dmg[gp], op=ALU.mult)],
                             reads=[b2T, dmg_T[gp]], writes=[sTa_T[g]])

                  def rec(d, h=h):
                      r = d * 4 + h
                      cur = 0
                      Cxd, CxTd = Cx[d], Cx_T[d]
                      for step, c in enumerate(orders[d]):
                          s = c // 2
                          start = (c % 2 == 0) if d == 0 else (c % 2 == 1)
                          if start:
                              if (d == 0 and s == 0) or (d == 1 and s == 3):
                                  cx.dma("sp", Cxd[cur][:, 0:129], Cinit_d[l, d, h], writes=[CxTd[cur]])
                              elif (d == 0 and s <= 3) or (d == 1 and s <= 2):
                                  cx.run("dve", [lambda e, cur=cur: e.tensor_scalar(out=Cxd[cur][:, 0:129], in0=Cxd[cur][:, 0:129], scalar1=chain[:, 0:1],
                                                                                     scalar2=None, op0=ALU.mult)], reads=[CxTd[cur], G], writes=[CxTd[cur]])
                              else:
                                  cx.run("dve", [lambda e, cur=cur: e.memset(Cxd[cur][:, 0:129], 0.0)], writes=[CxTd[cur]])
                          cx.run("act", [lambda e, cur=cur, step=step: e.activation(out=Cb_all[d][:, step, 0:129], in_=Cxd[cur][:, 0:129], func=AF.Copy)],
                                 reads=[CxTd[cur]], writes=[CbA_T[d][step]])
                          bU, bUT = psum()
                          cx.run("pe", [mm(bU[:, 0:129], kw[d][:, c, :], vext[:, c, 0:129], True, True)], reads=[kw_T[d][c], v_T], writes=[bUT])
                          yield
                          nxt = 1 - cur
                          ccol = cbc[:, r * 12 + c:r * 12 + c + 1]
                          cx.run("dve", [lambda e, cur=cur, nxt=nxt: e.scalar_tensor_tensor(
                              out=Cxd[nxt][:, 0:129], in0=Cxd[cur][:, 0:129], scalar=ccol, in1=bU[:, 0:129], op0=ALU.mult, op1=ALU.add)],
                              reads=[bUT, CxTd[cur], COL_T], writes=[CxTd[nxt]])
                          cur = nxt
                          end = (c % 2 == 1) if d == 0 else (c % 2 == 0)
                          if end:
                              cx.dma("sp", Cfin_d[l, s, d, h], Cxd[cur][:, 0:129], reads=[CxTd[cur]])
                          yield

                  for _ in itertools.zip_longest(rec(0), rec(1)):
                      pass

                  for d in range(2):
                      r = d * 4 + h
                      pr = prow(r)
                      for step, c in enumerate(orders[d]):
                          cs = slice(c * 128, (c + 1) * 128)
                          p = d * 12 + step
                          bA, bAT = psum()
                          cx.run("pe", [mm(bA[:, 0:129], sT_all[:, p * 128:(p + 1) * 128], vext[:, c, 0:129], True, True),
                                        mm(bA[:, 256:385], qT[:, cs], Cb_all[d][:, step, 0:129], True, True)],
                                 reads=[sTa_T[p // 4], v_T, qk_T, CbA_T[d][step]], writes=[bAT])
                          wcol = cols[:, 0, c * 36 + pr:c * 36 + pr + 1]
                          bp = step % 2
                          cx.run("act", [lambda e, bA=bA, bp=bp, wcol=wcol: e.activation(out=Bs2[bp][:, 0:129], in_=bA[:, 256:385], func=AF.Copy, scale=wcol)],
                                 reads=[bAT, COL_T], writes=[Bs2_T[bp]])
                          cx.run("dve", [lambda e, bA=bA, bp=bp, c=c: e.tensor_tensor(out=tot_all[:, c, 0:129], in0=bA[:, 0:129], in1=Bs2[bp][:, 0:129], op=ALU.add)],
                                 reads=[bAT, Bs2_T[bp]], writes=[tota_T])
                      den = tot_all[:, :, 128]
                      ecols = cols[:, 1, :].rearrange("p (c n) -> p c n", n=36)[:, :, pr]
                      cx.run("dve", [lambda e, den=den: e.tensor_scalar(out=dn12, in0=den, scalar1=-1.0, scalar2=None, op0=ALU.mult)], reads=[tota_T], writes=[dn_T])
                      cx.run("dve", [lambda e, den=den: e.tensor_tensor(out=dn12, in0=dn12, in1=den, op=ALU.max)], reads=[tota_T, dn_T], writes=[dn_T])
                      cx.run("dve", [lambda e, ecols=ecols: e.tensor_tensor(out=dn12, in0=dn12, in1=ecols, op=ALU.max)], reads=[dn_T, COL_T], writes=[dn_T])
                      cx.run("dve", [lambda e: e.reciprocal(out=dn12, in_=dn12)], reads=[dn_T], writes=[dn_T])
                      if d == 0:
                          cx.run("act", [lambda e, c=c: e.activation(out=hsum[:, c, :], in_=tot_all[:, c, 0:128], func=AF.Copy, scale=dn12[:, c:c + 1])
                                         for c in range(12)], reads=[tota_T, dn_T], writes=hs_T)
                      else:
                          cx.run("dve", [lambda e, c=c: e.scalar_tensor_tensor(out=hsum[:, c, :], in0=tot_all[:, c, 0:128], scalar=dn12[:, c:c + 1],
                                                                               in1=hsum[:, c, :], op0=ALU.mult, op1=ALU.add) for c in range(12)],
                                 reads=[tota_T, dn_T] + hs_T, writes=hs_T)
                  if l == 0 and h == 0:
                      dbg_out("hsum_l0h0", hsum, [128, 12, 128], hs_T)
                  if h + 1 < H:
                      proj_qk(h + 1)
                  head_norm_out(hsum, hs_T, mnw, h, osig, o_T, h_aT, haT_T, tot_all.rearrange("p c n -> p (c n)"), tota_T, sT_all[:, 0:1536], sTa_T[0:3])
                  if l == 0:
                      mod_groups(0, [4 + 2 * h, 5 + 2 * h])
                  phase_end("mlstm%d_h%d" % (l, h))
              dbg_out("haT_l%d" % l, h_aT, [128, 4, T], [haT_T])

              phase_end("mlstm%d" % l)
              aset(6144)
              acc = af32(4 * T).rearrange("p (a t) -> p a t", t=T)
              acc_T = TT("acc")
              upad = [abf(6 * 286).rearrange("p (s n) -> p s n", n=286) for _ in range(2)]
              up_T = [TT("upad0"), TT("upad1")]
              Dg = [abf(31 * 128).rearrange("p (k n) -> p k n", n=128) for _ in range(2)]
              Dg_T = [TT("dg0"), TT("dg1")]
              sg = [af32(512) for _ in range(2)]
              sg_T = [TT("sg0"), TT("sg1")]
              cx.run("dve", [lambda e: e.memset(upad[0], 0.0), lambda e: e.memset(upad[1], 0.0)], writes=up_T)
              wa, waT = wload(in_w_d[l][:, OFF_CA:OFF_CA + 512], 8, 512)
              wgc, wgcT = wload(in_w_d[l][:, OFF_CG:OFF_CG + 512], 8, 512)
              kst = {"k": 0}

              def conv_in(ct):
                  up, upT, dg, dgT = upad[ct % 2], up_T[ct % 2], Dg[ct % 2], Dg_T[ct % 2]
                  cx.run("act", [lambda e, kk=kk, ct=ct, dg=dg: e.activation(out=dg[:, kk, :], in_=identF, func=AF.Copy, scale=convp[:, ct, kk:kk + 1])
                                 for kk in range(CONV_K)], reads=[G], writes=[dgT])
                  for nb in range(3):
                      sl = slice(nb * 512, (nb + 1) * 512)
                      ba, baT = psum()
                      cx.run("pe", [mm(ba, wa[:, kc, ct * 128:(ct + 1) * 128], hT[:, kc, sl], kc == 0, kc == 7) for kc in range(8)], reads=[waT, hT_T], writes=[baT])
                      bg, bgT = psum()
                      cx.run("pe", [mm(bg, wgc[:, kc, ct * 128:(ct + 1) * 128], hT[:, kc, sl], kc == 0, kc == 7) for kc in range(8)], reads=[wgcT, hT_T], writes=[bgT])
                      pp = kst["k"] % 2
                      kst["k"] += 1
                      cx.run("act", [lambda e, bg=bg, pp=pp: e.activation(out=sg[pp], in_=bg, func=AF.Sigmoid)], reads=[bgT], writes=[sg_T[pp]])
                      cx.run("dve", [lambda e, ba=ba, pp=pp, nb=nb, up=up, s2=s2: e.tensor_tensor(
                          out=up[:, 2 * nb + s2, 15:271], in0=ba[:, s2 * 256:(s2 + 1) * 256],
                          in1=sg[pp][:, s2 * 256:(s2 + 1) * 256], op=ALU.mult) for s2 in range(2)], reads=[baT, sg_T[pp]], writes=[upT])
                  ths = []
                  for s in (1, 2, 3):
                      ths.append(lambda e, s=s, up=up: e.tensor_scalar(out=up[:, s, 0:15], in0=up[:, s - 1, 256:271], scalar1=chain[:, 0:1], scalar2=None, op0=ALU.mult))
                  for s in (0, 1, 2):
                      ths.append(lambda e, s=s, up=up: e.tensor_scalar(out=up[:, s, 271:286], in0=up[:, s + 1, 15:30], scalar1=chain[:, 0:1], scalar2=None, op0=ALU.mult))
                  cx.run("dve", ths, reads=[upT, G], writes=[upT])

              def conv_mm(ct):
                  up, upT, dg, dgT = upad[ct % 2], up_T[ct % 2], Dg[ct % 2], Dg_T[ct % 2]
                  for s in range(6):
                      bk, bkT = psum()
                      cx.run("pe", [mm(bk[:, 0:256], dg[:, kk, :], up[:, s, kk:kk + 256], kk == 0, kk == CONV_K - 1) for kk in range(CONV_K)],
                             reads=[dgT, upT], writes=[bkT])
                      cx.run("act", [lambda e, bk=bk, ct=ct, s=s: e.activation(out=acc[:, ct, s * 256:(s + 1) * 256], in_=bk[:, 0:256], func=AF.Identity,
                                                                               bias=convp[:, ct, 31:32])], reads=[bkT, G], writes=[acc_T])

              conv_in(0)
              for ct in range(4):
                  if ct + 1 < 4:
                      conv_in(ct + 1)
                  conv_mm(ct)
              dbg_out("conv_l%d" % l, acc, [128, 4, T], [acc_T])
              cb = abf(4 * 256).rearrange("p (a n) -> p a n", n=256)
              cq = abf(4 * 256).rearrange("p (a n) -> p a n", n=256)
              cb_T = TT("cb"); cq_T = TT("cq")
              mean = af32(256); msq = af32(256); rstd = af32(256)
              st_T = TT("lnst")
              for nb in range(6):
                  sl = slice(nb * 256, (nb + 1) * 256)
                  cx.run("act", [lambda e, ct=ct, sl=sl: e.activation(out=cb[:, ct, :], in_=acc[:, ct, sl], func=AF.Copy) for ct in range(4)], reads=[acc_T], writes=[cb_T])
                  cx.run("act", [lambda e, ct=ct, sl=sl: e.activation(out=cq[:, ct, :], in_=acc[:, ct, sl], func=AF.Square) for ct in range(4)], reads=[acc_T], writes=[cq_T])
                  b1, b1T = psum()
                  cx.run("pe", [mm(b1[:, 0:256], onesB, cb[:, ct, :], ct == 0, ct == 3) for ct in range(4)], reads=[cb_T, G], writes=[b1T])
                  b2, b2T = psum()
                  cx.run("pe", [mm(b2[:, 0:256], onesB, cq[:, ct, :], ct == 0, ct == 3) for ct in range(4)], reads=[cq_T, G], writes=[b2T])
                  ln_stats(b1[:, 0:256], b1T, b2[:, 0:256], b2T, mean, msq, rstd, st_T, 512)
                  cx.run("dve", [lambda e, ct=ct, sl=sl: e.tensor_tensor(out=acc[:, ct, sl], in0=acc[:, ct, sl], in1=mean, op=ALU.subtract) for ct in range(4)],
                         reads=[acc_T, st_T], writes=[acc_T])
                  cx.run("dve", [lambda e, ct=ct, sl=sl: e.tensor_tensor(out=acc[:, ct, sl], in0=acc[:, ct, sl], in1=rstd, op=ALU.mult) for ct in range(4)],
                         reads=[acc_T, st_T], writes=[acc_T])
                  cx.run("act", [lambda e, ct=ct, sl=sl: e.activation(out=acc[:, ct, sl], in_=acc[:, ct, sl], func=AF.Identity,
                                                                      scale=convp[:, ct, 32:33], bias=convp[:, ct, 33:34]) for ct in range(4)], reads=[acc_T, G], writes=[acc_T])
                  cx.run("act", [lambda e, ct=ct, sl=sl: e.activation(out=ubT[:, ct, sl], in_=acc[:, ct, sl], func=AF.Silu) for ct in range(4)], reads=[acc_T], writes=[ub_T])
              dbg_out("ubT_l%d" % l, ubT, [128, 4, T], [ub_T])

              phase_end("conv%d" % l)
              aset(9216)
              ropeCS = af32(1024); ropeSN = af32(1024)
              RP_T = TT("rope")
              cx.dma("sp", ropeCS, ropeCS_d, writes=[RP_T])
              cx.dma("sp", ropeSN, ropeSN_d, writes=[RP_T])
              qT = abf(T); kT = abf(T)
              qk_T = TT("rqk")
              vv = abf(T).rearrange("p (c n) -> p c n", n=128)
              v_T = TT("rv")
              gsil = af32(T).rearrange("p (c n) -> p c n", n=128)
              o_T = TT("gsil")
              ysum = af32(T).rearrange("p (c n) -> p c n", n=128)
              hs_T = [TT("ys%d" % c) for c in range(12)]
              kz = [abf(T).rearrange("p (c n) -> p c n", n=128) for _ in range(2)]
              kz_T = [[TT("kz") for c in range(12)] for _ in range(2)]
              Sx = [[af32(128) for _ in range(2)] for _ in range(2)]
              Sx_T = [[TT("sx"), TT("sx")] for _ in range(2)]
              t1 = af32(512); t2 = af32(512)
              sT_all = abf(24 * 128)
              sTa_T = [TT("rsTa%d" % i) for i in range(6)]
              Sb_all = [abf(12 * 128).rearrange("p (s n) -> p s n", n=128) for _ in range(2)]
              SbA_T = [[TT("sba") for _ in range(12)] for _ in range(2)]
              Bs2 = [af32(128) for _ in range(2)]
              Bs2_T = [TT("rbs2a"), TT("rbs2b")]
              t_T = TT("rt")
              def rproj_qk(h):
                  wv, wT = wload(in_w_d[l][:, OFF_RET + h * 512:OFF_RET + (h + 1) * 512], 8, 512)
                  for nb in range(3):
                      sl = slice(nb * 512, (nb + 1) * 512)
                      for which, dst in ((0, qT), (1, kT)):
                          bk, bkT = psum()
                          cx.run("pe", [mm(bk, wv[:, kc, which * 256:which * 256 + 128], hT[:, kc, sl], kc == 0, kc == 7) for kc in range(8)],
                                 reads=[wT, hT_T], writes=[bkT])
                          if nb < 2:
                              bs_, bsT = psum()
                              cx.run("pe", [mm(bs_, wv[:, kc, which * 256 + 128:which * 256 + 256], hT[:, kc, sl], kc == 0, kc == 7) for kc in range(8)],
                                     reads=[wT, hT_T], writes=[bsT])
                              cx.run("dve", [lambda e, bk=bk, sl=sl: e.tensor_tensor(out=t1, in0=bk, in1=ropeCS[:, sl], op=ALU.mult)], reads=[bkT, RP_T, t_T], writes=[t_T])
                              cx.run("dve", [lambda e, bs_=bs_, sl=sl: e.tensor_tensor(out=t2, in0=bs_, in1=ropeSN[:, sl], op=ALU.mult)], reads=[bsT, RP_T, t_T], writes=[t_T])
                              cx.run("dve", [lambda e, dst=dst, sl=sl: e.tensor_tensor(out=dst[:, sl], in0=t1, in1=t2, op=ALU.add)], reads=[t_T], writes=[qk_T, t_T])
                          else:
                              cx.run("act", [lambda e, bk=bk, dst=dst, sl=sl: e.activation(out=dst[:, sl], in_=bk, func=AF.Copy)], reads=[bkT], writes=[qk_T])
                  for c in range(12):
                      cs = slice(c * 128, (c + 1) * 128)
                      bk, bkT = psum()
                      bkb = bk.bitcast(BF16)
                      cx.run("pe", [lambda e, bkb=bkb, cs=cs: e.transpose(out=bkb[:, 0:128], in_=kT[:, cs], identity=identB)], reads=[qk_T, G], writes=[bkT])
                      for d in range(2):
                          r = d * 4 + h
                          cx.run("act", [lambda e, bkb=bkb, d=d, c=c, r=r: e.activation(out=kz[d][:, c, :], in_=bkb[:, 0:128], func=AF.Copy, scale=rcol[:, r, 1:2])],
                                 reads=[bkT, G], writes=[kz_T[d][c]])

              def rproj_vg(h):
                  wv2, wT2 = wload(in_w_d[l][:, OFF_RVG + h * 256:OFF_RVG + (h + 1) * 256], 8, 256)
                  for c in range(12):
                      cs = slice(c * 128, (c + 1) * 128)
                      bk, bkT = psum()
                      cx.run("pe", [mm(bk[:, 0:256], hT[:, kc, cs], wv2[:, kc, :], kc == 0, kc == 7) for kc in range(8)], reads=[wT2, hT_T], writes=[bkT])
                      cx.run("act", [lambda e, bk=bk, c=c: e.activation(out=vv[:, c, :], in_=bk[:, 0:128], func=AF.Copy)], reads=[bkT], writes=[v_T])
                      cx.run("act", [lambda e, bk=bk, c=c: e.activation(out=gsil[:, c, :], in_=bk[:, 128:256], func=AF.Silu)], reads=[bkT], writes=[o_T])

              rproj_qk(0)
              for h in range(H):
                  rproj_vg(h)
                  orders = [list(range(12)), list(range(11, -1, -1))]
                  for g in range(6):
                      d = g // 3
                      r = d * 4 + h
                      b2, b2T = psum()
                      css = [slice(orders[d][(g % 3) * 4 + i] * 128, (orders[d][(g % 3) * 4 + i] + 1) * 128) for i in range(4)]
                      cx.run("pe", [mm(b2[:, i * 128:(i + 1) * 128], kT[:, css[i]], qT[:, css[i]], True, True) for i in range(4)], reads=[qk_T], writes=[b2T])
                      cx.run("dve", [lambda e, b2=b2, g=g, i=i, r=r: e.tensor_tensor(out=sT_all[:, g * 512 + i * 128:g * 512 + (i + 1) * 128],
                                                                                   in0=b2[:, i * 128:(i + 1) * 128], in1=decT[:, r, :], op=ALU.mult) for i in range(4)],
                             reads=[b2T, G], writes=[sTa_T[g]])

                  def rrec(d, h=h):
                      r = d * 4 + h
                      cur = 0
                      Sxd, SxTd = Sx[d], Sx_T[d]
                      for step, c in enumerate(orders[d]):
                          s = c // 2
                          start = (c % 2 == 0) if d == 0 else (c % 2 == 1)
                          if start:
                              if (d == 0 and s == 0) or (d == 1 and s == 3):
                                  cx.dma("sp", Sxd[cur], Sinit_d[l, d, h], writes=[SxTd[cur]])
                              elif (d == 0 and s <= 3) or (d == 1 and s <= 2):
                                  cx.run("dve", [lambda e, cur=cur: e.tensor_scalar(out=Sxd[cur], in0=Sxd[cur], scalar1=chain[:, 0:1],
                                                                                     scalar2=None, op0=ALU.mult)], reads=[SxTd[cur], G], writes=[SxTd[cur]])
                              else:
                                  cx.run("dve", [lambda e, cur=cur: e.memset(Sxd[cur], 0.0)], writes=[SxTd[cur]])
                          cx.run("act", [lambda e, cur=cur, step=step: e.activation(out=Sb_all[d][:, step, :], in_=Sxd[cur], func=AF.Copy)],
                                 reads=[SxTd[cur]], writes=[SbA_T[d][step]])
                          bU, bUT = psum()
                          cx.run("pe", [mm(bU[:, 0:128], kz[d][:, c, :], vv[:, c, :], True, True)], reads=[kz_T[d][c], v_T], writes=[bUT])
                          yield
                          nxt = 1 - cur
                          cx.run("dve", [lambda e, cur=cur, nxt=nxt: e.scalar_tensor_tensor(
                              out=Sxd[nxt], in0=Sxd[cur], scalar=rcol[:, r, 2:3], in1=bU[:, 0:128], op0=ALU.mult, op1=ALU.add)],
                              reads=[bUT, SxTd[cur], G], writes=[SxTd[nxt]])
                          cur = nxt
                          end = (c % 2 == 1) if d == 0 else (c % 2 == 0)
                          if end:
                              cx.dma("sp", Sfin_d[l, s, d, h], Sxd[cur], reads=[SxTd[cur]])
                          yield

                  for _ in itertools.zip_longest(rrec(0), rrec(1)):
                      pass

                  for d in range(2):
                      r = d * 4 + h
                      for step, c in enumerate(orders[d]):
                          cs = slice(c * 128, (c + 1) * 128)
                          p = d * 12 + step
                          bA, bAT = psum()
                          cx.run("pe", [mm(bA[:, 0:128], sT_all[:, p * 128:(p + 1) * 128], vv[:, c, :], True, True),
                                        mm(bA[:, 256:384], qT[:, cs], Sb_all[d][:, step, :], True, True)],
                                 reads=[sTa_T[p // 4], v_T, qk_T, SbA_T[d][step]], writes=[bAT])
                          bp = step % 2
                          cx.run("act", [lambda e, bA=bA, bp=bp, r=r: e.activation(out=Bs2[bp], in_=bA[:, 256:384], func=AF.Copy, scale=rcol[:, r, 0:1])],
                                 reads=[bAT, G], writes=[Bs2_T[bp]])
                          if d == 0:
                              cx.run("dve", [lambda e, bA=bA, bp=bp, c=c: e.tensor_tensor(out=ysum[:, c, :], in0=bA[:, 0:128], in1=Bs2[bp], op=ALU.add)],
                                     reads=[bAT, Bs2_T[bp]], writes=[hs_T[c]])
                          else:
                              cx.run("dve", [lambda e, bA=bA, bp=bp: e.tensor_tensor(out=Bs2[bp], in0=bA[:, 0:128], in1=Bs2[bp], op=ALU.add)],
                                     reads=[bAT, Bs2_T[bp]], writes=[Bs2_T[bp]])
                              cx.run("dve", [lambda e, bp=bp, c=c: e.tensor_tensor(out=ysum[:, c, :], in0=ysum[:, c, :], in1=Bs2[bp], op=ALU.add)],
                                     reads=[Bs2_T[bp], hs_T[c]], writes=[hs_T[c]])
                  if h + 1 < H:
                      rproj_qk(h + 1)
                  head_norm_out(ysum, hs_T, rnw, h, gsil, o_T, h_cT, hcT_T, t1, t_T, sT_all[:, 0:1536], sTa_T[0:3])
                  if l + 1 < L:
                      mod_groups(l + 1, [3 * h, 3 * h + 1, 3 * h + 2])
              dbg_out("hcT_l%d" % l, h_cT, [128, 4, T], [hcT_T])

              phase_end("ret%d" % l)
              aset(15360)
              macc = af32(4 * T).rearrange("p (a t) -> p a t", t=T)
              macc_T = [TT("macc%d" % i) for i in range(4)]
              sgm = [af32(512) for _ in range(2)]
              sgm_T = [TT("sgm0"), TT("sgm1")]
              k = 0
              branches = ((mow_d, h_aT, haT_T), (cow_d, ubT, ub_T), (row_d, h_cT, hcT_T))
              for jg in range(2):
                  for b, (wd, src, srcT) in enumerate(branches):
                      wo, woT = wload(wd[l][:, jg * 512:(jg + 1) * 512], 4, 512)
                      wgm, wgmT = wload(in_w_d[l][:, OFF_GM + b * 1024 + jg * 512:OFF_GM + b * 1024 + (jg + 1) * 512], 8, 512)
                      for jj in range(4):
                          j = jg * 4 + jj
                          for nb in range(3):
                              sl = slice(nb * 512, (nb + 1) * 512)
                              by, byT = psum()
                              cx.run("pe", [mm(by, wo[:, kc, jj * 128:(jj + 1) * 128], src[:, kc, sl], kc == 0, kc == 3) for kc in range(4)], reads=[woT, srcT], writes=[byT])
                              bg, bgT = psum()
                              cx.run("pe", [mm(bg, wgm[:, kc, jj * 128:(jj + 1) * 128], hT[:, kc, sl], kc == 0, kc == 7) for kc in range(8)], reads=[wgmT, hT_T], writes=[bgT])
                              pp = k % 2
                              k += 1
                              cx.run("act", [lambda e, bg=bg, pp=pp: e.activation(out=sgm[pp], in_=bg, func=AF.Sigmoid)], reads=[bgT], writes=[sgm_T[pp]])
                              if b == 0:
                                  cx.run("dve", [lambda e, by=by, pp=pp, sl=sl, jj=jj: e.tensor_tensor(out=macc[:, jj, sl], in0=by, in1=sgm[pp], op=ALU.mult)],
                                         reads=[byT, sgm_T[pp]], writes=[macc_T[jj]])
                              else:
                                  cx.run("dve", [lambda e, by=by, pp=pp: e.tensor_tensor(out=sgm[pp], in0=by, in1=sgm[pp], op=ALU.mult)],
                                         reads=[byT, sgm_T[pp]], writes=[sgm_T[pp]])
                                  if b == 1:
                                      cx.run("dve", [lambda e, pp=pp, sl=sl, jj=jj: e.tensor_tensor(out=macc[:, jj, sl], in0=macc[:, jj, sl], in1=sgm[pp], op=ALU.add)],
                                             reads=[sgm_T[pp], macc_T[jj]], writes=[macc_T[jj]])
                                  else:
                                      cx.run("dve", [lambda e, pp=pp, sl=sl, j=j, jj=jj: e.tensor_tensor(out=merged[:, j, sl], in0=macc[:, jj, sl], in1=sgm[pp], op=ALU.add)],
                                             reads=[sgm_T[pp], macc_T[jj]], writes=[mg_T])
              dbg_out("merged_l%d" % l, merged, [128, 8, T], [mg_T])
              phase_end("merge%d" % l)
              aset(15360)
              xb = abf(8 * 512).rearrange("p (a n) -> p a n", n=512)
              xq = abf(8 * 512).rearrange("p (a n) -> p a n", n=512)
              lnt = (af32(512), af32(512), af32(512))
              cx.run("act", [lambda e, kc=kc: e.activation(out=x[:, kc, :], in_=x[:, kc, :], func=AF.Copy, scale=ALPHA) for kc in range(8)], reads=[x_T], writes=[x_T])
              for jg in range(2):
                  wo, woT = wload(outw_d[l][:, jg * 512:(jg + 1) * 512], 8, 512)
                  for jj in range(4):
                      j = jg * 4 + jj
                      for nb in range(3):
                          sl = slice(nb * 512, (nb + 1) * 512)
                          v = 1 if nb < 2 else 0
                          bk, bkT = psum()
                          cx.run("pe", [mm(bk, wo[:, kc, jj * 128:(jj + 1) * 128], merged[:, kc, sl], kc == 0, kc == 7) for kc in range(8)], reads=[woT, mg_T], writes=[bkT])
                          cx.run("dve", [lambda e, bk=bk, j=j, sl=sl, v=v: e.scalar_tensor_tensor(out=x[:, j, sl], in0=bk, scalar=modT[:, 16 + j, v:v + 1],
                                                                                                 in1=x[:, j, sl], op0=ALU.mult, op1=ALU.add)],
                                 reads=[bkT, x_T, M_T[l]], writes=[x_T])
              layer_norm_fm(0, [(0, 512), (512, 1024), (1024, 1536)], xb, xq, lnt)
              dbg_out("x1_l%d" % l, x, [128, 8, T], [x_T])
              phase_end("outp%d" % l)
              aset(0)
              ffT = abf(NFT * T).rearrange("p (a n) -> p a n", n=T)
              ff_T = TT("ffT")
              a_sb = abf(4 * T).rearrange("p (a n) -> p a n", n=T)
              asb_T = TT("a_sb")
              sgf = [af32(512) for _ in range(2)]
              sgf_T = [TT("sgf0"), TT("sgf1")]
              modulate(sc2p, 24)
              cx.run("act", [lambda e, kc=kc: e.activation(out=x[:, kc, :], in_=x[:, kc, :], func=AF.Copy, scale=ALPHA) for kc in range(8)], reads=[x_T], writes=[x_T])
              k = 0
              for g in range(6):
                  nt = 4 if g < 5 else 2
                  wa, waT = wload(w13_d[l][:, g * 512:g * 512 + nt * 128], 8, nt * 128)
                  wg2, wg2T = wload(w13_d[l][:, FF + g * 512:FF + g * 512 + nt * 128], 8, nt * 128)
                  for jj in range(nt):
                      for nb in range(3):
                          sl = slice(nb * 512, (nb + 1) * 512)
                          bk, bkT = psum()
                          cx.run("pe", [mm(bk, wa[:, kc, jj * 128:(jj + 1) * 128], hT[:, kc, sl], kc == 0, kc == 7) for kc in range(8)],
                                 reads=[waT, hT_T], writes=[bkT])
                          cx.run("act", [lambda e, bk=bk, jj=jj, sl=sl: e.activation(out=a_sb[:, jj, sl], in_=bk, func=AF.Copy)],
                                 reads=[bkT], writes=[asb_T])
                  for jj in range(nt):
                      for nb in range(3):
                          sl = slice(nb * 512, (nb + 1) * 512)
                          bk, bkT = psum()
                          cx.run("pe", [mm(bk, wg2[:, kc, jj * 128:(jj + 1) * 128], hT[:, kc, sl], kc == 0, kc == 7) for kc in range(8)],
                                 reads=[wg2T, hT_T], writes=[bkT])
                          pp = k % 2
                          k += 1
                          cx.run("act", [lambda e, bk=bk, pp=pp: e.activation(out=sgf[pp], in_=bk, func=AF.Silu)], reads=[bkT], writes=[sgf_T[pp]])
                          cx.run("dve", [lambda e, pp=pp, jj=jj, g=g, sl=sl: e.tensor_tensor(out=ffT[:, g * 4 + jj, sl], in0=sgf[pp], in1=a_sb[:, jj, sl], op=ALU.mult)],
                                 reads=[sgf_T[pp], asb_T], writes=[ff_T])
              for j in range(8):
                  w2v, w2T = wload(w2_d[l][:, j * 128:(j + 1) * 128], NFT, 128)
                  for nb in range(3):
                      sl = slice(nb * 512, (nb + 1) * 512)
                      v = 1 if nb < 2 else 0
                      bk, bkT = psum()
                      cx.run("pe", [mm(bk, w2v[:, kc, :], ffT[:, kc, sl], kc == 0, kc == NFT - 1) for kc in range(NFT)], reads=[w2T, ff_T], writes=[bkT])
                      cx.run("dve", [lambda e, bk=bk, j=j, sl=sl, v=v: e.scalar_tensor_tensor(
                          out=x[:, j, sl], in0=bk, scalar=modT[:, 40 + j, v:v + 1], in1=x[:, j, sl], op0=ALU.mult, op1=ALU.add)],
                          reads=[bkT, x_T, M_T[l]], writes=[x_T])
              aset(0)
              xb = abf(8 * 512).rearrange("p (a n) -> p a n", n=512)
              xq = abf(8 * 512).rearrange("p (a n) -> p a n", n=512)
              lnt = (af32(512), af32(512), af32(512))
              layer_norm_fm(1, [(0, 512), (512, 1024), (1024, 1536)], xb, xq, lnt)
              dbg_out("x2_l%d" % l, x, [128, 8, T], [x_T])

        try:
            layers()
        except _Stop:
            pass
        for kc in range(8):
            cx.dma("sp", yT_d[:, kc, :], x[:, kc, :], reads=[x_T])
        cx.final()

        with nc.Block() as block:
            def replay(name):
                def f(e):
                    for th in cx.prog[name]:
                        th(e)
                return f
            block.tensor(replay("pe"))
            block.scalar(replay("act"))
            block.vector(replay("dve"))
            block.gpsimd(replay("pool"))
            block.sync(replay("sp"))
    return nc


_IDX = np.concatenate([np.arange(0, 128, 2), np.arange(1, 128, 2)])
_IDXS = np.concatenate([np.arange(1, 128, 2), np.arange(0, 128, 2)])


def _prow(r):
    return (r // 4) * 32 + (r % 4)


def _in_cols():
    o = dict(mq=0, mk=512, mv=1024, mo=1536, mg=2048, ca=2064, cg=2576, rq=3088, rk=3600, rv=4112, rg=4624, gm=5136)
    cols = []
    for h in range(H):
        for nm in ("mq", "mk", "mv", "mo"):
            cols += list(range(o[nm] + h * 128, o[nm] + (h + 1) * 128))
    z = [-1] * 28
    mg = o["mg"]
    cols += [mg + 0 * 4 + h for h in range(4)] + z + [mg + 2 * 4 + h for h in range(4)]
    cols += [mg + 1 * 4 + h for h in range(4)] + z + [mg + 3 * 4 + h for h in range(4)]
    cols += list(range(o["ca"], o["ca"] + 512)) + list(range(o["cg"], o["cg"] + 512))
    for h in range(H):
        for nm in ("rq", "rk"):
            cols += list(o[nm] + h * 128 + _IDX) + list(o[nm] + h * 128 + _IDXS)
    for h in range(H):
        cols += list(range(o["rv"] + h * 128, o["rv"] + (h + 1) * 128)) + list(range(o["rg"] + h * 128, o["rg"] + (h + 1) * 128))
    cols += list(range(o["gm"], o["gm"] + 3072))
    cols = np.array(cols, dtype=np.int64)
    assert cols.shape[0] == NCOLS, cols.shape
    return cols


_NC_CACHE = {}


def kernel(x_prompt, x_sample, state_mlstm_C, state_mlstm_n, state_mlstm_m, state_ret_S, c, c_ctx,
           ada_w, ada_b, in_w, mlstm_gate_b, mlstm_norm_w, mlstm_out_w, conv_w, conv_b, conv_ln_w, conv_ln_b,
           conv_out_w, ret_decay, ret_norm_w, ret_out_w, out_w, ln1_w, ln1_b, ln2_w, ln2_b, ffn_w13, ffn_w2, _dbg=False):
    f32 = np.float32
    A = lambda a: np.ascontiguousarray(np.asarray(a, dtype=f32))
    x_prompt, x_sample = A(x_prompt), A(x_sample)
    sC, sn, smm, sS = A(state_mlstm_C), A(state_mlstm_n), A(state_mlstm_m), A(state_ret_S)
    c, c_ctx = A(c), A(c_ctx)
    in_w = A(in_w)
    cols = _in_cols()
    in_wp = np.zeros((L, D, NCOLS), f32)
    valid = cols >= 0
    in_wp[:, :, valid] = in_w[:, :, cols[valid]]
    gbias = A(mlstm_gate_b)
    gb = np.zeros((L, 36, 2), f32)
    for h in range(4):
        gb[:, h, 0] = gbias[:, 0, h]; gb[:, h, 1] = gbias[:, 1, h]
        gb[:, 32 + h, 0] = gbias[:, 2, h]; gb[:, 32 + h, 1] = gbias[:, 3, h]
    convp = np.zeros((L, 128, 4, 34), f32)
    cw = A(conv_w)
    convp[:, :, :, 0:31] = cw.reshape(L, CONV_K, 4, 128).transpose(0, 3, 2, 1)
    convp[:, :, :, 31] = A(conv_b).reshape(L, 4, 128).transpose(0, 2, 1)
    convp[:, :, :, 32] = A(conv_ln_w).reshape(L, 4, 128).transpose(0, 2, 1)
    convp[:, :, :, 33] = A(conv_ln_b).reshape(L, 4, 128).transpose(0, 2, 1)
    lnp = np.zeros((L, 128, 4, 8), f32)
    for i, a in enumerate((ln1_w, ln1_b, ln2_w, ln2_b)):
        lnp[:, :, i, :] = A(a).reshape(L, 8, 128).transpose(0, 2, 1)
    ada_bT = np.ascontiguousarray(A(ada_b).reshape(L, 48, 128).transpose(0, 2, 1))
    rdec = A(ret_decay).reshape(L, 1, 8)
    mnw = A(mlstm_norm_w).reshape(L, 1, 512)
    rnw = A(ret_norm_w).reshape(L, 1, 512)
    cst = np.zeros((128, 1024), f32)
    jj, ii = np.meshgrid(np.arange(128), np.arange(128), indexing="ij")
    cst[:, 0:128] = np.eye(128, dtype=f32)
    cst[:, 128:256] = np.where(jj <= ii, 0.0, NEG)
    cst[:, 256:384] = np.where(jj >= ii, 0.0, NEG)
    cst[:, 384:512] = np.maximum(ii - jj, 0)
    cst[:, 512:640] = np.maximum(jj - ii, 0)
    cst[:, 640:768] = (jj <= ii)
    cst[:, 768:896] = (jj >= ii)
    p = np.arange(128)
    cst[:, 896] = p + 1; cst[:, 897] = 128 - p; cst[:, 898] = 127 - p; cst[:, 899] = p; cst[:, 900] = 128
    sel = np.zeros((36, 8, 128), f32)
    for r in range(8):
        sel[_prow(r), r, :] = 1.0
    t = np.arange(1024)
    rows = (t // 64).astype(f32); colsg = (t % 64).astype(f32)
    freqs = (np.float32(10000.0) ** (-np.arange(32, dtype=f32) / np.float32(32))).astype(f32)
    ang = np.concatenate([rows[:, None] * freqs[None, :], colsg[:, None] * freqs[None, :]], -1).astype(f32)
    cs_lat = np.concatenate([np.cos(ang).T, np.cos(ang).T], 0).astype(f32)
    sn_lat = np.concatenate([-np.sin(ang).T, np.sin(ang).T], 0).astype(f32)
    cs_id = np.ones((128, 1024), f32); sn_id = np.zeros((128, 1024), f32)

    shared = dict(cst=cst, sel=sel, ada_w=A(ada_w), ada_bT=ada_bT, in_wp=in_wp, gb=gb, mnw=mnw, rnw=rnw,
                  mlstm_out_w=A(mlstm_out_w), conv_out_w=A(conv_out_w), ret_out_w=A(ret_out_w), convp=convp, rdec=rdec,
                  out_w=A(out_w), lnp=lnp, ffn_w13=A(ffn_w13), ffn_w2=A(ffn_w2))
    in_maps = []
    seg_prompt = []
    for core in range(8):
        if core < 4:
            xs = np.concatenate([x_sample[core], x_prompt[2 * core], x_prompt[2 * core + 1]], 0)
            seg_prompt.append({4: 2 * core, 5: 2 * core + 1})
            cvec = c[core]
            Cinit = np.concatenate([sC[core], sn[core][..., None]], -1)
            minit = np.zeros((L, 36, 1), f32)
            for h in range(4):
                minit[:, h, 0] = smm[core, :, 0, h]; minit[:, 32 + h, 0] = smm[core, :, 1, h]
            Sinit = sS[core][:, :, :, _IDX, :]
            chainv = 1.0
            rcs, rsn = cs_lat, sn_lat
        else:
            base = 8 + 6 * (core - 4)
            xs = np.concatenate([x_prompt[base + s] for s in range(6)], 0)
            seg_prompt.append({s: base + s for s in range(6)})
            cvec = c_ctx
            Cinit = np.zeros((L, 2, H, 128, 129), f32)
            minit = np.zeros((L, 36, 1), f32)
            Sinit = np.zeros((L, 2, H, 128, 128), f32)
            chainv = 0.0
            rcs, rsn = cs_id, sn_id
        xT = np.ascontiguousarray(xs.reshape(T, 8, 128).transpose(2, 1, 0))
        cv = np.stack([c_ctx.reshape(8, 128).T, cvec.reshape(8, 128).T], -1)
        m = dict(shared)
        m.update(xT=xT, cv=np.ascontiguousarray(cv, dtype=f32), chain=np.full((128, 1), chainv, f32),
                 Cinit=np.ascontiguousarray(Cinit, dtype=f32), minit=minit, Sinit=np.ascontiguousarray(Sinit, dtype=f32),
                 ropeCS=rcs, ropeSN=rsn)
        in_maps.append(m)

    key = bool(_dbg)
    nc = build(dbg=key)
    res = run_bass_kernel_spmd(nc, in_maps, core_ids=list(range(8)))
    R = res.results
    y_prompt = np.zeros((32, 256, D), f32)
    y_sample = np.zeros((4, 1024, D), f32)
    new_C = np.zeros((32, L, 2, H, 128, 128), f32)
    new_n = np.zeros((32, L, 2, H, 128), f32)
    new_m = np.zeros((32, L, 2, H), f32)
    new_S = np.zeros((32, L, 2, H, 128, 128), f32)
    for core in range(8):
        r = R[core]
        yt = np.asarray(r["yT"]).transpose(2, 1, 0).reshape(T, D)
        if core < 4:
            y_sample[core] = yt[0:1024]
        Cf = np.asarray(r["Cfin"]); mf = np.asarray(r["mfin"]); Sf = np.asarray(r["Sfin"])
        for s, b in seg_prompt[core].items():
            y_prompt[b] = yt[s * 256:(s + 1) * 256]
            new_C[b] = Cf[:, s, :, :, :, 0:128]
            new_n[b] = Cf[:, s, :, :, :, 128]
            for h in range(4):
                new_m[b, :, 0, h] = mf[:, h, 2 * s + 1]
                new_m[b, :, 1, h] = mf[:, 32 + h, 2 * s]
            Su = np.empty((L, 2, H, 128, 128), f32)
            Su[:, :, :, _IDX, :] = Sf[:, s]
            new_S[b] = Su
    if _dbg:
        return (y_prompt, y_sample, new_C, new_n, new_m, new_S), R
    return (y_prompt, y_sample, new_C, new_n, new_m, new_S)
```

```python
import math
import itertools
import numpy as np
import concourse.bass as bass
import concourse.mybir as mybir
from concourse.bass_utils import run_bass_kernel_spmd
from concourse.ap import AP

F32 = mybir.dt.float32
BF16 = mybir.dt.bfloat16
AF = mybir.ActivationFunctionType
ALU = mybir.AluOpType
AX = mybir.AxisListType

D = 1024
L = 2
T = 1536
NCH = 12
NSEG = 6
H = 4
HD = 128
FF = 2816
NFT = 22
CONV_K = 31
EPS = 1e-5
ALPHA = (2.0 * L) ** 0.25
KS = HD ** -0.5
LNKS = math.log(KS)
NEG = -30000.0
NCOLS = 9288
OFF_GATE = 2048
OFF_CA = 2120
OFF_CG = 2632
OFF_RET = 3144
OFF_RVG = 5192
OFF_GM = 6216

DEBUG = {}
STOP = None


class _Stop(Exception):
    pass


def phase_end(name):
    if STOP == name:
        raise _Stop()


def rev(ap):
    a = [list(x) for x in ap.ap]
    step, n = a[-1]
    off = ap.offset + step * (n - 1)
    a[-1] = [-step, n]
    return AP(ap.tensor, off, a)


import types


def _snap(th):
    cl = th.__closure__
    if not cl:
        return th
    cells = []
    for c in cl:
        try:
            cells.append(types.CellType(c.cell_contents))
        except ValueError:
            cells.append(c)
    return types.FunctionType(th.__code__, th.__globals__, th.__name__, th.__defaults__, tuple(cells))


class TT:
    __slots__ = ("name", "w", "r")

    def __init__(self, name=""):
        self.name = name
        self.w = None
        self.r = {}


class Ctx:
    ENG = ["pe", "act", "dve", "pool", "sp"]

    def __init__(self, nc, sems):
        self.nc = nc
        self.sems = sems
        self.prog = {e: [] for e in self.ENG}
        self.cnt = {e: 0 for e in self.ENG}
        self.seen = {e: {} for e in self.ENG}
        self.dkeys = [k for k in sems if k[0] == "d" and k[1:].isdigit()]
        self.wkeys = [k for k in sems if k[0] == "w" and k[1:].isdigit()]
        self.dtot = {k: 0 for k in sems}
        self.dn = 0
        self.wn = 0

    def _wait(self, eng, key, val):
        if val <= 0 or self.seen[eng].get(key, 0) >= val:
            return
        self.seen[eng][key] = val
        sem = self.sems[key]
        self.prog[eng].append(lambda e, sem=sem, val=val: e.wait_ge(sem, val))

    def _deps(self, eng, reads, writes):
        for t in reads:
            if t.w is not None:
                self._wait_dep(eng, t.w)
        for t in writes:
            if t.w is not None:
                self._wait_dep(eng, t.w)
            for k, v in t.r.items():
                self._wait_dep(eng, (k, v))

    def _wait_dep(self, eng, dep):
        k, v = dep
        if k == "pe" and eng == "pe":
            return
        self._wait(eng, k, v)

    def run(self, eng, thunks, reads=(), writes=()):
        if not isinstance(thunks, (list, tuple)):
            thunks = [thunks]
        thunks = [_snap(t) for t in thunks]
        self._deps(eng, reads, writes)
        sem = self.sems[eng]
        n = len(thunks)
        for i, th in enumerate(thunks):
            if i == n - 1:
                self.prog[eng].append(lambda e, th=th, sem=sem: th(e).then_inc(sem, 1))
            else:
                self.prog[eng].append(th)
        self.cnt[eng] += 1
        c = self.cnt[eng]
        for t in reads:
            t.r[eng] = c
        for t in writes:
            t.w = (eng, c)
            t.r = {}

    def dma(self, q, out, in_, reads=(), writes=()):
        if q == "pool":
            key = self.wkeys[self.wn % len(self.wkeys)]
            self.wn += 1
        else:
            key = self.dkeys[self.dn % len(self.dkeys)]
            self.dn += 1
        prev = self.dtot[key]
        self._wait(q, key, prev)
        self._deps(q, reads, writes)
        new = prev + 16
        self.dtot[key] = new
        sem = self.sems[key]
        self.prog[q].append(lambda e, out=out, in_=in_, sem=sem: e.dma_start(out=out, in_=in_).then_inc(sem, 16))
        for t in reads:
            t.r[key] = new
        for t in writes:
            t.w = (key, new)
            t.r = {}

    def barrier(self):
        engs = ["pe", "act", "dve", "sp"]
        for e in engs:
            for f in ["pe", "act", "dve"]:
                if f != e or e != "pe":
                    self._wait(e, f, self.cnt[f])
            for k in self.dkeys:
                self._wait(e, k, self.dtot[k])

    def final(self):
        for k in self.dkeys:
            self._wait("sp", k, self.dtot[k])
        for f in ["pe", "act", "dve"]:
            self._wait("sp", f, self.cnt[f])


def build(dbg=False):
    nc = bass.Bass("TRN2", target_bir_lowering=False)
    dram = {}

    def din(name, shape, dt=F32):
        dram[name] = nc.dram_tensor(name, list(shape), dt, kind="ExternalInput").ap()
        return dram[name]

    def dout(name, shape, dt=F32):
        dram[name] = nc.dram_tensor(name, list(shape), dt, kind="ExternalOutput").ap()
        return dram[name]

    xT_d = din("xT", [128, 8, T])
    cv_d = din("cv", [128, 8, 2])
    chain_d = din("chain", [128, 1])
    Cinit_d = din("Cinit", [L, 2, H, 128, 129])
    minit_d = din("minit", [L, 36, 1])
    Sinit_d = din("Sinit", [L, 2, H, 128, 128])
    ropeCS_d = din("ropeCS", [128, 1024])
    ropeSN_d = din("ropeSN", [128, 1024])
    cst_d = din("cst", [128, 1024])
    sel_d = din("sel", [36, 8, 128])
    ada_w_d = din("ada_w", [L, D, 6 * D])
    ada_b_d = din("ada_bT", [L, 128, 48])
    in_w_d = din("in_wp", [L, D, NCOLS])
    gb_d = din("gb", [L, 36, 2])
    mnw_d = din("mnw", [L, 1, 512])
    rnw_d = din("rnw", [L, 1, 512])
    mow_d = din("mlstm_out_w", [L, 512, D])
    cow_d = din("conv_out_w", [L, 512, D])
    row_d = din("ret_out_w", [L, 512, D])
    convp_d = din("convp", [L, 128, 4, 34])
    rdec_d = din("rdec", [L, 1, 8])
    outw_d = din("out_w", [L, D, D])
    lnp_d = din("lnp", [L, 128, 4, 8])
    w13_d = din("ffn_w13", [L, D, 2 * FF])
    w2_d = din("ffn_w2", [L, FF, D])

    yT_d = dout("yT", [128, 8, T])
    Cfin_d = dout("Cfin", [L, NSEG, 2, H, 128, 129])
    mfin_d = dout("mfin", [L, 36, 12])
    Sfin_d = dout("Sfin", [L, NSEG, 2, H, 128, 128])
    dbg_d = {}

    import contextlib
    es = contextlib.ExitStack()
    with es:
        def sb(name, shape, dt=F32):
            return es.enter_context(nc.sbuf_tensor("s_" + name, list(shape), dt))[:]

        x = sb("x", [128, 8, T])
        hT = sb("hT", [128, 8, T], BF16)
        NW = 3
        Wr = [sb("wr%d" % i, [128, 4096], BF16) for i in range(NW)]
        Wr_T = [TT("wr%d" % i) for i in range(NW)]
        cst = sb("cst", [128, 1024])
        sel = sb("sel", [36, 8, 128])
        identB = sb("identB", [128, 128], BF16)
        onesB = sb("onesB", [128, 128], BF16)
        chain = sb("chain", [128, 1])
        cvs = sb("cvs", [128, 8, 2], BF16)
        cvf = sb("cvf", [128, 8, 2])
        modTs = [sb("modT%d" % i, [128, 48, 2]) for i in range(L)]
        sc1ps = [sb("sc1p%d" % i, [128, 8, 2]) for i in range(L)]
        sc2ps = [sb("sc2p%d" % i, [128, 8, 2]) for i in range(L)]
        adabs = [sb("adab%d" % i, [128, 48]) for i in range(L)]
        M_T = [TT("mod%d" % i) for i in range(L)]
        gb = sb("gb", [36, 2])
        ngb = sb("ngb", [36, 1])
        minit = sb("minit", [36, 1])
        mnw = sb("mnw", [128, 512])
        rnw = sb("rnw", [128, 512])
        convp = sb("convp", [128, 4, 34])
        lnp = sb("lnp", [128, 4, 8])
        rdec = sb("rdec", [128, 8])
        lg = sb("lg", [128, 8])
        rcol = sb("rcol", [128, 8, 4])
        decT = sb("decT", [128, 8, 128])
        ones36 = sb("ones36", [36, 128])
        zeros36 = sb("zeros36", [36, 128])
        AR = 23168
        arena = sb("arena", [128, AR])
        ps = [es.enter_context(nc.psum_tensor("ps%d" % i, [128, 512], F32))[:] for i in range(8)]
        ps_T = [TT("ps%d" % i) for i in range(8)]

        keys = ["pe", "act", "dve", "pool", "sp"] + ["d%d" % i for i in range(8)] + ["w%d" % i for i in range(6)]
        sems = {k: es.enter_context(nc.semaphore(k)) for k in keys}
        cx = Ctx(nc, sems)
        G = TT("globals")
        x_T = TT("x")
        hT_T = TT("hT")

        identF = cst[:, 0:128]
        MASK = [cst[:, 128:256], cst[:, 256:384]]
        DIFF = [cst[:, 384:512], cst[:, 512:640]]
        M01 = [cst[:, 640:768], cst[:, 768:896]]
        POS = cst[:, 896:904]

        pstate = {"i": 0}

        def psum():
            i = pstate["i"] % 8
            pstate["i"] += 1
            return ps[i], ps_T[i]

        wstate = {"i": 0}

        def wload(src, kc, ncol):
            i = wstate["i"] % NW
            wstate["i"] += 1
            dst = Wr[i][:, 0:kc * ncol].rearrange("p (k n) -> p k n", n=ncol)
            cx.dma("pool", dst, src.rearrange("(k p) n -> p k n", p=128), writes=[Wr_T[i]])
            return dst, Wr_T[i]

        ast = {"o": 0}

        def aset(o):
            cx.barrier()
            ast["o"] = o

        def af32(n, shape=None):
            o = ast["o"]
            ast["o"] += n
            assert ast["o"] <= AR, ast["o"]
            v = arena[:, o:o + n]
            return v

        def abf(n):
            n2 = (n + 1) // 2
            return af32(n2).bitcast(BF16)

        def dbg_out(name, ap, shape, reads):
            if not dbg:
                return
            d = dout("dbg_" + name, shape, ap.dtype)
            DEBUG[name] = shape
            cx.dma("sp", d, ap, reads=reads)

        mm = lambda out, lhsT, rhs, st, sp: (lambda e: e.matmul(out, lhsT=lhsT, rhs=rhs, start=st, stop=sp))

        for kc in range(8):
            cx.dma("sp", x[:, kc, :], xT_d[:, kc, :], writes=[x_T])
        for dst, src in ((cst, cst_d), (sel, sel_d), (chain, chain_d), (cvf, cv_d)):
            cx.dma("sp", dst, src, writes=[G])
        cx.run("dve", [lambda e: e.memset(ones36, 1.0), lambda e: e.memset(zeros36, 0.0),
                       lambda e: e.memset(onesB, 1.0),
                       lambda e: e.tensor_copy(out=identB, in_=identF)],
               reads=[G], writes=[G])
        cx.run("act", [lambda e: e.activation(out=cvs, in_=cvf, func=AF.Silu)], reads=[G], writes=[G])
        for i in range(L):
            cx.dma("sp", adabs[i], ada_b_d[i], writes=[M_T[i]])

        def mod_groups(ll, gs):
            mT = modTs[ll]
            for g in gs:
                wv, wT = wload(ada_w_d[ll][:, g * 512:(g + 1) * 512], 8, 512)
                mb, mbT = psum()
                for jj in range(4):
                    cx.run("pe", [mm(mb[:, 2 * jj:2 * jj + 2], wv[:, kc, jj * 128:(jj + 1) * 128], cvs[:, kc, :], kc == 0, kc == 7) for kc in range(8)],
                           reads=[wT, G], writes=[mbT])
                j0 = g * 4
                cx.run("act", [lambda e, mb=mb, j0=j0: e.activation(out=mT[:, j0:j0 + 4, :].rearrange("p j v -> p (j v)"), in_=mb[:, 0:8], func=AF.Copy)],
                       reads=[mbT], writes=[M_T[ll]])
                cx.run("dve", [lambda e, j0=j0: e.tensor_tensor(out=mT[:, j0:j0 + 4, :], in0=mT[:, j0:j0 + 4, :],
                                                                in1=adabs[ll][:, j0:j0 + 4].unsqueeze(2).to_broadcast([128, 4, 2]), op=ALU.add)],
                       reads=[M_T[ll]], writes=[M_T[ll]])
                if g in (2, 3):
                    o = (g - 2) * 4
                    cx.run("dve", [lambda e, j0=j0, o=o: e.tensor_scalar(out=sc1ps[ll][:, o:o + 4, :], in0=mT[:, j0:j0 + 4, :], scalar1=1.0, scalar2=None, op0=ALU.add)],
                           reads=[M_T[ll]], writes=[M_T[ll]])
                if g in (8, 9):
                    o = (g - 8) * 4
                    cx.run("dve", [lambda e, j0=j0, o=o: e.tensor_scalar(out=sc2ps[ll][:, o:o + 4, :], in0=mT[:, j0:j0 + 4, :], scalar1=1.0, scalar2=None, op0=ALU.add)],
                           reads=[M_T[ll]], writes=[M_T[ll]])

        def layer_norm_fm(which, blocks, xb, xq, lnts):
            lw = lnp[:, 2 * which, :]
            lb = lnp[:, 2 * which + 1, :]
            xb_T, xq_T = TT("xb"), TT("xq")
            nblk = len(blocks)
            XB = []
            for _ in range(nblk):
                t = TT("xblk")
                t.w = x_T.w
                t.r = dict(x_T.r)
                XB.append(t)
            LT = [TT("lnt0"), TT("lnt1")]
            banks = {}

            def A(i):
                sl = slice(*blocks[i])
                cx.run("act", [lambda e, kc=kc: e.activation(out=xb[:, kc, :], in_=x[:, kc, sl], func=AF.Copy) for kc in range(8)],
                       reads=[XB[i]], writes=[xb_T])
                cx.run("act", [lambda e, kc=kc: e.activation(out=xq[:, kc, :], in_=x[:, kc, sl], func=AF.Square) for kc in range(8)],
                       reads=[XB[i]], writes=[xq_T])

            def P(i):
                b1, b1T = psum()
                cx.run("pe", [mm(b1, onesB, xb[:, kc, :], kc == 0, kc == 7) for kc in range(8)], reads=[xb_T], writes=[b1T])
                b2, b2T = psum()
                cx.run("pe", [mm(b2, onesB, xq[:, kc, :], kc == 0, kc == 7) for kc in range(8)], reads=[xq_T], writes=[b2T])
                banks[i] = (b1, b1T, b2, b2T)

            def S1(i):
                b1, b1T, b2, b2T = banks[i]
                mean, msq, rstd = lnts[i % 2]
                T_ = LT[i % 2]
                cx.run("act", [lambda e: e.activation(out=mean, in_=b1, func=AF.Copy, scale=1.0 / D)], reads=[b1T], writes=[T_])
                cx.run("dve", [lambda e: e.tensor_tensor(out=msq, in0=mean, in1=mean, op=ALU.mult)], reads=[T_], writes=[T_])
                cx.run("dve", [lambda e: e.scalar_tensor_tensor(out=msq, in0=b2, scalar=1.0 / D, in1=msq, op0=ALU.mult, op1=ALU.subtract)],
                       reads=[b2T, T_], writes=[T_])
                cx.run("dve", [lambda e: e.tensor_scalar(out=msq, in0=msq, scalar1=EPS, scalar2=None, op0=ALU.add)], reads=[T_], writes=[T_])

            def S2(i):
                mean, msq, rstd = lnts[i % 2]
                T_ = LT[i % 2]
                cx.run("act", [lambda e: e.activation(out=rstd, in_=msq, func=AF.Sqrt)], reads=[T_], writes=[T_])
                cx.run("dve", [lambda e: e.reciprocal(out=rstd, in_=rstd)], reads=[T_], writes=[T_])

            def Dd(i):
                sl = slice(*blocks[i])
                mean, msq, rstd = lnts[i % 2]
                T_ = LT[i % 2]
                cx.run("dve", [lambda e, kc=kc: e.tensor_tensor(out=x[:, kc, sl], in0=x[:, kc, sl], in1=mean, op=ALU.subtract) for kc in range(8)],
                       reads=[XB[i], T_], writes=[XB[i]])
                cx.run("dve", [lambda e, kc=kc: e.tensor_tensor(out=x[:, kc, sl], in0=x[:, kc, sl], in1=rstd, op=ALU.mult) for kc in range(8)],
                       reads=[XB[i], T_], writes=[XB[i]])

            def Da(i):
                sl = slice(*blocks[i])
                cx.run("act", [lambda e, kc=kc: e.activation(out=x[:, kc, sl], in_=x[:, kc, sl], func=AF.Identity,
                                                             scale=lw[:, kc:kc + 1], bias=lb[:, kc:kc + 1]) for kc in range(8)],
                       reads=[XB[i], G], writes=[XB[i]])

            assert nblk == 3
            A(0); P(0); S1(0); A(1); P(1); S2(0); Dd(0); S1(1); A(2); P(2); S2(1); Da(0); Dd(1); S1(2); S2(2); Da(1); Dd(2); Da(2)
            x_T.w = XB[2].w
            x_T.r = {}
            for t in XB:
                for k_, v_ in t.r.items():
                    x_T.r[k_] = max(x_T.r.get(k_, 0), v_)

        lnks = sb("lnks", [128, 1])
        cx.run("dve", [lambda e: e.memset(lnks, LNKS)], writes=[G])

        def ln_stats(b1, b1T, b2, b2T, mean, msq, rstd, st_T, n):
            cx.run("act", [lambda e: e.activation(out=mean, in_=b1, func=AF.Copy, scale=1.0 / n)], reads=[b1T], writes=[st_T])
            cx.run("dve", [lambda e: e.tensor_tensor(out=msq, in0=mean, in1=mean, op=ALU.mult)], reads=[st_T], writes=[st_T])
            cx.run("dve", [lambda e: e.scalar_tensor_tensor(out=msq, in0=b2, scalar=1.0 / n, in1=msq, op0=ALU.mult, op1=ALU.subtract)],
                   reads=[b2T, st_T], writes=[st_T])
            cx.run("dve", [lambda e: e.tensor_scalar(out=msq, in0=msq, scalar1=EPS, scalar2=None, op0=ALU.add)], reads=[st_T], writes=[st_T])
            cx.run("act", [lambda e: e.activation(out=rstd, in_=msq, func=AF.Sqrt)], reads=[st_T], writes=[st_T])
            cx.run("dve", [lambda e: e.reciprocal(out=rstd, in_=rstd)], reads=[st_T], writes=[st_T])

        def head_norm_out(src3, src_T, normw, h, gate3, gate_T, dstT, dst_T, scr, scr_T, hn_all, hn_Ts):
            stA = scr[:, 0:72].rearrange("p (c n) -> p c n", n=6)
            mvA = scr[:, 72:96].rearrange("p (c n) -> p c n", n=2)
            src_all = src3
            hn3 = hn_all.rearrange("p (c n) -> p c n", n=128)
            cx.run("dve", [lambda e, c=c: e.bn_stats(out=stA[:, c, :], in_=src3[:, c, :]) for c in range(12)], reads=list(src_T) + [scr_T], writes=[scr_T])
            cx.run("dve", [lambda e, c=c: e.bn_aggr(out=mvA[:, c, :], in_=stA[:, c, :]) for c in range(12)], reads=[scr_T], writes=[scr_T])
            rstd = mvA[:, :, 1]
            cx.run("dve", [lambda e: e.tensor_scalar(out=rstd, in0=rstd, scalar1=EPS, scalar2=None, op0=ALU.add)], reads=[scr_T], writes=[scr_T])
            cx.run("act", [lambda e: e.activation(out=rstd, in_=rstd, func=AF.Sqrt)], reads=[scr_T], writes=[scr_T])
            cx.run("dve", [lambda e: e.reciprocal(out=rstd, in_=rstd)], reads=[scr_T], writes=[scr_T])
            cx.run("dve", [lambda e, c=c: e.tensor_scalar(out=src3[:, c, :], in0=src3[:, c, :], scalar1=mvA[:, c, 0:1], scalar2=mvA[:, c, 1:2],
                                                          op0=ALU.subtract, op1=ALU.mult) for c in range(12)], reads=list(src_T) + [scr_T], writes=list(src_T))
            cx.run("dve", [lambda e, c=c: e.tensor_tensor(out=src3[:, c, :], in0=src3[:, c, :], in1=normw[:, h * 128:(h + 1) * 128], op=ALU.mult) for c in range(12)],
                   reads=list(src_T) + [G], writes=list(src_T))
            cx.run("dve", [lambda e: e.tensor_tensor(out=hn3, in0=src3, in1=gate3, op=ALU.mult)], reads=list(src_T) + [gate_T] + list(hn_Ts), writes=list(hn_Ts))
            for c in range(12):
                bk, bkT = psum()
                bkb = bk.bitcast(BF16)
                cx.run("pe", [lambda e, bkb=bkb, c=c: e.transpose(out=bkb[:, 0:128], in_=hn3[:, c, :], identity=identB)], reads=list(hn_Ts) + [G], writes=[bkT])
                cx.run("act", [lambda e, bkb=bkb, c=c: e.activation(out=dstT[:, h, c * 128:(c + 1) * 128], in_=bkb[:, 0:128], func=AF.Copy)], reads=[bkT], writes=[dst_T])

        def layers():
          for l in range(L):
              phase_end('pro%d' % l)
              aset(0)
              for dst, src in ((gb, gb_d[l]), (minit, minit_d[l]), (convp, convp_d[l]), (lnp, lnp_d[l]),
                               (mnw, mnw_d[l].partition_broadcast(128)), (rnw, rnw_d[l].partition_broadcast(128)),
                               (rdec, rdec_d[l].partition_broadcast(128))):
                  cx.dma("sp", dst, src, writes=[G])
              cx.run("dve", [lambda e: e.tensor_scalar(out=ngb, in0=gb[:, 1:2], scalar1=-1.0, scalar2=None, op0=ALU.mult)], reads=[G], writes=[G])
              cx.run("act", [lambda e: e.activation(out=lg, in_=rdec, func=AF.Exp, scale=-1.0)], reads=[G], writes=[G])
              cx.run("act", [lambda e: e.activation(out=lg, in_=lg, func=AF.Ln, bias=1.0)], reads=[G], writes=[G])
              cx.run("dve", [lambda e: e.tensor_scalar(out=lg, in0=lg, scalar1=-1.0, scalar2=None, op0=ALU.mult)], reads=[G], writes=[G])
              for r in range(8):
                  d = r // 4
                  lgc = lg[:, r:r + 1]
                  cx.run("act", [lambda e, r=r, d=d, lgc=lgc: e.activation(out=decT[:, r, :], in_=DIFF[d], func=AF.Exp, scale=lgc)], reads=[G], writes=[G])
                  cx.run("dve", [lambda e, r=r, d=d: e.scalar_tensor_tensor(out=decT[:, r, :], in0=decT[:, r, :], scalar=KS, in1=M01[d],
                                                                           op0=ALU.mult, op1=ALU.mult)], reads=[G], writes=[G])
                  cx.run("act", [lambda e, r=r, d=d, lgc=lgc: e.activation(out=rcol[:, r, 0:1], in_=POS[:, d:d + 1], func=AF.Exp, scale=lgc),
                                 lambda e, r=r, d=d, lgc=lgc: e.activation(out=rcol[:, r, 1:2], in_=POS[:, 2 + d:3 + d], func=AF.Exp, scale=lgc),
                                 lambda e, r=r, d=d, lgc=lgc: e.activation(out=rcol[:, r, 2:3], in_=POS[:, 4:5], func=AF.Exp, scale=lgc)],
                         reads=[G], writes=[G])
                  cx.run("dve", [lambda e, r=r: e.tensor_scalar(out=rcol[:, r, 1:2], in0=rcol[:, r, 1:2], scalar1=KS, scalar2=None, op0=ALU.mult)],
                         reads=[G], writes=[G])

              phase_end('small%d' % l)
              modT, sc1p, sc2p = modTs[l], sc1ps[l], sc2ps[l]
              if l == 0:
                  mod_groups(0, [0, 1, 2, 3])
              phase_end('modmm%d' % l)

              def modulate(scp, shoff):
                  ths = []
                  for kc in range(8):
                      for (a, b, v) in ((0, 1024, 1), (1024, T, 0)):
                          ths.append(lambda e, kc=kc, a=a, b=b, v=v: e.tensor_scalar(
                              out=hT[:, kc, a:b], in0=x[:, kc, a:b], scalar1=scp[:, kc, v:v + 1],
                              scalar2=modT[:, shoff + kc, v:v + 1], op0=ALU.mult, op1=ALU.add))
                  cx.run("dve", ths, reads=[x_T, M_T[l]], writes=[hT_T])

              dbg_out("modT_l%d" % l, modT, [128, 48, 2], [M_T[l]])
              phase_end("mod%d" % l)
              modulate(sc1p, 0)
              phase_end("h%d" % l)
              dbg_out("h_l%d" % l, hT, [128, 8, T], [hT_T])

              aset(3072)
              h_aT = arena[:, 0:3072].bitcast(BF16).rearrange("p (h t) -> p h t", t=T)
              ubT = arena[:, 3072:6144].bitcast(BF16).rearrange("p (h t) -> p h t", t=T)
              h_cT = arena[:, 6144:9216].bitcast(BF16).rearrange("p (h t) -> p h t", t=T)
              merged = arena[:, 9216:15360].bitcast(BF16).rearrange("p (h t) -> p h t", t=T)
              haT_T = TT("h_aT"); ub_T = TT("ubT"); hcT_T = TT("h_cT"); mg_T = TT("merged")
              rIG = af32(T); rA = af32(T); rP = af32(T); rCM = af32(T)
              R_T = TT("rows")
              sm = af32(96).rearrange("p (a c) -> p a c", c=12)
              SM_T = TT("sm")
              cols = af32(3 * 432).rearrange("p (q n) -> p q n", n=432)
              COL_T = TT("cols")
              cbc = af32(96)
              qT = abf(T); kT = abf(T)
              qk_T = TT("qk")
              vext = abf(12 * 130).rearrange("p (c n) -> p c n", n=130)
              v_T = TT("vext")
              osig = abf(T).rearrange("p (c n) -> p c n", n=128)
              o_T = TT("osig")
              hsum = af32(T).rearrange("p (c n) -> p c n", n=128)
              hs_T = [TT("hs%d" % c) for c in range(12)]
              kw = [abf(T).rearrange("p (c n) -> p c n", n=128) for _ in range(2)]
              kw_T = [[TT("kw") for c in range(12)] for _ in range(2)]
              Cx = [[af32(130) for _ in range(2)] for _ in range(2)]
              Cx_T = [[TT("cx"), TT("cx")] for _ in range(2)]
              sc2 = af32(8)
              dmg = [af32(512)] * 2
              dmg_T = [TT("dmg0")] * 2
              sT_all = abf(24 * 128)
              sTa_T = [TT("sTa%d" % i) for i in range(6)]
              Cb_all = [abf(12 * 130).rearrange("p (s n) -> p s n", n=130) for _ in range(2)]
              CbA_T = [[TT("cba") for _ in range(12)] for _ in range(2)]
              Bs2 = [af32(130) for _ in range(2)]
              Bs2_T = [TT("bs2a"), TT("bs2b")]
              tot_all = af32(12 * 130).rearrange("p (c n) -> p c n", n=130)
              tota_T = TT("tot_all")
              dn12 = af32(12)
              dn_T = TT("dn12")

              wg, wgT = wload(in_w_d[l][:, OFF_GATE:OFF_GATE + 72], 8, 72)
              cx.run("dve", [lambda e: e.memset(rP[0:36, :], 0.0), lambda e: e.memset(rCM[0:36, :], 0.0)], writes=[R_T])
              for which in (0, 1):
                  for nb in range(3):
                      bk, bkT = psum()
                      sl = slice(nb * 512, (nb + 1) * 512)
                      cx.run("pe", [mm(bk[0:36, :], wg[:, kc, which * 36:(which + 1) * 36], hT[:, kc, sl], kc == 0, kc == 7) for kc in range(8)],
                             reads=[wgT, hT_T], writes=[bkT])
                      if which == 0:
                          cx.run("act", [lambda e, bk=bk, sl=sl: e.activation(out=rIG[0:36, sl], in_=bk[0:36, :], func=AF.Identity, bias=gb[:, 0:1])],
                                 reads=[bkT, G], writes=[R_T])
                      else:
                          cx.run("act", [lambda e, bk=bk, sl=sl: e.activation(out=rA[0:36, sl], in_=bk[0:36, :], func=AF.Exp, scale=-1.0, bias=ngb[:, 0:1])],
                                 reads=[bkT, G], writes=[R_T])
              cx.run("act", [lambda e: e.activation(out=rA[0:36, :], in_=rA[0:36, :], func=AF.Ln, bias=1.0)], reads=[R_T], writes=[R_T])
              ths = []
              for c in range(12):
                  cs = slice(c * 128, (c + 1) * 128)
                  ths.append(lambda e, cs=cs: e.tensor_tensor_scan(out=rP[0:4, cs], data0=ones36[0:4, :], data1=rA[0:4, cs], initial=0.0,
                                                                   op0=ALU.mult, op1=ALU.add))
                  ths.append(lambda e, cs=cs: e.tensor_tensor_scan(out=rev(rP[32:36, cs]), data0=ones36[32:36, :], data1=rev(rA[32:36, cs]),
                                                                   initial=0.0, op0=ALU.mult, op1=ALU.add))
              cx.run("dve", ths, reads=[R_T], writes=[R_T])
              cx.run("dve", [lambda e: e.tensor_tensor(out=rA[0:36, :], in0=rIG[0:36, :], in1=rP[0:36, :], op=ALU.add)], reads=[R_T], writes=[R_T])
              ths = []
              for c in range(12):
                  cs = slice(c * 128, (c + 1) * 128)
                  ths.append(lambda e, cs=cs: e.tensor_tensor_scan(out=rCM[0:4, cs], data0=zeros36[0:4, :], data1=rA[0:4, cs], initial=-1e30,
                                                                   op0=ALU.add, op1=ALU.max))
                  ths.append(lambda e, cs=cs: e.tensor_tensor_scan(out=rev(rCM[32:36, cs]), data0=zeros36[32:36, :], data1=rev(rA[32:36, cs]),
                                                                   initial=-1e30, op0=ALU.add, op1=ALU.max))
              cx.run("dve", ths, reads=[R_T], writes=[R_T])
              CM3 = rCM.rearrange("p (c t) -> p c t", t=128)
              P3 = rP.rearrange("p (c t) -> p c t", t=128)
              A3 = rA.rearrange("p (c t) -> p c t", t=128)
              IG3 = rIG.rearrange("p (c t) -> p c t", t=128)
              cx.run("dve", [lambda e: e.memset(sm[0:36, :, :], 0.0)], writes=[SM_T])
              cx.run("dve", [lambda e: e.tensor_copy(out=sm[0:4, 0, :], in_=CM3[0:4, :, 127]),
                             lambda e: e.tensor_copy(out=sm[32:36, 0, :], in_=CM3[32:36, :, 0]),
                             lambda e: e.tensor_copy(out=sm[0:4, 1, :], in_=P3[0:4, :, 127]),
                             lambda e: e.tensor_copy(out=sm[32:36, 1, :], in_=P3[32:36, :, 0])], reads=[R_T, SM_T], writes=[SM_T])
              for d, r0 in ((0, 0), (1, 32)):
                  rs = slice(r0, r0 + 4)
                  order = list(range(12)) if d == 0 else list(range(11, -1, -1))
                  prev = None
                  for c in order:
                      s = c // 2
                      start = (c % 2 == 0) if d == 0 else (c % 2 == 1)
                      m0c = sm[rs, 2, c:c + 1]
                      if start:
                          if (d == 0 and s == 0) or (d == 1 and s == 3):
                              th = lambda e, m0c=m0c, rs=rs: e.tensor_copy(out=m0c, in_=minit[rs, :])
                          elif (d == 0 and s <= 3) or (d == 1 and s <= 2):
                              th = lambda e, m0c=m0c, rs=rs, prev=prev: e.tensor_scalar(out=m0c, in0=sm[rs, 4, prev:prev + 1], scalar1=chain[rs, :],
                                                                                       scalar2=None, op0=ALU.mult)
                          else:
                              th = lambda e, m0c=m0c: e.memset(m0c, 0.0)
                      else:
                          th = lambda e, m0c=m0c, rs=rs, prev=prev: e.tensor_copy(out=m0c, in_=sm[rs, 4, prev:prev + 1])
                      cx.run("dve", [th], reads=[SM_T, G], writes=[SM_T])
                      cx.run("dve", [lambda e, rs=rs, c=c: e.tensor_tensor(out=sm[rs, 3, c:c + 1], in0=sm[rs, 2, c:c + 1], in1=sm[rs, 0, c:c + 1], op=ALU.max)],
                             reads=[SM_T], writes=[SM_T])
                      cx.run("dve", [lambda e, rs=rs, c=c: e.tensor_tensor(out=sm[rs, 4, c:c + 1], in0=sm[rs, 3, c:c + 1], in1=sm[rs, 1, c:c + 1], op=ALU.subtract)],
                             reads=[SM_T], writes=[SM_T])
                      prev = c
              cx.dma("sp", mfin_d[l], sm[0:36, 4, :], reads=[SM_T])
              cx.run("dve", [lambda e: e.tensor_tensor(out=sm[0:36, 6, :], in0=sm[0:36, 3, :], in1=sm[0:36, 2, :], op=ALU.subtract)], reads=[SM_T], writes=[SM_T])
              cx.run("act", [lambda e: e.activation(out=sm[0:36, 5, :], in_=sm[0:36, 6, :], func=AF.Exp, scale=-1.0)], reads=[SM_T], writes=[SM_T])
              m0b = sm[0:36, 2, :].unsqueeze(2).to_broadcast([36, 12, 128])
              mxb = sm[0:36, 3, :].unsqueeze(2).to_broadcast([36, 12, 128])
              cx.run("dve", [lambda e: e.tensor_tensor(out=CM3[0:36], in0=CM3[0:36], in1=m0b, op=ALU.max)], reads=[R_T, SM_T], writes=[R_T])
              cx.run("dve", [lambda e: e.tensor_scalar(out=rCM[0:36, :], in0=rCM[0:36, :], scalar1=-1.0, scalar2=None, op0=ALU.mult)], reads=[R_T], writes=[R_T])
              dbg_out("rA_l%d" % l, rA[0:36, :], [36, T], [R_T])
              dbg_out("rNG_l%d" % l, rCM[0:36, :], [36, T], [R_T])

              def cols_from(q, rows):
                  bk, bkT = psum()
                  cx.run("pe", [lambda e, c=c, bk=bk: e.transpose(out=bk[:, c * 36:(c + 1) * 36], in_=rows[0:36, c * 128:(c + 1) * 128],
                                                                  identity=identF[0:36, 0:36]) for c in range(12)],
                         reads=[R_T, G], writes=[bkT])
                  cx.run("act", [lambda e, bk=bk: e.activation(out=cols[:, q, :], in_=bk[:, 0:432], func=AF.Copy)], reads=[bkT], writes=[COL_T])

              cx.run("dve", [lambda e: e.tensor_tensor(out=IG3[0:36], in0=CM3[0:36], in1=m0b, op=ALU.add)], reads=[R_T, SM_T], writes=[R_T])
              cx.run("act", [lambda e: e.activation(out=rIG[0:36, :], in_=rIG[0:36, :], func=AF.Exp)], reads=[R_T], writes=[R_T])
              cols_from(0, rIG)
              cx.run("dve", [lambda e: e.tensor_tensor(out=rP[0:36, :], in0=rP[0:36, :], in1=rCM[0:36, :], op=ALU.add)], reads=[R_T], writes=[R_T])
              cx.run("act", [lambda e: e.activation(out=rP[0:36, :], in_=rP[0:36, :], func=AF.Exp)], reads=[R_T], writes=[R_T])
              cols_from(1, rP)
              cx.run("dve", [lambda e: e.tensor_tensor(out=IG3[0:36], in0=A3[0:36], in1=mxb, op=ALU.subtract)], reads=[R_T, SM_T], writes=[R_T])
              cx.run("dve", [lambda e: e.tensor_scalar(out=rIG[0:36, :], in0=rIG[0:36, :], scalar1=LNKS, scalar2=None, op0=ALU.add)], reads=[R_T], writes=[R_T])
              cx.run("act", [lambda e: e.activation(out=rIG[0:36, :], in_=rIG[0:36, :], func=AF.Exp)], reads=[R_T], writes=[R_T])
              cols_from(2, rIG)
              bk, bkT = psum()
              cx.run("pe", [mm(bk[:, r * 12:(r + 1) * 12], sel[:, r, :], sm[0:36, 5, :], True, True) for r in range(8)], reads=[SM_T, G], writes=[bkT])
              cx.run("act", [lambda e, bk=bk: e.activation(out=cbc, in_=bk[:, 0:96], func=AF.Copy)], reads=[bkT], writes=[COL_T])
              dbg_out("sm_l%d" % l, sm[0:36, :, :], [36, 8, 12], [SM_T])
              dbg_out("cols_l%d" % l, cols, [128, 3, 432], [COL_T])

              phase_end("gates%d" % l)
              prow = lambda r: (r // 4) * 32 + (r % 4)
              lnks_col = None

              WH = {}

              def proj_qk(h):
                  wv, wT = wload(in_w_d[l][:, h * 512:(h + 1) * 512], 8, 512)
                  WH[h] = (wv, wT)
                  for nb in range(3):
                      sl = slice(nb * 512, (nb + 1) * 512)
                      for which, dst in ((0, qT), (1, kT)):
                          bk, bkT = psum()
                          cx.run("pe", [mm(bk, wv[:, kc, which * 128:(which + 1) * 128], hT[:, kc, sl], kc == 0, kc == 7) for kc in range(8)],
                                 reads=[wT, hT_T], writes=[bkT])
                          cx.run("act", [lambda e, bk=bk, dst=dst, sl=sl: e.activation(out=dst[:, sl], in_=bk, func=AF.Copy)], reads=[bkT], writes=[qk_T])
                  for c in range(12):
                      cs = slice(c * 128, (c + 1) * 128)
                      bk, bkT = psum()
                      bkb = bk.bitcast(BF16)
                      cx.run("pe", [lambda e, bkb=bkb, cs=cs: e.transpose(out=bkb[:, 0:128], in_=kT[:, cs], identity=identB)], reads=[qk_T, G], writes=[bkT])
                      for d in range(2):
                          r = d * 4 + h
                          col = cols[:, 2, c * 36 + prow(r):c * 36 + prow(r) + 1]
                          cx.run("act", [lambda e, bkb=bkb, d=d, c=c, col=col: e.activation(out=kw[d][:, c, :], in_=bkb[:, 0:128], func=AF.Copy, scale=col)],
                                 reads=[bkT, COL_T], writes=[kw_T[d][c]])

              def proj_vo(h):
                  wv, wT = WH[h]
                  cx.run("dve", [lambda e: e.memset(vext[:, :, 128:130], 1.0)], writes=[v_T])
                  for c in range(12):
                      cs = slice(c * 128, (c + 1) * 128)
                      bk, bkT = psum()
                      cx.run("pe", [mm(bk[:, 0:256], hT[:, kc, cs], wv[:, kc, 256:512], kc == 0, kc == 7) for kc in range(8)], reads=[wT, hT_T], writes=[bkT])
                      cx.run("act", [lambda e, bk=bk, c=c: e.activation(out=vext[:, c, 0:128], in_=bk[:, 0:128], func=AF.Copy)], reads=[bkT], writes=[v_T])
                      cx.run("act", [lambda e, bk=bk, c=c: e.activation(out=osig[:, c, :], in_=bk[:, 128:256], func=AF.Sigmoid)], reads=[bkT], writes=[o_T])

              proj_qk(0)
              for h in range(H):
                  proj_vo(h)
                  orders = [list(range(12)), list(range(11, -1, -1))]
                  for g in range(6):
                      d = g // 3
                      r = d * 4 + h
                      b1, b1T = psum()
                      ths = []
                      css = []
                      for i in range(4):
                          c = orders[d][(g % 3) * 4 + i]
                          cs = slice(c * 128, (c + 1) * 128)
                          css.append(cs)
                          o = b1[:, i * 128:(i + 1) * 128]
                          ths += [mm(o, sel[:, r, :], rCM[0:36, cs], True, False), mm(o, rA[0:36, cs], sel[:, r, :], False, False),
                                  mm(o, identF, MASK[d], False, True)]
                      cx.run("pe", ths, reads=[R_T, G], writes=[b1T])
                      gp = g % 2
                      cx.run("act", [lambda e, b1=b1, gp=gp: e.activation(out=dmg[gp], in_=b1, func=AF.Exp, bias=lnks[:, 0:1])], reads=[b1T, G], writes=[dmg_T[gp]])
                      b2, b2T = psum()
                      cx.run("pe", [mm(b2[:, i * 128:(i + 1) * 128], kT[:, css[i]], qT[:, css[i]], True, True) for i in range(4)], reads=[qk_T], writes=[b2T])
                      cx.run("dve", [lambda e, b2=b2, gp=gp, g=g: e.tensor_tensor(out=sT_all[:, g * 512:(g + 1) * 512], in0=b2, in1=dmg[gp], op=ALU.mult)],
                             reads=[b2T, dmg_T[gp]], writes=[sTa_T[g]])

                  def rec(d, h=h):
                      r = d * 4 + h
                      cur = 0
                      Cxd, CxTd = Cx[d], Cx_T[d]
                      for step, c in enumerate(orders[d]):
                          s = c // 2
                          start = (c % 2 == 0) if d == 0 else (c % 2 == 1)
                          if start:
                              if (d == 0 and s == 0) or (d == 1 and s == 3):
                                  cx.dma("sp", Cxd[cur][:, 0:129], Cinit_d[l, d, h], writes=[CxTd[cur]])
                              elif (d == 0 and s <= 3) or (d == 1 and s <= 2):
                                  cx.run("dve", [lambda e, cur=cur: e.tensor_scalar(out=Cxd[cur][:, 0:129], in0=Cxd[cur][:, 0:129], scalar1=chain[:, 0:1],
                                                                                     scalar2=None, op0=ALU.mult)], reads=[CxTd[cur], G], writes=[CxTd[cur]])
                              else:
                                  cx.run("dve", [lambda e, cur=cur: e.memset(Cxd[cur][:, 0:129], 0.0)], writes=[CxTd[cur]])
                          cx.run("act", [lambda e, cur=cur, step=step: e.activation(out=Cb_all[d][:, step, 0:129], in_=Cxd[cur][:, 0:129], func=AF.Copy)],
                                 reads=[CxTd[cur]], writes=[CbA_T[d][step]])
                          bU, bUT = psum()
                          cx.run("pe", [mm(bU[:, 0:129], kw[d][:, c, :], vext[:, c, 0:129], True, True)], reads=[kw_T[d][c], v_T], writes=[bUT])
                          yield
                          nxt = 1 - cur
                          ccol = cbc[:, r * 12 + c:r * 12 + c + 1]
                          cx.run("dve", [lambda e, cur=cur, nxt=nxt: e.scalar_tensor_tensor(
                              out=Cxd[nxt][:, 0:129], in0=Cxd[cur][:, 0:129], scalar=ccol, in1=bU[:, 0:129], op0=ALU.mult, op1=ALU.add)],
                              reads=[bUT, CxTd[cur], COL_T], writes=[CxTd[nxt]])
                          cur = nxt
                          end = (c % 2 == 1) if d == 0 else (c % 2 == 0)
                          if end:
                              cx.dma("sp", Cfin_d[l, s, d, h], Cxd[cur][:, 0:129], reads=[CxTd[cur]])
                          yield

                  for _ in itertools.zip_longest(rec(0), rec(1)):
                      pass

                  for d in range(2):
                      r = d * 4 + h
                      pr = prow(r)
                      for step, c in enumerate(orders[d]):
                          cs = slice(c * 128, (c + 1) * 128)
                          p = d * 12 + step
                          bA, bAT = psum()
                          cx.run("pe", [mm(bA[:, 0:129], sT_all[:, p * 128:(p + 1) * 128], vext[:, c, 0:129], True, True),
                                        mm(bA[:, 256:385], qT[:, cs], Cb_all[d][:, step, 0:129], True, True)],
                                 reads=[sTa_T[p // 4], v_T, qk_T, CbA_T[d][step]], writes=[bAT])
                          wcol = cols[:, 0, c * 36 + pr:c * 36 + pr + 1]
                          bp = step % 2
                          cx.run("act", [lambda e, bA=bA, bp=bp, wcol=wcol: e.activation(out=Bs2[bp][:, 0:129], in_=bA[:, 256:385], func=AF.Copy, scale=wcol)],
                                 reads=[bAT, COL_T], writes=[Bs2_T[bp]])
                          cx.run("dve", [lambda e, bA=bA, bp=bp, c=c: e.tensor_tensor(out=tot_all[:, c, 0:129], in0=bA[:, 0:129], in1=Bs2[bp][:, 0:129], op=ALU.add)],
                                 reads=[bAT, Bs2_T[bp]], writes=[tota_T])
                      den = tot_all[:, :, 128]
                      ecols = cols[:, 1, :].rearrange("p (c n) -> p c n", n=36)[:, :, pr]
                      cx.run("dve", [lambda e, den=den: e.tensor_scalar(out=dn12, in0=den, scalar1=-1.0, scalar2=None, op0=ALU.mult)], reads=[tota_T], writes=[dn_T])
                      cx.run("dve", [lambda e, den=den: e.tensor_tensor(out=dn12, in0=dn12, in1=den, op=ALU.max)], reads=[tota_T, dn_T], writes=[dn_T])
                      cx.run("dve", [lambda e, ecols=ecols: e.tensor_tensor(out=dn12, in0=dn12, in1=ecols, op=ALU.max)], reads=[dn_T, COL_T], writes=[dn_T])
                      cx.run("dve", [lambda e: e.reciprocal(out=dn12, in_=dn12)], reads=[dn_T], writes=[dn_T])
                      if d == 0:
                          cx.run("act", [lambda e, c=c: e.activation(out=hsum[:, c, :], in_=tot_all[:, c, 0:128], func=AF.Copy, scale=dn12[:, c:c + 1])
                                         for c in range(12)], reads=[tota_T, dn_T], writes=hs_T)
                      else:
                          cx.run("dve", [lambda e, c=c: e.scalar_tensor_tensor(out=hsum[:, c, :], in0=tot_all[:, c, 0:128], scalar=dn12[:, c:c + 1],
                                                                               in1=hsum[:, c, :], op0=ALU.mult, op1=ALU.add) for c in range(12)],
                                 reads=[tota_T, dn_T] + hs_T, writes=hs_T)
                  if l == 0 and h == 0:
                      dbg_out("hsum_l0h0", hsum, [128, 12, 128], hs_T)
                  if h + 1 < H:
                      proj_qk(h + 1)
                  head_norm_out(hsum, hs_T, mnw, h, osig, o_T, h_aT, haT_T, tot_all.rearrange("p c n -> p (c n)"), tota_T, sT_all[:, 0:1536], sTa_T[0:3])
                  if l == 0:
                      mod_groups(0, [4 + 2 * h, 5 + 2 * h])
                  phase_end("mlstm%d_h%d" % (l, h))
              dbg_out("haT_l%d" % l, h_aT, [128, 4, T], [haT_T])

              phase_end("mlstm%d" % l)
              aset(6144)
              acc = af32(4 * T).rearrange("p (a t) -> p a t", t=T)
              acc_T = TT("acc")
              upad = [abf(6 * 286).rearrange("p (s n) -> p s n", n=286) for _ in range(2)]
              up_T = [TT("upad0"), TT("upad1")]
              Dg = [abf(31 * 128).rearrange("p (k n) -> p k n", n=128) for _ in range(2)]
              Dg_T = [TT("dg0"), TT("dg1")]
              sg = [af32(512) for _ in range(2)]
              sg_T = [TT("sg0"), TT("sg1")]
              cx.run("dve", [lambda e: e.memset(upad[0], 0.0), lambda e: e.memset(upad[1], 0.0)], writes=up_T)
              wa, waT = wload(in_w_d[l][:, OFF_CA:OFF_CA + 512], 8, 512)
              wgc, wgcT = wload(in_w_d[l][:, OFF_CG:OFF_CG + 512], 8, 512)
              kst = {"k": 0}

              def conv_in(ct):
                  up, upT, dg, dgT = upad[ct % 2], up_T[ct % 2], Dg[ct % 2], Dg_T[ct % 2]
                  cx.run("act", [lambda e, kk=kk, ct=ct, dg=dg: e.activation(out=dg[:, kk, :], in_=identF, func=AF.Copy, scale=convp[:, ct, kk:kk + 1])
                                 for kk in range(CONV_K)], reads=[G], writes=[dgT])
                  for nb in range(3):
                      sl = slice(nb * 512, (nb + 1) * 512)
                      ba, baT = psum()
                      cx.run("pe", [mm(ba, wa[:, kc, ct * 128:(ct + 1) * 128], hT[:, kc, sl], kc == 0, kc == 7) for kc in range(8)], reads=[waT, hT_T], writes=[baT])
                      bg, bgT = psum()
                      cx.run("pe", [mm(bg, wgc[:, kc, ct * 128:(ct + 1) * 128], hT[:, kc, sl], kc == 0, kc == 7) for kc in range(8)], reads=[wgcT, hT_T], writes=[bgT])
                      pp = kst["k"] % 2
                      kst["k"] += 1
                      cx.run("act", [lambda e, bg=bg, pp=pp: e.activation(out=sg[pp], in_=bg, func=AF.Sigmoid)], reads=[bgT], writes=[sg_T[pp]])
                      cx.run("dve", [lambda e, ba=ba, pp=pp, nb=nb, up=up, s2=s2: e.tensor_tensor(
                          out=up[:, 2 * nb + s2, 15:271], in0=ba[:, s2 * 256:(s2 + 1) * 256],
                          in1=sg[pp][:, s2 * 256:(s2 + 1) * 256], op=ALU.mult) for s2 in range(2)], reads=[baT, sg_T[pp]], writes=[upT])
                  ths = []
                  for s in (1, 2, 3):
                      ths.append(lambda e, s=s, up=up: e.tensor_scalar(out=up[:, s, 0:15], in0=up[:, s - 1, 256:271], scalar1=chain[:, 0:1], scalar2=None, op0=ALU.mult))
                  for s in (0, 1, 2):
                      ths.append(lambda e, s=s, up=up: e.tensor_scalar(out=up[:, s, 271:286], in0=up[:, s + 1, 15:30], scalar1=chain[:, 0:1], scalar2=None, op0=ALU.mult))
                  cx.run("dve", ths, reads=[upT, G], writes=[upT])

              def conv_mm(ct):
                  up, upT, dg, dgT = upad[ct % 2], up_T[ct % 2], Dg[ct % 2], Dg_T[ct % 2]
                  for s in range(6):
                      bk, bkT = psum()
                      cx.run("pe", [mm(bk[:, 0:256], dg[:, kk, :], up[:, s, kk:kk + 256], kk == 0, kk == CONV_K - 1) for kk in range(CONV_K)],
                             reads=[dgT, upT], writes=[bkT])
                      cx.run("act", [lambda e, bk=bk, ct=ct, s=s: e.activation(out=acc[:, ct, s * 256:(s + 1) * 256], in_=bk[:, 0:256], func=AF.Identity,
                                                                               bias=convp[:, ct, 31:32])], reads=[bkT, G], writes=[acc_T])

              conv_in(0)
              for ct in range(4):
                  if ct + 1 < 4:
                      conv_in(ct + 1)
                  conv_mm(ct)
              dbg_out("conv_l%d" % l, acc, [128, 4, T], [acc_T])
              cb = abf(4 * 256).rearrange("p (a n) -> p a n", n=256)
              cq = abf(4 * 256).rearrange("p (a n) -> p a n", n=256)
              cb_T = TT("cb"); cq_T = TT("cq")
              mean = af32(256); msq = af32(256); rstd = af32(256)
              st_T = TT("lnst")
              for nb in range(6):
                  sl = slice(nb * 256, (nb + 1) * 256)
                  cx.run("act", [lambda e, ct=ct, sl=sl: e.activation(out=cb[:, ct, :], in_=acc[:, ct, sl], func=AF.Copy) for ct in range(4)], reads=[acc_T], writes=[cb_T])
                  cx.run("act", [lambda e, ct=ct, sl=sl: e.activation(out=cq[:, ct, :], in_=acc[:, ct, sl], func=AF.Square) for ct in range(4)], reads=[acc_T], writes=[cq_T])
                  b1, b1T = psum()
                  cx.run("pe", [mm(b1[:, 0:256], onesB, cb[:, ct, :], ct == 0, ct == 3) for ct in range(4)], reads=[cb_T, G], writes=[b1T])
                  b2, b2T = psum()
                  cx.run("pe", [mm(b2[:, 0:256], onesB, cq[:, ct, :], ct == 0, ct == 3) for ct in range(4)], reads=[cq_T, G], writes=[b2T])
                  ln_stats(b1[:, 0:256], b1T, b2[:, 0:256], b2T, mean, msq, rstd, st_T, 512)
                  cx.run("dve", [lambda e, ct=ct, sl=sl: e.tensor_tensor(out=acc[:, ct, sl], in0=acc[:, ct, sl], in1=mean, op=ALU.subtract) for ct in range(4)],
                         reads=[acc_T, st_T], writes=[acc_T])
                  cx.run("dve", [lambda e, ct=ct, sl=sl: e.tensor_tensor(out=acc[:, ct, sl], in0=acc[:, ct, sl], in1=rstd, op=ALU.mult) for ct in range(4)],
                         reads=[acc_T, st_T], writes=[acc_T])
                  cx.run("act", [lambda e, ct=ct, sl=sl: e.activation(out=acc[:, ct, sl], in_=acc[:, ct, sl], func=AF.Identity,
                                                                      scale=convp[:, ct, 32:33], bias=convp[:, ct, 33:34]) for ct in range(4)], reads=[acc_T, G], writes=[acc_T])
                  cx.run("act", [lambda e, ct=ct, sl=sl: e.activation(out=ubT[:, ct, sl], in_=acc[:, ct, sl], func=AF.Silu) for ct in range(4)], reads=[acc_T], writes=[ub_T])
              dbg_out("ubT_l%d" % l, ubT, [128, 4, T], [ub_T])

              phase_end("conv%d" % l)
              aset(9216)
              ropeCS = af32(1024); ropeSN = af32(1024)
              RP_T = TT("rope")
              cx.dma("sp", ropeCS, ropeCS_d, writes=[RP_T])
              cx.dma("sp", ropeSN, ropeSN_d, writes=[RP_T])
              qT = abf(T); kT = abf(T)
              qk_T = TT("rqk")
              vv = abf(T).rearrange("p (c n) -> p c n", n=128)
              v_T = TT("rv")
              gsil = af32(T).rearrange("p (c n) -> p c n", n=128)
              o_T = TT("gsil")
              ysum = af32(T).rearrange("p (c n) -> p c n", n=128)
              hs_T = [TT("ys%d" % c) for c in range(12)]
              kz = [abf(T).rearrange("p (c n) -> p c n", n=128) for _ in range(2)]
              kz_T = [[TT("kz") for c in range(12)] for _ in range(2)]
              Sx = [[af32(128) for _ in range(2)] for _ in range(2)]
              Sx_T = [[TT("sx"), TT("sx")] for _ in range(2)]
              t1 = af32(512); t2 = af32(512)
              sT_all = abf(24 * 128)
              sTa_T = [TT("rsTa%d" % i) for i in range(6)]
              Sb_all = [abf(12 * 128).rearrange("p (s n) -> p s n", n=128) for _ in range(2)]
              SbA_T = [[TT("sba") for _ in range(12)] for _ in range(2)]
              Bs2 = [af32(128) for _ in range(2)]
              Bs2_T = [TT("rbs2a"), TT("rbs2b")]
              t_T = TT("rt")
              def rproj_qk(h):
                  wv, wT = wload(in_w_d[l][:, OFF_RET + h * 512:OFF_RET + (h + 1) * 512], 8, 512)
                  for nb in range(3):
                      sl = slice(nb * 512, (nb + 1) * 512)
                      for which, dst in ((0, qT), (1, kT)):
                          bk, bkT = psum()
                          cx.run("pe", [mm(bk, wv[:, kc, which * 256:which * 256 + 128], hT[:, kc, sl], kc == 0, kc == 7) for kc in range(8)],
                                 reads=[wT, hT_T], writes=[bkT])
                          if nb < 2:
                              bs_, bsT = psum()
                              cx.run("pe", [mm(bs_, wv[:, kc, which * 256 + 128:which * 256 + 256], hT[:, kc, sl], kc == 0, kc == 7) for kc in range(8)],
                                     reads=[wT, hT_T], writes=[bsT])
                              cx.run("dve", [lambda e, bk=bk, sl=sl: e.tensor_tensor(out=t1, in0=bk, in1=ropeCS[:, sl], op=ALU.mult)], reads=[bkT, RP_T, t_T], writes=[t_T])
                              cx.run("dve", [lambda e, bs_=bs_, sl=sl: e.tensor_tensor(out=t2, in0=bs_, in1=ropeSN[:, sl], op=ALU.mult)], reads=[bsT, RP_T, t_T], writes=[t_T])
                              cx.run("dve", [lambda e, dst=dst, sl=sl: e.tensor_tensor(out=dst[:, sl], in0=t1, in1=t2, op=ALU.add)], reads=[t_T], writes=[qk_T, t_T])
                          else:
                              cx.run("act", [lambda e, bk=bk, dst=dst, sl=sl: e.activation(out=dst[:, sl], in_=bk, func=AF.Copy)], reads=[bkT], writes=[qk_T])
                  for c in range(12):
                      cs = slice(c * 128, (c + 1) * 128)
                      bk, bkT = psum()
                      bkb = bk.bitcast(BF16)
                      cx.run("pe", [lambda e, bkb=bkb, cs=cs: e.transpose(out=bkb[:, 0:128], in_=kT[:, cs], identity=identB)], reads=[qk_T, G], writes=[bkT])
                      for d in range(2):
                          r = d * 4 + h
                          cx.run("act", [lambda e, bkb=bkb, d=d, c=c, r=r: e.activation(out=kz[d][:, c, :], in_=bkb[:, 0:128], func=AF.Copy, scale=rcol[:, r, 1:2])],
                                 reads=[bkT, G], writes=[kz_T[d][c]])

              def rproj_vg(h):
                  wv2, wT2 = wload(in_w_d[l][:, OFF_RVG + h * 256:OFF_RVG + (h + 1) * 256], 8, 256)
                  for c in range(12):
                      cs = slice(c * 128, (c + 1) * 128)
                      bk, bkT = psum()
                      cx.run("pe", [mm(bk[:, 0:256], hT[:, kc, cs], wv2[:, kc, :], kc == 0, kc == 7) for kc in range(8)], reads=[wT2, hT_T], writes=[bkT])
                      cx.run("act", [lambda e, bk=bk, c=c: e.activation(out=vv[:, c, :], in_=bk[:, 0:128], func=AF.Copy)], reads=[bkT], writes=[v_T])
                      cx.run("act", [lambda e, bk=bk, c=c: e.activation(out=gsil[:, c, :], in_=bk[:, 128:256], func=AF.Silu)], reads=[bkT], writes=[o_T])

              rproj_qk(0)
              for h in range(H):
                  rproj_vg(h)
                  orders = [list(range(12)), list(range(11, -1, -1))]
                  for g in range(6):
                      d = g // 3
                      r = d * 4 + h
                      b2, b2T = psum()
                      css = [slice(orders[d][(g % 3) * 4 + i] * 128, (orders[d][(g % 3) * 4 + i] + 1) * 128) for i in range(4)]
                      cx.run("pe", [mm(b2[:, i * 128:(i + 1) * 128], kT[:, css[i]], qT[:, css[i]], True, True) for i in range(4)], reads=[qk_T], writes=[b2T])
                      cx.run("dve", [lambda e, b2=b2, g=g, i=i, r=r: e.tensor_tensor(out=sT_all[:, g * 512 + i * 128:g * 512 + (i + 1) * 128],
                                                                                   in0=b2[:, i * 128:(i + 1) * 128], in1=decT[:, r, :], op=ALU.mult) for i in range(4)],
                             reads=[b2T, G], writes=[sTa_T[g]])

                  def rrec(d, h=h):
                      r = d * 4 + h
                      cur = 0
                      Sxd, SxTd = Sx[d], Sx_T[d]
                      for step, c in enumerate(orders[d]):
                          s = c // 2
                          start = (c % 2 == 0) if d == 0 else (c % 2 == 1)
                          if start:
                              if (d == 0 and s == 0) or (d == 1 and s == 3):
                                  cx.dma("sp", Sxd[cur], Sinit_d[l, d, h], writes=[SxTd[cur]])
                              elif (d == 0 and s <= 3) or (d == 1 and s <= 2):
                                  cx.run("dve", [lambda e, cur=cur: e.tensor_scalar(out=Sxd[cur], in0=Sxd[cur], scalar1=chain[:, 0:1],
                                                                                     scalar2=None, op0=ALU.mult)], reads=[SxTd[cur], G], writes=[SxTd[cur]])
                              else:
                                  cx.run("dve", [lambda e, cur=cur: e.memset(Sxd[cur], 0.0)], writes=[SxTd[cur]])
                          cx.run("act", [lambda e, cur=cur, step=step: e.activation(out=Sb_all[d][:, step, :], in_=Sxd[cur], func=AF.Copy)],
                                 reads=[SxTd[cur]], writes=[SbA_T[d][step]])
                          bU, bUT = psum()
                          cx.run("pe", [mm(bU[:, 0:128], kz[d][:, c, :], vv[:, c, :], True, True)], reads=[kz_T[d][c], v_T], writes=[bUT])
                          yield
                          nxt = 1 - cur
                          cx.run("dve", [lambda e, cur=cur, nxt=nxt: e.scalar_tensor_tensor(
                              out=Sxd[nxt], in0=Sxd[cur], scalar=rcol[:, r, 2:3], in1=bU[:, 0:128], op0=ALU.mult, op1=ALU.add)],
                              reads=[bUT, SxTd[cur], G], writes=[SxTd[nxt]])
                          cur = nxt
                          end = (c % 2 == 1) if d == 0 else (c % 2 == 0)
                          if end:
                              cx.dma("sp", Sfin_d[l, s, d, h], Sxd[cur], reads=[SxTd[cur]])
                          yield

                  for _ in itertools.zip_longest(rrec(0), rrec(1)):
                      pass

                  for d in range(2):
                      r = d * 4 + h
                      for step, c in enumerate(orders[d]):
                          cs = slice(c * 128, (c + 1) * 128)
                          p = d * 12 + step
                          bA, bAT = psum()
                          cx.run("pe", [mm(bA[:, 0:128], sT_all[:, p * 128:(p + 1) * 128], vv[:, c, :], True, True),
                                        mm(bA[:, 256:384], qT[:, cs], Sb_all[d][:, step, :], True, True)],
                                 reads=[sTa_T[p // 4], v_T, qk_T, SbA_T[d][step]], writes=[bAT])
                          bp = step % 2
                          cx.run("act", [lambda e, bA=bA, bp=bp, r=r: e.activation(out=Bs2[bp], in_=bA[:, 256:384], func=AF.Copy, scale=rcol[:, r, 0:1])],
                                 reads=[bAT, G], writes=[Bs2_T[bp]])
                          if d == 0:
                              cx.run("dve", [lambda e, bA=bA, bp=bp, c=c: e.tensor_tensor(out=ysum[:, c, :], in0=bA[:, 0:128], in1=Bs2[bp], op=ALU.add)],
                                     reads=[bAT, Bs2_T[bp]], writes=[hs_T[c]])
                          else:
                              cx.run("dve", [lambda e, bA=bA, bp=bp: e.tensor_tensor(out=Bs2[bp], in0=bA[:, 0:128], in1=Bs2[bp], op=ALU.add)],
                                     reads=[bAT, Bs2_T[bp]], writes=[Bs2_T[bp]])
                              cx.run("dve", [lambda e, bp=bp, c=c: e.tensor_tensor(out=ysum[:, c, :], in0=ysum[:, c, :], in1=Bs2[bp], op=ALU.add)],
                                     reads=[Bs2_T[bp], hs_T[c]], writes=[hs_T[c]])
                  if h + 1 < H:
                      rproj_qk(h + 1)
                  head_norm_out(ysum, hs_T, rnw, h, gsil, o_T, h_cT, hcT_T, t1, t_T, sT_all[:, 0:1536], sTa_T[0:3])
                  if l + 1 < L:
                      mod_groups(l + 1, [3 * h, 3 * h + 1, 3 * h + 2])
              dbg_out("hcT_l%d" % l, h_cT, [128, 4, T], [hcT_T])

              phase_end("ret%d" % l)
              aset(15360)
              macc = af32(4 * T).rearrange("p (a t) -> p a t", t=T)
              macc_T = [TT("macc%d" % i) for i in range(4)]
              sgm = [af32(512) for _ in range(2)]
              sgm_T = [TT("sgm0"), TT("sgm1")]
              k = 0
              branches = ((mow_d, h_aT, haT_T), (cow_d, ubT, ub_T), (row_d, h_cT, hcT_T))
              for jg in range(2):
                  for b, (wd, src, srcT) in enumerate(branches):
                      wo, woT = wload(wd[l][:, jg * 512:(jg + 1) * 512], 4, 512)
                      wgm, wgmT = wload(in_w_d[l][:, OFF_GM + b * 1024 + jg * 512:OFF_GM + b * 1024 + (jg + 1) * 512], 8, 512)
                      for jj in range(4):
                          j = jg * 4 + jj
                          for nb in range(3):
                              sl = slice(nb * 512, (nb + 1) * 512)
                              by, byT = psum()
                              cx.run("pe", [mm(by, wo[:, kc, jj * 128:(jj + 1) * 128], src[:, kc, sl], kc == 0, kc == 3) for kc in range(4)], reads=[woT, srcT], writes=[byT])
                              bg, bgT = psum()
                              cx.run("pe", [mm(bg, wgm[:, kc, jj * 128:(jj + 1) * 128], hT[:, kc, sl], kc == 0, kc == 7) for kc in range(8)], reads=[wgmT, hT_T], writes=[bgT])
                              pp = k % 2
                              k += 1
                              cx.run("act", [lambda e, bg=bg, pp=pp: e.activation(out=sgm[pp], in_=bg, func=AF.Sigmoid)], reads=[bgT], writes=[sgm_T[pp]])
                              if b == 0:
                                  cx.run("dve", [lambda e, by=by, pp=pp, sl=sl, jj=jj: e.tensor_tensor(out=macc[:, jj, sl], in0=by, in1=sgm[pp], op=ALU.mult)],
                                         reads=[byT, sgm_T[pp]], writes=[macc_T[jj]])
                              else:
                                  cx.run("dve", [lambda e, by=by, pp=pp: e.tensor_tensor(out=sgm[pp], in0=by, in1=sgm[pp], op=ALU.mult)],
                                         reads=[byT, sgm_T[pp]], writes=[sgm_T[pp]])
                                  if b == 1:
                                      cx.run("dve", [lambda e, pp=pp, sl=sl, jj=jj: e.tensor_tensor(out=macc[:, jj, sl], in0=macc[:, jj, sl], in1=sgm[pp], op=ALU.add)],
                                             reads=[sgm_T[pp], macc_T[jj]], writes=[macc_T[jj]])
                                  else:
                                      cx.run("dve", [lambda e, pp=pp, sl=sl, j=j, jj=jj: e.tensor_tensor(out=merged[:, j, sl], in0=macc[:, jj, sl], in1=sgm[pp], op=ALU.add)],
                                             reads=[sgm_T[pp], macc_T[jj]], writes=[mg_T])
              dbg_out("merged_l%d" % l, merged, [128, 8, T], [mg_T])
              phase_end("merge%d" % l)
              aset(15360)
              xb = abf(8 * 512).rearrange("p (a n) -> p a n", n=512)
              xq = abf(8 * 512).rearrange("p (a n) -> p a n", n=512)
              lnt = [(af32(512), af32(512), af32(512)) for _ in range(2)]
              cx.run("act", [lambda e, kc=kc: e.activation(out=x[:, kc, :], in_=x[:, kc, :], func=AF.Copy, scale=ALPHA) for kc in range(8)], reads=[x_T], writes=[x_T])
              for jg in range(2):
                  wo, woT = wload(outw_d[l][:, jg * 512:(jg + 1) * 512], 8, 512)
                  for jj in range(4):
                      j = jg * 4 + jj
                      for nb in range(3):
                          sl = slice(nb * 512, (nb + 1) * 512)
                          v = 1 if nb < 2 else 0
                          bk, bkT = psum()
                          cx.run("pe", [mm(bk, wo[:, kc, jj * 128:(jj + 1) * 128], merged[:, kc, sl], kc == 0, kc == 7) for kc in range(8)], reads=[woT, mg_T], writes=[bkT])
                          cx.run("dve", [lambda e, bk=bk, j=j, sl=sl, v=v: e.scalar_tensor_tensor(out=x[:, j, sl], in0=bk, scalar=modT[:, 16 + j, v:v + 1],
                                                                                                 in1=x[:, j, sl], op0=ALU.mult, op1=ALU.add)],
                                 reads=[bkT, x_T, M_T[l]], writes=[x_T])
              layer_norm_fm(0, [(0, 512), (512, 1024), (1024, 1536)], xb, xq, lnt)
              dbg_out("x1_l%d" % l, x, [128, 8, T], [x_T])
              phase_end("outp%d" % l)
              aset(0)
              ffT = abf(NFT * T).rearrange("p (a n) -> p a n", n=T)
              ff_T = TT("ffT")
              a_sb = abf(4 * T).rearrange("p (a n) -> p a n", n=T)
              asb_T = TT("a_sb")
              sgf = [af32(512) for _ in range(2)]
              sgf_T = [TT("sgf0"), TT("sgf1")]
              modulate(sc2p, 24)
              cx.run("act", [lambda e, kc=kc: e.activation(out=x[:, kc, :], in_=x[:, kc, :], func=AF.Copy, scale=ALPHA) for kc in range(8)], reads=[x_T], writes=[x_T])
              k = 0
              for g in range(6):
                  nt = 4 if g < 5 else 2
                  wa, waT = wload(w13_d[l][:, g * 512:g * 512 + nt * 128], 8, nt * 128)
                  wg2, wg2T = wload(w13_d[l][:, FF + g * 512:FF + g * 512 + nt * 128], 8, nt * 128)
                  for jj in range(nt):
                      for nb in range(3):
                          sl = slice(nb * 512, (nb + 1) * 512)
                          bk, bkT = psum()
                          cx.run("pe", [mm(bk, wa[:, kc, jj * 128:(jj + 1) * 128], hT[:, kc, sl], kc == 0, kc == 7) for kc in range(8)],
                                 reads=[waT, hT_T], writes=[bkT])
                          cx.run("act", [lambda e, bk=bk, jj=jj, sl=sl: e.activation(out=a_sb[:, jj, sl], in_=bk, func=AF.Copy)],
                                 reads=[bkT], writes=[asb_T])
                  for jj in range(nt):
                      for nb in range(3):
                          sl = slice(nb * 512, (nb + 1) * 512)
                          bk, bkT = psum()
                          cx.run("pe", [mm(bk, wg2[:, kc, jj * 128:(jj + 1) * 128], hT[:, kc, sl], kc == 0, kc == 7) for kc in range(8)],
                                 reads=[wg2T, hT_T], writes=[bkT])
                          pp = k % 2
                          k += 1
                          cx.run("act", [lambda e, bk=bk, pp=pp: e.activation(out=sgf[pp], in_=bk, func=AF.Silu)], reads=[bkT], writes=[sgf_T[pp]])
                          cx.run("dve", [lambda e, pp=pp, jj=jj, g=g, sl=sl: e.tensor_tensor(out=ffT[:, g * 4 + jj, sl], in0=sgf[pp], in1=a_sb[:, jj, sl], op=ALU.mult)],
                                 reads=[sgf_T[pp], asb_T], writes=[ff_T])
              for j in range(8):
                  w2v, w2T = wload(w2_d[l][:, j * 128:(j + 1) * 128], NFT, 128)
                  for nb in range(3):
                      sl = slice(nb * 512, (nb + 1) * 512)
                      v = 1 if nb < 2 else 0
                      bk, bkT = psum()
                      cx.run("pe", [mm(bk, w2v[:, kc, :], ffT[:, kc, sl], kc == 0, kc == NFT - 1) for kc in range(NFT)], reads=[w2T, ff_T], writes=[bkT])
                      cx.run("dve", [lambda e, bk=bk, j=j, sl=sl, v=v: e.scalar_tensor_tensor(
                          out=x[:, j, sl], in0=bk, scalar=modT[:, 40 + j, v:v + 1], in1=x[:, j, sl], op0=ALU.mult, op1=ALU.add)],
                          reads=[bkT, x_T, M_T[l]], writes=[x_T])
              aset(0)
              xb = abf(8 * 512).rearrange("p (a n) -> p a n", n=512)
              xq = abf(8 * 512).rearrange("p (a n) -> p a n", n=512)
              lnt = [(af32(512), af32(512), af32(512)) for _ in range(2)]
              layer_norm_fm(1, [(0, 512), (512, 1024), (1024, 1536)], xb, xq, lnt)
              dbg_out("x2_l%d" % l, x, [128, 8, T], [x_T])

        try:
            layers()
        except _Stop:
            pass
        for kc in range(8):
            cx.dma("sp", yT_d[:, kc, :], x[:, kc, :], reads=[x_T])
        cx.final()

        with nc.Block() as block:
            def replay(name):
                def f(e):
                    for th in cx.prog[name]:
                        th(e)
                return f
            block.tensor(replay("pe"))
            block.scalar(replay("act"))
            block.vector(replay("dve"))
            block.gpsimd(replay("pool"))
            block.sync(replay("sp"))
    return nc


_IDX = np.concatenate([np.arange(0, 128, 2), np.arange(1, 128, 2)])
_IDXS = np.concatenate([np.arange(1, 128, 2), np.arange(0, 128, 2)])


def _prow(r):
    return (r // 4) * 32 + (r % 4)


def _in_cols():
    o = dict(mq=0, mk=512, mv=1024, mo=1536, mg=2048, ca=2064, cg=2576, rq=3088, rk=3600, rv=4112, rg=4624, gm=5136)
    cols = []
    for h in range(H):
        for nm in ("mq", "mk", "mv", "mo"):
            cols += list(range(o[nm] + h * 128, o[nm] + (h + 1) * 128))
    z = [-1] * 28
    mg = o["mg"]
    cols += [mg + 0 * 4 + h for h in range(4)] + z + [mg + 2 * 4 + h for h in range(4)]
    cols += [mg + 1 * 4 + h for h in range(4)] + z + [mg + 3 * 4 + h for h in range(4)]
    cols += list(range(o["ca"], o["ca"] + 512)) + list(range(o["cg"], o["cg"] + 512))
    for h in range(H):
        for nm in ("rq", "rk"):
            cols += list(o[nm] + h * 128 + _IDX) + list(o[nm] + h * 128 + _IDXS)
    for h in range(H):
        cols += list(range(o["rv"] + h * 128, o["rv"] + (h + 1) * 128)) + list(range(o["rg"] + h * 128, o["rg"] + (h + 1) * 128))
    cols += list(range(o["gm"], o["gm"] + 3072))
    cols = np.array(cols, dtype=np.int64)
    assert cols.shape[0] == NCOLS, cols.shape
    return cols


_NC_CACHE = {}


def kernel(x_prompt, x_sample, state_mlstm_C, state_mlstm_n, state_mlstm_m, state_ret_S, c, c_ctx,
           ada_w, ada_b, in_w, mlstm_gate_b, mlstm_norm_w, mlstm_out_w, conv_w, conv_b, conv_ln_w, conv_ln_b,
           conv_out_w, ret_decay, ret_norm_w, ret_out_w, out_w, ln1_w, ln1_b, ln2_w, ln2_b, ffn_w13, ffn_w2, _dbg=False):
    f32 = np.float32
    A = lambda a: np.ascontiguousarray(np.asarray(a, dtype=f32))
    x_prompt, x_sample = A(x_prompt), A(x_sample)
    sC, sn, smm, sS = A(state_mlstm_C), A(state_mlstm_n), A(state_mlstm_m), A(state_ret_S)
    c, c_ctx = A(c), A(c_ctx)
    in_w = A(in_w)
    cols = _in_cols()
    in_wp = np.zeros((L, D, NCOLS), f32)
    valid = cols >= 0
    in_wp[:, :, valid] = in_w[:, :, cols[valid]]
    gbias = A(mlstm_gate_b)
    gb = np.zeros((L, 36, 2), f32)
    for h in range(4):
        gb[:, h, 0] = gbias[:, 0, h]; gb[:, h, 1] = gbias[:, 1, h]
        gb[:, 32 + h, 0] = gbias[:, 2, h]; gb[:, 32 + h, 1] = gbias[:, 3, h]
    convp = np.zeros((L, 128, 4, 34), f32)
    cw = A(conv_w)
    convp[:, :, :, 0:31] = cw.reshape(L, CONV_K, 4, 128).transpose(0, 3, 2, 1)
    convp[:, :, :, 31] = A(conv_b).reshape(L, 4, 128).transpose(0, 2, 1)
    convp[:, :, :, 32] = A(conv_ln_w).reshape(L, 4, 128).transpose(0, 2, 1)
    convp[:, :, :, 33] = A(conv_ln_b).reshape(L, 4, 128).transpose(0, 2, 1)
    lnp = np.zeros((L, 128, 4, 8), f32)
    for i, a in enumerate((ln1_w, ln1_b, ln2_w, ln2_b)):
        lnp[:, :, i, :] = A(a).reshape(L, 8, 128).transpose(0, 2, 1)
    ada_bT = np.ascontiguousarray(A(ada_b).reshape(L, 48, 128).transpose(0, 2, 1))
    rdec = A(ret_decay).reshape(L, 1, 8)
    mnw = A(mlstm_norm_w).reshape(L, 1, 512)
    rnw = A(ret_norm_w).reshape(L, 1, 512)
    cst = np.zeros((128, 1024), f32)
    jj, ii = np.meshgrid(np.arange(128), np.arange(128), indexing="ij")
    cst[:, 0:128] = np.eye(128, dtype=f32)
    cst[:, 128:256] = np.where(jj <= ii, 0.0, NEG)
    cst[:, 256:384] = np.where(jj >= ii, 0.0, NEG)
    cst[:, 384:512] = np.maximum(ii - jj, 0)
    cst[:, 512:640] = np.maximum(jj - ii, 0)
    cst[:, 640:768] = (jj <= ii)
    cst[:, 768:896] = (jj >= ii)
    p = np.arange(128)
    cst[:, 896] = p + 1; cst[:, 897] = 128 - p; cst[:, 898] = 127 - p; cst[:, 899] = p; cst[:, 900] = 128
    sel = np.zeros((36, 8, 128), f32)
    for r in range(8):
        sel[_prow(r), r, :] = 1.0
    t = np.arange(1024)
    rows = (t // 64).astype(f32); colsg = (t % 64).astype(f32)
    freqs = (np.float32(10000.0) ** (-np.arange(32, dtype=f32) / np.float32(32))).astype(f32)
    ang = np.concatenate([rows[:, None] * freqs[None, :], colsg[:, None] * freqs[None, :]], -1).astype(f32)
    cs_lat = np.concatenate([np.cos(ang).T, np.cos(ang).T], 0).astype(f32)
    sn_lat = np.concatenate([-np.sin(ang).T, np.sin(ang).T], 0).astype(f32)
    cs_id = np.ones((128, 1024), f32); sn_id = np.zeros((128, 1024), f32)

    shared = dict(cst=cst, sel=sel, ada_w=A(ada_w), ada_bT=ada_bT, in_wp=in_wp, gb=gb, mnw=mnw, rnw=rnw,
                  mlstm_out_w=A(mlstm_out_w), conv_out_w=A(conv_out_w), ret_out_w=A(ret_out_w), convp=convp, rdec=rdec,
                  out_w=A(out_w), lnp=lnp, ffn_w13=A(ffn_w13), ffn_w2=A(ffn_w2))
    in_maps = []
    seg_prompt = []
    for core in range(8):
        if core < 4:
            xs = np.concatenate([x_sample[core], x_prompt[2 * core], x_prompt[2 * core + 1]], 0)
            seg_prompt.append({4: 2 * core, 5: 2 * core + 1})
            cvec = c[core]
            Cinit = np.concatenate([sC[core], sn[core][..., None]], -1)
            minit = np.zeros((L, 36, 1), f32)
            for h in range(4):
                minit[:, h, 0] = smm[core, :, 0, h]; minit[:, 32 + h, 0] = smm[core, :, 1, h]
            Sinit = sS[core][:, :, :, _IDX, :]
            chainv = 1.0
            rcs, rsn = cs_lat, sn_lat
        else:
            base = 8 + 6 * (core - 4)
            xs = np.concatenate([x_prompt[base + s] for s in range(6)], 0)
            seg_prompt.append({s: base + s for s in range(6)})
            cvec = c_ctx
            Cinit = np.zeros((L, 2, H, 128, 129), f32)
            minit = np.zeros((L, 36, 1), f32)
            Sinit = np.zeros((L, 2, H, 128, 128), f32)
            chainv = 0.0
            rcs, rsn = cs_id, sn_id
        xT = np.ascontiguousarray(xs.reshape(T, 8, 128).transpose(2, 1, 0))
        cv = np.stack([c_ctx.reshape(8, 128).T, cvec.reshape(8, 128).T], -1)
        m = dict(shared)
        m.update(xT=xT, cv=np.ascontiguousarray(cv, dtype=f32), chain=np.full((128, 1), chainv, f32),
                 Cinit=np.ascontiguousarray(Cinit, dtype=f32), minit=minit, Sinit=np.ascontiguousarray(Sinit, dtype=f32),
                 ropeCS=rcs, ropeSN=rsn)
        in_maps.append(m)

    key = bool(_dbg)
    nc = build(dbg=key)
    res = run_bass_kernel_spmd(nc, in_maps, core_ids=list(range(8)))
    R = res.results
    y_prompt = np.zeros((32, 256, D), f32)
    y_sample = np.zeros((4, 1024, D), f32)
    new_C = np.zeros((32, L, 2, H, 128, 128), f32)
    new_n = np.zeros((32, L, 2, H, 128), f32)
    new_m = np.zeros((32, L, 2, H), f32)
    new_S = np.zeros((32, L, 2, H, 128, 128), f32)
    for core in range(8):
        r = R[core]
        yt = np.asarray(r["yT"]).transpose(2, 1, 0).reshape(T, D)
        if core < 4:
            y_sample[core] = yt[0:1024]
        Cf = np.asarray(r["Cfin"]); mf = np.asarray(r["mfin"]); Sf = np.asarray(r["Sfin"])
        for s, b in seg_prompt[core].items():
            y_prompt[b] = yt[s * 256:(s + 1) * 256]
            new_C[b] = Cf[:, s, :, :, :, 0:128]
            new_n[b] = Cf[:, s, :, :, :, 128]
            for h in range(4):
                new_m[b, :, 0, h] = mf[:, h, 2 * s + 1]
                new_m[b, :, 1, h] = mf[:, 32 + h, 2 * s]
            Su = np.empty((L, 2, H, 128, 128), f32)
            Su[:, :, :, _IDX, :] = Sf[:, s]
            new_S[b] = Su
    if _dbg:
        return (y_prompt, y_sample, new_C, new_n, new_m, new_S), R
    return (y_prompt, y_sample, new_C, new_n, new_m, new_S)
```

```python
import math
import itertools
import numpy as np
import concourse.bass as bass
import concourse.mybir as mybir
from concourse.bass_utils import run_bass_kernel_spmd
from concourse.ap import AP

F32 = mybir.dt.float32
BF16 = mybir.dt.bfloat16
AF = mybir.ActivationFunctionType
ALU = mybir.AluOpType
AX = mybir.AxisListType

D = 1024
L = 2
T = 1536
NCH = 12
NSEG = 6
H = 4
HD = 128
FF = 2816
NFT = 22
CONV_K = 31
EPS = 1e-5
ALPHA = (2.0 * L) ** 0.25
KS = HD ** -0.5
LNKS = math.log(KS)
NEG = -30000.0
NCOLS = 9288
OFF_GATE = 2048
OFF_CA = 2120
OFF_CG = 2632
OFF_RET = 3144
OFF_RVG = 5192
OFF_GM = 6216

DEBUG = {}
STOP = None


class _Stop(Exception):
    pass


def phase_end(name):
    if STOP == name:
        raise _Stop()


def rev(ap):
    a = [list(x) for x in ap.ap]
    step, n = a[-1]
    off = ap.offset + step * (n - 1)
    a[-1] = [-step, n]
    return AP(ap.tensor, off, a)


import types


def _snap(th):
    cl = th.__closure__
    if not cl:
        return th
    cells = []
    for c in cl:
        try:
            cells.append(types.CellType(c.cell_contents))
        except ValueError:
            cells.append(c)
    return types.FunctionType(th.__code__, th.__globals__, th.__name__, th.__defaults__, tuple(cells))


class TT:
    __slots__ = ("name", "w", "r")

    def __init__(self, name=""):
        self.name = name
        self.w = None
        self.r = {}


class Ctx:
    ENG = ["pe", "act", "dve", "pool", "sp"]

    def __init__(self, nc, sems):
        self.nc = nc
        self.sems = sems
        self.prog = {e: [] for e in self.ENG}
        self.cnt = {e: 0 for e in self.ENG}
        self.seen = {e: {} for e in self.ENG}
        self.dkeys = [k for k in sems if k[0] == "d" and k[1:].isdigit()]
        self.wkeys = [k for k in sems if k[0] == "w" and k[1:].isdigit()]
        self.dtot = {k: 0 for k in sems}
        self.dn = 0
        self.wn = 0

    def _wait(self, eng, key, val):
        if val <= 0 or self.seen[eng].get(key, 0) >= val:
            return
        self.seen[eng][key] = val
        sem = self.sems[key]
        self.prog[eng].append(lambda e, sem=sem, val=val: e.wait_ge(sem, val))

    def _deps(self, eng, reads, writes):
        for t in reads:
            if t.w is not None:
                self._wait_dep(eng, t.w)
        for t in writes:
            if t.w is not None:
                self._wait_dep(eng, t.w)
            for k, v in t.r.items():
                self._wait_dep(eng, (k, v))

    def _wait_dep(self, eng, dep):
        k, v = dep
        if k == "pe" and eng == "pe":
            return
        self._wait(eng, k, v)

    def run(self, eng, thunks, reads=(), writes=()):
        if not isinstance(thunks, (list, tuple)):
            thunks = [thunks]
        thunks = [_snap(t) for t in thunks]
        self._deps(eng, reads, writes)
        sem = self.sems[eng]
        n = len(thunks)
        for i, th in enumerate(thunks):
            if i == n - 1:
                self.prog[eng].append(lambda e, th=th, sem=sem: th(e).then_inc(sem, 1))
            else:
                self.prog[eng].append(th)
        self.cnt[eng] += 1
        c = self.cnt[eng]
        for t in reads:
            t.r[eng] = c
        for t in writes:
            t.w = (eng, c)
            t.r = {}

    def dma(self, q, out, in_, reads=(), writes=()):
        if q == "pool":
            key = self.wkeys[self.wn % len(self.wkeys)]
            self.wn += 1
        else:
            key = self.dkeys[self.dn % len(self.dkeys)]
            self.dn += 1
        prev = self.dtot[key]
        self._wait(q, key, prev)
        self._deps(q, reads, writes)
        new = prev + 16
        self.dtot[key] = new
        sem = self.sems[key]
        self.prog[q].append(lambda e, out=out, in_=in_, sem=sem: e.dma_start(out=out, in_=in_).then_inc(sem, 16))
        for t in reads:
            t.r[key] = new
        for t in writes:
            t.w = (key, new)
            t.r = {}

    def barrier(self):
        engs = ["pe", "act", "dve", "sp"]
        for e in engs:
            for f in ["pe", "act", "dve"]:
                if f != e or e != "pe":
                    self._wait(e, f, self.cnt[f])
            for k in self.dkeys:
                self._wait(e, k, self.dtot[k])

    def final(self):
        for k in self.dkeys:
            self._wait("sp", k, self.dtot[k])
        for f in ["pe", "act", "dve"]:
            self._wait("sp", f, self.cnt[f])


def build(dbg=False):
    nc = bass.Bass("TRN2", target_bir_lowering=False)
    dram = {}

    def din(name, shape, dt=F32):
        dram[name] = nc.dram_tensor(name, list(shape), dt, kind="ExternalInput").ap()
        return dram[name]

    def dout(name, shape, dt=F32):
        dram[name] = nc.dram_tensor(name, list(shape), dt, kind="ExternalOutput").ap()
        return dram[name]

    xT_d = din("xT", [128, 8, T])
    cv_d = din("cv", [128, 8, 2])
    chain_d = din("chain", [128, 1])
    Cinit_d = din("Cinit", [L, 2, H, 128, 129])
    minit_d = din("minit", [L, 36, 1])
    Sinit_d = din("Sinit", [L, 2, H, 128, 128])
    ropeCS_d = din("ropeCS", [128, 1024])
    ropeSN_d = din("ropeSN", [128, 1024])
    cst_d = din("cst", [128, 1024])
    sel_d = din("sel", [36, 8, 128])
    ada_w_d = din("ada_w", [L, D, 6 * D])
    ada_b_d = din("ada_bT", [L, 128, 48])
    in_w_d = din("in_wp", [L, D, NCOLS])
    gb_d = din("gb", [L, 36, 2])
    mnw_d = din("mnw", [L, 1, 512])
    rnw_d = din("rnw", [L, 1, 512])
    mow_d = din("mlstm_out_w", [L, 512, D])
    cow_d = din("conv_out_w", [L, 512, D])
    row_d = din("ret_out_w", [L, 512, D])
    convp_d = din("convp", [L, 128, 4, 34])
    rdec_d = din("rdec", [L, 1, 8])
    outw_d = din("out_w", [L, D, D])
    lnp_d = din("lnp", [L, 128, 4, 8])
    w13_d = din("ffn_w13", [L, D, 2 * FF])
    w2_d = din("ffn_w2", [L, FF, D])

    yT_d = dout("yT", [128, 8, T])
    Cfin_d = dout("Cfin", [L, NSEG, 2, H, 128, 129])
    mfin_d = dout("mfin", [L, 36, 12])
    Sfin_d = dout("Sfin", [L, NSEG, 2, H, 128, 128])
    dbg_d = {}

    import contextlib
    es = contextlib.ExitStack()
    with es:
        def sb(name, shape, dt=F32):
            return es.enter_context(nc.sbuf_tensor("s_" + name, list(shape), dt))[:]

        x = sb("x", [128, 8, T])
        hT = sb("hT", [128, 8, T], BF16)
        NW = 3
        Wr = [sb("wr%d" % i, [128, 4096], BF16) for i in range(NW)]
        Wr_T = [TT("wr%d" % i) for i in range(NW)]
        cst = sb("cst", [128, 1024])
        sel = sb("sel", [36, 8, 128])
        identB = sb("identB", [128, 128], BF16)
        onesB = sb("onesB", [128, 128], BF16)
        chain = sb("chain", [128, 1])
        cvs = sb("cvs", [128, 8, 2], BF16)
        cvf = sb("cvf", [128, 8, 2])
        modTs = [sb("modT%d" % i, [128, 48, 2]) for i in range(L)]
        sc1ps = [sb("sc1p%d" % i, [128, 8, 2]) for i in range(L)]
        sc2ps = [sb("sc2p%d" % i, [128, 8, 2]) for i in range(L)]
        adabs = [sb("adab%d" % i, [128, 48]) for i in range(L)]
        M_T = [TT("mod%d" % i) for i in range(L)]
        gb = sb("gb", [36, 2])
        ngb = sb("ngb", [36, 1])
        minit = sb("minit", [36, 1])
        mnw = sb("mnw", [128, 512])
        rnw = sb("rnw", [128, 512])
        convp = sb("convp", [128, 4, 34])
        lnp = sb("lnp", [128, 4, 8])
        rdec = sb("rdec", [128, 8])
        lg = sb("lg", [128, 8])
        rcol = sb("rcol", [128, 8, 4])
        decT = sb("decT", [128, 8, 128])
        ones36 = sb("ones36", [36, 128])
        zeros36 = sb("zeros36", [36, 128])
        AR = 23168
        arena = sb("arena", [128, AR])
        ps = [es.enter_context(nc.psum_tensor("ps%d" % i, [128, 512], F32))[:] for i in range(8)]
        ps_T = [TT("ps%d" % i) for i in range(8)]

        keys = ["pe", "act", "dve", "pool", "sp"] + ["d%d" % i for i in range(8)] + ["w%d" % i for i in range(6)]
        sems = {k: es.enter_context(nc.semaphore(k)) for k in keys}
        cx = Ctx(nc, sems)
        G = TT("globals")
        x_T = TT("x")
        hT_T = TT("hT")

        identF = cst[:, 0:128]
        MASK = [cst[:, 128:256], cst[:, 256:384]]
        DIFF = [cst[:, 384:512], cst[:, 512:640]]
        M01 = [cst[:, 640:768], cst[:, 768:896]]
        POS = cst[:, 896:904]

        pstate = {"i": 0}

        def psum():
            i = pstate["i"] % 8
            pstate["i"] += 1
            return ps[i], ps_T[i]

        wstate = {"i": 0}

        def wload(src, kc, ncol):
            i = wstate["i"] % NW
            wstate["i"] += 1
            dst = Wr[i][:, 0:kc * ncol].rearrange("p (k n) -> p k n", n=ncol)
            cx.dma("pool", dst, src.rearrange("(k p) n -> p k n", p=128), writes=[Wr_T[i]])
            return dst, Wr_T[i]

        ast = {"o": 0}

        def aset(o):
            cx.barrier()
            ast["o"] = o

        def af32(n, shape=None):
            o = ast["o"]
            ast["o"] += n
            assert ast["o"] <= AR, ast["o"]
            v = arena[:, o:o + n]
            return v

        def abf(n):
            n2 = (n + 1) // 2
            return af32(n2).bitcast(BF16)

        def dbg_out(name, ap, shape, reads):
            if not dbg:
                return
            d = dout("dbg_" + name, shape, ap.dtype)
            DEBUG[name] = shape
            cx.dma("sp", d, ap, reads=reads)

        mm = lambda out, lhsT, rhs, st, sp: (lambda e: e.matmul(out, lhsT=lhsT, rhs=rhs, start=st, stop=sp))

        for kc in range(8):
            cx.dma("sp", x[:, kc, :], xT_d[:, kc, :], writes=[x_T])
        for dst, src in ((cst, cst_d), (sel, sel_d), (chain, chain_d), (cvf, cv_d)):
            cx.dma("sp", dst, src, writes=[G])
        cx.run("dve", [lambda e: e.memset(ones36, 1.0), lambda e: e.memset(zeros36, 0.0),
                       lambda e: e.memset(onesB, 1.0),
                       lambda e: e.tensor_copy(out=identB, in_=identF)],
               reads=[G], writes=[G])
        cx.run("act", [lambda e: e.activation(out=cvs, in_=cvf, func=AF.Silu)], reads=[G], writes=[G])
        for i in range(L):
            cx.dma("sp", adabs[i], ada_b_d[i], writes=[M_T[i]])

        def mod_groups(ll, gs):
            mT = modTs[ll]
            for g in gs:
                wv, wT = wload(ada_w_d[ll][:, g * 512:(g + 1) * 512], 8, 512)
                mb, mbT = psum()
                for jj in range(4):
                    cx.run("pe", [mm(mb[:, 2 * jj:2 * jj + 2], wv[:, kc, jj * 128:(jj + 1) * 128], cvs[:, kc, :], kc == 0, kc == 7) for kc in range(8)],
                           reads=[wT, G], writes=[mbT])
                j0 = g * 4
                cx.run("act", [lambda e, mb=mb, j0=j0: e.activation(out=mT[:, j0:j0 + 4, :].rearrange("p j v -> p (j v)"), in_=mb[:, 0:8], func=AF.Copy)],
                       reads=[mbT], writes=[M_T[ll]])
                cx.run("dve", [lambda e, j0=j0: e.tensor_tensor(out=mT[:, j0:j0 + 4, :], in0=mT[:, j0:j0 + 4, :],
                                                                in1=adabs[ll][:, j0:j0 + 4].unsqueeze(2).to_broadcast([128, 4, 2]), op=ALU.add)],
                       reads=[M_T[ll]], writes=[M_T[ll]])
                if g in (2, 3):
                    o = (g - 2) * 4
                    cx.run("dve", [lambda e, j0=j0, o=o: e.tensor_scalar(out=sc1ps[ll][:, o:o + 4, :], in0=mT[:, j0:j0 + 4, :], scalar1=1.0, scalar2=None, op0=ALU.add)],
                           reads=[M_T[ll]], writes=[M_T[ll]])
                if g in (8, 9):
                    o = (g - 8) * 4
                    cx.run("dve", [lambda e, j0=j0, o=o: e.tensor_scalar(out=sc2ps[ll][:, o:o + 4, :], in0=mT[:, j0:j0 + 4, :], scalar1=1.0, scalar2=None, op0=ALU.add)],
                           reads=[M_T[ll]], writes=[M_T[ll]])

        def layer_norm_fm(which, blocks, xb, xq, lnts):
            lw = lnp[:, 2 * which, :]
            lb = lnp[:, 2 * which + 1, :]
            xb_T, xq_T = TT("xb"), TT("xq")
            nblk = len(blocks)
            XB = []
            for _ in range(nblk):
                t = TT("xblk")
                t.w = x_T.w
                t.r = dict(x_T.r)
                XB.append(t)
            LT = [TT("lnt0"), TT("lnt1")]
            banks = {}

            def A(i):
                sl = slice(*blocks[i])
                cx.run("act", [lambda e, kc=kc: e.activation(out=xb[:, kc, :], in_=x[:, kc, sl], func=AF.Copy) for kc in range(8)],
                       reads=[XB[i]], writes=[xb_T])
                cx.run("act", [lambda e, kc=kc: e.activation(out=xq[:, kc, :], in_=x[:, kc, sl], func=AF.Square) for kc in range(8)],
                       reads=[XB[i]], writes=[xq_T])

            def P(i):
                b1, b1T = psum()
                cx.run("pe", [mm(b1, onesB, xb[:, kc, :], kc == 0, kc == 7) for kc in range(8)], reads=[xb_T], writes=[b1T])
                b2, b2T = psum()
                cx.run("pe", [mm(b2, onesB, xq[:, kc, :], kc == 0, kc == 7) for kc in range(8)], reads=[xq_T], writes=[b2T])
                banks[i] = (b1, b1T, b2, b2T)

            def S1(i):
                b1, b1T, b2, b2T = banks[i]
                mean, msq, rstd = lnts[i % 2]
                T_ = LT[i % 2]
                cx.run("act", [lambda e: e.activation(out=mean, in_=b1, func=AF.Copy, scale=1.0 / D)], reads=[b1T], writes=[T_])
                cx.run("dve", [lambda e: e.tensor_tensor(out=msq, in0=mean, in1=mean, op=ALU.mult)], reads=[T_], writes=[T_])
                cx.run("dve", [lambda e: e.scalar_tensor_tensor(out=msq, in0=b2, scalar=1.0 / D, in1=msq, op0=ALU.mult, op1=ALU.subtract)],
                       reads=[b2T, T_], writes=[T_])
                cx.run("dve", [lambda e: e.tensor_scalar(out=msq, in0=msq, scalar1=EPS, scalar2=None, op0=ALU.add)], reads=[T_], writes=[T_])

            def S2(i):
                mean, msq, rstd = lnts[i % 2]
                T_ = LT[i % 2]
                cx.run("act", [lambda e: e.activation(out=rstd, in_=msq, func=AF.Sqrt)], reads=[T_], writes=[T_])
                cx.run("dve", [lambda e: e.reciprocal(out=rstd, in_=rstd)], reads=[T_], writes=[T_])

            def Dd(i):
                sl = slice(*blocks[i])
                mean, msq, rstd = lnts[i % 2]
                T_ = LT[i % 2]
                cx.run("dve", [lambda e, kc=kc: e.tensor_tensor(out=x[:, kc, sl], in0=x[:, kc, sl], in1=mean, op=ALU.subtract) for kc in range(8)],
                       reads=[XB[i], T_], writes=[XB[i]])
                cx.run("dve", [lambda e, kc=kc: e.tensor_tensor(out=x[:, kc, sl], in0=x[:, kc, sl], in1=rstd, op=ALU.mult) for kc in range(8)],
                       reads=[XB[i], T_], writes=[XB[i]])

            def Da(i):
                sl = slice(*blocks[i])
                cx.run("act", [lambda e, kc=kc: e.activation(out=x[:, kc, sl], in_=x[:, kc, sl], func=AF.Identity,
                                                             scale=lw[:, kc:kc + 1], bias=lb[:, kc:kc + 1]) for kc in range(8)],
                       reads=[XB[i], G], writes=[XB[i]])

            assert nblk == 3
            A(0); P(0); S1(0); A(1); P(1); S2(0); Dd(0); S1(1); A(2); P(2); S2(1); Da(0); Dd(1); S1(2); S2(2); Da(1); Dd(2); Da(2)
            x_T.w = XB[2].w
            x_T.r = {}
            for t in XB:
                for k_, v_ in t.r.items():
                    x_T.r[k_] = max(x_T.r.get(k_, 0), v_)

        lnks = sb("lnks", [128, 1])
        cx.run("dve", [lambda e: e.memset(lnks, LNKS)], writes=[G])

        def ln_stats(b1, b1T, b2, b2T, mean, msq, rstd, st_T, n):
            cx.run("act", [lambda e: e.activation(out=mean, in_=b1, func=AF.Copy, scale=1.0 / n)], reads=[b1T], writes=[st_T])
            cx.run("dve", [lambda e: e.tensor_tensor(out=msq, in0=mean, in1=mean, op=ALU.mult)], reads=[st_T], writes=[st_T])
            cx.run("dve", [lambda e: e.scalar_tensor_tensor(out=msq, in0=b2, scalar=1.0 / n, in1=msq, op0=ALU.mult, op1=ALU.subtract)],
                   reads=[b2T, st_T], writes=[st_T])
            cx.run("dve", [lambda e: e.tensor_scalar(out=msq, in0=msq, scalar1=EPS, scalar2=None, op0=ALU.add)], reads=[st_T], writes=[st_T])
            cx.run("act", [lambda e: e.activation(out=rstd, in_=msq, func=AF.Sqrt)], reads=[st_T], writes=[st_T])
            cx.run("dve", [lambda e: e.reciprocal(out=rstd, in_=rstd)], reads=[st_T], writes=[st_T])

        def head_norm_out(src3, src_T, normw, h, gate3, gate_T, dstT, dst_T, scr, scr_T, hn_all, hn_Ts):
            stA = scr[:, 0:72].rearrange("p (c n) -> p c n", n=6)
            mvA = scr[:, 72:96].rearrange("p (c n) -> p c n", n=2)
            src_all = src3
            hn3 = hn_all.rearrange("p (c n) -> p c n", n=128)
            cx.run("dve", [lambda e, c=c: e.bn_stats(out=stA[:, c, :], in_=src3[:, c, :]) for c in range(12)], reads=list(src_T) + [scr_T], writes=[scr_T])
            cx.run("dve", [lambda e, c=c: e.bn_aggr(out=mvA[:, c, :], in_=stA[:, c, :]) for c in range(12)], reads=[scr_T], writes=[scr_T])
            rstd = mvA[:, :, 1]
            cx.run("dve", [lambda e: e.tensor_scalar(out=rstd, in0=rstd, scalar1=EPS, scalar2=None, op0=ALU.add)], reads=[scr_T], writes=[scr_T])
            cx.run("act", [lambda e: e.activation(out=rstd, in_=rstd, func=AF.Sqrt)], reads=[scr_T], writes=[scr_T])
            cx.run("dve", [lambda e: e.reciprocal(out=rstd, in_=rstd)], reads=[scr_T], writes=[scr_T])
            cx.run("dve", [lambda e, c=c: e.tensor_scalar(out=src3[:, c, :], in0=src3[:, c, :], scalar1=mvA[:, c, 0:1], scalar2=mvA[:, c, 1:2],
                                                          op0=ALU.subtract, op1=ALU.mult) for c in range(12)], reads=list(src_T) + [scr_T], writes=list(src_T))
            cx.run("dve", [lambda e, c=c: e.tensor_tensor(out=src3[:, c, :], in0=src3[:, c, :], in1=normw[:, h * 128:(h + 1) * 128], op=ALU.mult) for c in range(12)],
                   reads=list(src_T) + [G], writes=list(src_T))
            cx.run("dve", [lambda e: e.tensor_tensor(out=hn3, in0=src3, in1=gate3, op=ALU.mult)], reads=list(src_T) + [gate_T] + list(hn_Ts), writes=list(hn_Ts))
            for c in range(12):
                bk, bkT = psum()
                bkb = bk.bitcast(BF16)
                cx.run("pe", [lambda e, bkb=bkb, c=c: e.transpose(out=bkb[:, 0:128], in_=hn3[:, c, :], identity=identB)], reads=list(hn_Ts) + [G], writes=[bkT])
                cx.run("act", [lambda e, bkb=bkb, c=c: e.activation(out=dstT[:, h, c * 128:(c + 1) * 128], in_=bkb[:, 0:128], func=AF.Copy)], reads=[bkT], writes=[dst_T])

        def layers():
          for l in range(L):
              phase_end('pro%d' % l)
              aset(0)
              for dst, src in ((gb, gb_d[l]), (minit, minit_d[l]), (convp, convp_d[l]), (lnp, lnp_d[l]),
                               (mnw, mnw_d[l].partition_broadcast(128)), (rnw, rnw_d[l].partition_broadcast(128)),
                               (rdec, rdec_d[l].partition_broadcast(128))):
                  cx.dma("sp", dst, src, writes=[G])
              cx.run("dve", [lambda e: e.tensor_scalar(out=ngb, in0=gb[:, 1:2], scalar1=-1.0, scalar2=None, op0=ALU.mult)], reads=[G], writes=[G])
              cx.run("act", [lambda e: e.activation(out=lg, in_=rdec, func=AF.Exp, scale=-1.0)], reads=[G], writes=[G])
              cx.run("act", [lambda e: e.activation(out=lg, in_=lg, func=AF.Ln, bias=1.0)], reads=[G], writes=[G])
              cx.run("dve", [lambda e: e.tensor_scalar(out=lg, in0=lg, scalar1=-1.0, scalar2=None, op0=ALU.mult)], reads=[G], writes=[G])
              for r in range(8):
                  d = r // 4
                  lgc = lg[:, r:r + 1]
                  cx.run("act", [lambda e, r=r, d=d, lgc=lgc: e.activation(out=decT[:, r, :], in_=DIFF[d], func=AF.Exp, scale=lgc)], reads=[G], writes=[G])
                  cx.run("dve", [lambda e, r=r, d=d: e.scalar_tensor_tensor(out=decT[:, r, :], in0=decT[:, r, :], scalar=KS, in1=M01[d],
                                                                           op0=ALU.mult, op1=ALU.mult)], reads=[G], writes=[G])
                  cx.run("act", [lambda e, r=r, d=d, lgc=lgc: e.activation(out=rcol[:, r, 0:1], in_=POS[:, d:d + 1], func=AF.Exp, scale=lgc),
                                 lambda e, r=r, d=d, lgc=lgc: e.activation(out=rcol[:, r, 1:2], in_=POS[:, 2 + d:3 + d], func=AF.Exp, scale=lgc),
                                 lambda e, r=r, d=d, lgc=lgc: e.activation(out=rcol[:, r, 2:3], in_=POS[:, 4:5], func=AF.Exp, scale=lgc)],
                         reads=[G], writes=[G])
                  cx.run("dve", [lambda e, r=r: e.tensor_scalar(out=rcol[:, r, 1:2], in0=rcol[:, r, 1:2], scalar1=KS, scalar2=None, op0=ALU.mult)],
                         reads=[G], writes=[G])

              phase_end('small%d' % l)
              modT, sc1p, sc2p = modTs[l], sc1ps[l], sc2ps[l]
              if l == 0:
                  mod_groups(0, [0, 1, 2, 3])
              phase_end('modmm%d' % l)

              def modulate(scp, shoff):
                  ths = []
                  for kc in range(8):
                      for (a, b, v) in ((0, 1024, 1), (1024, T, 0)):
                          ths.append(lambda e, kc=kc, a=a, b=b, v=v: e.tensor_scalar(
                              out=hT[:, kc, a:b], in0=x[:, kc, a:b], scalar1=scp[:, kc, v:v + 1],
                              scalar2=modT[:, shoff + kc, v:v + 1], op0=ALU.mult, op1=ALU.add))
                  cx.run("dve", ths, reads=[x_T, M_T[l]], writes=[hT_T])

              dbg_out("modT_l%d" % l, modT, [128, 48, 2], [M_T[l]])
              phase_end("mod%d" % l)
              modulate(sc1p, 0)
              phase_end("h%d" % l)
              dbg_out("h_l%d" % l, hT, [128, 8, T], [hT_T])

              aset(3072)
              h_aT = arena[:, 0:3072].bitcast(BF16).rearrange("p (h t) -> p h t", t=T)
              ubT = arena[:, 3072:6144].bitcast(BF16).rearrange("p (h t) -> p h t", t=T)
              h_cT = arena[:, 6144:9216].bitcast(BF16).rearrange("p (h t) -> p h t", t=T)
              merged = arena[:, 9216:15360].bitcast(BF16).rearrange("p (h t) -> p h t", t=T)
              haT_T = TT("h_aT"); ub_T = TT("ubT"); hcT_T = TT("h_cT"); mg_T = TT("merged")
              rIG = af32(T); rA = af32(T); rP = af32(T); rCM = af32(T)
              R_T = TT("rows")
              sm = af32(96).rearrange("p (a c) -> p a c", c=12)
              SM_T = TT("sm")
              cols = af32(3 * 432).rearrange("p (q n) -> p q n", n=432)
              COL_T = TT("cols")
              cbc = af32(96)
              qT = abf(T); kT = abf(T)
              qk_T = TT("qk")
              vext = abf(12 * 130).rearrange("p (c n) -> p c n", n=130)
              v_T = TT("vext")
              osig = abf(T).rearrange("p (c n) -> p c n", n=128)
              o_T = TT("osig")
              hsum = af32(T).rearrange("p (c n) -> p c n", n=128)
              hs_T = [TT("hs%d" % c) for c in range(12)]
              kw = [abf(T).rearrange("p (c n) -> p c n", n=128) for _ in range(2)]
              kw_T = [[TT("kw") for c in range(12)] for _ in range(2)]
              Cx = [[af32(130) for _ in range(2)] for _ in range(2)]
              Cx_T = [[TT("cx"), TT("cx")] for _ in range(2)]
              sc2 = af32(8)
              dmg = [af32(512)] * 2
              dmg_T = [TT("dmg0")] * 2
              sT_all = abf(24 * 128)
              sTa_T = [TT("sTa%d" % i) for i in range(6)]
              Cb_all = [abf(12 * 130).rearrange("p (s n) -> p s n", n=130) for _ in range(2)]
              CbA_T = [[TT("cba") for _ in range(12)] for _ in range(2)]
              Bs2 = [af32(130) for _ in range(2)]
              Bs2_T = [TT("bs2a"), TT("bs2b")]
              tot_all = af32(12 * 130).rearrange("p (c n) -> p c n", n=130)
              tota_T = TT("tot_all")
              dn12 = af32(12)
              dn_T = TT("dn12")

              wg, wgT = wload(in_w_d[l][:, OFF_GATE:OFF_GATE + 72], 8, 72)
              cx.run("dve", [lambda e: e.memset(rP[0:36, :], 0.0), lambda e: e.memset(rCM[0:36, :], 0.0)], writes=[R_T])
              for which in (0, 1):
                  for nb in range(3):
                      bk, bkT = psum()
                      sl = slice(nb * 512, (nb + 1) * 512)
                      cx.run("pe", [mm(bk[0:36, :], wg[:, kc, which * 36:(which + 1) * 36], hT[:, kc, sl], kc == 0, kc == 7) for kc in range(8)],
                             reads=[wgT, hT_T], writes=[bkT])
                      if which == 0:
                          cx.run("act", [lambda e, bk=bk, sl=sl: e.activation(out=rIG[0:36, sl], in_=bk[0:36, :], func=AF.Identity, bias=gb[:, 0:1])],
                                 reads=[bkT, G], writes=[R_T])
                      else:
                          cx.run("act", [lambda e, bk=bk, sl=sl: e.activation(out=rA[0:36, sl], in_=bk[0:36, :], func=AF.Exp, scale=-1.0, bias=ngb[:, 0:1])],
                                 reads=[bkT, G], writes=[R_T])
              cx.run("act", [lambda e: e.activation(out=rA[0:36, :], in_=rA[0:36, :], func=AF.Ln, bias=1.0)], reads=[R_T], writes=[R_T])
              ths = []
              for c in range(12):
                  cs = slice(c * 128, (c + 1) * 128)
                  ths.append(lambda e, cs=cs: e.tensor_tensor_scan(out=rP[0:4, cs], data0=ones36[0:4, :], data1=rA[0:4, cs], initial=0.0,
                                                                   op0=ALU.mult, op1=ALU.add))
                  ths.append(lambda e, cs=cs: e.tensor_tensor_scan(out=rev(rP[32:36, cs]), data0=ones36[32:36, :], data1=rev(rA[32:36, cs]),
                                                                   initial=0.0, op0=ALU.mult, op1=ALU.add))
              cx.run("dve", ths, reads=[R_T], writes=[R_T])
              cx.run("dve", [lambda e: e.tensor_tensor(out=rA[0:36, :], in0=rIG[0:36, :], in1=rP[0:36, :], op=ALU.add)], reads=[R_T], writes=[R_T])
              ths = []
              for c in range(12):
                  cs = slice(c * 128, (c + 1) * 128)
                  ths.append(lambda e, cs=cs: e.tensor_tensor_scan(out=rCM[0:4, cs], data0=zeros36[0:4, :], data1=rA[0:4, cs], initial=-1e30,
                                                                   op0=ALU.add, op1=ALU.max))
                  ths.append(lambda e, cs=cs: e.tensor_tensor_scan(out=rev(rCM[32:36, cs]), data0=zeros36[32:36, :], data1=rev(rA[32:36, cs]),
                                                                   initial=-1e30, op0=ALU.add, op1=ALU.max))
              cx.run("dve", ths, reads=[R_T], writes=[R_T])
              CM3 = rCM.rearrange("p (c t) -> p c t", t=128)
              P3 = rP.rearrange("p (c t) -> p c t", t=128)
              A3 = rA.rearrange("p (c t) -> p c t", t=128)
              IG3 = rIG.rearrange("p (c t) -> p c t", t=128)
              cx.run("dve", [lambda e: e.memset(sm[0:36, :, :], 0.0)], writes=[SM_T])
              cx.run("dve", [lambda e: e.tensor_copy(out=sm[0:4, 0, :], in_=CM3[0:4, :, 127]),
                             lambda e: e.tensor_copy(out=sm[32:36, 0, :], in_=CM3[32:36, :, 0]),
                             lambda e: e.tensor_copy(out=sm[0:4, 1, :], in_=P3[0:4, :, 127]),
                             lambda e: e.tensor_copy(out=sm[32:36, 1, :], in_=P3[32:36, :, 0])], reads=[R_T, SM_T], writes=[SM_T])
              for d, r0 in ((0, 0), (1, 32)):
                  rs = slice(r0, r0 + 4)
                  order = list(range(12)) if d == 0 else list(range(11, -1, -1))
                  prev = None
                  for c in order:
                      s = c // 2
                      start = (c % 2 == 0) if d == 0 else (c % 2 == 1)
                      m0c = sm[rs, 2, c:c + 1]
                      if start:
                          if (d == 0 and s == 0) or (d == 1 and s == 3):
                              th = lambda e, m0c=m0c, rs=rs: e.tensor_copy(out=m0c, in_=minit[rs, :])
                          elif (d == 0 and s <= 3) or (d == 1 and s <= 2):
                              th = lambda e, m0c=m0c, rs=rs, prev=prev: e.tensor_scalar(out=m0c, in0=sm[rs, 4, prev:prev + 1], scalar1=chain[rs, :],
                                                                                       scalar2=None, op0=ALU.mult)
                          else:
                              th = lambda e, m0c=m0c: e.memset(m0c, 0.0)
                      else:
                          th = lambda e, m0c=m0c, rs=rs, prev=prev: e.tensor_copy(out=m0c, in_=sm[rs, 4, prev:prev + 1])
                      cx.run("dve", [th], reads=[SM_T, G], writes=[SM_T])
                      cx.run("dve", [lambda e, rs=rs, c=c: e.tensor_tensor(out=sm[rs, 3, c:c + 1], in0=sm[rs, 2, c:c + 1], in1=sm[rs, 0, c:c + 1], op=ALU.max)],
                             reads=[SM_T], writes=[SM_T])
                      cx.run("dve", [lambda e, rs=rs, c=c: e.tensor_tensor(out=sm[rs, 4, c:c + 1], in0=sm[rs, 3, c:c + 1], in1=sm[rs, 1, c:c + 1], op=ALU.subtract)],
                             reads=[SM_T], writes=[SM_T])
                      prev = c
              cx.dma("sp", mfin_d[l], sm[0:36, 4, :], reads=[SM_T])
              cx.run("dve", [lambda e: e.tensor_tensor(out=sm[0:36, 6, :], in0=sm[0:36, 3, :], in1=sm[0:36, 2, :], op=ALU.subtract)], reads=[SM_T], writes=[SM_T])
              cx.run("act", [lambda e: e.activation(out=sm[0:36, 5, :], in_=sm[0:36, 6, :], func=AF.Exp, scale=-1.0)], reads=[SM_T], writes=[SM_T])
              m0b = sm[0:36, 2, :].unsqueeze(2).to_broadcast([36, 12, 128])
              mxb = sm[0:36, 3, :].unsqueeze(2).to_broadcast([36, 12, 128])
              cx.run("dve", [lambda e: e.tensor_tensor(out=CM3[0:36], in0=CM3[0:36], in1=m0b, op=ALU.max)], reads=[R_T, SM_T], writes=[R_T])
              cx.run("dve", [lambda e: e.tensor_scalar(out=rCM[0:36, :], in0=rCM[0:36, :], scalar1=-1.0, scalar2=None, op0=ALU.mult)], reads=[R_T], writes=[R_T])
              dbg_out("rA_l%d" % l, rA[0:36, :], [36, T], [R_T])
              dbg_out("rNG_l%d" % l, rCM[0:36, :], [36, T], [R_T])

              def cols_from(q, rows):
                  bk, bkT = psum()
                  cx.run("pe", [lambda e, c=c, bk=bk: e.transpose(out=bk[:, c * 36:(c + 1) * 36], in_=rows[0:36, c * 128:(c + 1) * 128],
                                                                  identity=identF[0:36, 0:36]) for c in range(12)],
                         reads=[R_T, G], writes=[bkT])
                  cx.run("act", [lambda e, bk=bk: e.activation(out=cols[:, q, :], in_=bk[:, 0:432], func=AF.Copy)], reads=[bkT], writes=[COL_T])

              cx.run("dve", [lambda e: e.tensor_tensor(out=IG3[0:36], in0=CM3[0:36], in1=m0b, op=ALU.add)], reads=[R_T, SM_T], writes=[R_T])
              cx.run("act", [lambda e: e.activation(out=rIG[0:36, :], in_=rIG[0:36, :], func=AF.Exp)], reads=[R_T], writes=[R_T])
              cols_from(0, rIG)
              cx.run("dve", [lambda e: e.tensor_tensor(out=rP[0:36, :], in0=rP[0:36, :], in1=rCM[0:36, :], op=ALU.add)], reads=[R_T], writes=[R_T])
              cx.run("act", [lambda e: e.activation(out=rP[0:36, :], in_=rP[0:36, :], func=AF.Exp)], reads=[R_T], writes=[R_T])
              cols_from(1, rP)
              cx.run("dve", [lambda e: e.tensor_tensor(out=IG3[0:36], in0=A3[0:36], in1=mxb, op=ALU.subtract)], reads=[R_T, SM_T], writes=[R_T])
              cx.run("dve", [lambda e: e.tensor_scalar(out=rIG[0:36, :], in0=rIG[0:36, :], scalar1=LNKS, scalar2=None, op0=ALU.add)], reads=[R_T], writes=[R_T])
              cx.run("act", [lambda e: e.activation(out=rIG[0:36, :], in_=rIG[0:36, :], func=AF.Exp)], reads=[R_T], writes=[R_T])
              cols_from(2, rIG)
              bk, bkT = psum()
              cx.run("pe", [mm(bk[:, r * 12:(r + 1) * 12], sel[:, r, :], sm[0:36, 5, :], True, True) for r in range(8)], reads=[SM_T, G], writes=[bkT])
              cx.run("act", [lambda e, bk=bk: e.activation(out=cbc, in_=bk[:, 0:96], func=AF.Copy)], reads=[bkT], writes=[COL_T])
              dbg_out("sm_l%d" % l, sm[0:36, :, :], [36, 8, 12], [SM_T])
              dbg_out("cols_l%d" % l, cols, [128, 3, 432], [COL_T])

              phase_end("gates%d" % l)
              prow = lambda r: (r // 4) * 32 + (r % 4)
              lnks_col = None

              WH = {}

              def proj_qk(h):
                  wv, wT = wload(in_w_d[l][:, h * 512:(h + 1) * 512], 8, 512)
                  WH[h] = (wv, wT)
                  for nb in range(3):
                      sl = slice(nb * 512, (nb + 1) * 512)
                      for which, dst in ((0, qT), (1, kT)):
                          bk, bkT = psum()
                          cx.run("pe", [mm(bk, wv[:, kc, which * 128:(which + 1) * 128], hT[:, kc, sl], kc == 0, kc == 7) for kc in range(8)],
                                 reads=[wT, hT_T], writes=[bkT])
                          cx.run("act", [lambda e, bk=bk, dst=dst, sl=sl: e.activation(out=dst[:, sl], in_=bk, func=AF.Copy)], reads=[bkT], writes=[qk_T])
                  for c in range(12):
                      cs = slice(c * 128, (c + 1) * 128)
                      bk, bkT = psum()
                      bkb = bk.bitcast(BF16)
                      cx.run("pe", [lambda e, bkb=bkb, cs=cs: e.transpose(out=bkb[:, 0:128], in_=kT[:, cs], identity=identB)], reads=[qk_T, G], writes=[bkT])
                      for d in range(2):
                          r = d * 4 + h
                          col = cols[:, 2, c * 36 + prow(r):c * 36 + prow(r) + 1]
                          cx.run("act", [lambda e, bkb=bkb, d=d, c=c, col=col: e.activation(out=kw[d][:, c, :], in_=bkb[:, 0:128], func=AF.Copy, scale=col)],
                                 reads=[bkT, COL_T], writes=[kw_T[d][c]])

              def proj_vo(h):
                  wv, wT = WH[h]
                  cx.run("dve", [lambda e: e.memset(vext[:, :, 128:130], 1.0)], writes=[v_T])
                  for c in range(12):
                      cs = slice(c * 128, (c + 1) * 128)
                      bk, bkT = psum()
                      cx.run("pe", [mm(bk[:, 0:256], hT[:, kc, cs], wv[:, kc, 256:512], kc == 0, kc == 7) for kc in range(8)], reads=[wT, hT_T], writes=[bkT])
                      cx.run("act", [lambda e, bk=bk, c=c: e.activation(out=vext[:, c, 0:128], in_=bk[:, 0:128], func=AF.Copy)], reads=[bkT], writes=[v_T])
                      cx.run("act", [lambda e, bk=bk, c=c: e.activation(out=osig[:, c, :], in_=bk[:, 128:256], func=AF.Sigmoid)], reads=[bkT], writes=[o_T])

              proj_qk(0)
              for h in range(H):
                  proj_vo(h)
                  orders = [list(range(12)), list(range(11, -1, -1))]
                  for g in range(6):
                      d = g // 3
                      r = d * 4 + h
                      b1, b1T = psum()
                      ths = []
                      css = []
                      for i in range(4):
                          c = orders[d][(g % 3) * 4 + i]
                          cs = slice(c * 128, (c + 1) * 128)
                          css.append(cs)
                          o = b1[:, i * 128:(i + 1) * 128]
                          ths += [mm(o, sel[:, r, :], rCM[0:36, cs], True, False), mm(o, rA[0:36, cs], sel[:, r, :], False, False),
                                  mm(o, identF, MASK[d], False, True)]
                      cx.run("pe", ths, reads=[R_T, G], writes=[b1T])
                      gp = g % 2
                      cx.run("act", [lambda e, b1=b1, gp=gp: e.activation(out=dmg[gp], in_=b1, func=AF.Exp, bias=lnks[:, 0:1])], reads=[b1T, G], writes=[dmg_T[gp]])
                      b2, b2T = psum()
                      cx.run("pe", [mm(b2[:, i * 128:(i + 1) * 128], kT[:, css[i]], qT[:, css[i]], True, True) for i in range(4)], reads=[qk_T], writes=[b2T])
                      cx.run("dve", [lambda e, b2=b2, gp=gp, g=g: e.tensor_tensor(out=sT_all[:, g * 512:(g + 1) * 512], in0=b2, in1=dmg[gp], op=ALU.mult)],
                             reads=[b2T, dmg_T[gp]], writes=[sTa_T[g]])

                  def rec(d, h=h):
                      r = d * 4 + h
                      cur = 0
                      Cxd, CxTd = Cx[d], Cx_T[d]
                      for step, c in enumerate(orders[d]):
                          s = c // 2
                          start = (c % 2 == 0) if d == 0 else (c % 2 == 1)
                          if start:
                              if (d == 0 and s == 0) or (d == 1 and s == 3):
                                  cx.dma("sp", Cxd[cur][:, 0:129], Cinit_d[l, d, h], writes=[CxTd[cur]])
                              elif (d == 0 and s <= 3) or (d == 1 and s <= 2):
                                  cx.run("dve", [lambda e, cur=cur: e.tensor_scalar(out=Cxd[cur][:, 0:129], in0=Cxd[cur][:, 0:129], scalar1=chain[:, 0:1],
                                                                                     scalar2=None, op0=ALU.mult)], reads=[CxTd[cur], G], writes=[CxTd[cur]])
                              else:
                                  cx.run("dve", [lambda e, cur=cur: e.memset(Cxd[cur][:, 0:129], 0.0)], writes=[CxTd[cur]])
                          cx.run("act", [lambda e, cur=cur, step=step: e.activation(out=Cb_all[d][:, step, 0:129], in_=Cxd[cur][:, 0:129], func=AF.Copy)],
                                 reads=[CxTd[cur]], writes=[CbA_T[d][step]])
                          bU, bUT = psum()
                          cx.run("pe", [mm(bU[:, 0:129], kw[d][:, c, :], vext[:, c, 0:129], True, True)], reads=[kw_T[d][c], v_T], writes=[bUT])
                          yield
                          nxt = 1 - cur
                          ccol = cbc[:, r * 12 + c:r * 12 + c + 1]
                          cx.run("dve", [lambda e, cur=cur, nxt=nxt: e.scalar_tensor_tensor(
                              out=Cxd[nxt][:, 0:129], in0=Cxd[cur][:, 0:129], scalar=ccol, in1=bU[:, 0:129], op0=ALU.mult, op1=ALU.add)],
                              reads=[bUT, CxTd[cur], COL_T], writes=[CxTd[nxt]])
                          cur = nxt
                          end = (c % 2 == 1) if d == 0 else (c % 2 == 0)
                          if end:
                              cx.dma("sp", Cfin_d[l, s, d, h], Cxd[cur][:, 0:129], reads=[CxTd[cur]])
                          yield

                  for _ in itertools.zip_longest(rec(0), rec(1)):
                      pass

                  for d in range(2):
                      r = d * 4 + h
                      pr = prow(r)
                      for step, c in enumerate(orders[d]):
                          cs = slice(c * 128, (c + 1) * 128)
                          p = d * 12 + step
                          bA, bAT = psum()
                          cx.run("pe", [mm(bA[:, 0:129], sT_all[:, p * 128:(p + 1) * 128], vext[:, c, 0:129], True, True),
                                        mm(bA[:, 256:385], qT[:, cs], Cb_all[d][:, step, 0:129], True, True)],
                                 reads=[sTa_T[p // 4], v_T, qk_T, CbA_T[d][step]], writes=[bAT])
                          wcol = cols[:, 0, c * 36 + pr:c * 36 + pr + 1]
                          bp = step % 2
                          cx.run("act", [lambda e, bA=bA, bp=bp, wcol=wcol: e.activation(out=Bs2[bp][:, 0:129], in_=bA[:, 256:385], func=AF.Copy, scale=wcol)],
                                 reads=[bAT, COL_T], writes=[Bs2_T[bp]])
                          cx.run("dve", [lambda e, bA=bA, bp=bp, c=c: e.tensor_tensor(out=tot_all[:, c, 0:129], in0=bA[:, 0:129], in1=Bs2[bp][:, 0:129], op=ALU.add)],
                                 reads=[bAT, Bs2_T[bp]], writes=[tota_T])
                      den = tot_all[:, :, 128]
                      ecols = cols[:, 1, :].rearrange("p (c n) -> p c n", n=36)[:, :, pr]
                      cx.run("dve", [lambda e, den=den: e.tensor_scalar(out=dn12, in0=den, scalar1=-1.0, scalar2=None, op0=ALU.mult)], reads=[tota_T], writes=[dn_T])
                      cx.run("dve", [lambda e, den=den: e.tensor_tensor(out=dn12, in0=dn12, in1=den, op=ALU.max)], reads=[tota_T, dn_T], writes=[dn_T])
                      cx.run("dve", [lambda e, ecols=ecols: e.tensor_tensor(out=dn12, in0=dn12, in1=ecols, op=ALU.max)], reads=[dn_T, COL_T], writes=[dn_T])
                      cx.run("dve", [lambda e: e.reciprocal(out=dn12, in_=dn12)], reads=[dn_T], writes=[dn_T])
                      if d == 0:
                          cx.run("act", [lambda e, c=c: e.activation(out=hsum[:, c, :], in_=tot_all[:, c, 0:128], func=AF.Copy, scale=dn12[:, c:c + 1])
                                         for c in range(12)], reads=[tota_T, dn_T], writes=hs_T)
                      else:
                          cx.run("dve", [lambda e, c=c: e.scalar_tensor_tensor(out=hsum[:, c, :], in0=tot_all[:, c, 0:128], scalar=dn12[:, c:c + 1],
                                                                               in1=hsum[:, c, :], op0=ALU.mult, op1=ALU.add) for c in range(12)],
                                 reads=[tota_T, dn_T] + hs_T, writes=hs_T)
                  if l == 0 and h == 0:
                      dbg_out("hsum_l0h0", hsum, [128, 12, 128], hs_T)
                  if h + 1 < H:
                      proj_qk(h + 1)
                  head_norm_out(hsum, hs_T, mnw, h, osig, o_T, h_aT, haT_T, tot_all.rearrange("p c n -> p (c n)"), tota_T, sT_all[:, 0:1536], sTa_T[0:3])
                  if l == 0:
                      mod_groups(0, [4 + 2 * h, 5 + 2 * h])
                  phase_end("mlstm%d_h%d" % (l, h))
              dbg_out("haT_l%d" % l, h_aT, [128, 4, T], [haT_T])

              phase_end("mlstm%d" % l)
              aset(6144)
              acc = af32(4 * T).rearrange("p (a t) -> p a t", t=T)
              acc_T = TT("acc")
              upad = [abf(6 * 286).rearrange("p (s n) -> p s n", n=286) for _ in range(2)]
              up_T = [TT("upad0"), TT("upad1")]
              Dg = [abf(31 * 128).rearrange("p (k n) -> p k n", n=128) for _ in range(2)]
              Dg_T = [TT("dg0"), TT("dg1")]
              sg = [af32(512) for _ in range(2)]
              sg_T = [TT("sg0"), TT("sg1")]
              cx.run("dve", [lambda e: e.memset(upad[0], 0.0), lambda e: e.memset(upad[1], 0.0)], writes=up_T)
              wa, waT = wload(in_w_d[l][:, OFF_CA:OFF_CA + 512], 8, 512)
              wgc, wgcT = wload(in_w_d[l][:, OFF_CG:OFF_CG + 512], 8, 512)
              kst = {"k": 0}

              def conv_in(ct):
                  up, upT, dg, dgT = upad[ct % 2], up_T[ct % 2], Dg[ct % 2], Dg_T[ct % 2]
                  cx.run("act", [lambda e, kk=kk, ct=ct, dg=dg: e.activation(out=dg[:, kk, :], in_=identF, func=AF.Copy, scale=convp[:, ct, kk:kk + 1])
                                 for kk in range(CONV_K)], reads=[G], writes=[dgT])
                  for nb in range(3):
                      sl = slice(nb * 512, (nb + 1) * 512)
                      ba, baT = psum()
                      cx.run("pe", [mm(ba, wa[:, kc, ct * 128:(ct + 1) * 128], hT[:, kc, sl], kc == 0, kc == 7) for kc in range(8)], reads=[waT, hT_T], writes=[baT])
                      bg, bgT = psum()
                      cx.run("pe", [mm(bg, wgc[:, kc, ct * 128:(ct + 1) * 128], hT[:, kc, sl], kc == 0, kc == 7) for kc in range(8)], reads=[wgcT, hT_T], writes=[bgT])
                      pp = kst["k"] % 2
                      kst["k"] += 1
                      cx.run("act", [lambda e, bg=bg, pp=pp: e.activation(out=sg[pp], in_=bg, func=AF.Sigmoid)], reads=[bgT], writes=[sg_T[pp]])
                      cx.run("dve", [lambda e, ba=ba, pp=pp, nb=nb, up=up, s2=s2: e.tensor_tensor(
                          out=up[:, 2 * nb + s2, 15:271], in0=ba[:, s2 * 256:(s2 + 1) * 256],
                          in1=sg[pp][:, s2 * 256:(s2 + 1) * 256], op=ALU.mult) for s2 in range(2)], reads=[baT, sg_T[pp]], writes=[upT])
                  ths = []
                  for s in (1, 2, 3):
                      ths.append(lambda e, s=s, up=up: e.tensor_scalar(out=up[:, s, 0:15], in0=up[:, s - 1, 256:271], scalar1=chain[:, 0:1], scalar2=None, op0=ALU.mult))
                  for s in (0, 1, 2):
                      ths.append(lambda e, s=s, up=up: e.tensor_scalar(out=up[:, s, 271:286], in0=up[:, s + 1, 15:30], scalar1=chain[:, 0:1], scalar2=None, op0=ALU.mult))
                  cx.run("dve", ths, reads=[upT, G], writes=[upT])

              def conv_mm(ct):
                  up, upT, dg, dgT = upad[ct % 2], up_T[ct % 2], Dg[ct % 2], Dg_T[ct % 2]
                  for s in range(6):
                      bk, bkT = psum()
                      cx.run("pe", [mm(bk[:, 0:256], dg[:, kk, :], up[:, s, kk:kk + 256], kk == 0, kk == CONV_K - 1) for kk in range(CONV_K)],
                             reads=[dgT, upT], writes=[bkT])
                      cx.run("act", [lambda e, bk=bk, ct=ct, s=s: e.activation(out=acc[:, ct, s * 256:(s + 1) * 256], in_=bk[:, 0:256], func=AF.Identity,
                                                                               bias=convp[:, ct, 31:32])], reads=[bkT, G], writes=[acc_T])

              conv_in(0)
              for ct in range(4):
                  if ct + 1 < 4:
                      conv_in(ct + 1)
                  conv_mm(ct)
              dbg_out("conv_l%d" % l, acc, [128, 4, T], [acc_T])
              cb = abf(4 * 256).rearrange("p (a n) -> p a n", n=256)
              cq = abf(4 * 256).rearrange("p (a n) -> p a n", n=256)
              cb_T = TT("cb"); cq_T = TT("cq")
              cst2 = [(af32(256), af32(256), af32(256)) for _ in range(2)]
              CT = [TT("clnt0"), TT("clnt1")]
              AB = []
              for _ in range(6):
                  t_ = TT("accblk")
                  t_.w = acc_T.w
                  t_.r = dict(acc_T.r)
                  AB.append(t_)
              cbanks = {}

              def cA(i):
                  sl = slice(i * 256, (i + 1) * 256)
                  cx.run("act", [lambda e, ct=ct: e.activation(out=cb[:, ct, :], in_=acc[:, ct, sl], func=AF.Copy) for ct in range(4)], reads=[AB[i]], writes=[cb_T])
                  cx.run("act", [lambda e, ct=ct: e.activation(out=cq[:, ct, :], in_=acc[:, ct, sl], func=AF.Square) for ct in range(4)], reads=[AB[i]], writes=[cq_T])

              def cP(i):
                  b1, b1T = psum()
                  cx.run("pe", [mm(b1[:, 0:256], onesB, cb[:, ct, :], ct == 0, ct == 3) for ct in range(4)], reads=[cb_T, G], writes=[b1T])
                  b2, b2T = psum()
                  cx.run("pe", [mm(b2[:, 0:256], onesB, cq[:, ct, :], ct == 0, ct == 3) for ct in range(4)], reads=[cq_T, G], writes=[b2T])
                  cbanks[i] = (b1, b1T, b2, b2T)

              def cS1(i):
                  b1, b1T, b2, b2T = cbanks[i]
                  mean, msq, rstd = cst2[i % 2]
                  T_ = CT[i % 2]
                  cx.run("act", [lambda e: e.activation(out=mean, in_=b1[:, 0:256], func=AF.Copy, scale=1.0 / 512)], reads=[b1T], writes=[T_])
                  cx.run("dve", [lambda e: e.tensor_tensor(out=msq, in0=mean, in1=mean, op=ALU.mult)], reads=[T_], writes=[T_])
                  cx.run("dve", [lambda e: e.scalar_tensor_tensor(out=msq, in0=b2[:, 0:256], scalar=1.0 / 512, in1=msq, op0=ALU.mult, op1=ALU.subtract)],
                         reads=[b2T, T_], writes=[T_])
                  cx.run("dve", [lambda e: e.tensor_scalar(out=msq, in0=msq, scalar1=EPS, scalar2=None, op0=ALU.add)], reads=[T_], writes=[T_])

              def cS2(i):
                  mean, msq, rstd = cst2[i % 2]
                  T_ = CT[i % 2]
                  cx.run("act", [lambda e: e.activation(out=rstd, in_=msq, func=AF.Sqrt)], reads=[T_], writes=[T_])
                  cx.run("dve", [lambda e: e.reciprocal(out=rstd, in_=rstd)], reads=[T_], writes=[T_])

              def cDd(i):
                  sl = slice(i * 256, (i + 1) * 256)
                  mean, msq, rstd = cst2[i % 2]
                  T_ = CT[i % 2]
                  cx.run("dve", [lambda e, ct=ct: e.tensor_tensor(out=acc[:, ct, sl], in0=acc[:, ct, sl], in1=mean, op=ALU.subtract) for ct in range(4)],
                         reads=[AB[i], T_], writes=[AB[i]])
                  cx.run("dve", [lambda e, ct=ct: e.tensor_tensor(out=acc[:, ct, sl], in0=acc[:, ct, sl], in1=rstd, op=ALU.mult) for ct in range(4)],
                         reads=[AB[i], T_], writes=[AB[i]])

              def cDa(i):
                  sl = slice(i * 256, (i + 1) * 256)
                  cx.run("act", [lambda e, ct=ct: e.activation(out=acc[:, ct, sl], in_=acc[:, ct, sl], func=AF.Identity,
                                                              scale=convp[:, ct, 32:33], bias=convp[:, ct, 33:34]) for ct in range(4)], reads=[AB[i], G], writes=[AB[i]])
                  cx.run("act", [lambda e, ct=ct: e.activation(out=ubT[:, ct, sl], in_=acc[:, ct, sl], func=AF.Silu) for ct in range(4)], reads=[AB[i]], writes=[ub_T])

              NB6 = 6
              cA(0); cP(0); cS1(0)
              for i in range(NB6):
                  if i + 1 < NB6:
                      cA(i + 1); cP(i + 1)
                  cS2(i)
                  if i >= 1:
                      cDa(i - 1)
                  cDd(i)
                  if i + 1 < NB6:
                      cS1(i + 1)
              cDa(NB6 - 1)
              dbg_out("ubT_l%d" % l, ubT, [128, 4, T], [ub_T])

              phase_end("conv%d" % l)
              aset(9216)
              ropeCS = af32(1024); ropeSN = af32(1024)
              RP_T = TT("rope")
              cx.dma("sp", ropeCS, ropeCS_d, writes=[RP_T])
              cx.dma("sp", ropeSN, ropeSN_d, writes=[RP_T])
              qT = abf(T); kT = abf(T)
              qk_T = TT("rqk")
              vv = abf(T).rearrange("p (c n) -> p c n", n=128)
              v_T = TT("rv")
              gsil = af32(T).rearrange("p (c n) -> p c n", n=128)
              o_T = TT("gsil")
              ysum = af32(T).rearrange("p (c n) -> p c n", n=128)
              hs_T = [TT("ys%d" % c) for c in range(12)]
              kz = [abf(T).rearrange("p (c n) -> p c n", n=128) for _ in range(2)]
              kz_T = [[TT("kz") for c in range(12)] for _ in range(2)]
              Sx = [[af32(128) for _ in range(2)] for _ in range(2)]
              Sx_T = [[TT("sx"), TT("sx")] for _ in range(2)]
              t1 = af32(512); t2 = af32(512)
              sT_all = abf(24 * 128)
              sTa_T = [TT("rsTa%d" % i) for i in range(6)]
              Sb_all = [abf(12 * 128).rearrange("p (s n) -> p s n", n=128) for _ in range(2)]
              SbA_T = [[TT("sba") for _ in range(12)] for _ in range(2)]
              Bs2 = [af32(128) for _ in range(2)]
              Bs2_T = [TT("rbs2a"), TT("rbs2b")]
              t_T = TT("rt")
              def rproj_qk(h):
                  wv, wT = wload(in_w_d[l][:, OFF_RET + h * 512:OFF_RET + (h + 1) * 512], 8, 512)
                  for nb in range(3):
                      sl = slice(nb * 512, (nb + 1) * 512)
                      for which, dst in ((0, qT), (1, kT)):
                          bk, bkT = psum()
                          cx.run("pe", [mm(bk, wv[:, kc, which * 256:which * 256 + 128], hT[:, kc, sl], kc == 0, kc == 7) for kc in range(8)],
                                 reads=[wT, hT_T], writes=[bkT])
                          if nb < 2:
                              bs_, bsT = psum()
                              cx.run("pe", [mm(bs_, wv[:, kc, which * 256 + 128:which * 256 + 256], hT[:, kc, sl], kc == 0, kc == 7) for kc in range(8)],
                                     reads=[wT, hT_T], writes=[bsT])
                              cx.run("dve", [lambda e, bk=bk, sl=sl: e.tensor_tensor(out=t1, in0=bk, in1=ropeCS[:, sl], op=ALU.mult)], reads=[bkT, RP_T, t_T], writes=[t_T])
                              cx.run("dve", [lambda e, bs_=bs_, sl=sl: e.tensor_tensor(out=t2, in0=bs_, in1=ropeSN[:, sl], op=ALU.mult)], reads=[bsT, RP_T, t_T], writes=[t_T])
                              cx.run("dve", [lambda e, dst=dst, sl=sl: e.tensor_tensor(out=dst[:, sl], in0=t1, in1=t2, op=ALU.add)], reads=[t_T], writes=[qk_T, t_T])
                          else:
                              cx.run("act", [lambda e, bk=bk, dst=dst, sl=sl: e.activation(out=dst[:, sl], in_=bk, func=AF.Copy)], reads=[bkT], writes=[qk_T])
                  for c in range(12):
                      cs = slice(c * 128, (c + 1) * 128)
                      bk, bkT = psum()
                      bkb = bk.bitcast(BF16)
                      cx.run("pe", [lambda e, bkb=bkb, cs=cs: e.transpose(out=bkb[:, 0:128], in_=kT[:, cs], identity=identB)], reads=[qk_T, G], writes=[bkT])
                      for d in range(2):
                          r = d * 4 + h
                          cx.run("act", [lambda e, bkb=bkb, d=d, c=c, r=r: e.activation(out=kz[d][:, c, :], in_=bkb[:, 0:128], func=AF.Copy, scale=rcol[:, r, 1:2])],
                                 reads=[bkT, G], writes=[kz_T[d][c]])

              def rproj_vg(h):
                  wv2, wT2 = wload(in_w_d[l][:, OFF_RVG + h * 256:OFF_RVG + (h + 1) * 256], 8, 256)
                  for c in range(12):
                      cs = slice(c * 128, (c + 1) * 128)
                      bk, bkT = psum()
                      cx.run("pe", [mm(bk[:, 0:256], hT[:, kc, cs], wv2[:, kc, :], kc == 0, kc == 7) for kc in range(8)], reads=[wT2, hT_T], writes=[bkT])
                      cx.run("act", [lambda e, bk=bk, c=c: e.activation(out=vv[:, c, :], in_=bk[:, 0:128], func=AF.Copy)], reads=[bkT], writes=[v_T])
                      cx.run("act", [lambda e, bk=bk, c=c: e.activation(out=gsil[:, c, :], in_=bk[:, 128:256], func=AF.Silu)], reads=[bkT], writes=[o_T])

              rproj_qk(0)
              for h in range(H):
                  rproj_vg(h)
                  orders = [list(range(12)), list(range(11, -1, -1))]
                  for g in range(6):
                      d = g // 3
                      r = d * 4 + h
                      b2, b2T = psum()
                      css = [slice(orders[d][(g % 3) * 4 + i] * 128, (orders[d][(g % 3) * 4 + i] + 1) * 128) for i in range(4)]
                      cx.run("pe", [mm(b2[:, i * 128:(i + 1) * 128], kT[:, css[i]], qT[:, css[i]], True, True) for i in range(4)], reads=[qk_T], writes=[b2T])
                      cx.run("dve", [lambda e, b2=b2, g=g, i=i, r=r: e.tensor_tensor(out=sT_all[:, g * 512 + i * 128:g * 512 + (i + 1) * 128],
                                                                                   in0=b2[:, i * 128:(i + 1) * 128], in1=decT[:, r, :], op=ALU.mult) for i in range(4)],
                             reads=[b2T, G], writes=[sTa_T[g]])

                  def rrec(d, h=h):
                      r = d * 4 + h
                      cur = 0
                      Sxd, SxTd = Sx[d], Sx_T[d]
                      for step, c in enumerate(orders[d]):
                          s = c // 2
                          start = (c % 2 == 0) if d == 0 else (c % 2 == 1)
                          if start:
                              if (d == 0 and s == 0) or (d == 1 and s == 3):
                                  cx.dma("sp", Sxd[cur], Sinit_d[l, d, h], writes=[SxTd[cur]])
                              elif (d == 0 and s <= 3) or (d == 1 and s <= 2):
                                  cx.run("dve", [lambda e, cur=cur: e.tensor_scalar(out=Sxd[cur], in0=Sxd[cur], scalar1=chain[:, 0:1],
                                                                                     scalar2=None, op0=ALU.mult)], reads=[SxTd[cur], G], writes=[SxTd[cur]])
                              else:
                                  cx.run("dve", [lambda e, cur=cur: e.memset(Sxd[cur], 0.0)], writes=[SxTd[cur]])
                          cx.run("act", [lambda e, cur=cur, step=step: e.activation(out=Sb_all[d][:, step, :], in_=Sxd[cur], func=AF.Copy)],
                                 reads=[SxTd[cur]], writes=[SbA_T[d][step]])
                          bU, bUT = psum()
                          cx.run("pe", [mm(bU[:, 0:128], kz[d][:, c, :], vv[:, c, :], True, True)], reads=[kz_T[d][c], v_T], writes=[bUT])
                          yield
                          nxt = 1 - cur
                          cx.run("dve", [lambda e, cur=cur, nxt=nxt: e.scalar_tensor_tensor(
                              out=Sxd[nxt], in0=Sxd[cur], scalar=rcol[:, r, 2:3], in1=bU[:, 0:128], op0=ALU.mult, op1=ALU.add)],
                              reads=[bUT, SxTd[cur], G], writes=[SxTd[nxt]])
                          cur = nxt
                          end = (c % 2 == 1) if d == 0 else (c % 2 == 0)
                          if end:
                              cx.dma("sp", Sfin_d[l, s, d, h], Sxd[cur], reads=[SxTd[cur]])
                          yield

                  for _ in itertools.zip_longest(rrec(0), rrec(1)):
                      pass

                  for d in range(2):
                      r = d * 4 + h
                      for step, c in enumerate(orders[d]):
                          cs = slice(c * 128, (c + 1) * 128)
                          p = d * 12 + step
                          bA, bAT = psum()
                          cx.run("pe", [mm(bA[:, 0:128], sT_all[:, p * 128:(p + 1) * 128], vv[:, c, :], True, True),
                                        mm(bA[:, 256:384], qT[:, cs], Sb_all[d][:, step, :], True, True)],
                                 reads=[sTa_T[p // 4], v_T, qk_T, SbA_T[d][step]], writes=[bAT])
                          bp = step % 2
                          cx.run("act", [lambda e, bA=bA, bp=bp, r=r: e.activation(out=Bs2[bp], in_=bA[:, 256:384], func=AF.Copy, scale=rcol[:, r, 0:1])],
                                 reads=[bAT, G], writes=[Bs2_T[bp]])
                          if d == 0:
                              cx.run("dve", [lambda e, bA=bA, bp=bp, c=c: e.tensor_tensor(out=ysum[:, c, :], in0=bA[:, 0:128], in1=Bs2[bp], op=ALU.add)],
                                     reads=[bAT, Bs2_T[bp]], writes=[hs_T[c]])
                          else:
                              cx.run("dve", [lambda e, bA=bA, bp=bp: e.tensor_tensor(out=Bs2[bp], in0=bA[:, 0:128], in1=Bs2[bp], op=ALU.add)],
                                     reads=[bAT, Bs2_T[bp]], writes=[Bs2_T[bp]])
                              cx.run("dve", [lambda e, bp=bp, c=c: e.tensor_tensor(out=ysum[:, c, :], in0=ysum[:, c, :], in1=Bs2[bp], op=ALU.add)],
                                     reads=[Bs2_T[bp], hs_T[c]], writes=[hs_T[c]])
                  if h + 1 < H:
                      rproj_qk(h + 1)
                  head_norm_out(ysum, hs_T, rnw, h, gsil, o_T, h_cT, hcT_T, t1, t_T, sT_all[:, 0:1536], sTa_T[0:3])
                  if l + 1 < L:
                      mod_groups(l + 1, [3 * h, 3 * h + 1, 3 * h + 2])
              dbg_out("hcT_l%d" % l, h_cT, [128, 4, T], [hcT_T])

              phase_end("ret%d" % l)
              aset(15360)
              macc = af32(4 * T).rearrange("p (a t) -> p a t", t=T)
              macc_T = [TT("macc%d" % i) for i in range(4)]
              sgm = [af32(512) for _ in range(2)]
              sgm_T = [TT("sgm0"), TT("sgm1")]
              k = 0
              branches = ((mow_d, h_aT, haT_T), (cow_d, ubT, ub_T), (row_d, h_cT, hcT_T))
              for jg in range(2):
                  for b, (wd, src, srcT) in enumerate(branches):
                      wo, woT = wload(wd[l][:, jg * 512:(jg + 1) * 512], 4, 512)
                      wgm, wgmT = wload(in_w_d[l][:, OFF_GM + b * 1024 + jg * 512:OFF_GM + b * 1024 + (jg + 1) * 512], 8, 512)
                      for jj in range(4):
                          j = jg * 4 + jj
                          for nb in range(3):
                              sl = slice(nb * 512, (nb + 1) * 512)
                              by, byT = psum()
                              cx.run("pe", [mm(by, wo[:, kc, jj * 128:(jj + 1) * 128], src[:, kc, sl], kc == 0, kc == 3) for kc in range(4)], reads=[woT, srcT], writes=[byT])
                              bg, bgT = psum()
                              cx.run("pe", [mm(bg, wgm[:, kc, jj * 128:(jj + 1) * 128], hT[:, kc, sl], kc == 0, kc == 7) for kc in range(8)], reads=[wgmT, hT_T], writes=[bgT])
                              pp = k % 2
                              k += 1
                              cx.run("act", [lambda e, bg=bg, pp=pp: e.activation(out=sgm[pp], in_=bg, func=AF.Sigmoid)], reads=[bgT], writes=[sgm_T[pp]])
                              if b == 0:
                                  cx.run("dve", [lambda e, by=by, pp=pp, sl=sl, jj=jj: e.tensor_tensor(out=macc[:, jj, sl], in0=by, in1=sgm[pp], op=ALU.mult)],
                                         reads=[byT, sgm_T[pp]], writes=[macc_T[jj]])
                              else:
                                  cx.run("dve", [lambda e, by=by, pp=pp: e.tensor_tensor(out=sgm[pp], in0=by, in1=sgm[pp], op=ALU.mult)],
                                         reads=[byT, sgm_T[pp]], writes=[sgm_T[pp]])
                                  if b == 1:
                                      cx.run("dve", [lambda e, pp=pp, sl=sl, jj=jj: e.tensor_tensor(out=macc[:, jj, sl], in0=macc[:, jj, sl], in1=sgm[pp], op=ALU.add)],
                                             reads=[sgm_T[pp], macc_T[jj]], writes=[macc_T[jj]])
                                  else:
                                      cx.run("dve", [lambda e, pp=pp, sl=sl, j=j, jj=jj: e.tensor_tensor(out=merged[:, j, sl], in0=macc[:, jj, sl], in1=sgm[pp], op=ALU.add)],
                                             reads=[sgm_T[pp], macc_T[jj]], writes=[mg_T])
              dbg_out("merged_l%d" % l, merged, [128, 8, T], [mg_T])
              phase_end("merge%d" % l)
              aset(15360)
              xb = abf(8 * 512).rearrange("p (a n) -> p a n", n=512)
              xq = abf(8 * 512).rearrange("p (a n) -> p a n", n=512)
              lnt = [(af32(512), af32(512), af32(512)) for _ in range(2)]
              cx.run("act", [lambda e, kc=kc: e.activation(out=x[:, kc, :], in_=x[:, kc, :], func=AF.Copy, scale=ALPHA) for kc in range(8)], reads=[x_T], writes=[x_T])
              for jg in range(2):
                  wo, woT = wload(outw_d[l][:, jg * 512:(jg + 1) * 512], 8, 512)
                  for jj in range(4):
                      j = jg * 4 + jj
                      for nb in range(3):
                          sl = slice(nb * 512, (nb + 1) * 512)
                          v = 1 if nb < 2 else 0
                          bk, bkT = psum()
                          cx.run("pe", [mm(bk, wo[:, kc, jj * 128:(jj + 1) * 128], merged[:, kc, sl], kc == 0, kc == 7) for kc in range(8)], reads=[woT, mg_T], writes=[bkT])
                          cx.run("dve", [lambda e, bk=bk, j=j, sl=sl, v=v: e.scalar_tensor_tensor(out=x[:, j, sl], in0=bk, scalar=modT[:, 16 + j, v:v + 1],
                                                                                                 in1=x[:, j, sl], op0=ALU.mult, op1=ALU.add)],
                                 reads=[bkT, x_T, M_T[l]], writes=[x_T])
              layer_norm_fm(0, [(0, 512), (512, 1024), (1024, 1536)], xb, xq, lnt)
              dbg_out("x1_l%d" % l, x, [128, 8, T], [x_T])
              phase_end("outp%d" % l)
              aset(0)
              ffT = abf(NFT * T).rearrange("p (a n) -> p a n", n=T)
              ff_T = TT("ffT")
              a_sb = abf(4 * T).rearrange("p (a n) -> p a n", n=T)
              asb_T = TT("a_sb")
              sgf = [af32(512) for _ in range(2)]
              sgf_T = [TT("sgf0"), TT("sgf1")]
              modulate(sc2p, 24)
              cx.run("act", [lambda e, kc=kc: e.activation(out=x[:, kc, :], in_=x[:, kc, :], func=AF.Copy, scale=ALPHA) for kc in range(8)], reads=[x_T], writes=[x_T])
              k = 0
              for g in range(6):
                  nt = 4 if g < 5 else 2
                  wa, waT = wload(w13_d[l][:, g * 512:g * 512 + nt * 128], 8, nt * 128)
                  wg2, wg2T = wload(w13_d[l][:, FF + g * 512:FF + g * 512 + nt * 128], 8, nt * 128)
                  for jj in range(nt):
                      for nb in range(3):
                          sl = slice(nb * 512, (nb + 1) * 512)
                          bk, bkT = psum()
                          cx.run("pe", [mm(bk, wa[:, kc, jj * 128:(jj + 1) * 128], hT[:, kc, sl], kc == 0, kc == 7) for kc in range(8)],
                                 reads=[waT, hT_T], writes=[bkT])
                          cx.run("act", [lambda e, bk=bk, jj=jj, sl=sl: e.activation(out=a_sb[:, jj, sl], in_=bk, func=AF.Copy)],
                                 reads=[bkT], writes=[asb_T])
                  for jj in range(nt):
                      for nb in range(3):
                          sl = slice(nb * 512, (nb + 1) * 512)
                          bk, bkT = psum()
                          cx.run("pe", [mm(bk, wg2[:, kc, jj * 128:(jj + 1) * 128], hT[:, kc, sl], kc == 0, kc == 7) for kc in range(8)],
                                 reads=[wg2T, hT_T], writes=[bkT])
                          pp = k % 2
                          k += 1
                          cx.run("act", [lambda e, bk=bk, pp=pp: e.activation(out=sgf[pp], in_=bk, func=AF.Silu)], reads=[bkT], writes=[sgf_T[pp]])
                          cx.run("dve", [lambda e, pp=pp, jj=jj, g=g, sl=sl: e.tensor_tensor(out=ffT[:, g * 4 + jj, sl], in0=sgf[pp], in1=a_sb[:, jj, sl], op=ALU.mult)],
                                 reads=[sgf_T[pp], asb_T], writes=[ff_T])
              for j in range(8):
                  w2v, w2T = wload(w2_d[l][:, j * 128:(j + 1) * 128], NFT, 128)
                  for nb in range(3):
                      sl = slice(nb * 512, (nb + 1) * 512)
                      v = 1 if nb < 2 else 0
                      bk, bkT = psum()
                      cx.run("pe", [mm(bk, w2v[:, kc, :], ffT[:, kc, sl], kc == 0, kc == NFT - 1) for kc in range(NFT)], reads=[w2T, ff_T], writes=[bkT])
                      cx.run("dve", [lambda e, bk=bk, j=j, sl=sl, v=v: e.scalar_tensor_tensor(
                          out=x[:, j, sl], in0=bk, scalar=modT[:, 40 + j, v:v + 1], in1=x[:, j, sl], op0=ALU.mult, op1=ALU.add)],
                          reads=[bkT, x_T, M_T[l]], writes=[x_T])
              aset(0)
              xb = abf(8 * 512).rearrange("p (a n) -> p a n", n=512)
              xq = abf(8 * 512).rearrange("p (a n) -> p a n", n=512)
              lnt = [(af32(512), af32(512), af32(512)) for _ in range(2)]
              layer_norm_fm(1, [(0, 512), (512, 1024), (1024, 1536)], xb, xq, lnt)
              dbg_out("x2_l%d" % l, x, [128, 8, T], [x_T])

        try:
            layers()
        except _Stop:
            pass
        for kc in range(8):
            cx.dma("sp", yT_d[:, kc, :], x[:, kc, :], reads=[x_T])
        cx.final()

        with nc.Block() as block:
            def replay(name):
                def f(e):
                    for th in cx.prog[name]:
                        th(e)
                return f
            block.tensor(replay("pe"))
            block.scalar(replay("act"))
            block.vector(replay("dve"))
            block.gpsimd(replay("pool"))
            block.sync(replay("sp"))
    return nc


_IDX = np.concatenate([np.arange(0, 128, 2), np.arange(1, 128, 2)])
_IDXS = np.concatenate([np.arange(1, 128, 2), np.arange(0, 128, 2)])


def _prow(r):
    return (r // 4) * 32 + (r % 4)


def _in_cols():
    o = dict(mq=0, mk=512, mv=1024, mo=1536, mg=2048, ca=2064, cg=2576, rq=3088, rk=3600, rv=4112, rg=4624, gm=5136)
    cols = []
    for h in range(H):
        for nm in ("mq", "mk", "mv", "mo"):
            cols += list(range(o[nm] + h * 128, o[nm] + (h + 1) * 128))
    z = [-1] * 28
    mg = o["mg"]
    cols += [mg + 0 * 4 + h for h in range(4)] + z + [mg + 2 * 4 + h for h in range(4)]
    cols += [mg + 1 * 4 + h for h in range(4)] + z + [mg + 3 * 4 + h for h in range(4)]
    cols += list(range(o["ca"], o["ca"] + 512)) + list(range(o["cg"], o["cg"] + 512))
    for h in range(H):
        for nm in ("rq", "rk"):
            cols += list(o[nm] + h * 128 + _IDX) + list(o[nm] + h * 128 + _IDXS)
    for h in range(H):
        cols += list(range(o["rv"] + h * 128, o["rv"] + (h + 1) * 128)) + list(range(o["rg"] + h * 128, o["rg"] + (h + 1) * 128))
    cols += list(range(o["gm"], o["gm"] + 3072))
    cols = np.array(cols, dtype=np.int64)
    assert cols.shape[0] == NCOLS, cols.shape
    return cols


_NC_CACHE = {}


def kernel(x_prompt, x_sample, state_mlstm_C, state_mlstm_n, state_mlstm_m, state_ret_S, c, c_ctx,
           ada_w, ada_b, in_w, mlstm_gate_b, mlstm_norm_w, mlstm_out_w, conv_w, conv_b, conv_ln_w, conv_ln_b,
           conv_out_w, ret_decay, ret_norm_w, ret_out_w, out_w, ln1_w, ln1_b, ln2_w, ln2_b, ffn_w13, ffn_w2, _dbg=False):
    f32 = np.float32
    A = lambda a: np.ascontiguousarray(np.asarray(a, dtype=f32))
    x_prompt, x_sample = A(x_prompt), A(x_sample)
    sC, sn, smm, sS = A(state_mlstm_C), A(state_mlstm_n), A(state_mlstm_m), A(state_ret_S)
    c, c_ctx = A(c), A(c_ctx)
    in_w = A(in_w)
    cols = _in_cols()
    in_wp = np.zeros((L, D, NCOLS), f32)
    valid = cols >= 0
    in_wp[:, :, valid] = in_w[:, :, cols[valid]]
    gbias = A(mlstm_gate_b)
    gb = np.zeros((L, 36, 2), f32)
    for h in range(4):
        gb[:, h, 0] = gbias[:, 0, h]; gb[:, h, 1] = gbias[:, 1, h]
        gb[:, 32 + h, 0] = gbias[:, 2, h]; gb[:, 32 + h, 1] = gbias[:, 3, h]
    convp = np.zeros((L, 128, 4, 34), f32)
    cw = A(conv_w)
    convp[:, :, :, 0:31] = cw.reshape(L, CONV_K, 4, 128).transpose(0, 3, 2, 1)
    convp[:, :, :, 31] = A(conv_b).reshape(L, 4, 128).transpose(0, 2, 1)
    convp[:, :, :, 32] = A(conv_ln_w).reshape(L, 4, 128).transpose(0, 2, 1)
    convp[:, :, :, 33] = A(conv_ln_b).reshape(L, 4, 128).transpose(0, 2, 1)
    lnp = np.zeros((L, 128, 4, 8), f32)
    for i, a in enumerate((ln1_w, ln1_b, ln2_w, ln2_b)):
        lnp[:, :, i, :] = A(a).reshape(L, 8, 128).transpose(0, 2, 1)
    ada_bT = np.ascontiguousarray(A(ada_b).reshape(L, 48, 128).transpose(0, 2, 1))
    rdec = A(ret_decay).reshape(L, 1, 8)
    mnw = A(mlstm_norm_w).reshape(L, 1, 512)
    rnw = A(ret_norm_w).reshape(L, 1, 512)
    cst = np.zeros((128, 1024), f32)
    jj, ii = np.meshgrid(np.arange(128), np.arange(128), indexing="ij")
    cst[:, 0:128] = np.eye(128, dtype=f32)
    cst[:, 128:256] = np.where(jj <= ii, 0.0, NEG)
    cst[:, 256:384] = np.where(jj >= ii, 0.0, NEG)
    cst[:, 384:512] = np.maximum(ii - jj, 0)
    cst[:, 512:640] = np.maximum(jj - ii, 0)
    cst[:, 640:768] = (jj <= ii)
    cst[:, 768:896] = (jj >= ii)
    p = np.arange(128)
    cst[:, 896] = p + 1; cst[:, 897] = 128 - p; cst[:, 898] = 127 - p; cst[:, 899] = p; cst[:, 900] = 128
    sel = np.zeros((36, 8, 128), f32)
    for r in range(8):
        sel[_prow(r), r, :] = 1.0
    t = np.arange(1024)
    rows = (t // 64).astype(f32); colsg = (t % 64).astype(f32)
    freqs = (np.float32(10000.0) ** (-np.arange(32, dtype=f32) / np.float32(32))).astype(f32)
    ang = np.concatenate([rows[:, None] * freqs[None, :], colsg[:, None] * freqs[None, :]], -1).astype(f32)
    cs_lat = np.concatenate([np.cos(ang).T, np.cos(ang).T], 0).astype(f32)
    sn_lat = np.concatenate([-np.sin(ang).T, np.sin(ang).T], 0).astype(f32)
    cs_id = np.ones((128, 1024), f32); sn_id = np.zeros((128, 1024), f32)

    shared = dict(cst=cst, sel=sel, ada_w=A(ada_w), ada_bT=ada_bT, in_wp=in_wp, gb=gb, mnw=mnw, rnw=rnw,
                  mlstm_out_w=A(mlstm_out_w), conv_out_w=A(conv_out_w), ret_out_w=A(ret_out_w), convp=convp, rdec=rdec,
                  out_w=A(out_w), lnp=lnp, ffn_w13=A(ffn_w13), ffn_w2=A(ffn_w2))
    in_maps = []
    seg_prompt = []
    for core in range(8):
        if core < 4:
            xs = np.concatenate([x_sample[core], x_prompt[2 * core], x_prompt[2 * core + 1]], 0)
            seg_prompt.append({4: 2 * core, 5: 2 * core + 1})
            cvec = c[core]
            Cinit = np.concatenate([sC[core], sn[core][..., None]], -1)
            minit = np.zeros((L, 36, 1), f32)
            for h in range(4):
                minit[:, h, 0] = smm[core, :, 0, h]; minit[:, 32 + h, 0] = smm[core, :, 1, h]
            Sinit = sS[core][:, :, :, _IDX, :]
            chainv = 1.0
            rcs, rsn = cs_lat, sn_lat
        else:
            base = 8 + 6 * (core - 4)
            xs = np.concatenate([x_prompt[base + s] for s in range(6)], 0)
            seg_prompt.append({s: base + s for s in range(6)})
            cvec = c_ctx
            Cinit = np.zeros((L, 2, H, 128, 129), f32)
            minit = np.zeros((L, 36, 1), f32)
            Sinit = np.zeros((L, 2, H, 128, 128), f32)
            chainv = 0.0
            rcs, rsn = cs_id, sn_id
        xT = np.ascontiguousarray(xs.reshape(T, 8, 128).transpose(2, 1, 0))
        cv = np.stack([c_ctx.reshape(8, 128).T, cvec.reshape(8, 128).T], -1)
        m = dict(shared)
        m.update(xT=xT, cv=np.ascontiguousarray(cv, dtype=f32), chain=np.full((128, 1), chainv, f32),
                 Cinit=np.ascontiguousarray(Cinit, dtype=f32), minit=minit, Sinit=np.ascontiguousarray(Sinit, dtype=f32),
                 ropeCS=rcs, ropeSN=rsn)
        in_maps.append(m)

    key = bool(_dbg)
    nc = build(dbg=key)
    res = run_bass_kernel_spmd(nc, in_maps, core_ids=list(range(8)))
    R = res.results
    y_prompt = np.zeros((32, 256, D), f32)
    y_sample = np.zeros((4, 1024, D), f32)
    new_C = np.zeros((32, L, 2, H, 128, 128), f32)
    new_n = np.zeros((32, L, 2, H, 128), f32)
    new_m = np.zeros((32, L, 2, H), f32)
    new_S = np.zeros((32, L, 2, H, 128, 128), f32)
    for core in range(8):
        r = R[core]
        yt = np.asarray(r["yT"]).transpose(2, 1, 0).reshape(T, D)
        if core < 4:
            y_sample[core] = yt[0:1024]
        Cf = np.asarray(r["Cfin"]); mf = np.asarray(r["mfin"]); Sf = np.asarray(r["Sfin"])
        for s, b in seg_prompt[core].items():
            y_prompt[b] = yt[s * 256:(s + 1) * 256]
            new_C[b] = Cf[:, s, :, :, :, 0:128]
            new_n[b] = Cf[:, s, :, :, :, 128]
            for h in range(4):
                new_m[b, :, 0, h] = mf[:, h, 2 * s + 1]
                new_m[b, :, 1, h] = mf[:, 32 + h, 2 * s]
            Su = np.empty((L, 2, H, 128, 128), f32)
            Su[:, :, :, _IDX, :] = Sf[:, s]
            new_S[b] = Su
    if _dbg:
        return (y_prompt, y_sample, new_C, new_n, new_m, new_S), R
    return (y_prompt, y_sample, new_C, new_n, new_m, new_S)
```
